# Optimizing a Trainium2 kernel written in Bass

```python
import math
import jax
import jax.numpy as jnp
from jax import lax
import numpy as np

D_MODEL = 1024
BATCH = 4
SEQ = 4096
DEPTH = 2

GRID_W = 64
CTX_LEN = 256
HEAD_DIM = 64
ATT_HEADS = 6
ATT_KV_HEADS = 2
ATT_W = ATT_HEADS * HEAD_DIM
ATT_KV_W = ATT_KV_HEADS * HEAD_DIM
ATT_WINDOW = 128
ATT_BLOCK = 128
ROPE_THETA = 10000.0
ROPE_FREQS = HEAD_DIM // 4
RWKV_HEADS = 4
RWKV_W = RWKV_HEADS * HEAD_DIM
RWKV_LORA = 64
RWKV_SHIFT_W = 3 * RWKV_W + 4 * RWKV_LORA
RWKV_GN_EPS = 64e-5
SSM_HEADS = 6
SSM_W = SSM_HEADS * HEAD_DIM
SSM_STATE = 128
SSM_GROUPS = 2
SSM_CONV = 3
SSM_CHUNK = 128
SSM_CONV_W = SSM_W + 2 * SSM_GROUPS * SSM_STATE
SSM_NORM_EPS = 1e-5
MIX_W = ATT_W + RWKV_W + SSM_W
IN_SIZES = (ATT_W, ATT_KV_W, ATT_KV_W, ATT_W, RWKV_SHIFT_W, RWKV_W, SSM_W, SSM_CONV_W, 2 * SSM_HEADS)
N_IN = sum(IN_SIZES)
NORM_EPS = 1e-6

kernel_name = 'hybrid_parallel_groups_diffusion_block'


def split_cols(h, sizes):
    return jnp.split(h, [int(i) for i in np.cumsum(sizes)[:-1]], axis=-1)


def to_heads(t):
    return t.reshape(t.shape[:-1] + (-1, HEAD_DIM))


def flip_time(ts):
    return tuple(jnp.flip(t, axis=1) for t in ts)


def rms_norm(x, w, eps=NORM_EPS):
    xf = x.astype(jnp.float32)
    y = xf * lax.rsqrt(jnp.mean(xf * xf, axis=-1, keepdims=True) + eps)
    return (y * w.astype(jnp.float32)).astype(x.dtype)


def axial_rope_tables(rows, dtype):
    row = jnp.repeat(jnp.arange(rows), GRID_W)
    col = jnp.tile(jnp.arange(GRID_W), rows)
    pos = jnp.stack([row, col], axis=-1).astype(jnp.float32)
    inv = ROPE_THETA ** (-jnp.arange(ROPE_FREQS, dtype=jnp.float32) / ROPE_FREQS)
    ang = pos[:, :, None] * inv
    return jnp.cos(ang).astype(dtype), jnp.sin(ang).astype(dtype)


def apply_axial_rope(u, cos, sin):
    shp = u.shape
    u = u.reshape(shp[:-1] + (2, 2, ROPE_FREQS))
    u0, u1 = u[..., 0, :], u[..., 1, :]
    cs, sn = cos[None, :, None], sin[None, :, None]
    return jnp.stack([u0 * cs - u1 * sn, u0 * sn + u1 * cs], axis=-2).reshape(shp)


def centred_shift(u, mu):
    prev = jnp.pad(u[:, :-1], ((0, 0), (1, 0), (0, 0)))
    nxt = jnp.pad(u[:, 1:], ((0, 0), (0, 1), (0, 0)))
    return u + mu[0] * (prev - u) + mu[1] * (nxt - u)


def centred_dwconv(u, w, b):
    k = w.shape[0]
    y = lax.conv_general_dilated(u, w[:, None, :], window_strides=(1,), padding=[(k // 2, k // 2)],
                                 dimension_numbers=('NWC', 'WIO', 'NWC'), feature_group_count=u.shape[-1])
    return y + b


def sink_softmax(logits, sink):
    sink_col = jnp.broadcast_to(sink.astype(jnp.float32).reshape(ATT_KV_HEADS, -1, 1, 1), logits.shape[:-1] + (1,))
    return jax.nn.softmax(jnp.concatenate([logits, sink_col], axis=-1), axis=-1)[..., :-1]


def windowed_attention(q, k, v, k_c, v_c, sink):
    bsz, t = q.shape[:2]
    nb = t // ATT_BLOCK
    span = 3 * ATT_BLOCK
    qb = q.reshape(bsz, nb, ATT_BLOCK, ATT_KV_HEADS, -1, HEAD_DIM)

    def band(u):
        up = jnp.pad(u, ((0, 0), (ATT_BLOCK, ATT_BLOCK), (0, 0), (0, 0)))
        up = up.reshape(bsz, nb + 2, ATT_BLOCK, ATT_KV_HEADS, HEAD_DIM)
        return jnp.concatenate([up[:, :-2], up[:, 1:-1], up[:, 2:]], axis=2)

    kb, vb = band(k), band(v)
    scale = HEAD_DIM ** -0.5
    s_loc = jnp.einsum('bnqhgd,bnshd->bnhgqs', qb, kb).astype(jnp.float32) * scale
    s_ctx = jnp.einsum('bnqhgd,bchd->bnhgqc', qb, k_c).astype(jnp.float32) * scale
    qi = jnp.arange(ATT_BLOCK)[:, None]
    kj = jnp.arange(span)[None, :]
    key_pos = jnp.arange(nb)[:, None, None] * ATT_BLOCK + kj - ATT_BLOCK
    valid = (jnp.abs(kj - ATT_BLOCK - qi) <= ATT_WINDOW) & (key_pos >= 0) & (key_pos < t)
    s_loc = jnp.where(valid[None, :, None, None], s_loc, -jnp.inf)
    p = sink_softmax(jnp.concatenate([s_loc, s_ctx], axis=-1), sink).astype(v.dtype)
    o = (jnp.einsum('bnhgqs,bnshd->bnqhgd', p[..., :span], vb)
         + jnp.einsum('bnhgqc,bchd->bnqhgd', p[..., span:], v_c))
    return o.reshape(bsz, t, ATT_W)


def context_attention(q_c, k_c, v_c, sink):
    bsz, lc = q_c.shape[:2]
    qg = q_c.reshape(bsz, lc, ATT_KV_HEADS, -1, HEAD_DIM)
    s = jnp.einsum('bqhgd,bshd->bhgqs', qg, k_c).astype(jnp.float32) * HEAD_DIM ** -0.5
    p = sink_softmax(s, sink).astype(v_c.dtype)
    return jnp.einsum('bhgqs,bshd->bqhgd', p, v_c).reshape(bsz, lc, ATT_W)


def attention_mixer(q, k, v, g, q_c, k_c, v_c, g_c, sink, cos, sin, ctx_out):
    bsz, t = q.shape[:2]
    q = apply_axial_rope(q.reshape(bsz, t, ATT_HEADS, HEAD_DIM), cos, sin)
    k = apply_axial_rope(k.reshape(bsz, t, ATT_KV_HEADS, HEAD_DIM), cos, sin)
    v = v.reshape(bsz, t, ATT_KV_HEADS, HEAD_DIM)
    k_c = k_c.reshape(bsz, -1, ATT_KV_HEADS, HEAD_DIM)
    v_c = v_c.reshape(bsz, -1, ATT_KV_HEADS, HEAD_DIM)
    o_lat = windowed_attention(q, k, v, k_c, v_c, sink) * jax.nn.silu(g)
    o_ctx = None
    if ctx_out:
        o_ctx = context_attention(q_c.reshape(bsz, -1, ATT_HEADS, HEAD_DIM), k_c, v_c, sink) * jax.nn.silu(g_c)
    return o_lat, o_ctx


def rwkv_scan_inputs(r, k, v, wd, ad, w0, w_up, a0, a_up, k_k, k_a):
    w_log = -jax.nn.softplus(-(w0 + jnp.tanh(wd) @ w_up)) - 0.5
    decay = jnp.exp(-jnp.exp(w_log))
    a = jax.nn.sigmoid(a0 + ad @ a_up)
    kk = to_heads(k * k_k)
    kk = kk / jnp.maximum(jnp.sqrt(jnp.sum(kk * kk, axis=-1, keepdims=True)), 1e-12)
    kd = k * (1.0 + (a - 1.0) * k_a)
    return to_heads(r), to_heads(decay), to_heads(kd), to_heads(v), kk, to_heads(a)


def rwkv7_scan(r, w, k, v, kk, a, s0):
    def step(s, inp):
        rt, wt, kt, vt, kkt, at = inp
        sa = jnp.einsum('bhvk,bhk->bhv', s, kkt)
        s = s * wt[:, :, None, :] - sa[..., None] * (kkt * at)[:, :, None, :] + vt[..., None] * kt[:, :, None, :]
        return s, jnp.einsum('bhvk,bhk->bhv', s, rt)

    xs = tuple(jnp.moveaxis(t, 1, 0) for t in (r, w, k, v, kk, a))
    s_final, y = lax.scan(step, s0, xs)
    return jnp.moveaxis(y, 0, 1), s_final


def rwkv_mixer(u_lat, g_lat, u_ctx, g_ctx, mu, w0, w_up, a0, a_up, k_k, k_a, r_k, ln_w, ln_b, ctx_out):
    streams = []
    for u in (u_ctx, u_lat):
        u = centred_shift(u, mu).astype(jnp.float32)
        streams.append(split_cols(u, (RWKV_W, RWKV_W, RWKV_W, 2 * RWKV_LORA, 2 * RWKV_LORA)))
    bsz = u_lat.shape[0]
    y_ctx, y_lat = 0.0, 0.0
    for d in range(2):
        lo = slice(d * RWKV_LORA, (d + 1) * RWKV_LORA)
        ins = [rwkv_scan_inputs(r, k, v, wd[..., lo], ad[..., lo], w0[d], w_up[d], a0[d], a_up[d], k_k[d], k_a[d])
               for r, k, v, wd, ad in streams]
        if d == 1:
            ins = [flip_time(s) for s in ins]
        s0 = jnp.zeros((bsz, RWKV_HEADS, HEAD_DIM, HEAD_DIM), jnp.float32)
        yc, s_ctx = rwkv7_scan(*ins[0], s0)
        yl, _ = rwkv7_scan(*ins[1], s_ctx)
        if d == 1:
            yc, yl = flip_time((yc, yl))
        y_ctx, y_lat = y_ctx + yc, y_lat + yl

    def finish(y, stream, g):
        r, k, v = (to_heads(t) for t in stream[:3])
        mean = jnp.mean(y, axis=-1, keepdims=True)
        var = jnp.mean(jnp.square(y - mean), axis=-1, keepdims=True)
        y = (y - mean) * lax.rsqrt(var + RWKV_GN_EPS)
        bonus = jnp.sum(r * k * r_k, axis=-1, keepdims=True) * v
        out = y.reshape(y.shape[:2] + (RWKV_W,)) * ln_w + ln_b + bonus.reshape(y.shape[:2] + (RWKV_W,))
        return out.astype(g.dtype) * jax.nn.silu(g)

    o_lat = finish(y_lat, streams[1], g_lat)
    o_ctx = finish(y_ctx, streams[0], g_ctx) if ctx_out else None
    return o_lat, o_ctx


def ssd_scan(x, dt, a_neg, b_mat, c_mat, s0):
    bsz, t, h, p = x.shape
    n = b_mat.shape[-1]
    nc, cl = t // SSM_CHUNK, SSM_CHUNK
    rep = h // b_mat.shape[2]
    bh = jnp.repeat(b_mat, rep, axis=2).reshape(bsz, nc, cl, h, n)
    ch = jnp.repeat(c_mat, rep, axis=2).reshape(bsz, nc, cl, h, n)
    xdt = (x * dt[..., None]).reshape(bsz, nc, cl, h, p)
    cum = jnp.cumsum((dt * a_neg).reshape(bsz, nc, cl, h), axis=2)
    idx = jnp.arange(cl)
    lower = (idx[:, None] >= idx[None, :])[None, None, :, :, None]
    decay = jnp.exp(jnp.where(lower, cum[:, :, :, None, :] - cum[:, :, None, :, :], -jnp.inf))
    scores = jnp.einsum('bcihn,bcjhn->bcijh', ch, bh) * decay
    y_diag = jnp.einsum('bcijh,bcjhp->bcihp', scores, xdt)
    tail = jnp.exp(cum[:, :, -1:, :] - cum)
    chunk_states = jnp.einsum('bcjhn,bcjh,bcjhp->bchpn', bh, tail, xdt)
    chunk_decay = jnp.exp(cum[:, :, -1, :])

    def step(s, inp):
        st, dec = inp
        return s * dec[:, :, None, None] + st, s

    s_final, prev = lax.scan(step, s0, (jnp.moveaxis(chunk_states, 1, 0), jnp.moveaxis(chunk_decay, 1, 0)))
    prev = jnp.moveaxis(prev, 0, 1)
    y_off = jnp.einsum('bcihn,bchpn->bcihp', ch, prev) * jnp.exp(cum)[..., None]
    return (y_diag + y_off).reshape(bsz, t, h, p), s_final


def ssd_mixer(z_lat, xbc_lat, dt_lat, z_ctx, xbc_ctx, dt_ctx, conv_w, conv_b, a_log, dt_bias, d_skip, norm_w,
              ctx_out):
    streams = []
    for xbc, dt in ((xbc_ctx, dt_ctx), (xbc_lat, dt_lat)):
        xbc = jax.nn.silu(centred_dwconv(xbc, conv_w, conv_b)).astype(jnp.float32)
        xs, bm, cm = split_cols(xbc, (SSM_W, SSM_GROUPS * SSM_STATE, SSM_GROUPS * SSM_STATE))
        bsz, t = xs.shape[:2]
        dtp = jax.nn.softplus(dt.astype(jnp.float32).reshape(bsz, t, 2, SSM_HEADS) + dt_bias)
        streams.append((to_heads(xs), bm.reshape(bsz, t, SSM_GROUPS, SSM_STATE),
                        cm.reshape(bsz, t, SSM_GROUPS, SSM_STATE), dtp))
    bsz = z_lat.shape[0]
    y_ctx, y_lat = 0.0, 0.0
    for d in range(2):
        a_neg = -jnp.exp(a_log[d].astype(jnp.float32))
        ins = [(xs, dtp[:, :, d], bm, cm) for xs, bm, cm, dtp in streams]
        if d == 1:
            ins = [flip_time(s) for s in ins]
        s0 = jnp.zeros((bsz, SSM_HEADS, HEAD_DIM, SSM_STATE), jnp.float32)
        yc, s_ctx = ssd_scan(ins[0][0], ins[0][1], a_neg, ins[0][2], ins[0][3], s0)
        yl, _ = ssd_scan(ins[1][0], ins[1][1], a_neg, ins[1][2], ins[1][3], s_ctx)
        if d == 1:
            yc, yl = flip_time((yc, yl))
        y_ctx, y_lat = y_ctx + yc, y_lat + yl

    def finish(y, xs, z):
        b_, t_ = y.shape[:2]
        y = (y + d_skip[:, None] * xs).reshape(b_, t_, SSM_W) * jax.nn.silu(z.astype(jnp.float32))
        yg = y.reshape(b_, t_, SSM_GROUPS, SSM_W // SSM_GROUPS)
        yg = yg * lax.rsqrt(jnp.mean(yg * yg, axis=-1, keepdims=True) + SSM_NORM_EPS)
        return (yg.reshape(b_, t_, SSM_W) * norm_w).astype(z.dtype)

    o_lat = finish(y_lat, streams[1][0], z_lat)
    o_ctx = finish(y_ctx, streams[0][0], z_ctx) if ctx_out else None
    return o_lat, o_ctx


def trunk_layer(x, ctx, c_act, cctx_act, cos, sin, ada_w, ada_b, norm_w, w_in, w_out, attn_sink,
                rwkv_mu, rwkv_w0, rwkv_w_up, rwkv_a0, rwkv_a_up, rwkv_k_k, rwkv_k_a, rwkv_r_k, rwkv_ln_w,
                rwkv_ln_b, ssm_conv_w, ssm_conv_b, ssm_a_log, ssm_dt_bias, ssm_d, ssm_norm_w, ctx_out):
    shift, scale, gate = jnp.split((c_act @ ada_w + ada_b)[:, None, :], 3, axis=-1)
    shift_c, scale_c, gate_c = jnp.split(cctx_act @ ada_w + ada_b, 3, axis=-1)
    h_lat = rms_norm(x, norm_w) * (1.0 + scale) + shift
    h_ctx = rms_norm(ctx, norm_w) * (1.0 + scale_c) + shift_c
    lat = split_cols(h_lat @ w_in, IN_SIZES)
    cx = split_cols(h_ctx @ w_in, IN_SIZES)
    att_lat, att_ctx = attention_mixer(lat[0], lat[1], lat[2], lat[3], cx[0], cx[1], cx[2], cx[3],
                                       attn_sink, cos, sin, ctx_out)
    rwkv_lat, rwkv_ctx = rwkv_mixer(lat[4], lat[5], cx[4], cx[5], rwkv_mu, rwkv_w0, rwkv_w_up, rwkv_a0,
                                    rwkv_a_up, rwkv_k_k, rwkv_k_a, rwkv_r_k, rwkv_ln_w, rwkv_ln_b, ctx_out)
    ssm_lat, ssm_ctx = ssd_mixer(lat[6], lat[7], lat[8], cx[6], cx[7], cx[8], ssm_conv_w, ssm_conv_b,
                                 ssm_a_log, ssm_dt_bias, ssm_d, ssm_norm_w, ctx_out)
    x = x + gate * (jnp.concatenate([att_lat, rwkv_lat, ssm_lat], axis=-1) @ w_out)
    if ctx_out:
        ctx = ctx + gate_c * (jnp.concatenate([att_ctx, rwkv_ctx, ssm_ctx], axis=-1) @ w_out)
    return x, ctx


def setup_inputs(seed: int = 0) -> dict:
    key = jax.random.key(seed)
    ks = iter(jax.random.split(key, 32))
    f32 = jnp.float32

    def nrm(shape, s):
        return jax.random.normal(next(ks), shape, f32) * s

    def uni(shape, lo, hi):
        return jax.random.uniform(next(ks), shape, f32, lo, hi)

    L = DEPTH
    x = nrm((BATCH, SEQ, D_MODEL), 1.0)
    c = nrm((BATCH, D_MODEL), 1.0)
    ctx = nrm((BATCH, CTX_LEN, D_MODEL), 1.0)
    c_ctx = nrm((D_MODEL,), 1.0)
    ada_w = nrm((L, D_MODEL, 3 * D_MODEL), 0.5 * D_MODEL ** -0.5)
    ada_b = nrm((L, 3 * D_MODEL), 0.02)
    norm_w = 1.0 + nrm((L, D_MODEL), 0.05)
    w_in = nrm((L, D_MODEL, N_IN), D_MODEL ** -0.5)
    w_out = nrm((L, MIX_W, D_MODEL), MIX_W ** -0.5)
    attn_sink = nrm((L, ATT_HEADS), 0.5)
    rwkv_mu = uni((L, 2, RWKV_SHIFT_W), 0.0, 0.5)
    rwkv_w0 = uni((L, 2, RWKV_W), -6.5, -1.5)
    rwkv_w_up = nrm((L, 2, RWKV_LORA, RWKV_W), 0.5 * RWKV_LORA ** -0.5)
    rwkv_a0 = nrm((L, 2, RWKV_W), 0.1)
    rwkv_a_up = nrm((L, 2, RWKV_LORA, RWKV_W), 0.5 * RWKV_LORA ** -0.5)
    rwkv_k_k = 0.85 + nrm((L, 2, RWKV_W), 0.05)
    rwkv_k_a = 1.0 + nrm((L, 2, RWKV_W), 0.05)
    rwkv_r_k = nrm((L, RWKV_HEADS, HEAD_DIM), 0.1)
    rwkv_ln_w = 1.0 + nrm((L, RWKV_W), 0.05)
    rwkv_ln_b = nrm((L, RWKV_W), 0.02)
    ssm_conv_w = nrm((L, SSM_CONV, SSM_CONV_W), SSM_CONV ** -0.5)
    ssm_conv_b = nrm((L, SSM_CONV_W), 0.02)
    ssm_a_log = jnp.log(uni((L, 2, SSM_HEADS), 1.0, 16.0))
    dt0 = jnp.exp(uni((L, 2, SSM_HEADS), math.log(1e-3), math.log(1e-1)))
    ssm_dt_bias = dt0 + jnp.log(-jnp.expm1(-dt0))
    ssm_d = 1.0 + nrm((L, SSM_HEADS), 0.1)
    ssm_norm_w = 1.0 + nrm((L, SSM_W), 0.05)
    final_norm_w = 1.0 + nrm((D_MODEL,), 0.05)
    return {'x': x, 'c': c, 'ctx': ctx, 'c_ctx': c_ctx, 'ada_w': ada_w, 'ada_b': ada_b, 'norm_w': norm_w,
            'w_in': w_in, 'w_out': w_out, 'attn_sink': attn_sink, 'rwkv_mu': rwkv_mu, 'rwkv_w0': rwkv_w0,
            'rwkv_w_up': rwkv_w_up, 'rwkv_a0': rwkv_a0, 'rwkv_a_up': rwkv_a_up, 'rwkv_k_k': rwkv_k_k,
            'rwkv_k_a': rwkv_k_a, 'rwkv_r_k': rwkv_r_k, 'rwkv_ln_w': rwkv_ln_w, 'rwkv_ln_b': rwkv_ln_b,
            'ssm_conv_w': ssm_conv_w, 'ssm_conv_b': ssm_conv_b, 'ssm_a_log': ssm_a_log,
            'ssm_dt_bias': ssm_dt_bias, 'ssm_d': ssm_d, 'ssm_norm_w': ssm_norm_w, 'final_norm_w': final_norm_w}


def reference(x, c, ctx, c_ctx, ada_w, ada_b, norm_w, w_in, w_out, attn_sink, rwkv_mu, rwkv_w0, rwkv_w_up,
              rwkv_a0, rwkv_a_up, rwkv_k_k, rwkv_k_a, rwkv_r_k, rwkv_ln_w, rwkv_ln_b, ssm_conv_w, ssm_conv_b,
              ssm_a_log, ssm_dt_bias, ssm_d, ssm_norm_w, final_norm_w):
    rows = x.shape[1] // GRID_W
    cos, sin = axial_rope_tables(rows, x.dtype)
    c_act = jax.nn.silu(c)
    cctx_act = jax.nn.silu(c_ctx)
    for i in range(DEPTH):
        x, ctx = trunk_layer(x, ctx, c_act, cctx_act, cos, sin, ada_w[i], ada_b[i], norm_w[i], w_in[i], w_out[i],
                             attn_sink[i], rwkv_mu[i], rwkv_w0[i], rwkv_w_up[i], rwkv_a0[i], rwkv_a_up[i],
                             rwkv_k_k[i], rwkv_k_a[i], rwkv_r_k[i], rwkv_ln_w[i], rwkv_ln_b[i], ssm_conv_w[i],
                             ssm_conv_b[i], ssm_a_log[i], ssm_dt_bias[i], ssm_d[i], ssm_norm_w[i],
                             ctx_out=(i < DEPTH - 1))
    return rms_norm(x, final_norm_w)
```

```python
import numpy as np
import concourse.bass as bass
import concourse.mybir as mybir

F32 = mybir.dt.float32
BF16 = mybir.dt.bfloat16
ALU = mybir.AluOpType
AF = mybir.ActivationFunctionType
AX = mybir.AxisListType

import os
DUMP = os.environ.get("MK_DUMP", "")
ENGS = ("pe", "dve", "act", "pool", "sp")
N_DMA_SEMS = 20


class Trk:
    __slots__ = ("lw", "rd", "ps")

    def __init__(self, ps=False):
        self.lw = None
        self.rd = []
        self.ps = ps


class V:
    __slots__ = ("ap", "trk")

    def __init__(self, ap, trk):
        self.ap = ap
        self.trk = trk

    def __getitem__(self, idx):
        return V(self.ap[idx], self.trk)

    def m(self, fn):
        return V(fn(self.ap), self.trk)

    def re(self, pat, **kw):
        return V(self.ap.rearrange(pat, **kw), self.trk)

    def bc(self, shape):
        return V(self.ap.broadcast_to(shape), self.trk)

    @property
    def shape(self):
        return self.ap.shape


class Tile:
    def __init__(self, handle):
        self.h = handle
        self.trk = Trk()

    def __getitem__(self, idx):
        return V(self.h[idx], (self.trk,))

    def ap(self):
        return V(self.h.ap() if hasattr(self.h, "ap") else self.h[:], (self.trk,))


class DTile:
    def __init__(self, handle):
        self.h = handle
        self.reg = {}

    def v(self, key, idx=None, fn=None):
        t = self.reg.setdefault(key, Trk())
        ap = self.h.ap()
        if fn is not None:
            ap = fn(ap)
        if idx is not None:
            ap = ap[idx]
        return V(ap, (t,))


class Op:
    __slots__ = ("eng", "fn", "reads", "writes", "idx", "deps", "signal", "sig_count",
                 "is_dma", "dma_sem", "dma_val", "dma_prev", "kind")


class Prog:
    def __init__(self, nc):
        self.nc = nc
        self.ops = {e: [] for e in ENGS}
        self.n_t = 0
        self.all_ops = []

    ARENA = 207 * 1024

    def sb(self, shape, dtype, name=None):
        self.n_t += 1
        if not hasattr(self, "_abase"):
            a = self.nc.alloc_sbuf_tensor("arena", [128, self.ARENA], mybir.dt.uint8)
            self._abase = self.nc.lookup_mloc(a).addr
            self._aoff = 0
        esz = 2 if dtype == BF16 else 4
        n = esz
        for d in shape[1:]:
            n *= d
        off = (self._aoff + 63) // 64 * 64
        assert off + n <= self.ARENA, f"SBUF arena overflow allocating {name} {shape}: {off + n}"
        self._aoff = off + n
        return Tile(self.nc.alloc_sbuf_tensor_at(f"{name or 'sb'}_{self.n_t}", list(shape), dtype, offset=self._abase + off))

    def mark(self):
        return getattr(self, "_aoff", 0)

    def release(self, mark):
        self.barrier()
        self._aoff = mark

    def barrier(self):
        lasts = []
        for e in ENGS:
            for op in reversed(self.ops[e]):
                if not op.is_dma and op.kind != "bar":
                    lasts.append(op)
                    break
        dmas = [op for op in self.all_ops[getattr(self, "_bar_pos", 0):] if op.is_dma]
        self._bar_pos = len(self.all_ops)
        for e in ENGS:
            op = self.rec(e, lambda eh: eh.nop(nofuse=True), [], [], kind="bar")
            op.deps = [d for d in lasts if d.eng != e] + dmas
            for d in op.deps:
                d.signal = True

    def ps(self, shape, dtype=F32, name=None):
        self.n_t += 1
        return Tile(self.nc.alloc_psum_tensor(name or f"ps{self.n_t}", list(shape), dtype))

    def dram(self, name, shape, dtype, kind="Internal"):
        return DTile(self.nc.dram_tensor(name, list(shape), dtype, kind=kind))

    def rec(self, eng, fn, reads, writes, is_dma=False, kind=""):
        op = Op()
        op.eng = eng
        op.fn = fn
        op.kind = kind
        op.is_dma = is_dma
        op.idx = len(self.ops[eng])
        op.signal = False
        op.deps = []
        rt = []
        for v in reads:
            if v is None or not isinstance(v, V):
                continue
            rt.extend(v.trk)
        wt = []
        for v in writes:
            if v is None or not isinstance(v, V):
                continue
            wt.extend(v.trk)
        deps = set()
        for t in rt:
            if t.lw is not None:
                deps.add(t.lw)
            if t.ps:
                for r in t.rd:
                    if r.eng != eng:
                        deps.add(r)
        for t in wt:
            if t.lw is not None:
                deps.add(t.lw)
            for r in t.rd:
                deps.add(r)
        deps.discard(op)
        for t in rt:
            t.rd.append(op)
        for t in wt:
            t.lw = op
            t.rd = []
        final = []
        for d in deps:
            if d.eng == "pe" and eng == "pe" and not d.is_dma and not is_dma:
                continue
            final.append(d)
        op.deps = final
        for d in final:
            d.signal = True
        self.ops[eng].append(op)
        self.all_ops.append(op)
        return op

    def mm(self, out, lhsT, rhs, start=True, stop=True):
        return self.rec("pe", lambda e: e.matmul(out.ap, lhsT.ap, rhs.ap, start=start, stop=stop),
                        [lhsT, rhs], [out], kind="mm")

    def tr(self, out, in_, ident):
        return self.rec("pe", lambda e: e.transpose(out.ap, in_.ap, ident.ap), [in_, ident], [out], kind="tr")

    def act(self, out, in_, func, bias=None, scale=None, accum_out=None, eng="act"):
        kw = {}
        rd = [in_]
        if bias is not None:
            kw["bias"] = bias.ap if isinstance(bias, V) else bias
            rd.append(bias)
        if scale is not None:
            kw["scale"] = scale.ap if isinstance(scale, V) else scale
            rd.append(scale)
        wr = [out]
        if accum_out is not None:
            kw["accum_out"] = accum_out.ap
            wr.append(accum_out)
        return self.rec("act", lambda e: e.activation(out.ap, in_.ap, func, **kw), rd, wr, kind="act")

    def tt(self, eng, out, in0, in1, op):
        return self.rec(eng, lambda e: e.tensor_tensor(out.ap, in0.ap, in1.ap, op), [in0, in1], [out], kind="tt")

    def ts(self, eng, out, in0, s1, s2, op0, op1=None, accum_out=None):
        rd = [in0, s1, s2]
        a1 = s1.ap if isinstance(s1, V) else s1
        a2 = s2.ap if isinstance(s2, V) else s2
        kw = {}
        wr = [out]
        if accum_out is not None:
            kw["accum_out"] = accum_out.ap
            wr.append(accum_out)
        if op1 is None:
            return self.rec(eng, lambda e: e.tensor_scalar(out.ap, in0.ap, a1, None, op0, **kw), rd, wr, kind="ts")
        return self.rec(eng, lambda e: e.tensor_scalar(out.ap, in0.ap, a1, a2, op0, op1, **kw), rd, wr, kind="ts")

    def stt(self, eng, out, in0, scalar, in1, op0, op1):
        a = scalar.ap if isinstance(scalar, V) else scalar
        return self.rec(eng, lambda e: e.scalar_tensor_tensor(out.ap, in0.ap, a, in1.ap, op0, op1),
                        [in0, scalar, in1], [out], kind="stt")

    def copy(self, eng, out, in_):
        if eng == "act":
            return self.rec("act", lambda e: e.copy(out.ap, in_.ap), [in_], [out], kind="copy")
        return self.rec(eng, lambda e: e.tensor_copy(out.ap, in_.ap), [in_], [out], kind="copy")

    def memset(self, eng, out, val):
        return self.rec(eng, lambda e: e.memset(out.ap, val), [], [out], kind="memset")

    def reduce(self, eng, out, in_, op, axis=AX.X):
        return self.rec(eng, lambda e: e.tensor_reduce(out.ap, in_.ap, axis, op), [in_], [out], kind="red")

    def recip(self, out, in_):
        return self.rec("dve", lambda e: e.reciprocal(out.ap, in_.ap), [in_], [out], kind="recip")

    def scan(self, eng, out, d0, d1, initial, op0, op1):
        ini = initial.ap if isinstance(initial, V) else initial
        return self.rec(eng, lambda e: e.tensor_tensor_scan(out.ap, d0.ap, d1.ap, ini, op0, op1),
                        [d0, d1, initial], [out], kind="scan")

    def dma(self, q, out, in_, **kw):
        return self.rec(q, lambda e: e.dma_start(out.ap, in_.ap, **kw), [in_], [out], is_dma=True, kind="dma")

    def emit(self):
        nc = self.nc
        eng_sem = {e: nc.alloc_semaphore(f"s_{e}") for e in ENGS}
        dma_sems = {e: ([nc.alloc_semaphore(f"d_{e}_{i}") for i in range(N_DMA_SEMS)]
                        if any(o.is_dma for o in self.ops[e]) else []) for e in ENGS}
        for e in ENGS:
            cnt = 0
            nd = 0
            last_on_sem = {}
            for op in self.ops[e]:
                if op.is_dma:
                    j = nd % N_DMA_SEMS
                    nd += 1
                    op.dma_sem = dma_sems[e][j]
                    k = nd_k = (nd - 1) // N_DMA_SEMS + 1
                    op.dma_val = 16 * k
                    op.dma_prev = last_on_sem.get(j)
                    last_on_sem[j] = op
                else:
                    if op.signal:
                        cnt += 1
                        op.sig_count = cnt
        self._eng_sem = eng_sem
        handles = {"pe": "tensor", "dve": "vector", "act": "scalar", "pool": "gpsimd", "sp": "sync"}

        def run_engine(ename, eh):
            seen = {}

            def wait(sem, val):
                key = id(sem)
                if seen.get(key, 0) >= val:
                    return
                seen[key] = val
                eh.wait_ge(sem, val)

            for op in self.ops[ename]:
                for d in op.deps:
                    if d.is_dma:
                        wait(d.dma_sem, d.dma_val)
                    else:
                        wait(eng_sem[d.eng], d.sig_count)
                if op.is_dma:
                    if op.dma_prev is not None:
                        wait(op.dma_prev.dma_sem, op.dma_prev.dma_val)
                    ins = op.fn(eh)
                    ins.then_inc(op.dma_sem, 16)
                else:
                    ins = op.fn(eh)
                    if op.signal:
                        ins.then_inc(eng_sem[ename], 1)
                    if DUMP and ename in DUMP:
                        print(ename, op.idx, op.kind, ins.concise(), flush=True)
            last = {}
            for op in self.ops[ename]:
                if op.is_dma:
                    last[id(op.dma_sem)] = op
            for op in last.values():
                wait(op.dma_sem, op.dma_val)

        with nc.Block() as block:
            @block.tensor
            def _(eh):
                run_engine("pe", eh)

            @block.vector
            def _(eh):
                run_engine("dve", eh)

            @block.scalar
            def _(eh):
                run_engine("act", eh)

            @block.gpsimd
            def _(eh):
                run_engine("pool", eh)

            @block.sync
            def _(eh):
                run_engine("sp", eh)

    def stats(self):
        return {e: len(self.ops[e]) for e in ENGS}

from concourse.bass_utils import run_bass_kernel_spmd

S = 4352
CTX = 256
TLAT = 4096
DM = 1024
NIN = 3596
NBLK = 34
DEPTH = 2
TILES = [(0, 256)] + [(256 + 512 * i, 512) for i in range(8)]
NEG = -30000.0

C_Q, C_K, C_V, C_G = 0, 384, 512, 640
C_RW = 1024
C_RG = 2048
C_Z = 2304
C_XBC = 2688
C_DT = 3584


def _slots(spec):
    d, o = {}, 0
    for n, w in spec:
        d[n] = (o, w)
        o += w
    return d, o


PP, NPP = _slots([("adab", 24), ("normw", 8), ("sink", 6), ("mu0", 8), ("mu1", 8), ("w0", 4), ("a0", 4),
                  ("kk", 4), ("ka", 4), ("rk", 2), ("lnw", 2), ("lnb", 2), ("convw", 21), ("convb", 7),
                  ("ssmd", 3), ("ssmnw", 3)])
PR, NPR = _slots([("adabg", 1024), ("dtb", 12), ("alog", 12), ("fnw", 1024), ("convr", 3 * 896), ("shr", 3 * 1024)])
CS, NCS = _slots([("ident", 128), ("mband", 3 * 384), ("perm", 128), ("cos", 4096), ("sin", 4096),
                  ("blk", 128), ("rmf", 256), ("rmb", 256), ("snf", 128), ("snb", 128), ("trif", 128),
                  ("trib", 128), ("ones", 128), ("hlo", 128), ("hhi", 128), ("lowf", 128), ("lowb", 128)])


def fm(v):
    v = np.asarray(v, np.float32)
    return np.ascontiguousarray(v.reshape(-1, 128).T)


def rep(v):
    v = np.asarray(v, np.float32).reshape(1, -1)
    return np.ascontiguousarray(np.broadcast_to(v, (128, v.shape[1])))


def make_consts():
    c = np.zeros((128, NCS), np.float32)

    def put(n, a):
        o, w = CS[n]
        c[:, o:o + w] = a
    put("ident", np.eye(128, dtype=np.float32))
    qi = np.arange(128)[:, None]
    kj = np.arange(384)[None, :]
    band = np.abs(kj - 128 - qi) <= 128
    mb = []
    for var in range(3):
        ok = band.copy()
        if var == 0:
            ok &= kj >= 128
        if var == 2:
            ok &= kj < 256
        mb.append(np.where(ok, 0.0, NEG))
    put("mband", np.concatenate(mb, 1))
    perm = np.zeros((128, 128), np.float32)
    for m in range(128):
        d = m % 64
        half = (d % 32) // 16
        partner = m + 16 if half == 0 else m - 16
        perm[partner, m] = 1.0
    put("perm", perm)
    rows = TLAT // 64
    row = np.repeat(np.arange(rows), 64).astype(np.float32)
    col = np.tile(np.arange(64), rows).astype(np.float32)
    pos = np.stack([row, col], -1)
    inv = (np.float32(10000.0) ** (-np.arange(16, dtype=np.float32) / np.float32(16))).astype(np.float32)
    ang = (pos[:, :, None] * inv).astype(np.float32)
    cosv, sinv = np.cos(ang).astype(np.float32), np.sin(ang).astype(np.float32)
    ct = np.zeros((128, TLAT), np.float32)
    st = np.zeros((128, TLAT), np.float32)
    for m in range(128):
        d = m % 64
        ax, half, f = d // 32, (d % 32) // 16, d % 16
        ct[m] = cosv[:, ax, f]
        st[m] = -sinv[:, ax, f] if half == 0 else sinv[:, ax, f]
    put("cos", ct)
    put("sin", st)
    blk = np.zeros((128, 128), np.float32)
    blk[:64, :64] = 1
    blk[64:, 64:] = 1
    put("blk", blk)
    s = np.arange(128)[:, None]
    t = np.arange(128)[None, :]
    same = (s // 64) == (t // 64)
    put("rmf", np.concatenate([(same & (s < t)), (same & (s <= t))], 1).astype(np.float32))
    put("rmb", np.concatenate([(same & (s > t)), (same & (s >= t))], 1).astype(np.float32))
    put("lowf", (same & (t < s)).astype(np.float32))
    put("lowb", (same & (t > s)).astype(np.float32))
    put("snf", np.where(s <= t, 0.0, NEG))
    put("snb", np.where(s >= t, 0.0, NEG))
    put("trif", (s <= t).astype(np.float32))
    put("trib", (s >= t).astype(np.float32))
    put("ones", np.ones((128, 128), np.float32))
    hlo = np.zeros((128, 128), np.float32)
    hlo[:64] = 1
    put("hlo", hlo)
    put("hhi", 1 - hlo)
    return c


def make_pp(inp, l):
    p = np.zeros((128, NPP), np.float32)

    def put(n, a):
        o, w = PP[n]
        assert a.shape == (128, w), (n, a.shape)
        p[:, o:o + w] = a
    put("adab", fm(inp["ada_b"][l]))
    put("normw", fm(inp["norm_w"][l]))
    put("sink", rep(inp["attn_sink"][l]))
    put("mu0", fm(inp["rwkv_mu"][l, 0]))
    put("mu1", fm(inp["rwkv_mu"][l, 1]))
    for n, k in (("w0", "rwkv_w0"), ("a0", "rwkv_a0"), ("kk", "rwkv_k_k"), ("ka", "rwkv_k_a")):
        put(n, np.concatenate([fm(inp[k][l, 0]), fm(inp[k][l, 1])], 1))
    put("rk", fm(inp["rwkv_r_k"][l].reshape(-1)))
    put("lnw", fm(inp["rwkv_ln_w"][l]))
    put("lnb", fm(inp["rwkv_ln_b"][l]))
    cw = inp["ssm_conv_w"][l]
    put("convw", np.stack([fm(cw[0]), fm(cw[1]), fm(cw[2])], -1).reshape(128, 21))
    put("convb", fm(inp["ssm_conv_b"][l]))
    put("ssmd", fm(np.repeat(inp["ssm_d"][l], 64)))
    put("ssmnw", fm(inp["ssm_norm_w"][l]))
    return p


def make_pr(inp, l):
    p = np.zeros((128, NPR), np.float32)

    def put(n, a):
        o, w = PR[n]
        p[:, o:o + w] = a
    put("adabg", rep(inp["ada_b"][l, 2048:3072]))
    put("dtb", rep(inp["ssm_dt_bias"][l].reshape(-1)))
    put("alog", rep(inp["ssm_a_log"][l].reshape(-1)))
    put("fnw", rep(inp["final_norm_w"]))
    put("convr", rep(inp["ssm_conv_w"][l].reshape(-1)))
    mu = inp["rwkv_mu"][l]
    put("shr", rep(np.concatenate([mu[0], mu[1], mu[1]], 0)))
    return p


class MK:
    def __init__(self, nc, stop_after=None, dbg=False):
        self.nc = nc
        self.P = P = Prog(nc)
        self.dbg = dbg
        self.stop_after = stop_after
        ok = "ExternalOutput" if dbg else "Internal"
        self.xin = P.dram("xin", [S, DM], F32, kind="ExternalInput")
        self.cvec = P.dram("cvec", [128, 16], F32, kind="ExternalInput")
        self.ada_w = P.dram("ada_w", [DEPTH, DM, 3 * DM], F32, kind="ExternalInput")
        self.w_in = P.dram("w_in", [DEPTH, DM, NIN], F32, kind="ExternalInput")
        self.w_out = P.dram("w_out", [DEPTH, DM, DM], F32, kind="ExternalInput")
        self.wup = P.dram("wup", [DEPTH, 2, 64, 256], F32, kind="ExternalInput")
        self.aup = P.dram("aup", [DEPTH, 2, 64, 256], F32, kind="ExternalInput")
        self.pp = P.dram("pp", [DEPTH, 128, NPP], F32, kind="ExternalInput")
        self.pr = P.dram("pr", [DEPTH, 128, NPR], F32, kind="ExternalInput")
        self.cst = P.dram("cst", [128, NCS], F32, kind="ExternalInput")
        self.y = P.dram("y", [TLAT, DM], F32, kind="ExternalOutput")
        self.wbin = P.dram("wbin", [DEPTH, 128, 8, NIN], BF16)
        self.wbout = P.dram("wbout", [DEPTH, 128, 8, DM], BF16)
        self.hT = P.dram("hT", [128, 8, S], BF16, kind=ok)
        self.mix = P.dram("mixT", [DM, S], BF16, kind=ok)
        self.xres = P.dram("xres", [S, DM], F32, kind=ok)
        self.yf = P.dram("yfwd", [384, S], F32)
        self.psh = nc.alloc_psum_tensor("psum", [128, 8, 512], F32)
        self.bt = [Trk(ps=True) for _ in range(8)]
        self.c_ident = P.sb([128, 128], F32, "c_ident")
        self.c_identb = P.sb([128, 128], BF16, "c_identb")
        self.load_c(self.c_ident, "ident")
        P.copy("dve", self.c_identb[:], self.c_ident[:])

    def ps(self, b0, nb=1, dt=F32):
        ap = self.psh[:, b0:b0 + nb, :]
        if dt is not F32:
            ap = ap.bitcast(dt)
        return V(ap, tuple(self.bt[b0:b0 + nb]))

    def load_c(self, tile, name, q="sp", sub=None):
        o, w = CS[name]
        if sub is not None:
            o, w = o + sub[0], sub[1]
        self.P.dma(q, tile[:], self.cst.v("c", (slice(None), slice(o, o + w))))

    def phase_w(self):
        P = self.P
        st = [P.sb([128, 8, 512], F32, f"w_st{i}") for i in range(2)]
        sb = [P.sb([128, 8, 512], BF16, f"w_sb{i}") for i in range(2)]
        it = 0
        for l in range(DEPTH):
            for (src, dst, n) in ((self.w_in, self.wbin, NIN), (self.w_out, self.wbout, DM)):
                for c0 in range(0, n, 512):
                    cw = min(512, n - c0)
                    a, b = st[it % 2], sb[it % 2]
                    q = "sp" if it % 2 == 0 else "act"
                    P.dma(q, a[:, :, 0:cw], src.v(("w", l), fn=lambda ap, l=l, c0=c0, cw=cw:
                                                   ap[l].rearrange("(kc p) n -> p kc n", p=128)[:, :, c0:c0 + cw]))
                    P.copy("dve" if it % 2 == 0 else "pool", b[:, :, 0:cw], a[:, :, 0:cw])
                    P.dma(q, dst.v(("wb", l, c0), fn=lambda ap, l=l, c0=c0, cw=cw: ap[l][:, :, c0:c0 + cw]),
                          b[:, :, 0:cw])
                    it += 1

    def wb(self, l, c0, cw):
        t0 = (c0 // 512) * 512
        trk = []
        ap = self.wbin.h.ap()[l][:, :, c0:c0 + cw]
        for t in range(t0, c0 + cw, 512):
            trk.append(self.wbin.reg.setdefault(("wb", l, t), Trk()))
        return V(ap, tuple(trk))

    def wbo(self, l, c0, cw):
        t0 = (c0 // 512) * 512
        trk = []
        ap = self.wbout.h.ap()[l][:, :, c0:c0 + cw]
        for t in range(t0, c0 + cw, 512):
            trk.append(self.wbout.reg.setdefault(("wb", l, t), Trk()))
        return V(ap, tuple(trk))

    def phase_mod(self, l):
        P = self.P
        if not hasattr(self, "ppt"):
            self.ppt = P.sb([128, NPP], F32, "ppt")
            self.prt = P.sb([128, NPR], F32, "prt")
            self.cact = P.sb([128, 8, 2], F32, "cact")
            self.modT = P.sb([128, 24, 2], F32, "modT")
            self.s1T = P.sb([128, 8, 2], F32, "s1T")
            self.gate_bc = P.sb([128, 2, 1024], F32, "gate_bc")
            craw = P.sb([128, 16], F32, "craw")
            P.dma("sp", craw[:], self.cvec.v(0))
            P.act(self.cact[:].re("p k w -> p w k"), craw[:].re("p (w k) -> p w k", w=2), AF.Silu)
        mk_ = P.mark()
        self.crep = P.sb([128, 8, 2, 128], F32, "crep")
        self.aw = [P.sb([128, 8, 512], F32, f"aw{i}") for i in range(2)]
        P.copy("dve", self.crep[:], self.cact[:].m(lambda a: a.unsqueeze(3)).bc([128, 8, 2, 128]))
        P.dma("sp", self.ppt[:], self.pp.v(l, fn=lambda a: a[l]))
        P.dma("act", self.prt[:], self.pr.v(l, fn=lambda a: a[l]))
        for t in range(6):
            a = self.aw[t % 2]
            P.dma("sp" if t % 2 == 0 else "act", a[:],
                  self.ada_w.v(("aw", l), fn=lambda ap, t=t: ap[l].rearrange("(kc p) n -> p kc n", p=128)[:, :, t * 512:(t + 1) * 512]))
            pm = self.ps(0)
            for j in range(4):
                for kc in range(8):
                    P.mm(pm[:, 0, j * 2:(j + 1) * 2], a[:, kc, j * 128:(j + 1) * 128], self.cact[:, kc, :],
                         start=(kc == 0), stop=(kc == 7))
            P.copy("dve", self.modT[:, t * 4:(t + 1) * 4, :], pm[:, 0, 0:8].re("p (j w) -> p j w", w=2))
            if t >= 4:
                for w in range(2):
                    pg = self.ps(1 + w)
                    for kc in range(8):
                        P.mm(pg[:, 0, :], self.crep[:, kc, w, :], a[:, kc, :], start=(kc == 0), stop=(kc == 7))
                    o = PR["adabg"][0] + (t - 4) * 512
                    P.tt("dve", self.gate_bc[:, w, (t - 4) * 512:(t - 3) * 512], pg[:, 0, :], self.prt[:, o:o + 512], ALU.add)
        o = PP["adab"][0]
        P.tt("dve", self.modT[:], self.modT[:], self.ppt[:, o:o + 24].m(lambda a: a.unsqueeze(2)).bc([128, 24, 2]), ALU.add)
        o = PP["normw"][0]
        P.stt("dve", self.s1T[:], self.modT[:, 8:16, :], 1.0,
              self.ppt[:, o:o + 8].m(lambda a: a.unsqueeze(2)).bc([128, 8, 2]), ALU.add, ALU.mult)
        P.release(mk_)

    def phase_norm(self, l):
        P = self.P
        if not hasattr(self, "n_x"):
            self.n_x = [P.sb([128, DM], F32, f"n_x{i}") for i in range(2)]
            self.n_junk = P.sb([128, DM], F32, "n_junk")
            self.n_xb = [P.sb([128, DM], BF16, f"n_xb{i}") for i in range(2)]
            self.n_ss = [P.sb([128, 4], F32, f"n_ss{i}") for i in range(2)]
            self.n_h = [P.sb([128, 8, 512], BF16, f"n_h{i}") for i in range(2)]
            self.n_t = [P.sb([128, 8, 128], F32, f"n_t{i}") for i in range(2)]
        src = self.xin if l == 0 else self.xres
        for ti, (t0, tn) in enumerate(TILES):
            w = 1 if ti == 0 else 0
            hb = self.n_h[ti % 2]
            for bi in range(tn // 128):
                blk = (t0 // 128) + bi
                xt, xb, ss = self.n_x[blk % 2], self.n_xb[blk % 2], self.n_ss[blk % 2]
                P.dma("sp" if blk % 2 == 0 else "act", xt[:], src.v(("x", blk), (slice(blk * 128, blk * 128 + 128), slice(None))))
                P.act(self.n_junk[:], xt[:], AF.Square, accum_out=ss[:, 0:1])
                P.act(ss[:, 1:2], ss[:, 0:1], AF.Sqrt, bias=1e-6, scale=1.0 / DM)
                P.recip(ss[:, 2:3], ss[:, 1:2])
                P.ts("dve", xb[:], xt[:], ss[:, 2:3], None, ALU.mult)
                pt = self.ps(2 + blk % 2, 1, BF16)
                for kc in range(8):
                    P.tr(pt[:, 0, kc * 128:(kc + 1) * 128], xb[:, kc * 128:(kc + 1) * 128], self.c_identb[:])
                tmp = self.n_t[blk % 2]
                P.tt("dve", tmp[:], pt[:, 0, :].re("p (k t) -> p k t", k=8),
                     self.s1T[:, :, w:w + 1].bc([128, 8, 128]), ALU.mult)
                P.tt("pool", hb[:, :, bi * 128:(bi + 1) * 128], tmp[:],
                     self.modT[:, 0:8, w:w + 1].bc([128, 8, 128]), ALU.add)
            P.dma("sp", self.hT.v(("h", ti), (slice(None), slice(None), slice(t0, t0 + tn))), hb[:, :, 0:tn])

    def phase_out(self, l):
        P = self.P
        last = (l == DEPTH - 1)
        if not hasattr(self, "o_w"):
            self.o_w = P.sb([128, 8, DM], BF16, "o_w")
            self.o_m = [P.sb([128, 8, 128], BF16, f"o_m{i}") for i in range(2)]
            self.o_x = [P.sb([128, DM], F32, f"o_x{i}") for i in range(2)]
            self.o_t = [P.sb([128, DM], F32, f"o_t{i}") for i in range(2)]
            self.o_ss = [P.sb([128, 4], F32, f"o_ss{i}") for i in range(2)]
        P.dma("sp", self.o_w[:], self.wbo(l, 0, DM))
        src = self.xin if l == 0 else self.xres
        for blk in range(NBLK):
            if last and blk < 2:
                continue
            w = 1 if blk < 2 else 0
            m, xt, tt_, ss = self.o_m[blk % 2], self.o_x[blk % 2], self.o_t[blk % 2], self.o_ss[blk % 2]
            tsl = slice(blk * 128, blk * 128 + 128)
            P.dma("sp", m[:], self.mix.v(("m", blk), fn=lambda ap, tsl=tsl: ap.rearrange("(kc p) t -> p kc t", p=128)[:, :, tsl]))
            P.dma("act", xt[:], src.v(("x", blk), (tsl, slice(None))))
            for hf in range(2):
                po = self.ps(4 + 2 * (blk % 2) + hf)
                for kc in range(8):
                    P.mm(po[:, 0, :], m[:, kc, :], self.o_w[:, kc, hf * 512:(hf + 1) * 512], start=(kc == 0), stop=(kc == 7))
                P.tt("dve", tt_[:, hf * 512:(hf + 1) * 512], po[:, 0, :], self.gate_bc[:, w, hf * 512:(hf + 1) * 512], ALU.mult)
            P.tt("pool", tt_[:], tt_[:], xt[:], ALU.add)
            if not last:
                P.dma("sp", self.xres.v(("x", blk), (tsl, slice(None))), tt_[:])
            else:
                P.act(xt[:], tt_[:], AF.Square, accum_out=ss[:, 0:1])
                P.act(ss[:, 1:2], ss[:, 0:1], AF.Sqrt, bias=1e-6, scale=1.0 / DM)
                P.recip(ss[:, 2:3], ss[:, 1:2])
                o = PR["fnw"][0]
                P.stt("dve", xt[:], tt_[:], ss[:, 2:3], self.prt[:, o:o + DM], ALU.mult, ALU.mult)
                P.dma("sp", self.y.v(("y", blk), (slice(blk * 128 - CTX, blk * 128 - CTX + 128), slice(None))), xt[:])

    def load_h(self, tile, ti, q="sp"):
        t0, tn = TILES[ti]
        self.P.dma(q, tile[:, :, 0:tn], self.hT.v(("h", ti), (slice(None), slice(None), slice(t0, t0 + tn))))

    def phase_att(self, l):
        P = self.P
        ctx_out = (l < DEPTH - 1)
        import os
        sm = int(os.environ.get("MK_SET", "63"))
        wq = P.sb([128, 8, 384], BF16, "a_wq")
        wg = P.sb([128, 8, 384], BF16, "a_wg")
        wk = P.sb([128, 8, 128], BF16, "a_wk")
        wv = P.sb([128, 8, 128], BF16, "a_wv")
        if sm & 1:
            for j in range(3):
                for hh in range(2):
                    h = j + 3 * hh
                    P.dma("sp", wq[:, :, j * 128 + hh * 64: j * 128 + hh * 64 + 64], self.wb(l, C_Q + h * 64, 64))
                    P.dma("act", wg[:, :, j * 128 + hh * 64: j * 128 + hh * 64 + 64], self.wb(l, C_G + h * 64, 64))
            P.dma("sp", wk[:], self.wb(l, C_K, 128))
            P.dma("act", wv[:], self.wb(l, C_V, 128))
        cos = P.sb([128, TLAT], F32, "a_cos")
        sin = P.sb([128, TLAT], F32, "a_sin")
        if sm & 2:
            self.load_c(cos, "cos", "sp")
            self.load_c(sin, "sin", "act")
        pf = P.sb([128, 128], F32, "a_pf")
        permb = P.sb([128, 128], BF16, "a_permb")
        if sm & 4:
            self.load_c(pf, "perm")
            P.copy("dve", permb[:], pf[:])
        mband = P.sb([128, 3, 384], F32, "a_mband")
        o, w_ = CS["mband"]
        if sm & 8:
            P.dma("sp", mband[:].re("p a b -> p (a b)"), self.cst.v("c", (slice(None), slice(o, o + w_))))
        sink8 = P.sb([128, 6], F32, "a_sink8")
        o = PP["sink"][0]
        if sm & 16:
            P.ts("dve", sink8[:], self.ppt[:, o:o + 6], 8.0, None, ALU.mult)
        KB = P.sb([128, 4608], BF16, "a_KB")
        Vtm = P.sb([128, 36, 128], BF16, "a_V")
        if sm & 32:
            P.memset("pool", KB[:, 256:384], 0.0)
            P.memset("pool", KB[:, 4480:4608], 0.0)
            P.memset("pool", Vtm[:, 2, :], 0.0)
            P.memset("pool", Vtm[:, 35, :], 0.0)
        hb = [P.sb([128, 8, 512], BF16, f"a_h{i}") for i in range(2)]
        kf = P.sb([128, 512], F32, "a_kf")
        t1 = [P.sb([128, 512], F32, f"a_t1{i}") for i in range(2)]
        t2 = [P.sb([128, 512], F32, f"a_t2{i}") for i in range(2)]

        def kcol(tok):
            return tok if tok < CTX else tok + 128

        def proj(dst, wt, c0, cw, hbt, tn, lat_t0, bank, func=None):
            pp_ = self.ps(bank)
            for kc in range(8):
                P.mm(pp_[:, 0, 0:tn], wt[:, kc, c0:c0 + cw], hbt[:, kc, 0:tn], start=(kc == 0), stop=(kc == 7))
            if func is not None:
                P.act(dst, pp_[:, 0, 0:tn], func)
                return
            if lat_t0 is None:
                P.copy("act", dst, pp_[:, 0, 0:tn])
                return
            P.copy("act", kf[:, 0:tn], pp_[:, 0, 0:tn])
            pr_ = self.ps(2)
            P.mm(pr_[:, 0, 0:tn], pf[:], kf[:, 0:tn])
            a, b = t1[bank % 2], t2[bank % 2]
            P.tt("pool", a[:, 0:tn], kf[:, 0:tn], cos[:, lat_t0:lat_t0 + tn], ALU.mult)
            P.tt("dve", b[:, 0:tn], pr_[:, 0, 0:tn], sin[:, lat_t0:lat_t0 + tn], ALU.mult)
            P.tt("pool", dst, a[:, 0:tn], b[:, 0:tn], ALU.add)

        stage = int(os.environ.get("MK_ATT", "9"))
        if stage <= 0:
            return
        for ti, (t0, tn) in enumerate(TILES):
            h = hb[ti % 2]
            self.load_h(h, ti, "sp" if ti % 2 == 0 else "act")
            kc0 = kcol(t0)
            ma = int(os.environ.get("MK_A", "3"))
            if ti >= int(os.environ.get("MK_AT", "9")):
                break
            if ma & 1:
                proj(KB[:, kc0:kc0 + tn], wk, 0, 128, h, tn, None if ti == 0 else t0 - CTX, ti % 2)
            for bi in range(tn // 128):
                if not (ma & 2):
                    break
                vb = (t0 // 128 + bi)
                vb = vb if vb < 2 else vb + 1
                pv = self.ps(3 + bi % 2)
                for kc in range(8):
                    P.mm(pv[:, 0, 0:128], h[:, kc, bi * 128:(bi + 1) * 128], wv[:, kc, :], start=(kc == 0), stop=(kc == 7))
                P.copy("dve" if bi % 2 == 0 else "act", Vtm[:, vb, :], pv[:, 0, 0:128])

        import os
        stage = int(os.environ.get("MK_ATT", "9"))
        if stage <= 1:
            return
        qT = [P.sb([128, 3, 512], BF16, f"a_q{i}") for i in range(2)]
        gT = [P.sb([128, 3, 512], BF16, f"a_g{i}") for i in range(2)]
        om = [P.sb([128, 3, 512], BF16, f"a_om{i}") for i in range(2)]
        sc = [P.sb([128, 3, 640], F32, f"a_sc{i}") for i in range(2)]
        pe = [P.sb([128, 3, 640], F32, f"a_pe{i}") for i in range(2)]
        pn = [P.sb([128, 3, 640], BF16, f"a_pn{i}") for i in range(2)]
        pTs = [P.sb([128, 5, 128], BF16, f"a_pT{i}") for i in range(3)]
        st = [P.sb([128, 8, 3], F32, f"a_st{i}") for i in range(2)]
        it = 0
        npt = 0
        for ti, (t0, tn) in enumerate(TILES):
            if ti == 0 and not ctx_out:
                continue
            h = hb[ti % 2]
            self.load_h(h, ti, "sp" if ti % 2 == 0 else "act")
            q_, g_, o_ = qT[ti % 2], gT[ti % 2], om[ti % 2]
            for j in range(3):
                proj(q_[:, j, 0:tn], wq, j * 128, 128, h, tn, None if ti == 0 else t0 - CTX, j % 2)
                proj(g_[:, j, 0:tn], wg, j * 128, 128, h, tn, None, (j + 1) % 2, func=AF.Silu)
            if stage <= 2:
                break
            for bi in range(tn // 128):
                if stage <= 5 and (bi > 0 or ti > 1):
                    break
                qs = slice(bi * 128, bi * 128 + 128)
                isctx = (ti == 0)
                n = (t0 - CTX) // 128 + bi if not isctx else None
                lo = 384 if isctx else 0
                po = self.ps(2)
                for kg in range(2):
                    ps_ = slice(kg * 64, kg * 64 + 64)
                    s_, e_, n_, stt_ = sc[it % 2], pe[it % 2], pn[it % 2], st[it % 2]
                    it += 1
                    for j in range(3):
                        pb = self.ps(3 + 2 * (j % 2))
                        pc = self.ps(4 + 2 * (j % 2))
                        if not isctx:
                            kb0 = 256 + n * 128
                            P.mm(pb[:, 0, 0:384], q_[ps_, j, qs], KB[ps_, kb0:kb0 + 384])
                            var = 0 if n == 0 else (2 if n == 31 else 1)
                            P.tt("dve", s_[:, j, 0:384], pb[:, 0, 0:384], mband[:, var, :], ALU.add)
                        P.mm(pc[:, 0, 0:256], q_[ps_, j, qs], KB[ps_, 0:256])
                        P.copy("act", s_[:, j, 384:640], pc[:, 0, 0:256])
                    if stage <= 3:
                        continue
                    P.reduce("dve", stt_[:, 0, :], s_[:, :, lo:640], ALU.max)
                    P.tt("dve", stt_[:, 1, :], stt_[:, 0, :], sink8[:, kg * 3:kg * 3 + 3], ALU.max)
                    P.ts("dve", stt_[:, 2, :], stt_[:, 1, :], -0.125, None, ALU.mult)
                    for j in range(3):
                        P.act(e_[:, j, lo:640], s_[:, j, lo:640], AF.Exp, bias=stt_[:, 2, j:j + 1], scale=0.125,
                              accum_out=stt_[:, 3, j:j + 1])
                    P.stt("dve", stt_[:, 4, :], sink8[:, kg * 3:kg * 3 + 3], 0.125, stt_[:, 2, :], ALU.mult, ALU.add)
                    P.act(stt_[:, 5, :], stt_[:, 4, :], AF.Exp)
                    P.tt("dve", stt_[:, 6, :], stt_[:, 5, :], stt_[:, 3, :], ALU.add)
                    P.recip(stt_[:, 7, :], stt_[:, 6, :])
                    P.tt("dve", n_[:, :, lo:640], e_[:, :, lo:640],
                         stt_[:, 7, :].m(lambda a: a.unsqueeze(2)).bc([128, 3, 640 - lo]), ALU.mult)
                    if stage <= 4:
                        continue
                    for j in range(3):
                        pt = self.ps(7, 1, BF16)
                        chunks = ([] if isctx else [0, 1, 2]) + [3, 4]
                        for c in chunks:
                            P.tr(pt[:, 0, c * 128:(c + 1) * 128], n_[:, j, c * 128:(c + 1) * 128], self.c_identb[:])
                        pts = pTs[npt % 3]
                        npt += 1
                        c0 = chunks[0]
                        if npt % 2 == 0:
                            P.copy("act", pts[:, c0:5, :], pt[:, 0, c0 * 128:640].re("p (c t) -> p c t", t=128))
                        else:
                            P.copy("dve", pts[:, c0:5, :], pt[:, 0, c0 * 128:640].re("p (c t) -> p c t", t=128))
                        for ci, c in enumerate(chunks):
                            vb = (2 + n + c) if c < 3 else (c - 3)
                            P.mm(po[ps_, 0, j * 128:(j + 1) * 128], Vtm[:, vb, kg * 64:kg * 64 + 64], pts[:, c, :],
                                 start=(ci == 0), stop=(ci == len(chunks) - 1))
                P.tt("dve", o_[:, :, qs], po[:, 0, 0:384].re("p (j t) -> p j t", j=3), g_[:, :, qs], ALU.mult)
            for j in range(3):
                for hh in range(2):
                    hd = j + 3 * hh
                    P.dma("sp" if hh == 0 else "act",
                          self.mix.v(("ma", ti, hd), (slice(hd * 64, hd * 64 + 64), slice(t0, t0 + tn))),
                          o_[hh * 64:hh * 64 + 64, j, 0:tn])

    def load_h_halo(self, tile, ti, q="sp"):
        P = self.P
        t0, tn = TILES[ti]
        lo, hi = t0 - 1, t0 + tn + 1
        if ti <= 1:
            P.memset("pool", tile[:, :, 0:1], 0.0)
            lo = t0
        if ti == 0 or ti == len(TILES) - 1:
            P.memset("pool", tile[:, :, tn + 1:tn + 2], 0.0)
            hi = t0 + tn
        P.dma(q, tile[:, :, lo - t0 + 1:hi - t0 + 1], self.hT.v(("h", ti), (slice(None), slice(None), slice(lo, hi))))

    def scaled_w(self, l, c0, ncol, rows, name, roff=0):
        P = self.P
        rows = [r[:, roff:roff + ncol] for r in rows]
        w3 = P.sb([128, 3, 8, ncol], BF16, name)
        mk_ = P.mark()
        half = ncol // 2
        stg = [P.sb([128, 8, half], F32, f"{name}_st{i}") for i in range(2)]
        for hf in range(2):
            st = stg[hf]
            P.dma("sp" if hf == 0 else "act", st[:],
                  self.w_in.v(("w", l), fn=lambda ap, hf=hf: ap[l].rearrange("(kc p) n -> p kc n", p=128)[:, :, c0 + hf * half:c0 + (hf + 1) * half]))
            for tap in range(3):
                P.tt("dve" if tap != 1 else "pool", w3[:, tap, :, hf * half:(hf + 1) * half], st[:],
                     rows[tap][:, hf * half:(hf + 1) * half].m(lambda a: a.unsqueeze(1)).bc([128, 8, half]), ALU.mult)
        P.release(mk_)
        return w3

    def phase_ssd(self, l):
        P = self.P
        ctx_out = (l < DEPTH - 1)
        o = PR["convr"][0]
        rows = [self.prt[:, o + k * 896:o + (k + 1) * 896] for k in range(3)]
        xsT = P.sb([128, 3, S], F32, "s_xsT")
        BT = P.sb([128, 2, S], BF16, "s_BT")
        CT = P.sb([128, 2, S], BF16, "s_CT")
        dt_tm = P.sb([128, NBLK, 12], F32, "s_dt")
        dtA_tm = P.sb([128, NBLK, 12], F32, "s_dtA")
        aneg = P.sb([128, 12], F32, "s_aneg")
        o = PR["alog"][0]
        P.act(aneg[:], self.prt[:, o:o + 12], AF.Exp)
        P.ts("dve", aneg[:], aneg[:], -1.0, None, ALU.mult)
        mk1 = P.mark()
        wx3 = self.scaled_w(l, C_XBC, 896, rows, "s_wx3")
        wdt = P.sb([128, 8, 12], BF16, "s_wdt")
        P.dma("sp", wdt[:], self.wb(l, C_DT, 12))
        hh = [P.sb([128, 8, 514], BF16, f"s_hh{i}") for i in range(2)]
        dtmp = [P.sb([128, 12], F32, f"s_dtmp{i}") for i in range(2)]
        ocb = PP["convb"][0]
        nb = 0
        for ti, (t0, tn) in enumerate(TILES):
            h = hh[ti % 2]
            self.load_h_halo(h, ti, "sp" if ti % 2 == 0 else "act")
            for c in range(7):
                pb = self.ps(c % 2)
                n = 0
                for tap in range(3):
                    for kc in range(8):
                        P.mm(pb[:, 0, 0:tn], wx3[:, tap, kc, c * 128:(c + 1) * 128], h[:, kc, tap:tap + tn],
                             start=(n == 0), stop=(n == 23))
                        n += 1
                if c < 3:
                    dst = xsT[:, c, t0:t0 + tn]
                elif c < 5:
                    dst = BT[:, c - 3, t0:t0 + tn]
                else:
                    dst = CT[:, c - 5, t0:t0 + tn]
                P.act(dst, pb[:, 0, 0:tn], AF.Silu, bias=self.ppt[:, ocb + c:ocb + c + 1])
            for bi in range(tn // 128):
                blk = t0 // 128 + bi
                pd = self.ps(2 + blk % 2)
                for kc in range(8):
                    P.mm(pd[:, 0, 0:12], h[:, kc, 1 + bi * 128:1 + (bi + 1) * 128], wdt[:, kc, :], start=(kc == 0), stop=(kc == 7))
                d_ = dtmp[blk % 2]
                o = PR["dtb"][0]
                P.tt("dve", d_[:], pd[:, 0, 0:12], self.prt[:, o:o + 12], ALU.add)
                P.act(d_[:], d_[:], AF.Exp)
                P.act(dt_tm[:, blk, :], d_[:], AF.Ln, bias=1.0)
                P.tt("dve", dtA_tm[:, blk, :], dt_tm[:, blk, :], aneg[:], ALU.mult)
        P.release(mk1)
        if int(os.environ.get("MK_SSD", "9")) <= 1:
            self.dbg_xsT = xsT
            return
        wz = P.sb([128, 8, 384], BF16, "s_wz")
        P.dma("sp", wz[:], self.wb(l, C_Z, 384))
        cf = {}
        for nme in ("snf", "snb", "trif", "trib", "ones", "hlo", "hhi"):
            cf[nme] = P.sb([128, 128], F32, "s_c_" + nme)
            self.load_c(cf[nme], nme, "act")
        St = P.sb([128, 6, 64], F32, "s_S")
        NB2 = 2
        xs_sb = [P.sb([128, 384], F32, f"s_xs{i}") for i in range(NB2)]
        Bt_sb = [P.sb([128, 2, 128], BF16, f"s_Bt{i}") for i in range(NB2)]
        Gs = [P.sb([128, 2, 128], F32, f"s_G{i}") for i in range(NB2)]
        cumc = [P.sb([128, 6], F32, f"s_cc{i}") for i in range(NB2)]
        dbc = [P.sb([128, 6, 128], F32, f"s_dbc{i}") for i in range(NB2)]
        Dm = [P.sb([128, 6, 128], F32, f"s_D{i}") for i in range(NB2)]
        Em = [P.sb([128, 6, 128], F32, f"s_E{i}") for i in range(NB2)]
        scT = [P.sb([128, 6, 128], BF16, f"s_sc{i}") for i in range(NB2)]
        Er = [P.sb([128, 6, 128], F32, f"s_Er{i}") for i in range(NB2)]
        Cd = [P.sb([128, 6, 128], F32, f"s_Cd{i}") for i in range(NB2)]
        xdt = [P.sb([128, 6, 64], BF16, f"s_xdt{i}") for i in range(NB2)]
        xdtt = [P.sb([128, 6, 64], BF16, f"s_xdtt{i}") for i in range(NB2)]
        ysb = [P.sb([128, 3, 128], F32, f"s_y{i}") for i in range(NB2)]
        yfl = [P.sb([128, 3, 128], F32, f"s_yf{i}") for i in range(NB2)]
        hz = [P.sb([128, 8, 128], BF16, f"s_hz{i}") for i in range(NB2)]
        zs = [P.sb([128, 3, 128], F32, f"s_z{i}") for i in range(NB2)]
        sq = [P.sb([128, 3, 128], F32, f"s_sq{i}") for i in range(NB2)]
        rs = [P.sb([128, 2, 128], F32, f"s_rs{i}") for i in range(NB2)]
        mo = [P.sb([128, 3, 128], BF16, f"s_mo{i}") for i in range(NB2)]
        osd, onw = PP["ssmd"][0], PP["ssmnw"][0]
        it = 0
        for d in range(2):
            order = list(range(NBLK)) if d == 0 else [1, 0] + list(range(NBLK - 1, 1, -1))
            tri = cf["trif"] if d == 0 else cf["trib"]
            sneg = cf["snf"] if d == 0 else cf["snb"]
            last = 127 if d == 0 else 0
            P.memset("dve", St[:], 0.0)
            for c in order:
                if d == 1 and c < 2 and not ctx_out:
                    pass
                k = it % NB2
                it += 1
                cs_ = slice(c * 128, (c + 1) * 128)
                pxs = self.ps(0)
                for j in range(3):
                    P.tr(pxs[:, 0, j * 128:(j + 1) * 128], xsT[:, j, cs_], self.c_ident[:])
                P.copy("act", xs_sb[k][:], pxs[:, 0, 0:384])
                pbt = self.ps(1, 1, BF16)
                for g in range(2):
                    P.tr(pbt[:, 0, g * 128:(g + 1) * 128], BT[:, g, cs_], self.c_identb[:])
                P.copy("dve", Bt_sb[k][:].re("p g n -> p (g n)"), pbt[:, 0, 0:256])
                pg = self.ps(2)
                for g in range(2):
                    P.mm(pg[:, 0, g * 128:(g + 1) * 128], BT[:, g, cs_], CT[:, g, cs_])
                P.copy("act", Gs[k][:].re("p g n -> p (g n)"), pg[:, 0, 0:256])
                dA = dtA_tm[:, c, d * 6:(d + 1) * 6]
                P.mm(pg[:, 0, 256:262], tri[:], dA)
                P.copy("dve", cumc[k][:], pg[:, 0, 256:262])
                P.copy("pool", dbc[k][:], dA.m(lambda a: a.unsqueeze(2)).bc([128, 6, 128]))
                pcr = self.ps(3, 2)
                for h in range(6):
                    P.mm(pcr[:, h // 4, (h % 4) * 128:(h % 4 + 1) * 128], dbc[k][:, h, :], tri[:])
                pcr_v = pcr.re("p b n -> p (b n)")[:, 0:768].re("p (h n) -> p h n", h=6)
                P.tt("dve", Dm[k][:], pcr_v, cumc[k][:].m(lambda a: a.unsqueeze(2)).bc([128, 6, 128]), ALU.subtract)
                P.act(Er[k][:], pcr_v, AF.Exp)
                P.tt("pool", Dm[k][:], Dm[k][:], sneg[:].m(lambda a: a.unsqueeze(1)).bc([128, 6, 128]), ALU.add)
                P.act(Em[k][:], Dm[k][:], AF.Exp)
                P.tt("pool", scT[k][:].re("p (g h) n -> p g h n", g=2), Em[k][:].re("p (g h) n -> p g h n", g=2),
                     Gs[k][:].m(lambda a: a.unsqueeze(2)).bc([128, 2, 3, 128]), ALU.mult)
                P.tt("dve", Cd[k][:].re("p (g h) n -> p g h n", g=2), Er[k][:].re("p (g h) n -> p g h n", g=2),
                     CT[:, :, cs_].m(lambda a: a.unsqueeze(2)).bc([128, 2, 3, 128]), ALU.mult)
                dtc = dt_tm[:, c, d * 6:(d + 1) * 6]
                P.tt("dve", xdt[k][:], xs_sb[k][:].re("p (h q) -> p h q", h=6),
                     dtc.m(lambda a: a.unsqueeze(2)).bc([128, 6, 64]), ALU.mult)
                P.tt("pool", xdtt[k][:], xdt[k][:], Em[k][:, :, last:last + 1].bc([128, 6, 64]), ALU.mult)
                py = self.ps(5)
                for h in range(6):
                    o_ = py[(h % 2) * 64:(h % 2) * 64 + 64, 0, (h // 2) * 128:(h // 2 + 1) * 128]
                    P.mm(o_, xdt[k][:, h, :], scT[k][:, h, :], start=True, stop=False)
                    P.mm(o_, St[:, h, :], Cd[k][:, h, :], start=False, stop=True)
                pcs = self.ps(6)
                for h in range(6):
                    P.mm(pcs[:, 0, h * 64:(h + 1) * 64], Bt_sb[k][:, h // 3, :], xdtt[k][:, h, :])
                P.tt("dve", St[:], St[:], Er[k][:, :, last:last + 1].bc([128, 6, 64]), ALU.mult)
                P.tt("dve", St[:], St[:], pcs[:, 0, 0:384].re("p (h q) -> p h q", h=6), ALU.add)
                if d == 0:
                    P.copy("act", ysb[k][:].re("p j n -> p (j n)"), py[:, 0, 0:384])
                    P.dma("sp", self.yf.v(("yf", c), fn=lambda ap, cs_=cs_: ap.rearrange("(j p) t -> p j t", p=128)[:, :, cs_]), ysb[k][:])
                    continue
                if c < 2 and not ctx_out:
                    continue
                P.dma("act", yfl[k][:], self.yf.v(("yf", c), fn=lambda ap, cs_=cs_: ap.rearrange("(j p) t -> p j t", p=128)[:, :, cs_]))
                P.dma("sp", hz[k][:], self.hT.v(("h", "z"), (slice(None), slice(None), cs_)))
                pz = self.ps(7)
                for j in range(3):
                    for kc in range(8):
                        P.mm(pz[:, 0, j * 128:(j + 1) * 128], wz[:, kc, j * 128:(j + 1) * 128], hz[k][:, kc, :],
                             start=(kc == 0), stop=(kc == 7))
                P.act(zs[k][:].re("p j n -> p (j n)"), pz[:, 0, 0:384], AF.Silu)
                y_ = ysb[k]
                P.tt("dve", y_[:].re("p j n -> p (j n)"), py[:, 0, 0:384], yfl[k][:].re("p j n -> p (j n)"), ALU.add)
                P.tt("pool", yfl[k][:], xsT[:, :, cs_], self.ppt[:, osd:osd + 3].m(lambda a: a.unsqueeze(2)).bc([128, 3, 128]), ALU.mult)
                P.tt("pool", y_[:], y_[:], yfl[k][:], ALU.add)
                P.tt("dve", y_[:], y_[:], zs[k][:], ALU.mult)
                P.act(sq[k][:], y_[:], AF.Square)
                pgs = self.ps(7)
                P.mm(pgs[:, 0, 384:512].m(lambda a: a), cf["ones"][:], sq[k][:, 0, :], start=True, stop=False)
                P.mm(pgs[:, 0, 384:512], cf["hlo"][:], sq[k][:, 1, :], start=False, stop=True)
                pgs2 = self.ps(6)
                P.mm(pgs2[:, 0, 384:512], cf["hhi"][:], sq[k][:, 1, :], start=True, stop=False)
                P.mm(pgs2[:, 0, 384:512], cf["ones"][:], sq[k][:, 2, :], start=False, stop=True)
                P.act(rs[k][:, 0, :], pgs[:, 0, 384:512], AF.Sqrt, bias=1e-5, scale=1.0 / 192)
                P.act(rs[k][:, 1, :], pgs2[:, 0, 384:512], AF.Sqrt, bias=1e-5, scale=1.0 / 192)
                P.recip(rs[k][:], rs[k][:])
                m_ = mo[k]
                P.stt("dve", m_[:, 0, :], y_[:, 0, :], self.ppt[:, onw:onw + 1], rs[k][:, 0, :], ALU.mult, ALU.mult)
                P.stt("dve", m_[0:64, 1, :], y_[0:64, 1, :], self.ppt[0:64, onw + 1:onw + 2], rs[k][0:64, 0, :], ALU.mult, ALU.mult)
                P.stt("dve", m_[64:128, 1, :], y_[64:128, 1, :], self.ppt[64:128, onw + 1:onw + 2], rs[k][64:128, 1, :], ALU.mult, ALU.mult)
                P.stt("dve", m_[:, 2, :], y_[:, 2, :], self.ppt[:, onw + 2:onw + 3], rs[k][:, 1, :], ALU.mult, ALU.mult)
                P.dma("sp", self.mix.v(("ms", c), fn=lambda ap, cs_=cs_: ap[640:1024].rearrange("(j p) t -> p j t", p=128)[:, :, cs_]), m_[:])

    def phase_rwkv(self, l):
        P = self.P
        ctx_out = (l < DEPTH - 1)
        LWC = -0.6065306597126334
        o = PR["shr"][0]
        r0, r2, r1 = (self.prt[:, o:o + 1024], self.prt[:, o + 1024:o + 2048], self.prt[:, o + 2048:o + 3072])
        P.tt("dve", r1, r0, r2, ALU.add)
        P.ts("dve", r1, r1, -1.0, 1.0, ALU.mult, ALU.add)
        rows = [r0, r1, r2]
        cf = {}
        for nme in ("blk", "lowf", "lowb"):
            cf[nme] = P.sb([128, 128], F32, "r_c_" + nme)
            self.load_c(cf[nme], nme, "act")
        m4 = {}
        for nme in ("rmf", "rmb"):
            t_ = P.sb([128, 256], F32, "r_c_" + nme)
            self.load_c(t_, nme, "act")
            m4[nme] = P.sb([128, 2, 256], F32, "r_m4_" + nme)
            P.copy("dve", m4[nme][:], t_[:].m(lambda a: a.unsqueeze(1)).bc([128, 2, 256]))
        ones64 = P.sb([128, 64], F32, "r_ones")
        P.memset("pool", ones64[:], 1.0)
        omka = P.sb([128, 4], F32, "r_omka")
        oka = PP["ka"][0]
        P.ts("dve", omka[:], self.ppt[:, oka:oka + 4], -1.0, 1.0, ALU.mult, ALU.add)
        wupf = P.sb([128, 256], F32, "r_wupf")
        aupf = P.sb([128, 256], F32, "r_aupf")
        wupb = P.sb([128, 256], BF16, "r_wupb")
        aupb = P.sb([128, 256], BF16, "r_aupb")
        for d in range(2):
            P.dma("sp", wupf[d * 64:(d + 1) * 64, :], self.wup.v(("u", l, d), fn=lambda ap, d=d: ap[l][d]))
            P.dma("act", aupf[d * 64:(d + 1) * 64, :], self.aup.v(("u", l, d), fn=lambda ap, d=d: ap[l][d]))
        P.copy("dve", wupb[:], wupf[:])
        P.copy("pool", aupb[:], aupf[:])
        wdT = P.sb([128, S], BF16, "r_wdT")
        adT = P.sb([128, S], BF16, "r_adT")
        mk1 = P.mark()
        hh = [P.sb([128, 8, 514], BF16, f"r_hh{i}") for i in range(2)]
        wl3 = self.scaled_w(l, C_RW + 768, 256, rows, "r_wl3", roff=768)
        for ti, (t0, tn) in enumerate(TILES):
            h = hh[ti % 2]
            self.load_h_halo(h, ti, "sp" if ti % 2 == 0 else "act")
            for c in range(2):
                pb = self.ps(c)
                n = 0
                for tap in range(3):
                    for kc in range(8):
                        P.mm(pb[:, 0, 0:tn], wl3[:, tap, kc, c * 128:(c + 1) * 128], h[:, kc, tap:tap + tn],
                             start=(n == 0), stop=(n == 23))
                        n += 1
                if c == 0:
                    P.act(wdT[:, t0:t0 + tn], pb[:, 0, 0:tn], AF.Tanh)
                else:
                    P.copy("act", adT[:, t0:t0 + tn], pb[:, 0, 0:tn])
        P.release(mk1)
        RW = int(os.environ.get("MK_RW", "99"))
        if RW <= 1:
            return
        mk_hp = P.mark()
        for hp in range(2):
            rT = P.sb([128, S], F32, "r_rT")
            kT = P.sb([128, S], F32, "r_kT")
            vT = P.sb([128, S], F32, "r_vT")
            ysum = P.sb([128, S], F32, "r_ysum")
            mk2 = P.mark()
            hh = [P.sb([128, 8, 514], BF16, f"r_hh{i}") for i in range(2)]
            w3 = [self.scaled_w(l, C_RW + j * 256 + hp * 128, 128, rows, f"r_w3{j}", roff=j * 256 + hp * 128) for j in range(3)]
            for ti, (t0, tn) in enumerate(TILES):
                h = hh[ti % 2]
                self.load_h_halo(h, ti, "sp" if ti % 2 == 0 else "act")
                for j, dst in enumerate((rT, kT, vT)):
                    pb = self.ps(j % 2)
                    n = 0
                    for tap in range(3):
                        for kc in range(8):
                            P.mm(pb[:, 0, 0:tn], w3[j][:, tap, kc, :], h[:, kc, tap:tap + tn], start=(n == 0), stop=(n == 23))
                            n += 1
                    P.copy("act", dst[:, t0:t0 + tn], pb[:, 0, 0:tn])
            P.release(mk2)
            if RW <= 2:
                return
            mk3 = P.mark()
            self.rwkv_scan(l, hp, rT, kT, vT, wdT, adT, ysum, wupb, aupb, cf, m4, ones64, omka, LWC)
            P.release(mk3)
            self.rwkv_finish(l, hp, rT, kT, vT, ysum, cf, ctx_out)
            P.release(mk_hp)

    def rwkv_scan(self, l, hp, rT, kT, vT, wdT, adT, ysum, wupb, aupb, cf, m4, ones64, omka, LWC):
        P = self.P
        NB = 2
        f32t = lambda n, w=512: [P.sb([128, w], F32, f"r_{n}")] * NB
        sg, aa, cum = f32t("sg"), f32t("aa"), f32t("cum")
        lw = sg
        cumx = [P.sb([128, 8], F32, "r_tot")] * NB
        e1, e2, e3, e4 = f32t("e1"), f32t("e2"), f32t("e3"), f32t("e4")
        kkr, sqk, nrm, kd, bq = f32t("kkr"), f32t("sqk"), f32t("nrm"), f32t("kd"), f32t("bq")
        kk, tmpa = kkr, nrm
        WC = [P.sb([128, 8], F32, f"r_WC{i}") for i in range(NB)]
        arb = [P.sb([128, 4, 2, 128], F32, "r_arb")] * NB
        kb = [P.sb([128, 512], F32, "r_kb")] * NB
        bb = [P.sb([128, 512], F32, "r_bb")] * NB
        kt = [P.sb([128, 512], F32, "r_kt")] * NB
        btl = [P.sb([128, 512], F32, "r_btl")] * NB
        TM = [P.sb([128, 4, 128], F32, f"r_TM{i}") for i in range(NB)]
        AT = [P.sb([128, 2, 4, 128], F32, f"r_AT{i}") for i in range(NB)]
        X1 = [P.sb([128, 2, 128], F32, f"r_X1{i}") for i in range(NB)]
        X1b = [P.sb([128, 2, 128], BF16, f"r_X1b{i}") for i in range(NB)]
        XT1b = [P.sb([128, 2, 128], BF16, f"r_XT1b{i}") for i in range(NB)]
        XX = [P.sb([128, 2, 2, 128], F32, f"r_XX{i}") for i in range(2)]
        Qf = [P.sb([128, 2, 128], F32, f"r_Qf{i}") for i in range(NB)]
        Qb = [P.sb([128, 2, 128], BF16, f"r_Q{i}") for i in range(2)]
        W1 = [P.sb([128, 128], F32, f"r_W1{i}") for i in range(NB)]
        AU = [P.sb([128, 256], F32, f"r_AU{i}") for i in range(NB)]
        Y0T = [P.sb([128, 128], F32, f"r_Y0T{i}") for i in range(NB)]
        RhT = [P.sb([128, 128], F32, f"r_RhT{i}") for i in range(NB)]
        MTt = [P.sb([128, 128], F32, f"r_MTt{i}") for i in range(2)]
        MT = [P.sb([128, 128], F32, f"r_MT{i}") for i in range(4)]
        Ns = [P.sb([128, 64], F32, f"r_Ns{i}") for i in range(4)]
        ytmp = [P.sb([128, 64], F32, f"r_yt{i}") for i in range(2)]
        Sseq = P.sb([128, 4, 64], F32, "r_Sseq")
        ow0, oa0, okk, oka = PP["w0"][0], PP["a0"][0], PP["kk"][0], PP["ka"][0]
        nblk_done = 0
        for d in range(2):
            if d == 1 and hp == 0 and os.environ.get("MK_RWDBG"):
                if not hasattr(self, "dbgy"):
                    self.dbgy = P.dram("dbgy", [128, S], F32, kind="ExternalOutput")
                P.dma("sp", self.dbgy.v(0), ysum[:])
            col = d * 2 + hp
            mask4 = m4["rmf"] if d == 0 else m4["rmb"]
            low = cf["lowf"] if d == 0 else cf["lowb"]
            P.memset("dve", Sseq[:, 0, :], 0.0)
            si = 0
            tiles = list(range(len(TILES))) if d == 0 else [0] + list(range(len(TILES) - 1, 0, -1))
            for tix, ti in enumerate(tiles):
                t0, tn = TILES[ti]
                nch = tn // 64
                nbk = tn // 128
                k_ = tix % NB
                ts_ = slice(t0, t0 + tn)
                ds_ = slice(d * 64, (d + 1) * 64)
                pxw, pxa = self.ps(0), self.ps(1)
                P.mm(pxw[:, 0, 0:tn], wupb[ds_, hp * 128:(hp + 1) * 128], wdT[ds_, ts_])
                P.mm(pxa[:, 0, 0:tn], aupb[ds_, hp * 128:(hp + 1) * 128], adT[ds_, ts_])
                P.act(sg[k_][:, 0:tn], pxw[:, 0, 0:tn], AF.Sigmoid, bias=self.ppt[:, ow0 + col:ow0 + col + 1])
                P.act(aa[k_][:, 0:tn], pxa[:, 0, 0:tn], AF.Sigmoid, bias=self.ppt[:, oa0 + col:oa0 + col + 1])
                P.ts("dve", lw[k_][:, 0:tn], sg[k_][:, 0:tn], LWC, None, ALU.mult)
                for c in range(nch):
                    cs = slice(c * 64, (c + 1) * 64)
                    P.scan("dve", cum[k_][:, cs], ones64[:], lw[k_][:, cs], 0.0, ALU.mult, ALU.add)
                c3 = lambda t_: t_[:, 0:tn].re("p (c n) -> p c n", n=64)
                P.act(WC[k_][:, 0:nch].m(lambda a: a.unsqueeze(2)), c3(cum[k_])[:, :, 63:64], AF.Exp)
                if d == 1:
                    P.copy("pool", cumx[k_][:, 0:nch].m(lambda a: a.unsqueeze(2)), c3(cum[k_])[:, :, 63:64])
                    P.tt("dve", cum[k_][:, 0:tn], lw[k_][:, 0:tn], cum[k_][:, 0:tn], ALU.subtract)
                    P.tt("dve", c3(cum[k_]), c3(cum[k_]), cumx[k_][:, 0:nch].m(lambda a: a.unsqueeze(2)).bc([128, nch, 64]), ALU.add)
                    tot = None
                P.tt("pool", e4[k_][:, 0:tn], cum[k_][:, 0:tn], lw[k_][:, 0:tn], ALU.subtract)
                P.act(e1[k_][:, 0:tn], cum[k_][:, 0:tn], AF.Exp)
                P.act(e2[k_][:, 0:tn], cum[k_][:, 0:tn], AF.Exp, scale=-1.0)
                P.act(e3[k_][:, 0:tn], e4[k_][:, 0:tn], AF.Exp)
                P.tt("dve", c3(e4[k_]), c3(e2[k_]), WC[k_][:, 0:nch].m(lambda a: a.unsqueeze(2)).bc([128, nch, 64]), ALU.mult)
                P.ts("dve", kkr[k_][:, 0:tn], kT[:, ts_], self.ppt[:, okk + col:okk + col + 1], None, ALU.mult)
                P.act(sqk[k_][:, 0:tn], kkr[k_][:, 0:tn], AF.Square)
                pss = self.ps(2)
                P.mm(pss[:, 0, 0:tn], cf["blk"][:], sqk[k_][:, 0:tn])
                P.act(nrm[k_][:, 0:tn], pss[:, 0, 0:tn], AF.Sqrt)
                P.ts("dve", nrm[k_][:, 0:tn], nrm[k_][:, 0:tn], 1e-12, None, ALU.max)
                P.recip(nrm[k_][:, 0:tn], nrm[k_][:, 0:tn])
                P.tt("dve", kk[k_][:, 0:tn], kkr[k_][:, 0:tn], nrm[k_][:, 0:tn], ALU.mult)
                P.ts("pool", tmpa[k_][:, 0:tn], aa[k_][:, 0:tn], self.ppt[:, oka + col:oka + col + 1], omka[:, col:col + 1], ALU.mult, ALU.add)
                P.tt("pool", kd[k_][:, 0:tn], kT[:, ts_], tmpa[k_][:, 0:tn], ALU.mult)
                P.tt("dve", bq[k_][:, 0:tn], kk[k_][:, 0:tn], aa[k_][:, 0:tn], ALU.mult)
                b3 = lambda t_: t_[:, 0:tn].re("p (b n) -> p b n", n=128)
                P.ts("pool", sqk[k_][:, 0:tn], kk[k_][:, 0:tn], -1.0, None, ALU.mult)
                P.tt("dve", arb[k_][:, 0:nbk, 0, :], b3(sqk[k_]), b3(e3[k_]), ALU.mult)
                P.tt("pool", arb[k_][:, 0:nbk, 1, :], rT[:, ts_].re("p (b n) -> p b n", n=128), b3(e1[k_]), ALU.mult)
                P.tt("dve", kb[k_][:, 0:tn], kd[k_][:, 0:tn], e2[k_][:, 0:tn], ALU.mult)
                P.tt("pool", bb[k_][:, 0:tn], bq[k_][:, 0:tn], e2[k_][:, 0:tn], ALU.mult)
                P.tt("dve", kt[k_][:, 0:tn], kd[k_][:, 0:tn], e4[k_][:, 0:tn], ALU.mult)
                P.tt("pool", btl[k_][:, 0:tn], bq[k_][:, 0:tn], e4[k_][:, 0:tn], ALU.mult)
                RW = int(os.environ.get("MK_RW", "99"))
                if RW <= 3:
                    return
                blocks = list(range(nbk)) if d == 0 else list(range(nbk - 1, -1, -1))
                for bi in blocks:
                    j_ = nblk_done % NB
                    nblk_done += 1
                    bs = slice(bi * 128, (bi + 1) * 128)
                    pt = self.ps(3)
                    P.tr(pt[:, 0, 0:128], arb[k_][:, bi, 0, :], self.c_ident[:])
                    P.tr(pt[:, 0, 128:256], btl[k_][:, bs], self.c_ident[:])
                    P.tr(pt[:, 0, 256:384], kt[k_][:, bs], self.c_ident[:])
                    P.tr(pt[:, 0, 384:512], vT[:, t0 + bi * 128:t0 + (bi + 1) * 128], self.c_ident[:])
                    tm = TM[j_]
                    P.copy("act", tm[:].re("p a n -> p (a n)"), pt[:, 0, 0:512])
                    RWB = int(os.environ.get("MK_RWB", "9"))
                    if RWB <= 1:
                        return
                    pa = self.ps(4, 2)
                    px = self.ps(6)
                    px2 = self.ps(6, 2)
                    for h in range(2):
                        hs = slice(h * 64, (h + 1) * 64)
                        ar_ = arb[k_][hs, bi, :, :].re("p a n -> p (a n)")
                        P.mm(pa[:, h, 0:256], bb[k_][hs, bs], ar_)
                        P.mm(pa[:, h, 256:512], kb[k_][hs, bs], ar_)
                        P.mm(px2[:, h, 0:128], arb[k_][hs, bi, 0, :], bb[k_][hs, bs])
                    at = AT[j_]
                    if RWB <= 2:
                        return
                    P.tt("dve", at[:].re("p h a n -> p h (a n)"), pa,
                         mask4[:].re("p a n -> p (a n)").m(lambda a: a.unsqueeze(1)).bc([128, 2, 512]), ALU.mult)
                    x1 = X1[j_]
                    if RWB <= 3:
                        return
                    P.tt("dve", x1[:], px2[:, :, 0:128], low[:].m(lambda a: a.unsqueeze(1)).bc([128, 2, 128]), ALU.mult)
                    if RW <= 4:
                        return
                    qf = Qf[j_]
                    P.tt("pool", qf[:], at[:, :, 0, :], self.c_ident[:].m(lambda a: a.unsqueeze(1)).bc([128, 2, 128]), ALU.add)
                    qb = qf
                    xk = [x1[:, h, :] for h in range(2)]
                    xtk = [at[:, h, 0, :] for h in range(2)]
                    for lev in range(5):
                        pn = self.ps(7)
                        for h in range(2):
                            P.mm(pn[:, 0, h * 256:h * 256 + 128], xtk[h], xk[h])
                            if lev < 4:
                                P.mm(pn[:, 0, h * 256 + 128:h * 256 + 256], xk[h], xtk[h])
                        xx = XX[lev % 2]
                        P.copy("act", xx[:].re("p h a n -> p (h a n)"), pn[:, 0, :])
                        xk = [xx[:, h, 0, :] for h in range(2)]
                        xtk = [xx[:, h, 1, :] for h in range(2)]
                        pq = px
                        for h in range(2):
                            P.mm(pq[:, 0, 256 + h * 128:256 + (h + 1) * 128], xk[h], qb[:, h, :])
                        P.tt("dve", qf[:], pq[:, 0, 256:512].re("p (h n) -> p h n", h=2), qf[:], ALU.add)
                    q = qf
                    if RW <= 5:
                        return
                    pw = self.ps(2)
                    for h in range(2):
                        P.mm(pw[:, 0, h * 64:(h + 1) * 64], at[:, h, 2, :], tm[:, 3, h * 64:(h + 1) * 64])
                    P.copy("act", W1[j_][:], pw[:, 0, 0:128])
                    pau = self.ps(3)
                    for h in range(2):
                        P.mm(pau[:, 0, h * 64:(h + 1) * 64], q[:, h, :], tm[:, 0, h * 64:(h + 1) * 64])
                        P.mm(pau[:, 0, 128 + h * 64:128 + (h + 1) * 64], q[:, h, :], W1[j_][:, h * 64:(h + 1) * 64])
                    au = AU[j_]
                    P.copy("act", au[:], pau[:, 0, 0:256])
                    py = self.ps(6)
                    for h in range(2):
                        hs = slice(h * 64, (h + 1) * 64)
                        P.mm(py[hs, 0, 0:128], au[:, 128 + h * 64:128 + (h + 1) * 64], at[:, h, 1, :], start=True, stop=False)
                        P.mm(py[hs, 0, 0:128], tm[:, 3, h * 64:(h + 1) * 64], at[:, h, 3, :], start=False, stop=True)
                        P.mm(py[hs, 0, 128:256], au[:, h * 64:(h + 1) * 64], at[:, h, 1, :])
                    P.copy("act", Y0T[j_][:], py[:, 0, 0:128])
                    P.tt("dve", RhT[j_][:], py[:, 0, 128:256], arb[k_][:, bi, 1, :], ALU.add)
                    if RW <= 6:
                        return
                    for cc in range(2):
                        cs = slice(cc * 64, (cc + 1) * 64)
                        mi = (nblk_done * 2 + cc) % 4
                        pmn = self.ps(7 if cc == 0 else 3)
                        P.mm(pmn[:, 0, 0:128], au[cs, 0:128], tm[cs, 1, :])
                        for h in range(2):
                            hs = slice(h * 64, (h + 1) * 64)
                            P.mm(pmn[hs, 0, 128:192], tm[cs, 1, hs], au[cs, 128 + h * 64:128 + (h + 1) * 64], start=True, stop=False)
                            P.mm(pmn[hs, 0, 128:192], tm[cs, 2, hs], tm[cs, 3, hs], start=False, stop=True)
                        mt_ = MTt[cc]
                        P.tt("dve", mt_[:], pmn[:, 0, 0:128], cf["blk"][:], ALU.mult)
                        wcol = bi * 2 + cc
                        P.stt("dve", MT[mi][:], self.c_ident[:], WC[k_][:, wcol:wcol + 1], mt_[:], ALU.mult, ALU.add)
                        P.copy("act", Ns[mi][:], pmn[:, 0, 128:192])
                    if RW <= 7:
                        return
                    for cc in ([0, 1] if d == 0 else [1, 0]):
                        cs = slice(cc * 64, (cc + 1) * 64)
                        mi = (nblk_done * 2 + cc) % 4
                        tok = slice(t0 + bi * 128 + cc * 64, t0 + bi * 128 + cc * 64 + 64)
                        pyy = self.ps(2)
                        pyh = [self.ps(0), self.ps(1)]
                        for h in range(2):
                            hs = slice(h * 64, (h + 1) * 64)
                            P.mm(pyh[h][hs, 0, 0:64], Sseq[hs, si % 4, :], RhT[j_][hs, cs])
                        P.mm(pyy[:, 0, 64:128], MT[mi][:], Sseq[:, si % 4, :])
                        for h in range(2):
                            hs = slice(h * 64, (h + 1) * 64)
                            if d == 0:
                                P.tt("dve", ysum[hs, tok], pyh[h][hs, 0, 0:64], Y0T[j_][hs, cs], ALU.add)
                            else:
                                yt_ = ytmp[cc]
                                P.tt("dve", yt_[hs, :], pyh[h][hs, 0, 0:64], Y0T[j_][hs, cs], ALU.add)
                                P.tt("pool", ysum[hs, tok], ysum[hs, tok], yt_[hs, :], ALU.add)
                        P.tt("dve", Sseq[:, (si + 1) % 4, :], pyy[:, 0, 64:128], Ns[mi][:], ALU.add)
                        si += 1
                    if RW <= 8:
                        return

    def rwkv_finish(self, l, hp, rT, kT, vT, ysum, cf, ctx_out):
        P = self.P
        wg = P.sb([128, 8, 128], BF16, "rf_wg")
        P.dma("sp", wg[:], self.wb(l, C_RG + hp * 128, 128))
        hb = [P.sb([128, 8, 512], BF16, f"rf_h{i}") for i in range(2)]
        f = lambda n: [P.sb([128, 512], F32, f"rf_{n}{i}") for i in range(2)]
        yc, sq, rstd, rk, gs = f("yc"), f("sq"), f("rstd"), f("rk"), f("gs")
        mo = [P.sb([128, 512], BF16, f"rf_mo{i}") for i in range(2)]
        olw, olb, ork = PP["lnw"][0] + hp, PP["lnb"][0] + hp, PP["rk"][0] + hp
        for ti, (t0, tn) in enumerate(TILES):
            if ti == 0 and not ctx_out:
                continue
            k_ = ti % 2
            ts_ = slice(t0, t0 + tn)
            self.load_h(hb[k_], ti, "sp" if k_ == 0 else "act")
            pg = self.ps(0)
            for kc in range(8):
                P.mm(pg[:, 0, 0:tn], wg[:, kc, :], hb[k_][:, kc, 0:tn], start=(kc == 0), stop=(kc == 7))
            P.act(gs[k_][:, 0:tn], pg[:, 0, 0:tn], AF.Silu)
            pm = self.ps(1)
            P.mm(pm[:, 0, 0:tn], cf["blk"][:], ysum[:, ts_])
            P.stt("dve", yc[k_][:, 0:tn], pm[:, 0, 0:tn], -1.0 / 64, ysum[:, ts_], ALU.mult, ALU.add)
            P.act(sq[k_][:, 0:tn], yc[k_][:, 0:tn], AF.Square)
            pv = self.ps(2)
            P.mm(pv[:, 0, 0:tn], cf["blk"][:], sq[k_][:, 0:tn])
            P.act(rstd[k_][:, 0:tn], pv[:, 0, 0:tn], AF.Sqrt, bias=64e-5, scale=1.0 / 64)
            P.recip(rstd[k_][:, 0:tn], rstd[k_][:, 0:tn])
            P.tt("dve", yc[k_][:, 0:tn], yc[k_][:, 0:tn], rstd[k_][:, 0:tn], ALU.mult)
            P.ts("dve", yc[k_][:, 0:tn], yc[k_][:, 0:tn], self.ppt[:, olw:olw + 1], self.ppt[:, olb:olb + 1], ALU.mult, ALU.add)
            P.stt("dve", rk[k_][:, 0:tn], rT[:, ts_], self.ppt[:, ork:ork + 1], kT[:, ts_], ALU.mult, ALU.mult)
            pb = self.ps(3)
            P.mm(pb[:, 0, 0:tn], cf["blk"][:], rk[k_][:, 0:tn])
            P.tt("dve", rk[k_][:, 0:tn], pb[:, 0, 0:tn], vT[:, ts_], ALU.mult)
            P.tt("pool", yc[k_][:, 0:tn], yc[k_][:, 0:tn], rk[k_][:, 0:tn], ALU.add)
            P.tt("dve", mo[k_][:, 0:tn], yc[k_][:, 0:tn], gs[k_][:, 0:tn], ALU.mult)
            r0_ = 384 + hp * 128
            P.dma("sp", self.mix.v(("mr", hp, ti), (slice(r0_, r0_ + 128), ts_)), mo[k_][:, 0:tn])

    def zero_mix(self, r0, r1):
        P = self.P
        z = P.sb([128, S], BF16, "zmix")
        P.memset("pool", z[:], 0.0)
        for kc in range(r0 // 128, r1 // 128):
            P.dma("sp", self.mix.v(("mz", kc), (slice(kc * 128, kc * 128 + 128), slice(None))), z[:])

    def build(self, layers=DEPTH):
        P = self.P
        base = P.mark()
        self.phase_w()
        P.release(base)
        for l in range(layers):
            self.phase_mod(l)
            keep = P.mark()
            self.phase_norm(l)
            P.release(keep)
            if self.stop_after == "norm":
                return
            if not os.environ.get("MK_SKIP_ATT"):
                self.phase_att(l)
                P.release(keep)
            if self.stop_after == "att":
                return
            if os.environ.get("MK_SKIP_RWKV"):
                self.zero_mix(384, 640)
            else:
                self.phase_rwkv(l)
            P.release(keep)
            if self.stop_after == "rwkv":
                return
            self.phase_ssd(l)
            P.release(keep)
            if self.stop_after == "ssd":
                return
            self.phase_out(l)
            P.release(keep)
            for a in ("n_x", "o_w"):
                if hasattr(self, a):
                    delattr(self, a)

def build_program(stop_after=None, dbg=False, layers=DEPTH):
    nc = bass.Bass("TRN2", target_bir_lowering=False)
    k = MK(nc, stop_after=stop_after, dbg=dbg)
    k.build(layers)
    k.P.emit()
    return nc, k


def prep_inputs(inp, b):
    inp = {k: np.asarray(v) for k, v in inp.items()}
    d = {}
    d["xin"] = np.ascontiguousarray(np.concatenate([inp["ctx"][b], inp["x"][b]], 0).astype(np.float32))
    d["cvec"] = np.ascontiguousarray(np.concatenate([fm(inp["c"][b]), fm(inp["c_ctx"])], 1))
    d["ada_w"] = np.ascontiguousarray(inp["ada_w"], np.float32)
    d["w_in"] = np.ascontiguousarray(inp["w_in"], np.float32)
    d["w_out"] = np.ascontiguousarray(inp["w_out"], np.float32)
    d["wup"] = np.ascontiguousarray(inp["rwkv_w_up"], np.float32)
    d["aup"] = np.ascontiguousarray(inp["rwkv_a_up"], np.float32)
    d["pp"] = np.stack([make_pp(inp, l) for l in range(DEPTH)])
    d["pr"] = np.stack([make_pr(inp, l) for l in range(DEPTH)])
    d["cst"] = make_consts()
    return d


def kernel(**inputs):
    nc, _ = build_program()
    maps = [prep_inputs(inputs, b % 4) for b in range(4)]
    in_maps = [maps[i % 4] for i in range(8)]
    res = run_bass_kernel_spmd(nc, in_maps, core_ids=list(range(8)))
    return np.stack([np.asarray(res.results[b]["y"], np.float32) for b in range(4)], 0)
```

```python
import numpy as np
import concourse.bass as bass
import concourse.mybir as mybir

F32 = mybir.dt.float32
BF16 = mybir.dt.bfloat16
ALU = mybir.AluOpType
AF = mybir.ActivationFunctionType
AX = mybir.AxisListType

import os
DUMP = os.environ.get("MK_DUMP", "")
NOSYNC = set(filter(None, os.environ.get("MK_NOSYNC", "").split(",")))
ENGS = ("pe", "dve", "act", "pool", "sp")
N_DMA_SEMS = 20


class Trk:
    __slots__ = ("lw", "rd", "ps")

    def __init__(self, ps=False):
        self.lw = None
        self.rd = []
        self.ps = ps


class V:
    __slots__ = ("ap", "trk")

    def __init__(self, ap, trk):
        self.ap = ap
        self.trk = trk

    def __getitem__(self, idx):
        return V(self.ap[idx], self.trk)

    def m(self, fn):
        return V(fn(self.ap), self.trk)

    def re(self, pat, **kw):
        return V(self.ap.rearrange(pat, **kw), self.trk)

    def bc(self, shape):
        return V(self.ap.broadcast_to(shape), self.trk)

    @property
    def shape(self):
        return self.ap.shape


class Tile:
    def __init__(self, handle):
        self.h = handle
        self.trk = Trk()

    def __getitem__(self, idx):
        return V(self.h[idx], (self.trk,))

    def ap(self):
        return V(self.h.ap() if hasattr(self.h, "ap") else self.h[:], (self.trk,))


class DTile:
    def __init__(self, handle):
        self.h = handle
        self.reg = {}

    def v(self, key, idx=None, fn=None):
        t = self.reg.setdefault(key, Trk())
        ap = self.h.ap()
        if fn is not None:
            ap = fn(ap)
        if idx is not None:
            ap = ap[idx]
        return V(ap, (t,))


class Op:
    __slots__ = ("eng", "fn", "reads", "writes", "idx", "deps", "signal", "sig_count",
                 "is_dma", "dma_sem", "dma_val", "dma_prev", "kind")


class Prog:
    def __init__(self, nc):
        self.nc = nc
        self.ops = {e: [] for e in ENGS}
        self.n_t = 0
        self.all_ops = []

    ARENA = 207 * 1024

    def sb(self, shape, dtype, name=None):
        self.n_t += 1
        if not hasattr(self, "_abase"):
            a = self.nc.alloc_sbuf_tensor("arena", [128, self.ARENA], mybir.dt.uint8)
            self._abase = self.nc.lookup_mloc(a).addr
            self._aoff = 0
        esz = 2 if dtype == BF16 else 4
        n = esz
        for d in shape[1:]:
            n *= d
        off = (self._aoff + 63) // 64 * 64
        assert off + n <= self.ARENA, f"SBUF arena overflow allocating {name} {shape}: {off + n}"
        self._aoff = off + n
        return Tile(self.nc.alloc_sbuf_tensor_at(f"{name or 'sb'}_{self.n_t}", list(shape), dtype, offset=self._abase + off))

    def mark(self):
        return getattr(self, "_aoff", 0)

    def release(self, mark):
        self.barrier()
        self._aoff = mark

    def barrier(self):
        lasts = []
        for e in ENGS:
            for op in reversed(self.ops[e]):
                if not op.is_dma and op.kind != "bar":
                    lasts.append(op)
                    break
        dmas = [op for op in self.all_ops[getattr(self, "_bar_pos", 0):] if op.is_dma]
        self._bar_pos = len(self.all_ops)
        for e in ENGS:
            op = self.rec(e, lambda eh: eh.nop(nofuse=True), [], [], kind="bar")
            op.deps = [d for d in lasts if d.eng != e] + dmas
            for d in op.deps:
                d.signal = True

    def ps(self, shape, dtype=F32, name=None):
        self.n_t += 1
        return Tile(self.nc.alloc_psum_tensor(name or f"ps{self.n_t}", list(shape), dtype))

    def dram(self, name, shape, dtype, kind="Internal"):
        return DTile(self.nc.dram_tensor(name, list(shape), dtype, kind=kind))

    def rec(self, eng, fn, reads, writes, is_dma=False, kind=""):
        op = Op()
        op.eng = eng
        op.fn = fn
        op.kind = kind
        op.is_dma = is_dma
        op.idx = len(self.ops[eng])
        op.signal = False
        op.deps = []
        rt = []
        for v in reads:
            if v is None or not isinstance(v, V):
                continue
            rt.extend(v.trk)
        wt = []
        for v in writes:
            if v is None or not isinstance(v, V):
                continue
            wt.extend(v.trk)
        deps = set()
        for t in rt:
            if t.lw is not None:
                deps.add(t.lw)
            if t.ps:
                for r in t.rd:
                    if r.eng != eng:
                        deps.add(r)
        for t in wt:
            if t.lw is not None:
                deps.add(t.lw)
            for r in t.rd:
                deps.add(r)
        deps.discard(op)
        for t in rt:
            t.rd.append(op)
        for t in wt:
            t.lw = op
            t.rd = []
        final = []
        for d in deps:
            if d.eng == "pe" and eng == "pe" and not d.is_dma and not is_dma:
                continue
            if d.eng == eng and eng in NOSYNC and not d.is_dma and not is_dma:
                continue
            final.append(d)
        op.deps = final
        for d in final:
            d.signal = True
        self.ops[eng].append(op)
        self.all_ops.append(op)
        return op

    def mm(self, out, lhsT, rhs, start=True, stop=True):
        return self.rec("pe", lambda e: e.matmul(out.ap, lhsT.ap, rhs.ap, start=start, stop=stop),
                        [lhsT, rhs], [out], kind="mm")

    def tr(self, out, in_, ident):
        return self.rec("pe", lambda e: e.transpose(out.ap, in_.ap, ident.ap), [in_, ident], [out], kind="tr")

    def act(self, out, in_, func, bias=None, scale=None, accum_out=None, eng="act"):
        kw = {}
        rd = [in_]
        if bias is not None:
            kw["bias"] = bias.ap if isinstance(bias, V) else bias
            rd.append(bias)
        if scale is not None:
            kw["scale"] = scale.ap if isinstance(scale, V) else scale
            rd.append(scale)
        wr = [out]
        if accum_out is not None:
            kw["accum_out"] = accum_out.ap
            wr.append(accum_out)
        return self.rec("act", lambda e: e.activation(out.ap, in_.ap, func, **kw), rd, wr, kind="act")

    def tt(self, eng, out, in0, in1, op):
        return self.rec(eng, lambda e: e.tensor_tensor(out.ap, in0.ap, in1.ap, op), [in0, in1], [out], kind="tt")

    def ts(self, eng, out, in0, s1, s2, op0, op1=None, accum_out=None):
        rd = [in0, s1, s2]
        a1 = s1.ap if isinstance(s1, V) else s1
        a2 = s2.ap if isinstance(s2, V) else s2
        kw = {}
        wr = [out]
        if accum_out is not None:
            kw["accum_out"] = accum_out.ap
            wr.append(accum_out)
        if op1 is None:
            return self.rec(eng, lambda e: e.tensor_scalar(out.ap, in0.ap, a1, None, op0, **kw), rd, wr, kind="ts")
        return self.rec(eng, lambda e: e.tensor_scalar(out.ap, in0.ap, a1, a2, op0, op1, **kw), rd, wr, kind="ts")

    def stt(self, eng, out, in0, scalar, in1, op0, op1):
        a = scalar.ap if isinstance(scalar, V) else scalar
        return self.rec(eng, lambda e: e.scalar_tensor_tensor(out.ap, in0.ap, a, in1.ap, op0, op1),
                        [in0, scalar, in1], [out], kind="stt")

    def copy(self, eng, out, in_):
        if eng == "act":
            return self.rec("act", lambda e: e.copy(out.ap, in_.ap), [in_], [out], kind="copy")
        return self.rec(eng, lambda e: e.tensor_copy(out.ap, in_.ap), [in_], [out], kind="copy")

    def memset(self, eng, out, val):
        return self.rec(eng, lambda e: e.memset(out.ap, val), [], [out], kind="memset")

    def reduce(self, eng, out, in_, op, axis=AX.X):
        return self.rec(eng, lambda e: e.tensor_reduce(out.ap, in_.ap, axis, op), [in_], [out], kind="red")

    def recip(self, out, in_):
        return self.rec("dve", lambda e: e.reciprocal(out.ap, in_.ap), [in_], [out], kind="recip")

    def scan(self, eng, out, d0, d1, initial, op0, op1):
        ini = initial.ap if isinstance(initial, V) else initial
        return self.rec(eng, lambda e: e.tensor_tensor_scan(out.ap, d0.ap, d1.ap, ini, op0, op1),
                        [d0, d1, initial], [out], kind="scan")

    def dma(self, q, out, in_, **kw):
        return self.rec(q, lambda e: e.dma_start(out.ap, in_.ap, **kw), [in_], [out], is_dma=True, kind="dma")

    def emit(self):
        nc = self.nc
        eng_sem = {e: nc.alloc_semaphore(f"s_{e}") for e in ENGS}
        dma_sems = {e: ([nc.alloc_semaphore(f"d_{e}_{i}") for i in range(N_DMA_SEMS)]
                        if any(o.is_dma for o in self.ops[e]) else []) for e in ENGS}
        for e in ENGS:
            cnt = 0
            nd = 0
            last_on_sem = {}
            for op in self.ops[e]:
                if op.is_dma:
                    j = nd % N_DMA_SEMS
                    nd += 1
                    op.dma_sem = dma_sems[e][j]
                    k = nd_k = (nd - 1) // N_DMA_SEMS + 1
                    op.dma_val = 16 * k
                    op.dma_prev = last_on_sem.get(j)
                    last_on_sem[j] = op
                else:
                    if op.signal:
                        cnt += 1
                        op.sig_count = cnt
        self._eng_sem = eng_sem
        handles = {"pe": "tensor", "dve": "vector", "act": "scalar", "pool": "gpsimd", "sp": "sync"}

        def run_engine(ename, eh):
            seen = {}

            def wait(sem, val):
                key = id(sem)
                if seen.get(key, 0) >= val:
                    return
                seen[key] = val
                eh.wait_ge(sem, val)

            for op in self.ops[ename]:
                for d in op.deps:
                    if d.is_dma:
                        wait(d.dma_sem, d.dma_val)
                    else:
                        wait(eng_sem[d.eng], d.sig_count)
                if op.is_dma:
                    if op.dma_prev is not None:
                        wait(op.dma_prev.dma_sem, op.dma_prev.dma_val)
                    ins = op.fn(eh)
                    ins.then_inc(op.dma_sem, 16)
                else:
                    ins = op.fn(eh)
                    if op.signal:
                        ins.then_inc(eng_sem[ename], 1)
                    if DUMP and ename in DUMP:
                        print(ename, op.idx, op.kind, ins.concise(), flush=True)
            last = {}
            for op in self.ops[ename]:
                if op.is_dma:
                    last[id(op.dma_sem)] = op
            for op in last.values():
                wait(op.dma_sem, op.dma_val)

        with nc.Block() as block:
            @block.tensor
            def _(eh):
                run_engine("pe", eh)

            @block.vector
            def _(eh):
                run_engine("dve", eh)

            @block.scalar
            def _(eh):
                run_engine("act", eh)

            @block.gpsimd
            def _(eh):
                run_engine("pool", eh)

            @block.sync
            def _(eh):
                run_engine("sp", eh)

    def stats(self):
        return {e: len(self.ops[e]) for e in ENGS}

from concourse.bass_utils import run_bass_kernel_spmd

S = 4352
CTX = 256
TLAT = 4096
DM = 1024
NIN = 3596
NBLK = 34
DEPTH = 2
TILES = [(0, 256)] + [(256 + 512 * i, 512) for i in range(8)]
NEG = -30000.0

C_Q, C_K, C_V, C_G = 0, 384, 512, 640
C_RW = 1024
C_RG = 2048
C_Z = 2304
C_XBC = 2688
C_DT = 3584


def _slots(spec):
    d, o = {}, 0
    for n, w in spec:
        d[n] = (o, w)
        o += w
    return d, o


PP, NPP = _slots([("adab", 24), ("normw", 8), ("sink", 6), ("mu0", 8), ("mu1", 8), ("w0", 4), ("a0", 4),
                  ("kk", 4), ("ka", 4), ("rk", 2), ("lnw", 2), ("lnb", 2), ("convw", 21), ("convb", 7),
                  ("ssmd", 3), ("ssmnw", 3)])
PR, NPR = _slots([("adabg", 1024), ("dtb", 12), ("alog", 12), ("fnw", 1024)])
PR2, NPR2 = _slots([("convr", 3 * 896), ("shr", 3 * 1024)])
CS, NCS = _slots([("ident", 128), ("mband", 3 * 384), ("perm", 128), ("cos", 4096), ("sin", 4096),
                  ("blk", 128), ("rmf", 256), ("rmb", 256), ("snf", 128), ("snb", 128), ("trif", 128),
                  ("trib", 128), ("ones", 128), ("hlo", 128), ("hhi", 128), ("lowf", 128), ("lowb", 128)])


def fm(v):
    v = np.asarray(v, np.float32)
    return np.ascontiguousarray(v.reshape(-1, 128).T)


def rep(v):
    v = np.asarray(v, np.float32).reshape(1, -1)
    return np.ascontiguousarray(np.broadcast_to(v, (128, v.shape[1])))


def make_consts():
    c = np.zeros((128, NCS), np.float32)

    def put(n, a):
        o, w = CS[n]
        c[:, o:o + w] = a
    put("ident", np.eye(128, dtype=np.float32))
    qi = np.arange(128)[:, None]
    kj = np.arange(384)[None, :]
    band = np.abs(kj - 128 - qi) <= 128
    mb = []
    for var in range(3):
        ok = band.copy()
        if var == 0:
            ok &= kj >= 128
        if var == 2:
            ok &= kj < 256
        mb.append(np.where(ok, 0.0, NEG))
    put("mband", np.concatenate(mb, 1))
    perm = np.zeros((128, 128), np.float32)
    for m in range(128):
        d = m % 64
        half = (d % 32) // 16
        partner = m + 16 if half == 0 else m - 16
        perm[partner, m] = 1.0
    put("perm", perm)
    rows = TLAT // 64
    row = np.repeat(np.arange(rows), 64).astype(np.float32)
    col = np.tile(np.arange(64), rows).astype(np.float32)
    pos = np.stack([row, col], -1)
    inv = (np.float32(10000.0) ** (-np.arange(16, dtype=np.float32) / np.float32(16))).astype(np.float32)
    ang = (pos[:, :, None] * inv).astype(np.float32)
    cosv, sinv = np.cos(ang).astype(np.float32), np.sin(ang).astype(np.float32)
    ct = np.zeros((128, TLAT), np.float32)
    st = np.zeros((128, TLAT), np.float32)
    for m in range(128):
        d = m % 64
        ax, half, f = d // 32, (d % 32) // 16, d % 16
        ct[m] = cosv[:, ax, f]
        st[m] = -sinv[:, ax, f] if half == 0 else sinv[:, ax, f]
    put("cos", ct)
    put("sin", st)
    blk = np.zeros((128, 128), np.float32)
    blk[:64, :64] = 1
    blk[64:, 64:] = 1
    put("blk", blk)
    s = np.arange(128)[:, None]
    t = np.arange(128)[None, :]
    same = (s // 64) == (t // 64)
    put("rmf", np.concatenate([(same & (s < t)), (same & (s <= t))], 1).astype(np.float32))
    put("rmb", np.concatenate([(same & (s > t)), (same & (s >= t))], 1).astype(np.float32))
    put("lowf", (same & (t < s)).astype(np.float32))
    put("lowb", (same & (t > s)).astype(np.float32))
    put("snf", np.where(s <= t, 0.0, NEG))
    put("snb", np.where(s >= t, 0.0, NEG))
    put("trif", (s <= t).astype(np.float32))
    put("trib", (s >= t).astype(np.float32))
    put("ones", np.ones((128, 128), np.float32))
    hlo = np.zeros((128, 128), np.float32)
    hlo[:64] = 1
    put("hlo", hlo)
    put("hhi", 1 - hlo)
    return c


def make_pp(inp, l):
    p = np.zeros((128, NPP), np.float32)

    def put(n, a):
        o, w = PP[n]
        assert a.shape == (128, w), (n, a.shape)
        p[:, o:o + w] = a
    put("adab", fm(inp["ada_b"][l]))
    put("normw", fm(inp["norm_w"][l]))
    put("sink", rep(inp["attn_sink"][l]))
    put("mu0", fm(inp["rwkv_mu"][l, 0]))
    put("mu1", fm(inp["rwkv_mu"][l, 1]))
    for n, k in (("w0", "rwkv_w0"), ("a0", "rwkv_a0"), ("kk", "rwkv_k_k"), ("ka", "rwkv_k_a")):
        put(n, np.concatenate([fm(inp[k][l, 0]), fm(inp[k][l, 1])], 1))
    put("rk", fm(inp["rwkv_r_k"][l].reshape(-1)))
    put("lnw", fm(inp["rwkv_ln_w"][l]))
    put("lnb", fm(inp["rwkv_ln_b"][l]))
    cw = inp["ssm_conv_w"][l]
    put("convw", np.stack([fm(cw[0]), fm(cw[1]), fm(cw[2])], -1).reshape(128, 21))
    put("convb", fm(inp["ssm_conv_b"][l]))
    put("ssmd", fm(np.repeat(inp["ssm_d"][l], 64)))
    put("ssmnw", fm(inp["ssm_norm_w"][l]))
    return p


def make_pr(inp, l):
    p = np.zeros((128, NPR), np.float32)

    def put(n, a):
        o, w = PR[n]
        p[:, o:o + w] = a
    put("adabg", rep(inp["ada_b"][l, 2048:3072]))
    put("dtb", rep(inp["ssm_dt_bias"][l].reshape(-1)))
    put("alog", rep(inp["ssm_a_log"][l].reshape(-1)))
    put("fnw", rep(inp["final_norm_w"]))
    return p


def make_pr2(inp, l):
    p = np.zeros((128, NPR2), np.float32)

    def put(n, a):
        o, w = PR2[n]
        p[:, o:o + w] = a
    put("convr", rep(inp["ssm_conv_w"][l].reshape(-1)))
    mu = inp["rwkv_mu"][l]
    put("shr", rep(np.concatenate([mu[0], mu[1], mu[1]], 0)))
    return p


class MK:
    def __init__(self, nc, stop_after=None, dbg=False):
        self.nc = nc
        self.P = P = Prog(nc)
        self.dbg = dbg
        self.stop_after = stop_after
        ok = "ExternalOutput" if dbg else "Internal"
        self.xin = P.dram("xin", [S, DM], F32, kind="ExternalInput")
        self.cvec = P.dram("cvec", [128, 16], F32, kind="ExternalInput")
        self.ada_w = P.dram("ada_w", [DEPTH, DM, 3 * DM], F32, kind="ExternalInput")
        self.w_in = P.dram("w_in", [DEPTH, DM, NIN], F32, kind="ExternalInput")
        self.w_out = P.dram("w_out", [DEPTH, DM, DM], F32, kind="ExternalInput")
        self.wup = P.dram("wup", [DEPTH, 2, 64, 256], F32, kind="ExternalInput")
        self.aup = P.dram("aup", [DEPTH, 2, 64, 256], F32, kind="ExternalInput")
        self.pp = P.dram("pp", [DEPTH, 128, NPP], F32, kind="ExternalInput")
        self.pr = P.dram("pr", [DEPTH, 128, NPR], F32, kind="ExternalInput")
        self.pr2 = P.dram("pr2", [DEPTH, 128, NPR2], F32, kind="ExternalInput")
        self.cst = P.dram("cst", [128, NCS], F32, kind="ExternalInput")
        self.y = P.dram("y", [TLAT, DM], F32, kind="ExternalOutput")
        self.wbin = P.dram("wbin", [DEPTH, 128, 8, NIN], BF16)
        self.wbout = P.dram("wbout", [DEPTH, 128, 8, DM], BF16)
        self.hT = P.dram("hT", [128, 8, S], BF16, kind=ok)
        self.mix = P.dram("mixT", [DM, S], BF16, kind=ok)
        self.xres = P.dram("xres", [S, DM], F32, kind=ok)
        self.yf = P.dram("yfwd", [384, S], F32)
        self.psh = nc.alloc_psum_tensor("psum", [128, 8, 512], F32)
        self.bt = [Trk(ps=True) for _ in range(8)]
        self.c_ident = P.sb([128, 128], F32, "c_ident")
        self.c_identb = P.sb([128, 128], BF16, "c_identb")
        self.load_c(self.c_ident, "ident")
        P.copy("dve", self.c_identb[:], self.c_ident[:])

    def ps(self, b0, nb=1, dt=F32):
        ap = self.psh[:, b0:b0 + nb, :]
        if dt is not F32:
            ap = ap.bitcast(dt)
        return V(ap, tuple(self.bt[b0:b0 + nb]))

    def load_c(self, tile, name, q="sp", sub=None):
        o, w = CS[name]
        if sub is not None:
            o, w = o + sub[0], sub[1]
        self.P.dma(q, tile[:], self.cst.v("c", (slice(None), slice(o, o + w))))

    def phase_w(self):
        P = self.P
        st = [P.sb([128, 8, 512], F32, f"w_st{i}") for i in range(2)]
        sb = [P.sb([128, 8, 512], BF16, f"w_sb{i}") for i in range(2)]
        it = 0
        for l in range(DEPTH):
            for (src, dst, n) in ((self.w_in, self.wbin, NIN), (self.w_out, self.wbout, DM)):
                for c0 in range(0, n, 512):
                    cw = min(512, n - c0)
                    a, b = st[it % 2], sb[it % 2]
                    q = "sp" if it % 2 == 0 else "act"
                    P.dma(q, a[:, :, 0:cw], src.v(("w", l), fn=lambda ap, l=l, c0=c0, cw=cw:
                                                   ap[l].rearrange("(kc p) n -> p kc n", p=128)[:, :, c0:c0 + cw]))
                    P.copy("dve" if it % 2 == 0 else "pool", b[:, :, 0:cw], a[:, :, 0:cw])
                    P.dma(q, dst.v(("wb", l, c0), fn=lambda ap, l=l, c0=c0, cw=cw: ap[l][:, :, c0:c0 + cw]),
                          b[:, :, 0:cw])
                    it += 1

    def wb(self, l, c0, cw):
        t0 = (c0 // 512) * 512
        trk = []
        ap = self.wbin.h.ap()[l][:, :, c0:c0 + cw]
        for t in range(t0, c0 + cw, 512):
            trk.append(self.wbin.reg.setdefault(("wb", l, t), Trk()))
        return V(ap, tuple(trk))

    def wbo(self, l, c0, cw):
        t0 = (c0 // 512) * 512
        trk = []
        ap = self.wbout.h.ap()[l][:, :, c0:c0 + cw]
        for t in range(t0, c0 + cw, 512):
            trk.append(self.wbout.reg.setdefault(("wb", l, t), Trk()))
        return V(ap, tuple(trk))

    def phase_mod(self, l):
        P = self.P
        if not hasattr(self, "ppt"):
            self.ppt = P.sb([128, NPP], F32, "ppt")
            self.prt = P.sb([128, NPR], F32, "prt")
            self.cact = P.sb([128, 8, 2], F32, "cact")
            self.modT = P.sb([128, 24, 2], F32, "modT")
            self.s1T = P.sb([128, 8, 2], F32, "s1T")
            self.gate_bc = P.sb([128, 2, 1024], F32, "gate_bc")
            craw = P.sb([128, 16], F32, "craw")
            P.dma("sp", craw[:], self.cvec.v(0))
            P.act(self.cact[:].re("p k w -> p w k"), craw[:].re("p (w k) -> p w k", w=2), AF.Silu)
        mk_ = P.mark()
        self.crep = P.sb([128, 8, 2, 128], F32, "crep")
        self.aw = [P.sb([128, 8, 512], F32, f"aw{i}") for i in range(2)]
        P.copy("dve", self.crep[:], self.cact[:].m(lambda a: a.unsqueeze(3)).bc([128, 8, 2, 128]))
        P.dma("sp", self.ppt[:], self.pp.v(l, fn=lambda a: a[l]))
        P.dma("act", self.prt[:], self.pr.v(l, fn=lambda a: a[l]))
        for t in range(6):
            a = self.aw[t % 2]
            P.dma("sp" if t % 2 == 0 else "act", a[:],
                  self.ada_w.v(("aw", l), fn=lambda ap, t=t: ap[l].rearrange("(kc p) n -> p kc n", p=128)[:, :, t * 512:(t + 1) * 512]))
            pm = self.ps(0)
            for j in range(4):
                for kc in range(8):
                    P.mm(pm[:, 0, j * 2:(j + 1) * 2], a[:, kc, j * 128:(j + 1) * 128], self.cact[:, kc, :],
                         start=(kc == 0), stop=(kc == 7))
            P.copy("dve", self.modT[:, t * 4:(t + 1) * 4, :], pm[:, 0, 0:8].re("p (j w) -> p j w", w=2))
            if t >= 4:
                for w in range(2):
                    pg = self.ps(1 + w)
                    for kc in range(8):
                        P.mm(pg[:, 0, :], self.crep[:, kc, w, :], a[:, kc, :], start=(kc == 0), stop=(kc == 7))
                    o = PR["adabg"][0] + (t - 4) * 512
                    P.tt("dve", self.gate_bc[:, w, (t - 4) * 512:(t - 3) * 512], pg[:, 0, :], self.prt[:, o:o + 512], ALU.add)
        o = PP["adab"][0]
        P.tt("dve", self.modT[:], self.modT[:], self.ppt[:, o:o + 24].m(lambda a: a.unsqueeze(2)).bc([128, 24, 2]), ALU.add)
        o = PP["normw"][0]
        P.stt("dve", self.s1T[:], self.modT[:, 8:16, :], 1.0,
              self.ppt[:, o:o + 8].m(lambda a: a.unsqueeze(2)).bc([128, 8, 2]), ALU.add, ALU.mult)
        P.release(mk_)

    def phase_norm(self, l):
        P = self.P
        if not hasattr(self, "n_x"):
            self.n_x = [P.sb([128, DM], F32, f"n_x{i}") for i in range(2)]
            self.n_junk = P.sb([128, DM], F32, "n_junk")
            self.n_xb = [P.sb([128, DM], BF16, f"n_xb{i}") for i in range(2)]
            self.n_ss = [P.sb([128, 4], F32, f"n_ss{i}") for i in range(2)]
            self.n_h = [P.sb([128, 8, 512], BF16, f"n_h{i}") for i in range(2)]
            self.n_t = [P.sb([128, 8, 128], F32, f"n_t{i}") for i in range(2)]
        src = self.xin if l == 0 else self.xres
        for ti, (t0, tn) in enumerate(TILES):
            w = 1 if ti == 0 else 0
            hb = self.n_h[ti % 2]
            for bi in range(tn // 128):
                blk = (t0 // 128) + bi
                xt, xb, ss = self.n_x[blk % 2], self.n_xb[blk % 2], self.n_ss[blk % 2]
                P.dma("sp" if blk % 2 == 0 else "act", xt[:], src.v(("x", blk), (slice(blk * 128, blk * 128 + 128), slice(None))))
                P.act(self.n_junk[:], xt[:], AF.Square, accum_out=ss[:, 0:1])
                P.act(ss[:, 1:2], ss[:, 0:1], AF.Sqrt, bias=1e-6, scale=1.0 / DM)
                P.recip(ss[:, 2:3], ss[:, 1:2])
                P.ts("dve", xb[:], xt[:], ss[:, 2:3], None, ALU.mult)
                pt = self.ps(2 + blk % 2, 1, BF16)
                for kc in range(8):
                    P.tr(pt[:, 0, kc * 128:(kc + 1) * 128], xb[:, kc * 128:(kc + 1) * 128], self.c_identb[:])
                tmp = self.n_t[blk % 2]
                P.tt("dve", tmp[:], pt[:, 0, :].re("p (k t) -> p k t", k=8),
                     self.s1T[:, :, w:w + 1].bc([128, 8, 128]), ALU.mult)
                P.tt("pool", hb[:, :, bi * 128:(bi + 1) * 128], tmp[:],
                     self.modT[:, 0:8, w:w + 1].bc([128, 8, 128]), ALU.add)
            P.dma("sp", self.hT.v(("h", ti), (slice(None), slice(None), slice(t0, t0 + tn))), hb[:, :, 0:tn])

    def phase_out(self, l):
        P = self.P
        last = (l == DEPTH - 1)
        if not hasattr(self, "o_w"):
            self.o_w = P.sb([128, 8, DM], BF16, "o_w")
            self.o_m = [P.sb([128, 8, 128], BF16, f"o_m{i}") for i in range(2)]
            self.o_x = [P.sb([128, DM], F32, f"o_x{i}") for i in range(2)]
            self.o_t = [P.sb([128, DM], F32, f"o_t{i}") for i in range(2)]
            self.o_ss = [P.sb([128, 4], F32, f"o_ss{i}") for i in range(2)]
        P.dma("sp", self.o_w[:], self.wbo(l, 0, DM))
        src = self.xin if l == 0 else self.xres
        for blk in range(NBLK):
            if last and blk < 2:
                continue
            w = 1 if blk < 2 else 0
            m, xt, tt_, ss = self.o_m[blk % 2], self.o_x[blk % 2], self.o_t[blk % 2], self.o_ss[blk % 2]
            tsl = slice(blk * 128, blk * 128 + 128)
            P.dma("sp", m[:], self.mix.v(("m", blk), fn=lambda ap, tsl=tsl: ap.rearrange("(kc p) t -> p kc t", p=128)[:, :, tsl]))
            P.dma("act", xt[:], src.v(("x", blk), (tsl, slice(None))))
            for hf in range(2):
                po = self.ps(4 + 2 * (blk % 2) + hf)
                for kc in range(8):
                    P.mm(po[:, 0, :], m[:, kc, :], self.o_w[:, kc, hf * 512:(hf + 1) * 512], start=(kc == 0), stop=(kc == 7))
                P.tt("dve", tt_[:, hf * 512:(hf + 1) * 512], po[:, 0, :], self.gate_bc[:, w, hf * 512:(hf + 1) * 512], ALU.mult)
            P.tt("pool", tt_[:], tt_[:], xt[:], ALU.add)
            if not last:
                P.dma("sp", self.xres.v(("x", blk), (tsl, slice(None))), tt_[:])
            else:
                P.act(xt[:], tt_[:], AF.Square, accum_out=ss[:, 0:1])
                P.act(ss[:, 1:2], ss[:, 0:1], AF.Sqrt, bias=1e-6, scale=1.0 / DM)
                P.recip(ss[:, 2:3], ss[:, 1:2])
                o = PR["fnw"][0]
                P.stt("dve", xt[:], tt_[:], ss[:, 2:3], self.prt[:, o:o + DM], ALU.mult, ALU.mult)
                P.dma("sp", self.y.v(("y", blk), (slice(blk * 128 - CTX, blk * 128 - CTX + 128), slice(None))), xt[:])

    def load_h(self, tile, ti, q="sp"):
        t0, tn = TILES[ti]
        self.P.dma(q, tile[:, :, 0:tn], self.hT.v(("h", ti), (slice(None), slice(None), slice(t0, t0 + tn))))

    def phase_att(self, l):
        P = self.P
        ctx_out = (l < DEPTH - 1)
        import os
        sm = int(os.environ.get("MK_SET", "63"))
        wq = P.sb([128, 8, 384], BF16, "a_wq")
        wg = P.sb([128, 8, 384], BF16, "a_wg")
        wk = P.sb([128, 8, 128], BF16, "a_wk")
        wv = P.sb([128, 8, 128], BF16, "a_wv")
        if sm & 1:
            for j in range(3):
                for hh in range(2):
                    h = j + 3 * hh
                    P.dma("sp", wq[:, :, j * 128 + hh * 64: j * 128 + hh * 64 + 64], self.wb(l, C_Q + h * 64, 64))
                    P.dma("act", wg[:, :, j * 128 + hh * 64: j * 128 + hh * 64 + 64], self.wb(l, C_G + h * 64, 64))
            P.dma("sp", wk[:], self.wb(l, C_K, 128))
            P.dma("act", wv[:], self.wb(l, C_V, 128))
        cos = P.sb([128, TLAT], F32, "a_cos")
        sin = P.sb([128, TLAT], F32, "a_sin")
        if sm & 2:
            self.load_c(cos, "cos", "sp")
            self.load_c(sin, "sin", "act")
        pf = P.sb([128, 128], F32, "a_pf")
        permb = P.sb([128, 128], BF16, "a_permb")
        if sm & 4:
            self.load_c(pf, "perm")
            P.copy("dve", permb[:], pf[:])
        mband = P.sb([128, 3, 384], F32, "a_mband")
        o, w_ = CS["mband"]
        if sm & 8:
            P.dma("sp", mband[:].re("p a b -> p (a b)"), self.cst.v("c", (slice(None), slice(o, o + w_))))
        sink8 = P.sb([128, 6], F32, "a_sink8")
        o = PP["sink"][0]
        if sm & 16:
            P.ts("dve", sink8[:], self.ppt[:, o:o + 6], 8.0, None, ALU.mult)
        KB = P.sb([128, 4608], BF16, "a_KB")
        Vtm = P.sb([128, 36, 128], BF16, "a_V")
        if sm & 32:
            P.memset("pool", KB[:, 256:384], 0.0)
            P.memset("pool", KB[:, 4480:4608], 0.0)
            P.memset("pool", Vtm[:, 2, :], 0.0)
            P.memset("pool", Vtm[:, 35, :], 0.0)
        hb = [P.sb([128, 8, 512], BF16, f"a_h{i}") for i in range(2)]
        kf = P.sb([128, 512], F32, "a_kf")
        t1 = [P.sb([128, 512], F32, f"a_t1{i}") for i in range(2)]
        t2 = [P.sb([128, 512], F32, f"a_t2{i}") for i in range(2)]

        def kcol(tok):
            return tok if tok < CTX else tok + 128

        def proj(dst, wt, c0, cw, hbt, tn, lat_t0, bank, func=None):
            pp_ = self.ps(bank)
            for kc in range(8):
                P.mm(pp_[:, 0, 0:tn], wt[:, kc, c0:c0 + cw], hbt[:, kc, 0:tn], start=(kc == 0), stop=(kc == 7))
            if func is not None:
                P.act(dst, pp_[:, 0, 0:tn], func)
                return
            if lat_t0 is None:
                P.copy("act", dst, pp_[:, 0, 0:tn])
                return
            P.copy("act", kf[:, 0:tn], pp_[:, 0, 0:tn])
            pr_ = self.ps(2)
            P.mm(pr_[:, 0, 0:tn], pf[:], kf[:, 0:tn])
            a, b = t1[bank % 2], t2[bank % 2]
            P.tt("pool", a[:, 0:tn], kf[:, 0:tn], cos[:, lat_t0:lat_t0 + tn], ALU.mult)
            P.tt("dve", b[:, 0:tn], pr_[:, 0, 0:tn], sin[:, lat_t0:lat_t0 + tn], ALU.mult)
            P.tt("pool", dst, a[:, 0:tn], b[:, 0:tn], ALU.add)

        stage = int(os.environ.get("MK_ATT", "9"))
        if stage <= 0:
            return
        for ti, (t0, tn) in enumerate(TILES):
            h = hb[ti % 2]
            self.load_h(h, ti, "sp" if ti % 2 == 0 else "act")
            kc0 = kcol(t0)
            ma = int(os.environ.get("MK_A", "3"))
            if ti >= int(os.environ.get("MK_AT", "9")):
                break
            if ma & 1:
                proj(KB[:, kc0:kc0 + tn], wk, 0, 128, h, tn, None if ti == 0 else t0 - CTX, ti % 2)
            for bi in range(tn // 128):
                if not (ma & 2):
                    break
                vb = (t0 // 128 + bi)
                vb = vb if vb < 2 else vb + 1
                pv = self.ps(3 + bi % 2)
                for kc in range(8):
                    P.mm(pv[:, 0, 0:128], h[:, kc, bi * 128:(bi + 1) * 128], wv[:, kc, :], start=(kc == 0), stop=(kc == 7))
                P.copy("dve" if bi % 2 == 0 else "act", Vtm[:, vb, :], pv[:, 0, 0:128])

        import os
        stage = int(os.environ.get("MK_ATT", "9"))
        if stage <= 1:
            return
        qT = [P.sb([128, 3, 512], BF16, f"a_q{i}") for i in range(2)]
        gT = [P.sb([128, 3, 512], BF16, f"a_g{i}") for i in range(2)]
        om = [P.sb([128, 3, 512], BF16, f"a_om{i}") for i in range(2)]
        sc = [P.sb([128, 3, 640], F32, f"a_sc{i}") for i in range(2)]
        pe = [P.sb([128, 3, 640], F32, f"a_pe{i}") for i in range(2)]
        pn = [P.sb([128, 3, 640], BF16, f"a_pn{i}") for i in range(2)]
        pTs = [P.sb([128, 5, 128], BF16, f"a_pT{i}") for i in range(3)]
        st = [P.sb([128, 8, 3], F32, f"a_st{i}") for i in range(2)]
        it = 0
        npt = 0
        for ti, (t0, tn) in enumerate(TILES):
            if ti == 0 and not ctx_out:
                continue
            h = hb[ti % 2]
            self.load_h(h, ti, "sp" if ti % 2 == 0 else "act")
            q_, g_, o_ = qT[ti % 2], gT[ti % 2], om[ti % 2]
            for j in range(3):
                proj(q_[:, j, 0:tn], wq, j * 128, 128, h, tn, None if ti == 0 else t0 - CTX, j % 2)
                proj(g_[:, j, 0:tn], wg, j * 128, 128, h, tn, None, (j + 1) % 2, func=AF.Silu)
            if stage <= 2:
                break
            for bi in range(tn // 128):
                if stage <= 5 and (bi > 0 or ti > 1):
                    break
                qs = slice(bi * 128, bi * 128 + 128)
                isctx = (ti == 0)
                n = (t0 - CTX) // 128 + bi if not isctx else None
                lo = 384 if isctx else 0
                po = self.ps(2)
                for kg in range(2):
                    ps_ = slice(kg * 64, kg * 64 + 64)
                    s_, e_, n_, stt_ = sc[it % 2], pe[it % 2], pn[it % 2], st[it % 2]
                    it += 1
                    for j in range(3):
                        pb = self.ps(3 + 2 * (j % 2))
                        pc = self.ps(4 + 2 * (j % 2))
                        if not isctx:
                            kb0 = 256 + n * 128
                            P.mm(pb[:, 0, 0:384], q_[ps_, j, qs], KB[ps_, kb0:kb0 + 384])
                            var = 0 if n == 0 else (2 if n == 31 else 1)
                            P.tt("dve", s_[:, j, 0:384], pb[:, 0, 0:384], mband[:, var, :], ALU.add)
                        P.mm(pc[:, 0, 0:256], q_[ps_, j, qs], KB[ps_, 0:256])
                        P.copy("act", s_[:, j, 384:640], pc[:, 0, 0:256])
                    if stage <= 3:
                        continue
                    P.reduce("dve", stt_[:, 0, :], s_[:, :, lo:640], ALU.max)
                    P.tt("dve", stt_[:, 1, :], stt_[:, 0, :], sink8[:, kg * 3:kg * 3 + 3], ALU.max)
                    P.ts("dve", stt_[:, 2, :], stt_[:, 1, :], -0.125, None, ALU.mult)
                    for j in range(3):
                        P.act(e_[:, j, lo:640], s_[:, j, lo:640], AF.Exp, bias=stt_[:, 2, j:j + 1], scale=0.125,
                              accum_out=stt_[:, 3, j:j + 1])
                    P.stt("dve", stt_[:, 4, :], sink8[:, kg * 3:kg * 3 + 3], 0.125, stt_[:, 2, :], ALU.mult, ALU.add)
                    P.act(stt_[:, 5, :], stt_[:, 4, :], AF.Exp)
                    P.tt("dve", stt_[:, 6, :], stt_[:, 5, :], stt_[:, 3, :], ALU.add)
                    P.recip(stt_[:, 7, :], stt_[:, 6, :])
                    P.tt("dve", n_[:, :, lo:640], e_[:, :, lo:640],
                         stt_[:, 7, :].m(lambda a: a.unsqueeze(2)).bc([128, 3, 640 - lo]), ALU.mult)
                    if stage <= 4:
                        continue
                    for j in range(3):
                        pt = self.ps(7, 1, BF16)
                        chunks = ([] if isctx else [0, 1, 2]) + [3, 4]
                        for c in chunks:
                            P.tr(pt[:, 0, c * 128:(c + 1) * 128], n_[:, j, c * 128:(c + 1) * 128], self.c_identb[:])
                        pts = pTs[npt % 3]
                        npt += 1
                        c0 = chunks[0]
                        if npt % 2 == 0:
                            P.copy("act", pts[:, c0:5, :], pt[:, 0, c0 * 128:640].re("p (c t) -> p c t", t=128))
                        else:
                            P.copy("dve", pts[:, c0:5, :], pt[:, 0, c0 * 128:640].re("p (c t) -> p c t", t=128))
                        for ci, c in enumerate(chunks):
                            vb = (2 + n + c) if c < 3 else (c - 3)
                            P.mm(po[ps_, 0, j * 128:(j + 1) * 128], Vtm[:, vb, kg * 64:kg * 64 + 64], pts[:, c, :],
                                 start=(ci == 0), stop=(ci == len(chunks) - 1))
                P.tt("dve", o_[:, :, qs], po[:, 0, 0:384].re("p (j t) -> p j t", j=3), g_[:, :, qs], ALU.mult)
            for j in range(3):
                for hh in range(2):
                    hd = j + 3 * hh
                    P.dma("sp" if hh == 0 else "act",
                          self.mix.v(("ma", ti, hd), (slice(hd * 64, hd * 64 + 64), slice(t0, t0 + tn))),
                          o_[hh * 64:hh * 64 + 64, j, 0:tn])

    def load_h_halo(self, tile, ti, q="sp"):
        P = self.P
        t0, tn = TILES[ti]
        lo, hi = t0 - 1, t0 + tn + 1
        if ti <= 1:
            P.memset("pool", tile[:, :, 0:1], 0.0)
            lo = t0
        if ti == 0 or ti == len(TILES) - 1:
            P.memset("pool", tile[:, :, tn + 1:tn + 2], 0.0)
            hi = t0 + tn
        P.dma(q, tile[:, :, lo - t0 + 1:hi - t0 + 1], self.hT.v(("h", ti), (slice(None), slice(None), slice(lo, hi))))

    def scaled_w(self, l, c0, ncol, rows, name, roff=0):
        P = self.P
        rows = [r[:, roff:roff + ncol] for r in rows]
        w3 = P.sb([128, 3, 8, ncol], BF16, name)
        mk_ = P.mark()
        half = ncol // 2
        stg = [P.sb([128, 8, half], F32, f"{name}_st{i}") for i in range(2)]
        for hf in range(2):
            st = stg[hf]
            P.dma("sp" if hf == 0 else "act", st[:],
                  self.w_in.v(("w", l), fn=lambda ap, hf=hf: ap[l].rearrange("(kc p) n -> p kc n", p=128)[:, :, c0 + hf * half:c0 + (hf + 1) * half]))
            for tap in range(3):
                P.tt("dve" if tap != 1 else "pool", w3[:, tap, :, hf * half:(hf + 1) * half], st[:],
                     rows[tap][:, hf * half:(hf + 1) * half].m(lambda a: a.unsqueeze(1)).bc([128, 8, half]), ALU.mult)
        P.release(mk_)
        return w3

    def phase_ssd(self, l):
        P = self.P
        ctx_out = (l < DEPTH - 1)
        xsT = P.sb([128, 3, S], F32, "s_xsT")
        BT = P.sb([128, 2, S], BF16, "s_BT")
        CT = P.sb([128, 2, S], BF16, "s_CT")
        dt_tm = P.sb([128, NBLK, 12], F32, "s_dt")
        dtA_tm = P.sb([128, NBLK, 12], F32, "s_dtA")
        aneg = P.sb([128, 12], F32, "s_aneg")
        o = PR["alog"][0]
        P.act(aneg[:], self.prt[:, o:o + 12], AF.Exp)
        P.ts("dve", aneg[:], aneg[:], -1.0, None, ALU.mult)
        mk1 = P.mark()
        o = PR2["convr"][0]
        crow = P.sb([128, 3 * 896], F32, "s_crow")
        P.dma("sp", crow[:], self.pr2.v(l, fn=lambda a: a[l][:, o:o + 3 * 896]))
        rows = [crow[:, k * 896:(k + 1) * 896] for k in range(3)]
        wx3 = self.scaled_w(l, C_XBC, 896, rows, "s_wx3")
        wdt = P.sb([128, 8, 12], BF16, "s_wdt")
        P.dma("sp", wdt[:], self.wb(l, C_DT, 12))
        hh = [P.sb([128, 8, 514], BF16, f"s_hh{i}") for i in range(2)]
        dtmp = [P.sb([128, 12], F32, f"s_dtmp{i}") for i in range(2)]
        ocb = PP["convb"][0]
        nb = 0
        for ti, (t0, tn) in enumerate(TILES):
            h = hh[ti % 2]
            self.load_h_halo(h, ti, "sp" if ti % 2 == 0 else "act")
            for c in range(7):
                pb = self.ps(c % 2)
                n = 0
                for tap in range(3):
                    for kc in range(8):
                        P.mm(pb[:, 0, 0:tn], wx3[:, tap, kc, c * 128:(c + 1) * 128], h[:, kc, tap:tap + tn],
                             start=(n == 0), stop=(n == 23))
                        n += 1
                if c < 3:
                    dst = xsT[:, c, t0:t0 + tn]
                elif c < 5:
                    dst = BT[:, c - 3, t0:t0 + tn]
                else:
                    dst = CT[:, c - 5, t0:t0 + tn]
                P.act(dst, pb[:, 0, 0:tn], AF.Silu, bias=self.ppt[:, ocb + c:ocb + c + 1])
            for bi in range(tn // 128):
                blk = t0 // 128 + bi
                pd = self.ps(2 + blk % 2)
                for kc in range(8):
                    P.mm(pd[:, 0, 0:12], h[:, kc, 1 + bi * 128:1 + (bi + 1) * 128], wdt[:, kc, :], start=(kc == 0), stop=(kc == 7))
                d_ = dtmp[blk % 2]
                o = PR["dtb"][0]
                P.tt("dve", d_[:], pd[:, 0, 0:12], self.prt[:, o:o + 12], ALU.add)
                P.act(d_[:], d_[:], AF.Exp)
                P.act(dt_tm[:, blk, :], d_[:], AF.Ln, bias=1.0)
                P.tt("dve", dtA_tm[:, blk, :], dt_tm[:, blk, :], aneg[:], ALU.mult)
        P.release(mk1)
        if int(os.environ.get("MK_SSD", "9")) <= 1:
            self.dbg_xsT = xsT
            return
        wz = P.sb([128, 8, 384], BF16, "s_wz")
        P.dma("sp", wz[:], self.wb(l, C_Z, 384))
        cf = {}
        for nme in ("snf", "snb", "trif", "trib", "ones", "hlo", "hhi"):
            cf[nme] = P.sb([128, 128], F32, "s_c_" + nme)
            self.load_c(cf[nme], nme, "act")
        St = P.sb([128, 6, 64], F32, "s_S")
        NB2 = 2
        xs_sb = [P.sb([128, 384], F32, f"s_xs{i}") for i in range(NB2)]
        Bt_sb = [P.sb([128, 2, 128], BF16, f"s_Bt{i}") for i in range(NB2)]
        Gs = [P.sb([128, 2, 128], F32, f"s_G{i}") for i in range(NB2)]
        cumc = [P.sb([128, 6], F32, f"s_cc{i}") for i in range(NB2)]
        dbc = [P.sb([128, 6, 128], F32, f"s_dbc{i}") for i in range(NB2)]
        Dm = [P.sb([128, 6, 128], F32, f"s_D{i}") for i in range(NB2)]
        Em = [P.sb([128, 6, 128], F32, f"s_E{i}") for i in range(NB2)]
        scT = [P.sb([128, 6, 128], BF16, f"s_sc{i}") for i in range(NB2)]
        Er = [P.sb([128, 6, 128], F32, f"s_Er{i}") for i in range(NB2)]
        Cd = [P.sb([128, 6, 128], F32, f"s_Cd{i}") for i in range(NB2)]
        xdt = [P.sb([128, 6, 64], BF16, f"s_xdt{i}") for i in range(NB2)]
        xdtt = [P.sb([128, 6, 64], BF16, f"s_xdtt{i}") for i in range(NB2)]
        ysb = [P.sb([128, 3, 128], F32, f"s_y{i}") for i in range(NB2)]
        yfl = [P.sb([128, 3, 128], F32, f"s_yf{i}") for i in range(NB2)]
        hz = [P.sb([128, 8, 128], BF16, f"s_hz{i}") for i in range(NB2)]
        zs = [P.sb([128, 3, 128], F32, f"s_z{i}") for i in range(NB2)]
        sq = [P.sb([128, 3, 128], F32, f"s_sq{i}") for i in range(NB2)]
        rs = [P.sb([128, 2, 128], F32, f"s_rs{i}") for i in range(NB2)]
        mo = [P.sb([128, 3, 128], BF16, f"s_mo{i}") for i in range(NB2)]
        osd, onw = PP["ssmd"][0], PP["ssmnw"][0]
        it = 0
        for d in range(2):
            order = list(range(NBLK)) if d == 0 else [1, 0] + list(range(NBLK - 1, 1, -1))
            tri = cf["trif"] if d == 0 else cf["trib"]
            sneg = cf["snf"] if d == 0 else cf["snb"]
            last = 127 if d == 0 else 0
            P.memset("dve", St[:], 0.0)
            for c in order:
                if d == 1 and c < 2 and not ctx_out:
                    pass
                k = it % NB2
                it += 1
                cs_ = slice(c * 128, (c + 1) * 128)
                pxs = self.ps(0)
                for j in range(3):
                    P.tr(pxs[:, 0, j * 128:(j + 1) * 128], xsT[:, j, cs_], self.c_ident[:])
                P.copy("act", xs_sb[k][:], pxs[:, 0, 0:384])
                pbt = self.ps(1, 1, BF16)
                for g in range(2):
                    P.tr(pbt[:, 0, g * 128:(g + 1) * 128], BT[:, g, cs_], self.c_identb[:])
                P.copy("dve", Bt_sb[k][:].re("p g n -> p (g n)"), pbt[:, 0, 0:256])
                pg = self.ps(2)
                for g in range(2):
                    P.mm(pg[:, 0, g * 128:(g + 1) * 128], BT[:, g, cs_], CT[:, g, cs_])
                P.copy("act", Gs[k][:].re("p g n -> p (g n)"), pg[:, 0, 0:256])
                dA = dtA_tm[:, c, d * 6:(d + 1) * 6]
                P.mm(pg[:, 0, 256:262], tri[:], dA)
                P.copy("dve", cumc[k][:], pg[:, 0, 256:262])
                P.copy("pool", dbc[k][:], dA.m(lambda a: a.unsqueeze(2)).bc([128, 6, 128]))
                pcr = self.ps(3, 2)
                for h in range(6):
                    P.mm(pcr[:, h // 4, (h % 4) * 128:(h % 4 + 1) * 128], dbc[k][:, h, :], tri[:])
                pcr_v = pcr.re("p b n -> p (b n)")[:, 0:768].re("p (h n) -> p h n", h=6)
                P.tt("dve", Dm[k][:], pcr_v, cumc[k][:].m(lambda a: a.unsqueeze(2)).bc([128, 6, 128]), ALU.subtract)
                P.act(Er[k][:], pcr_v, AF.Exp)
                P.tt("pool", Dm[k][:], Dm[k][:], sneg[:].m(lambda a: a.unsqueeze(1)).bc([128, 6, 128]), ALU.add)
                P.act(Em[k][:], Dm[k][:], AF.Exp)
                P.tt("pool", scT[k][:].re("p (g h) n -> p g h n", g=2), Em[k][:].re("p (g h) n -> p g h n", g=2),
                     Gs[k][:].m(lambda a: a.unsqueeze(2)).bc([128, 2, 3, 128]), ALU.mult)
                P.tt("dve", Cd[k][:].re("p (g h) n -> p g h n", g=2), Er[k][:].re("p (g h) n -> p g h n", g=2),
                     CT[:, :, cs_].m(lambda a: a.unsqueeze(2)).bc([128, 2, 3, 128]), ALU.mult)
                dtc = dt_tm[:, c, d * 6:(d + 1) * 6]
                P.tt("dve", xdt[k][:], xs_sb[k][:].re("p (h q) -> p h q", h=6),
                     dtc.m(lambda a: a.unsqueeze(2)).bc([128, 6, 64]), ALU.mult)
                P.tt("pool", xdtt[k][:], xdt[k][:], Em[k][:, :, last:last + 1].bc([128, 6, 64]), ALU.mult)
                py = self.ps(5)
                for h in range(6):
                    o_ = py[(h % 2) * 64:(h % 2) * 64 + 64, 0, (h // 2) * 128:(h // 2 + 1) * 128]
                    P.mm(o_, xdt[k][:, h, :], scT[k][:, h, :], start=True, stop=False)
                    P.mm(o_, St[:, h, :], Cd[k][:, h, :], start=False, stop=True)
                pcs = self.ps(6)
                for h in range(6):
                    P.mm(pcs[:, 0, h * 64:(h + 1) * 64], Bt_sb[k][:, h // 3, :], xdtt[k][:, h, :])
                P.tt("dve", St[:], St[:], Er[k][:, :, last:last + 1].bc([128, 6, 64]), ALU.mult)
                P.tt("dve", St[:], St[:], pcs[:, 0, 0:384].re("p (h q) -> p h q", h=6), ALU.add)
                if d == 0:
                    P.copy("act", ysb[k][:].re("p j n -> p (j n)"), py[:, 0, 0:384])
                    P.dma("sp", self.yf.v(("yf", c), fn=lambda ap, cs_=cs_: ap.rearrange("(j p) t -> p j t", p=128)[:, :, cs_]), ysb[k][:])
                    continue
                if c < 2 and not ctx_out:
                    continue
                P.dma("act", yfl[k][:], self.yf.v(("yf", c), fn=lambda ap, cs_=cs_: ap.rearrange("(j p) t -> p j t", p=128)[:, :, cs_]))
                P.dma("sp", hz[k][:], self.hT.v(("h", "z"), (slice(None), slice(None), cs_)))
                pz = self.ps(7)
                for j in range(3):
                    for kc in range(8):
                        P.mm(pz[:, 0, j * 128:(j + 1) * 128], wz[:, kc, j * 128:(j + 1) * 128], hz[k][:, kc, :],
                             start=(kc == 0), stop=(kc == 7))
                P.act(zs[k][:].re("p j n -> p (j n)"), pz[:, 0, 0:384], AF.Silu)
                y_ = ysb[k]
                P.tt("dve", y_[:].re("p j n -> p (j n)"), py[:, 0, 0:384], yfl[k][:].re("p j n -> p (j n)"), ALU.add)
                P.tt("pool", yfl[k][:], xsT[:, :, cs_], self.ppt[:, osd:osd + 3].m(lambda a: a.unsqueeze(2)).bc([128, 3, 128]), ALU.mult)
                P.tt("pool", y_[:], y_[:], yfl[k][:], ALU.add)
                P.tt("dve", y_[:], y_[:], zs[k][:], ALU.mult)
                P.act(sq[k][:], y_[:], AF.Square)
                pgs = self.ps(7)
                P.mm(pgs[:, 0, 384:512].m(lambda a: a), cf["ones"][:], sq[k][:, 0, :], start=True, stop=False)
                P.mm(pgs[:, 0, 384:512], cf["hlo"][:], sq[k][:, 1, :], start=False, stop=True)
                pgs2 = self.ps(6)
                P.mm(pgs2[:, 0, 384:512], cf["hhi"][:], sq[k][:, 1, :], start=True, stop=False)
                P.mm(pgs2[:, 0, 384:512], cf["ones"][:], sq[k][:, 2, :], start=False, stop=True)
                P.act(rs[k][:, 0, :], pgs[:, 0, 384:512], AF.Sqrt, bias=1e-5, scale=1.0 / 192)
                P.act(rs[k][:, 1, :], pgs2[:, 0, 384:512], AF.Sqrt, bias=1e-5, scale=1.0 / 192)
                P.recip(rs[k][:], rs[k][:])
                m_ = mo[k]
                P.stt("dve", m_[:, 0, :], y_[:, 0, :], self.ppt[:, onw:onw + 1], rs[k][:, 0, :], ALU.mult, ALU.mult)
                P.stt("dve", m_[0:64, 1, :], y_[0:64, 1, :], self.ppt[0:64, onw + 1:onw + 2], rs[k][0:64, 0, :], ALU.mult, ALU.mult)
                P.stt("dve", m_[64:128, 1, :], y_[64:128, 1, :], self.ppt[64:128, onw + 1:onw + 2], rs[k][64:128, 1, :], ALU.mult, ALU.mult)
                P.stt("dve", m_[:, 2, :], y_[:, 2, :], self.ppt[:, onw + 2:onw + 3], rs[k][:, 1, :], ALU.mult, ALU.mult)
                P.dma("sp", self.mix.v(("ms", c), fn=lambda ap, cs_=cs_: ap[640:1024].rearrange("(j p) t -> p j t", p=128)[:, :, cs_]), m_[:])

    def phase_rwkv(self, l):
        P = self.P
        ctx_out = (l < DEPTH - 1)
        LWC = -0.6065306597126334
        cf = {}
        for nme in ("blk", "lowf", "lowb"):
            cf[nme] = P.sb([128, 128], F32, "r_c_" + nme)
            self.load_c(cf[nme], nme, "act")
        m4 = {nme: P.sb([128, 2, 256], F32, "r_m4_" + nme) for nme in ("rmf", "rmb")}
        mk0 = P.mark()
        for nme in ("rmf", "rmb"):
            t_ = P.sb([128, 256], F32, "r_c_" + nme)
            self.load_c(t_, nme, "act")
            P.copy("dve", m4[nme][:], t_[:].m(lambda a: a.unsqueeze(1)).bc([128, 2, 256]))
        P.release(mk0)
        ones64 = P.sb([128, 64], F32, "r_ones")
        P.memset("pool", ones64[:], 1.0)
        omka = P.sb([128, 4], F32, "r_omka")
        oka = PP["ka"][0]
        P.ts("dve", omka[:], self.ppt[:, oka:oka + 4], -1.0, 1.0, ALU.mult, ALU.add)
        wupf = P.sb([128, 256], F32, "r_wupf")
        aupf = P.sb([128, 256], F32, "r_aupf")
        wupb = P.sb([128, 256], BF16, "r_wupb")
        aupb = P.sb([128, 256], BF16, "r_aupb")
        for d in range(2):
            P.dma("sp", wupf[d * 64:(d + 1) * 64, :], self.wup.v(("u", l, d), fn=lambda ap, d=d: ap[l][d]))
            P.dma("act", aupf[d * 64:(d + 1) * 64, :], self.aup.v(("u", l, d), fn=lambda ap, d=d: ap[l][d]))
        P.copy("dve", wupb[:], wupf[:])
        P.copy("pool", aupb[:], aupf[:])
        wdT = P.sb([128, S], BF16, "r_wdT")
        adT = P.sb([128, S], BF16, "r_adT")
        mk1 = P.mark()
        rows = self.shift_rows(l)
        hh = [P.sb([128, 8, 514], BF16, f"r_hh{i}") for i in range(2)]
        wl3 = self.scaled_w(l, C_RW + 768, 256, rows, "r_wl3", roff=768)
        for ti, (t0, tn) in enumerate(TILES):
            h = hh[ti % 2]
            self.load_h_halo(h, ti, "sp" if ti % 2 == 0 else "act")
            for c in range(2):
                pb = self.ps(c)
                n = 0
                for tap in range(3):
                    for kc in range(8):
                        P.mm(pb[:, 0, 0:tn], wl3[:, tap, kc, c * 128:(c + 1) * 128], h[:, kc, tap:tap + tn],
                             start=(n == 0), stop=(n == 23))
                        n += 1
                if c == 0:
                    P.act(wdT[:, t0:t0 + tn], pb[:, 0, 0:tn], AF.Tanh)
                else:
                    P.copy("act", adT[:, t0:t0 + tn], pb[:, 0, 0:tn])
        P.release(mk1)
        RW = int(os.environ.get("MK_RW", "99"))
        if RW <= 1:
            return
        mk_hp = P.mark()
        for hp in range(2):
            rT = P.sb([128, S], F32, "r_rT")
            kT = P.sb([128, S], F32, "r_kT")
            vT = P.sb([128, S], F32, "r_vT")
            ysum = P.sb([128, S], F32, "r_ysum")
            mk2 = P.mark()
            rows = self.shift_rows(l)
            hh = [P.sb([128, 8, 514], BF16, f"r_hh{i}") for i in range(2)]
            w3 = [self.scaled_w(l, C_RW + j * 256 + hp * 128, 128, rows, f"r_w3{j}", roff=j * 256 + hp * 128) for j in range(3)]
            for ti, (t0, tn) in enumerate(TILES):
                h = hh[ti % 2]
                self.load_h_halo(h, ti, "sp" if ti % 2 == 0 else "act")
                for j, dst in enumerate((rT, kT, vT)):
                    pb = self.ps(j % 2)
                    n = 0
                    for tap in range(3):
                        for kc in range(8):
                            P.mm(pb[:, 0, 0:tn], w3[j][:, tap, kc, :], h[:, kc, tap:tap + tn], start=(n == 0), stop=(n == 23))
                            n += 1
                    P.copy("act", dst[:, t0:t0 + tn], pb[:, 0, 0:tn])
            P.release(mk2)
            if RW <= 2:
                return
            mk3 = P.mark()
            self.rwkv_scan(l, hp, rT, kT, vT, wdT, adT, ysum, wupb, aupb, cf, m4, ones64, omka, LWC)
            P.release(mk3)
            self.rwkv_finish(l, hp, rT, kT, vT, ysum, cf, ctx_out)
            P.release(mk_hp)

    def shift_rows(self, l):
        P = self.P
        o = PR2["shr"][0]
        sr = P.sb([128, 3 * 1024], F32, "r_shr")
        P.dma("sp", sr[:], self.pr2.v(l, fn=lambda a: a[l][:, o:o + 3 * 1024]))
        r0, r2, r1 = sr[:, 0:1024], sr[:, 1024:2048], sr[:, 2048:3072]
        P.tt("dve", r1, r0, r2, ALU.add)
        P.ts("dve", r1, r1, -1.0, 1.0, ALU.mult, ALU.add)
        return [r0, r1, r2]

    def rwkv_scan(self, l, hp, rT, kT, vT, wdT, adT, ysum, wupb, aupb, cf, m4, ones64, omka, LWC):
        P = self.P
        NB = 2
        f32t = lambda n, w=512: [P.sb([128, w], F32, f"r_{n}")] * NB
        sg, aa, cum = f32t("sg"), f32t("aa"), f32t("cum")
        lw = sg
        cumx = [P.sb([128, 8], F32, "r_tot")] * NB
        e1, e2, e3, e4 = f32t("e1"), f32t("e2"), f32t("e3"), f32t("e4")
        kkr, sqk, nrm, kd, bq = f32t("kkr"), f32t("sqk"), f32t("nrm"), f32t("kd"), f32t("bq")
        kk, tmpa = kkr, nrm
        WC = [P.sb([128, 8], F32, f"r_WC{i}") for i in range(NB)]
        arb = [P.sb([128, 4, 2, 128], F32, "r_arb")] * NB
        kb = [P.sb([128, 512], F32, "r_kb")] * NB
        bb = [P.sb([128, 512], F32, "r_bb")] * NB
        kt = [P.sb([128, 512], F32, "r_kt")] * NB
        btl = [P.sb([128, 512], F32, "r_btl")] * NB
        G4 = 4
        TM = [P.sb([128, 4, 128], F32, f"r_TM{i}") for i in range(G4)]
        AT = [P.sb([128, 2, 4, 128], F32, f"r_AT{i}") for i in range(G4)]
        X1 = [P.sb([128, 2, 128], F32, f"r_X1{i}") for i in range(G4)]
        XX = [P.sb([128, 2, 2, 128], F32, f"r_XX{i}") for i in range(G4)]
        Qf = [P.sb([128, 2, 128], F32, f"r_Qf{i}") for i in range(G4)]
        W1 = [P.sb([128, 128], F32, f"r_W1{i}") for i in range(G4)]
        AU = [P.sb([128, 256], F32, f"r_AU{i}") for i in range(G4)]
        Y0T = [P.sb([128, 128], F32, f"r_Y0T{i}") for i in range(G4)]
        RhT = [P.sb([128, 128], F32, f"r_RhT{i}") for i in range(G4)]
        MTt = [P.sb([128, 128], F32, "r_MTt")] * 2
        MT = [[P.sb([128, 128], F32, f"r_MT{i}_{c}") for c in range(2)] for i in range(G4)]
        Ns = [[P.sb([128, 64], F32, f"r_Ns{i}_{c}") for c in range(2)] for i in range(G4)]
        ytmp = [P.sb([128, 64], F32, f"r_yt{i}") for i in range(2)]
        Sseq = P.sb([128, 4, 64], F32, "r_Sseq")
        ow0, oa0, okk, oka = PP["w0"][0], PP["a0"][0], PP["kk"][0], PP["ka"][0]
        if os.environ.get("MK_MEM"):
            print("rwkv_scan arena used", P._aoff, "of", P.ARENA)
        nblk_done = 0
        for d in range(2):
            if d == 1 and hp == 0 and os.environ.get("MK_RWDBG"):
                if not hasattr(self, "dbgy"):
                    self.dbgy = P.dram("dbgy", [128, S], F32, kind="ExternalOutput")
                P.dma("sp", self.dbgy.v(0), ysum[:])
            col = d * 2 + hp
            mask4 = m4["rmf"] if d == 0 else m4["rmb"]
            low = cf["lowf"] if d == 0 else cf["lowb"]
            P.memset("dve", Sseq[:, 0, :], 0.0)
            si = 0
            tiles = list(range(len(TILES))) if d == 0 else [0] + list(range(len(TILES) - 1, 0, -1))
            for tix, ti in enumerate(tiles):
                t0, tn = TILES[ti]
                nch = tn // 64
                nbk = tn // 128
                k_ = tix % NB
                ts_ = slice(t0, t0 + tn)
                ds_ = slice(d * 64, (d + 1) * 64)
                pxw, pxa = self.ps(0), self.ps(1)
                P.mm(pxw[:, 0, 0:tn], wupb[ds_, hp * 128:(hp + 1) * 128], wdT[ds_, ts_])
                P.mm(pxa[:, 0, 0:tn], aupb[ds_, hp * 128:(hp + 1) * 128], adT[ds_, ts_])
                P.act(sg[k_][:, 0:tn], pxw[:, 0, 0:tn], AF.Sigmoid, bias=self.ppt[:, ow0 + col:ow0 + col + 1])
                P.act(aa[k_][:, 0:tn], pxa[:, 0, 0:tn], AF.Sigmoid, bias=self.ppt[:, oa0 + col:oa0 + col + 1])
                P.ts("dve", lw[k_][:, 0:tn], sg[k_][:, 0:tn], LWC, None, ALU.mult)
                for c in range(nch):
                    cs = slice(c * 64, (c + 1) * 64)
                    P.scan("dve", cum[k_][:, cs], ones64[:], lw[k_][:, cs], 0.0, ALU.mult, ALU.add)
                c3 = lambda t_: t_[:, 0:tn].re("p (c n) -> p c n", n=64)
                P.act(WC[k_][:, 0:nch].m(lambda a: a.unsqueeze(2)), c3(cum[k_])[:, :, 63:64], AF.Exp)
                if d == 1:
                    P.copy("pool", cumx[k_][:, 0:nch].m(lambda a: a.unsqueeze(2)), c3(cum[k_])[:, :, 63:64])
                    P.tt("dve", cum[k_][:, 0:tn], lw[k_][:, 0:tn], cum[k_][:, 0:tn], ALU.subtract)
                    P.tt("dve", c3(cum[k_]), c3(cum[k_]), cumx[k_][:, 0:nch].m(lambda a: a.unsqueeze(2)).bc([128, nch, 64]), ALU.add)
                    tot = None
                P.tt("pool", e4[k_][:, 0:tn], cum[k_][:, 0:tn], lw[k_][:, 0:tn], ALU.subtract)
                P.act(e1[k_][:, 0:tn], cum[k_][:, 0:tn], AF.Exp)
                P.act(e2[k_][:, 0:tn], cum[k_][:, 0:tn], AF.Exp, scale=-1.0)
                P.act(e3[k_][:, 0:tn], e4[k_][:, 0:tn], AF.Exp)
                P.tt("dve", c3(e4[k_]), c3(e2[k_]), WC[k_][:, 0:nch].m(lambda a: a.unsqueeze(2)).bc([128, nch, 64]), ALU.mult)
                P.ts("dve", kkr[k_][:, 0:tn], kT[:, ts_], self.ppt[:, okk + col:okk + col + 1], None, ALU.mult)
                P.act(sqk[k_][:, 0:tn], kkr[k_][:, 0:tn], AF.Square)
                pss = self.ps(2)
                P.mm(pss[:, 0, 0:tn], cf["blk"][:], sqk[k_][:, 0:tn])
                P.act(nrm[k_][:, 0:tn], pss[:, 0, 0:tn], AF.Sqrt)
                P.ts("dve", nrm[k_][:, 0:tn], nrm[k_][:, 0:tn], 1e-12, None, ALU.max)
                P.recip(nrm[k_][:, 0:tn], nrm[k_][:, 0:tn])
                P.tt("dve", kk[k_][:, 0:tn], kkr[k_][:, 0:tn], nrm[k_][:, 0:tn], ALU.mult)
                P.ts("pool", tmpa[k_][:, 0:tn], aa[k_][:, 0:tn], self.ppt[:, oka + col:oka + col + 1], omka[:, col:col + 1], ALU.mult, ALU.add)
                P.tt("pool", kd[k_][:, 0:tn], kT[:, ts_], tmpa[k_][:, 0:tn], ALU.mult)
                P.tt("dve", bq[k_][:, 0:tn], kk[k_][:, 0:tn], aa[k_][:, 0:tn], ALU.mult)
                b3 = lambda t_: t_[:, 0:tn].re("p (b n) -> p b n", n=128)
                P.ts("pool", sqk[k_][:, 0:tn], kk[k_][:, 0:tn], -1.0, None, ALU.mult)
                P.tt("dve", arb[k_][:, 0:nbk, 0, :], b3(sqk[k_]), b3(e3[k_]), ALU.mult)
                P.tt("pool", arb[k_][:, 0:nbk, 1, :], rT[:, ts_].re("p (b n) -> p b n", n=128), b3(e1[k_]), ALU.mult)
                P.tt("dve", kb[k_][:, 0:tn], kd[k_][:, 0:tn], e2[k_][:, 0:tn], ALU.mult)
                P.tt("pool", bb[k_][:, 0:tn], bq[k_][:, 0:tn], e2[k_][:, 0:tn], ALU.mult)
                P.tt("dve", kt[k_][:, 0:tn], kd[k_][:, 0:tn], e4[k_][:, 0:tn], ALU.mult)
                P.tt("pool", btl[k_][:, 0:tn], bq[k_][:, 0:tn], e4[k_][:, 0:tn], ALU.mult)
                blocks = list(range(nbk)) if d == 0 else list(range(nbk - 1, -1, -1))
                G = len(blocks)
                idm = lambda t_: t_[:].m(lambda a: a.unsqueeze(1)).bc([128, 2, 128])
                ev = 0
                for g, bi in enumerate(blocks):
                    bs = slice(bi * 128, (bi + 1) * 128)
                    pt = self.ps(2 + g)
                    P.tr(pt[:, 0, 0:128], arb[k_][:, bi, 0, :], self.c_ident[:])
                    P.tr(pt[:, 0, 128:256], btl[k_][:, bs], self.c_ident[:])
                    P.tr(pt[:, 0, 256:384], kt[k_][:, bs], self.c_ident[:])
                    P.tr(pt[:, 0, 384:512], vT[:, t0 + bi * 128:t0 + (bi + 1) * 128], self.c_ident[:])
                for g, bi in enumerate(blocks):
                    P.copy("act" if g % 2 == 0 else "dve", TM[g][:].re("p a n -> p (a n)"), self.ps(2 + g)[:, 0, 0:512])
                px2 = self.ps(6, 2)
                for half in range(0, G, 2):
                    gs = list(range(half, min(half + 2, G)))
                    for g in gs:
                        bi = blocks[g]
                        bs = slice(bi * 128, (bi + 1) * 128)
                        pa = self.ps(2 + 2 * (g % 2), 2)
                        for h in range(2):
                            hs = slice(h * 64, (h + 1) * 64)
                            ar_ = arb[k_][hs, bi, :, :].re("p a n -> p (a n)")
                            P.mm(pa[:, h, 0:256], bb[k_][hs, bs], ar_)
                            P.mm(pa[:, h, 256:512], kb[k_][hs, bs], ar_)
                            P.mm(px2[:, h, g * 128:(g + 1) * 128], arb[k_][hs, bi, 0, :], bb[k_][hs, bs])
                    for g in gs:
                        pa = self.ps(2 + 2 * (g % 2), 2)
                        P.tt("dve", AT[g][:].re("p h a n -> p h (a n)"), pa,
                             mask4[:].re("p a n -> p (a n)").m(lambda a: a.unsqueeze(1)).bc([128, 2, 512]), ALU.mult)
                for g in range(G):
                    P.tt("dve", X1[g][:], px2[:, :, g * 128:(g + 1) * 128], idm(low), ALU.mult)
                    P.tt("pool", Qf[g][:], AT[g][:, :, 0, :], idm(self.c_ident), ALU.add)
                xk = [[X1[g][:, h, :] for h in range(2)] for g in range(G)]
                xtk = [[AT[g][:, h, 0, :] for h in range(2)] for g in range(G)]
                for lev in range(5):
                    for g in range(G):
                        pn = self.ps(2 + g)
                        for h in range(2):
                            P.mm(pn[:, 0, h * 256:h * 256 + 128], xtk[g][h], xk[g][h])
                            if lev < 4:
                                P.mm(pn[:, 0, h * 256 + 128:h * 256 + 256], xk[g][h], xtk[g][h])
                    for g in range(G):
                        P.copy("act", XX[g][:].re("p h a n -> p (h a n)"), self.ps(2 + g)[:, 0, :])
                        xk[g] = [XX[g][:, h, 0, :] for h in range(2)]
                        xtk[g] = [XX[g][:, h, 1, :] for h in range(2)]
                    for g in range(G):
                        pq = self.ps(6 + g // 2)
                        for h in range(2):
                            c0 = (g % 2) * 256 + h * 128
                            P.mm(pq[:, 0, c0:c0 + 128], xk[g][h], Qf[g][:, h, :])
                    for g in range(G):
                        pq = self.ps(6 + g // 2)
                        c0 = (g % 2) * 256
                        P.tt("dve", Qf[g][:], pq[:, 0, c0:c0 + 256].re("p (h n) -> p h n", h=2), Qf[g][:], ALU.add)
                pw = self.ps(0)
                for g in range(G):
                    for h in range(2):
                        P.mm(pw[:, 0, g * 128 + h * 64:g * 128 + (h + 1) * 64], AT[g][:, h, 2, :], TM[g][:, 3, h * 64:(h + 1) * 64])
                for g in range(G):
                    P.copy("act", W1[g][:], pw[:, 0, g * 128:(g + 1) * 128])
                for g in range(G):
                    pau = self.ps(2 + g // 2)
                    c0 = (g % 2) * 256
                    for h in range(2):
                        P.mm(pau[:, 0, c0 + h * 64:c0 + (h + 1) * 64], Qf[g][:, h, :], TM[g][:, 0, h * 64:(h + 1) * 64])
                        P.mm(pau[:, 0, c0 + 128 + h * 64:c0 + 128 + (h + 1) * 64], Qf[g][:, h, :], W1[g][:, h * 64:(h + 1) * 64])
                for g in range(G):
                    pau = self.ps(2 + g // 2)
                    c0 = (g % 2) * 256
                    P.copy("act" if g % 2 == 0 else "dve", AU[g][:], pau[:, 0, c0:c0 + 256])
                for g, bi in enumerate(blocks):
                    py = self.ps(4 + g // 2)
                    c0 = (g % 2) * 256
                    for h in range(2):
                        hs = slice(h * 64, (h + 1) * 64)
                        P.mm(py[hs, 0, c0:c0 + 128], AU[g][:, 128 + h * 64:128 + (h + 1) * 64], AT[g][:, h, 1, :], start=True, stop=False)
                        P.mm(py[hs, 0, c0:c0 + 128], TM[g][:, 3, h * 64:(h + 1) * 64], AT[g][:, h, 3, :], start=False, stop=True)
                        P.mm(py[hs, 0, c0 + 128:c0 + 256], AU[g][:, h * 64:(h + 1) * 64], AT[g][:, h, 1, :])
                for g, bi in enumerate(blocks):
                    py = self.ps(4 + g // 2)
                    c0 = (g % 2) * 256
                    P.copy("act", Y0T[g][:], py[:, 0, c0:c0 + 128])
                    P.tt("dve", RhT[g][:], py[:, 0, c0 + 128:c0 + 256], arb[k_][:, bi, 1, :], ALU.add)
                for g, bi in enumerate(blocks):
                    for cc in range(2):
                        cs = slice(cc * 64, (cc + 1) * 64)
                        pmn = self.ps((6 if g < 2 else 2) + cc)
                        c0 = (g % 2) * 192
                        P.mm(pmn[:, 0, c0:c0 + 128], AU[g][cs, 0:128], TM[g][cs, 1, :])
                        for h in range(2):
                            hs = slice(h * 64, (h + 1) * 64)
                            P.mm(pmn[hs, 0, c0 + 128:c0 + 192], TM[g][cs, 1, hs], AU[g][cs, 128 + h * 64:128 + (h + 1) * 64], start=True, stop=False)
                            P.mm(pmn[hs, 0, c0 + 128:c0 + 192], TM[g][cs, 2, hs], TM[g][cs, 3, hs], start=False, stop=True)
                for g, bi in enumerate(blocks):
                    for cc in range(2):
                        pmn = self.ps((6 if g < 2 else 2) + cc)
                        c0 = (g % 2) * 192
                        mt_ = MTt[cc]
                        P.tt("dve", mt_[:], pmn[:, 0, c0:c0 + 128], cf["blk"][:], ALU.mult)
                        wcol = bi * 2 + cc
                        P.stt("dve", MT[g][cc][:], self.c_ident[:], WC[k_][:, wcol:wcol + 1], mt_[:], ALU.mult, ALU.add)
                        P.copy("act", Ns[g][cc][:], pmn[:, 0, c0 + 128:c0 + 192])
                for g, bi in enumerate(blocks):
                    for cc in ([0, 1] if d == 0 else [1, 0]):
                        cs = slice(cc * 64, (cc + 1) * 64)
                        tok = slice(t0 + bi * 128 + cc * 64, t0 + bi * 128 + cc * 64 + 64)
                        pyh = [self.ps(4), self.ps(5)]
                        pss_ = self.ps(0)
                        for h in range(2):
                            hs = slice(h * 64, (h + 1) * 64)
                            P.mm(pyh[h][hs, 0, 0:64], Sseq[hs, si % 4, :], RhT[g][hs, cs])
                        P.mm(pss_[:, 0, 0:64], MT[g][cc][:], Sseq[:, si % 4, :])
                        P.tt("dve", Sseq[:, (si + 1) % 4, :], pss_[:, 0, 0:64], Ns[g][cc][:], ALU.add)
                        for h in range(2):
                            hs = slice(h * 64, (h + 1) * 64)
                            if d == 0:
                                P.tt("pool" if False else "dve", ysum[hs, tok], pyh[h][hs, 0, 0:64], Y0T[g][hs, cs], ALU.add)
                            else:
                                yt_ = ytmp[cc]
                                P.tt("dve", yt_[hs, :], pyh[h][hs, 0, 0:64], Y0T[g][hs, cs], ALU.add)
                                P.tt("pool", ysum[hs, tok], ysum[hs, tok], yt_[hs, :], ALU.add)
                        si += 1

    def rwkv_finish(self, l, hp, rT, kT, vT, ysum, cf, ctx_out):
        P = self.P
        wg = P.sb([128, 8, 128], BF16, "rf_wg")
        P.dma("sp", wg[:], self.wb(l, C_RG + hp * 128, 128))
        hb = [P.sb([128, 8, 512], BF16, f"rf_h{i}") for i in range(2)]
        f = lambda n: [P.sb([128, 512], F32, f"rf_{n}{i}") for i in range(2)]
        yc, sq, rstd, rk, gs = f("yc"), f("sq"), f("rstd"), f("rk"), f("gs")
        mo = [P.sb([128, 512], BF16, f"rf_mo{i}") for i in range(2)]
        olw, olb, ork = PP["lnw"][0] + hp, PP["lnb"][0] + hp, PP["rk"][0] + hp
        for ti, (t0, tn) in enumerate(TILES):
            if ti == 0 and not ctx_out:
                continue
            k_ = ti % 2
            ts_ = slice(t0, t0 + tn)
            self.load_h(hb[k_], ti, "sp" if k_ == 0 else "act")
            pg = self.ps(0)
            for kc in range(8):
                P.mm(pg[:, 0, 0:tn], wg[:, kc, :], hb[k_][:, kc, 0:tn], start=(kc == 0), stop=(kc == 7))
            P.act(gs[k_][:, 0:tn], pg[:, 0, 0:tn], AF.Silu)
            pm = self.ps(1)
            P.mm(pm[:, 0, 0:tn], cf["blk"][:], ysum[:, ts_])
            P.stt("dve", yc[k_][:, 0:tn], pm[:, 0, 0:tn], -1.0 / 64, ysum[:, ts_], ALU.mult, ALU.add)
            P.act(sq[k_][:, 0:tn], yc[k_][:, 0:tn], AF.Square)
            pv = self.ps(2)
            P.mm(pv[:, 0, 0:tn], cf["blk"][:], sq[k_][:, 0:tn])
            P.act(rstd[k_][:, 0:tn], pv[:, 0, 0:tn], AF.Sqrt, bias=64e-5, scale=1.0 / 64)
            P.recip(rstd[k_][:, 0:tn], rstd[k_][:, 0:tn])
            P.tt("dve", yc[k_][:, 0:tn], yc[k_][:, 0:tn], rstd[k_][:, 0:tn], ALU.mult)
            P.ts("dve", yc[k_][:, 0:tn], yc[k_][:, 0:tn], self.ppt[:, olw:olw + 1], self.ppt[:, olb:olb + 1], ALU.mult, ALU.add)
            P.stt("dve", rk[k_][:, 0:tn], rT[:, ts_], self.ppt[:, ork:ork + 1], kT[:, ts_], ALU.mult, ALU.mult)
            pb = self.ps(3)
            P.mm(pb[:, 0, 0:tn], cf["blk"][:], rk[k_][:, 0:tn])
            P.tt("dve", rk[k_][:, 0:tn], pb[:, 0, 0:tn], vT[:, ts_], ALU.mult)
            P.tt("pool", yc[k_][:, 0:tn], yc[k_][:, 0:tn], rk[k_][:, 0:tn], ALU.add)
            P.tt("dve", mo[k_][:, 0:tn], yc[k_][:, 0:tn], gs[k_][:, 0:tn], ALU.mult)
            r0_ = 384 + hp * 128
            P.dma("sp", self.mix.v(("mr", hp, ti), (slice(r0_, r0_ + 128), ts_)), mo[k_][:, 0:tn])

    def zero_mix(self, r0, r1):
        P = self.P
        z = P.sb([128, S], BF16, "zmix")
        P.memset("pool", z[:], 0.0)
        for kc in range(r0 // 128, r1 // 128):
            P.dma("sp", self.mix.v(("mz", kc), (slice(kc * 128, kc * 128 + 128), slice(None))), z[:])

    def build(self, layers=DEPTH):
        P = self.P
        base = P.mark()
        self.phase_w()
        P.release(base)
        for l in range(layers):
            self.phase_mod(l)
            keep = P.mark()
            self.phase_norm(l)
            P.release(keep)
            if self.stop_after == "norm":
                return
            if not os.environ.get("MK_SKIP_ATT"):
                self.phase_att(l)
                P.release(keep)
            if self.stop_after == "att":
                return
            if os.environ.get("MK_SKIP_RWKV"):
                self.zero_mix(384, 640)
            else:
                self.phase_rwkv(l)
            P.release(keep)
            if self.stop_after == "rwkv":
                return
            self.phase_ssd(l)
            P.release(keep)
            if self.stop_after == "ssd":
                return
            self.phase_out(l)
            P.release(keep)
            for a in ("n_x", "o_w"):
                if hasattr(self, a):
                    delattr(self, a)

def build_program(stop_after=None, dbg=False, layers=DEPTH):
    nc = bass.Bass("TRN2", target_bir_lowering=False)
    k = MK(nc, stop_after=stop_after, dbg=dbg)
    k.build(layers)
    k.P.emit()
    return nc, k


def prep_inputs(inp, b):
    inp = {k: np.asarray(v) for k, v in inp.items()}
    d = {}
    d["xin"] = np.ascontiguousarray(np.concatenate([inp["ctx"][b], inp["x"][b]], 0).astype(np.float32))
    d["cvec"] = np.ascontiguousarray(np.concatenate([fm(inp["c"][b]), fm(inp["c_ctx"])], 1))
    d["ada_w"] = np.ascontiguousarray(inp["ada_w"], np.float32)
    d["w_in"] = np.ascontiguousarray(inp["w_in"], np.float32)
    d["w_out"] = np.ascontiguousarray(inp["w_out"], np.float32)
    d["wup"] = np.ascontiguousarray(inp["rwkv_w_up"], np.float32)
    d["aup"] = np.ascontiguousarray(inp["rwkv_a_up"], np.float32)
    d["pp"] = np.stack([make_pp(inp, l) for l in range(DEPTH)])
    d["pr"] = np.stack([make_pr(inp, l) for l in range(DEPTH)])
    d["pr2"] = np.stack([make_pr2(inp, l) for l in range(DEPTH)])
    d["cst"] = make_consts()
    return d


def kernel(**inputs):
    nc, _ = build_program()
    maps = [prep_inputs(inputs, b % 4) for b in range(4)]
    in_maps = [maps[i % 4] for i in range(8)]
    res = run_bass_kernel_spmd(nc, in_maps, core_ids=list(range(8)))
    return np.stack([np.asarray(res.results[b]["y"], np.float32) for b in range(4)], 0)
```

```python
import numpy as np
import concourse.bass as bass
import concourse.mybir as mybir

F32 = mybir.dt.float32
BF16 = mybir.dt.bfloat16
ALU = mybir.AluOpType
AF = mybir.ActivationFunctionType
AX = mybir.AxisListType

import os
DUMP = os.environ.get("MK_DUMP", "")
NOSYNC = set(filter(None, os.environ.get("MK_NOSYNC", "").split(",")))
ENGS = ("pe", "dve", "act", "pool", "sp")
N_DMA_SEMS = 20


class Trk:
    __slots__ = ("lw", "rd", "ps")

    def __init__(self, ps=False):
        self.lw = None
        self.rd = []
        self.ps = ps


class V:
    __slots__ = ("ap", "trk")

    def __init__(self, ap, trk):
        self.ap = ap
        self.trk = trk

    def __getitem__(self, idx):
        return V(self.ap[idx], self.trk)

    def m(self, fn):
        return V(fn(self.ap), self.trk)

    def re(self, pat, **kw):
        return V(self.ap.rearrange(pat, **kw), self.trk)

    def bc(self, shape):
        return V(self.ap.broadcast_to(shape), self.trk)

    @property
    def shape(self):
        return self.ap.shape


class Tile:
    def __init__(self, handle):
        self.h = handle
        self.trk = Trk()

    def __getitem__(self, idx):
        return V(self.h[idx], (self.trk,))

    def ap(self):
        return V(self.h.ap() if hasattr(self.h, "ap") else self.h[:], (self.trk,))


class DTile:
    def __init__(self, handle):
        self.h = handle
        self.reg = {}

    def v(self, key, idx=None, fn=None):
        t = self.reg.setdefault(key, Trk())
        ap = self.h.ap()
        if fn is not None:
            ap = fn(ap)
        if idx is not None:
            ap = ap[idx]
        return V(ap, (t,))


class Op:
    __slots__ = ("eng", "fn", "reads", "writes", "idx", "deps", "signal", "sig_count",
                 "is_dma", "dma_sem", "dma_val", "dma_prev", "kind")


class Prog:
    def __init__(self, nc):
        self.nc = nc
        self.ops = {e: [] for e in ENGS}
        self.n_t = 0
        self.all_ops = []

    ARENA = 207 * 1024

    def sb(self, shape, dtype, name=None):
        self.n_t += 1
        if not hasattr(self, "_abase"):
            a = self.nc.alloc_sbuf_tensor("arena", [128, self.ARENA], mybir.dt.uint8)
            self._abase = self.nc.lookup_mloc(a).addr
            self._aoff = 0
        esz = 2 if dtype == BF16 else 4
        n = esz
        for d in shape[1:]:
            n *= d
        off = (self._aoff + 63) // 64 * 64
        assert off + n <= self.ARENA, f"SBUF arena overflow allocating {name} {shape}: {off + n}"
        self._aoff = off + n
        return Tile(self.nc.alloc_sbuf_tensor_at(f"{name or 'sb'}_{self.n_t}", list(shape), dtype, offset=self._abase + off))

    def mark(self):
        return getattr(self, "_aoff", 0)

    def release(self, mark):
        self.barrier()
        self._aoff = mark

    def barrier(self):
        lasts = []
        for e in ENGS:
            for op in reversed(self.ops[e]):
                if not op.is_dma and op.kind != "bar":
                    lasts.append(op)
                    break
        dmas = [op for op in self.all_ops[getattr(self, "_bar_pos", 0):] if op.is_dma]
        self._bar_pos = len(self.all_ops)
        for e in ENGS:
            op = self.rec(e, lambda eh: eh.nop(nofuse=True), [], [], kind="bar")
            op.deps = [d for d in lasts if d.eng != e] + dmas
            for d in op.deps:
                d.signal = True

    def ps(self, shape, dtype=F32, name=None):
        self.n_t += 1
        return Tile(self.nc.alloc_psum_tensor(name or f"ps{self.n_t}", list(shape), dtype))

    def dram(self, name, shape, dtype, kind="Internal"):
        return DTile(self.nc.dram_tensor(name, list(shape), dtype, kind=kind))

    def rec(self, eng, fn, reads, writes, is_dma=False, kind=""):
        op = Op()
        op.eng = eng
        op.fn = fn
        op.kind = kind
        op.is_dma = is_dma
        op.idx = len(self.ops[eng])
        op.signal = False
        op.deps = []
        rt = []
        for v in reads:
            if v is None or not isinstance(v, V):
                continue
            rt.extend(v.trk)
        wt = []
        for v in writes:
            if v is None or not isinstance(v, V):
                continue
            wt.extend(v.trk)
        deps = set()
        for t in rt:
            if t.lw is not None:
                deps.add(t.lw)
            if t.ps:
                for r in t.rd:
                    if r.eng != eng:
                        deps.add(r)
        for t in wt:
            if t.lw is not None:
                deps.add(t.lw)
            for r in t.rd:
                deps.add(r)
        deps.discard(op)
        for t in rt:
            t.rd.append(op)
        for t in wt:
            t.lw = op
            t.rd = []
        final = []
        for d in deps:
            if d.eng == "pe" and eng == "pe" and not d.is_dma and not is_dma:
                continue
            if d.eng == eng and eng in NOSYNC and not d.is_dma and not is_dma:
                continue
            final.append(d)
        op.deps = final
        for d in final:
            d.signal = True
        self.ops[eng].append(op)
        self.all_ops.append(op)
        return op

    def mm(self, out, lhsT, rhs, start=True, stop=True):
        return self.rec("pe", lambda e: e.matmul(out.ap, lhsT.ap, rhs.ap, start=start, stop=stop),
                        [lhsT, rhs], [out], kind="mm")

    def tr(self, out, in_, ident):
        return self.rec("pe", lambda e: e.transpose(out.ap, in_.ap, ident.ap), [in_, ident], [out], kind="tr")

    def act(self, out, in_, func, bias=None, scale=None, accum_out=None, eng="act"):
        kw = {}
        rd = [in_]
        if bias is not None:
            kw["bias"] = bias.ap if isinstance(bias, V) else bias
            rd.append(bias)
        if scale is not None:
            kw["scale"] = scale.ap if isinstance(scale, V) else scale
            rd.append(scale)
        wr = [out]
        if accum_out is not None:
            kw["accum_out"] = accum_out.ap
            wr.append(accum_out)
        return self.rec("act", lambda e: e.activation(out.ap, in_.ap, func, **kw), rd, wr, kind="act")

    def tt(self, eng, out, in0, in1, op):
        return self.rec(eng, lambda e: e.tensor_tensor(out.ap, in0.ap, in1.ap, op), [in0, in1], [out], kind="tt")

    def ts(self, eng, out, in0, s1, s2, op0, op1=None, accum_out=None):
        rd = [in0, s1, s2]
        a1 = s1.ap if isinstance(s1, V) else s1
        a2 = s2.ap if isinstance(s2, V) else s2
        kw = {}
        wr = [out]
        if accum_out is not None:
            kw["accum_out"] = accum_out.ap
            wr.append(accum_out)
        if op1 is None:
            return self.rec(eng, lambda e: e.tensor_scalar(out.ap, in0.ap, a1, None, op0, **kw), rd, wr, kind="ts")
        return self.rec(eng, lambda e: e.tensor_scalar(out.ap, in0.ap, a1, a2, op0, op1, **kw), rd, wr, kind="ts")

    def stt(self, eng, out, in0, scalar, in1, op0, op1):
        a = scalar.ap if isinstance(scalar, V) else scalar
        return self.rec(eng, lambda e: e.scalar_tensor_tensor(out.ap, in0.ap, a, in1.ap, op0, op1),
                        [in0, scalar, in1], [out], kind="stt")

    def copy(self, eng, out, in_):
        if eng == "act":
            return self.rec("act", lambda e: e.copy(out.ap, in_.ap), [in_], [out], kind="copy")
        return self.rec(eng, lambda e: e.tensor_copy(out.ap, in_.ap), [in_], [out], kind="copy")

    def memset(self, eng, out, val):
        return self.rec(eng, lambda e: e.memset(out.ap, val), [], [out], kind="memset")

    def reduce(self, eng, out, in_, op, axis=AX.X):
        return self.rec(eng, lambda e: e.tensor_reduce(out.ap, in_.ap, axis, op), [in_], [out], kind="red")

    def recip(self, out, in_):
        return self.rec("dve", lambda e: e.reciprocal(out.ap, in_.ap), [in_], [out], kind="recip")

    def scan(self, eng, out, d0, d1, initial, op0, op1):
        ini = initial.ap if isinstance(initial, V) else initial
        return self.rec(eng, lambda e: e.tensor_tensor_scan(out.ap, d0.ap, d1.ap, ini, op0, op1),
                        [d0, d1, initial], [out], kind="scan")

    def dma(self, q, out, in_, **kw):
        return self.rec(q, lambda e: e.dma_start(out.ap, in_.ap, **kw), [in_], [out], is_dma=True, kind="dma")

    def emit(self):
        nc = self.nc
        eng_sem = {e: nc.alloc_semaphore(f"s_{e}") for e in ENGS}
        dma_sems = {e: ([nc.alloc_semaphore(f"d_{e}_{i}") for i in range(N_DMA_SEMS)]
                        if any(o.is_dma for o in self.ops[e]) else []) for e in ENGS}
        for e in ENGS:
            cnt = 0
            nd = 0
            last_on_sem = {}
            for op in self.ops[e]:
                if op.is_dma:
                    j = nd % N_DMA_SEMS
                    nd += 1
                    op.dma_sem = dma_sems[e][j]
                    k = nd_k = (nd - 1) // N_DMA_SEMS + 1
                    op.dma_val = 16 * k
                    op.dma_prev = last_on_sem.get(j)
                    last_on_sem[j] = op
                else:
                    if op.signal:
                        cnt += 1
                        op.sig_count = cnt
        self._eng_sem = eng_sem
        handles = {"pe": "tensor", "dve": "vector", "act": "scalar", "pool": "gpsimd", "sp": "sync"}

        def run_engine(ename, eh):
            seen = {}

            def wait(sem, val):
                key = id(sem)
                if seen.get(key, 0) >= val:
                    return
                seen[key] = val
                eh.wait_ge(sem, val)

            for op in self.ops[ename]:
                for d in op.deps:
                    if d.is_dma:
                        wait(d.dma_sem, d.dma_val)
                    else:
                        wait(eng_sem[d.eng], d.sig_count)
                if op.is_dma:
                    if op.dma_prev is not None:
                        wait(op.dma_prev.dma_sem, op.dma_prev.dma_val)
                    ins = op.fn(eh)
                    ins.then_inc(op.dma_sem, 16)
                else:
                    ins = op.fn(eh)
                    if op.signal:
                        ins.then_inc(eng_sem[ename], 1)
                    if DUMP and ename in DUMP:
                        print(ename, op.idx, op.kind, ins.concise(), flush=True)
            last = {}
            for op in self.ops[ename]:
                if op.is_dma:
                    last[id(op.dma_sem)] = op
            for op in last.values():
                wait(op.dma_sem, op.dma_val)

        with nc.Block() as block:
            @block.tensor
            def _(eh):
                run_engine("pe", eh)

            @block.vector
            def _(eh):
                run_engine("dve", eh)

            @block.scalar
            def _(eh):
                run_engine("act", eh)

            @block.gpsimd
            def _(eh):
                run_engine("pool", eh)

            @block.sync
            def _(eh):
                run_engine("sp", eh)

    def stats(self):
        return {e: len(self.ops[e]) for e in ENGS}

from concourse.bass_utils import run_bass_kernel_spmd

S = 4352
CTX = 256
TLAT = 4096
DM = 1024
NIN = 3596
NBLK = 34
DEPTH = 2
TILES = [(0, 256)] + [(256 + 512 * i, 512) for i in range(8)]
NEG = -30000.0

C_Q, C_K, C_V, C_G = 0, 384, 512, 640
C_RW = 1024
C_RG = 2048
C_Z = 2304
C_XBC = 2688
C_DT = 3584


def _slots(spec):
    d, o = {}, 0
    for n, w in spec:
        d[n] = (o, w)
        o += w
    return d, o


PP, NPP = _slots([("adab", 24), ("normw", 8), ("sink", 6), ("mu0", 8), ("mu1", 8), ("w0", 4), ("a0", 4),
                  ("kk", 4), ("ka", 4), ("rk", 2), ("lnw", 2), ("lnb", 2), ("convw", 21), ("convb", 7),
                  ("ssmd", 3), ("ssmnw", 3)])
PR, NPR = _slots([("adabg", 1024), ("dtb", 12), ("alog", 12), ("fnw", 1024)])
PR2, NPR2 = _slots([("convr", 3 * 896), ("shr", 3 * 1024)])
CS, NCS = _slots([("ident", 128), ("mband", 3 * 384), ("perm", 128), ("cos", 4096), ("sin", 4096),
                  ("blk", 128), ("rmf", 256), ("rmb", 256), ("snf", 128), ("snb", 128), ("trif", 128),
                  ("trib", 128), ("ones", 128), ("hlo", 128), ("hhi", 128), ("lowf", 128), ("lowb", 128)])


def fm(v):
    v = np.asarray(v, np.float32)
    return np.ascontiguousarray(v.reshape(-1, 128).T)


def rep(v):
    v = np.asarray(v, np.float32).reshape(1, -1)
    return np.ascontiguousarray(np.broadcast_to(v, (128, v.shape[1])))


def make_consts():
    c = np.zeros((128, NCS), np.float32)

    def put(n, a):
        o, w = CS[n]
        c[:, o:o + w] = a
    put("ident", np.eye(128, dtype=np.float32))
    qi = np.arange(128)[:, None]
    kj = np.arange(384)[None, :]
    band = np.abs(kj - 128 - qi) <= 128
    mb = []
    for var in range(3):
        ok = band.copy()
        if var == 0:
            ok &= kj >= 128
        if var == 2:
            ok &= kj < 256
        mb.append(np.where(ok, 0.0, NEG))
    put("mband", np.concatenate(mb, 1))
    perm = np.zeros((128, 128), np.float32)
    for m in range(128):
        d = m % 64
        half = (d % 32) // 16
        partner = m + 16 if half == 0 else m - 16
        perm[partner, m] = 1.0
    put("perm", perm)
    rows = TLAT // 64
    row = np.repeat(np.arange(rows), 64).astype(np.float32)
    col = np.tile(np.arange(64), rows).astype(np.float32)
    pos = np.stack([row, col], -1)
    inv = (np.float32(10000.0) ** (-np.arange(16, dtype=np.float32) / np.float32(16))).astype(np.float32)
    ang = (pos[:, :, None] * inv).astype(np.float32)
    cosv, sinv = np.cos(ang).astype(np.float32), np.sin(ang).astype(np.float32)
    ct = np.zeros((128, TLAT), np.float32)
    st = np.zeros((128, TLAT), np.float32)
    for m in range(128):
        d = m % 64
        ax, half, f = d // 32, (d % 32) // 16, d % 16
        ct[m] = cosv[:, ax, f]
        st[m] = -sinv[:, ax, f] if half == 0 else sinv[:, ax, f]
    put("cos", ct)
    put("sin", st)
    blk = np.zeros((128, 128), np.float32)
    blk[:64, :64] = 1
    blk[64:, 64:] = 1
    put("blk", blk)
    s = np.arange(128)[:, None]
    t = np.arange(128)[None, :]
    same = (s // 64) == (t // 64)
    put("rmf", np.concatenate([(same & (s < t)), (same & (s <= t))], 1).astype(np.float32))
    put("rmb", np.concatenate([(same & (s > t)), (same & (s >= t))], 1).astype(np.float32))
    put("lowf", (same & (t < s)).astype(np.float32))
    put("lowb", (same & (t > s)).astype(np.float32))
    put("snf", np.where(s <= t, 0.0, NEG))
    put("snb", np.where(s >= t, 0.0, NEG))
    put("trif", (s <= t).astype(np.float32))
    put("trib", (s >= t).astype(np.float32))
    put("ones", np.ones((128, 128), np.float32))
    hlo = np.zeros((128, 128), np.float32)
    hlo[:64] = 1
    put("hlo", hlo)
    put("hhi", 1 - hlo)
    return c


def make_pp(inp, l):
    p = np.zeros((128, NPP), np.float32)

    def put(n, a):
        o, w = PP[n]
        assert a.shape == (128, w), (n, a.shape)
        p[:, o:o + w] = a
    put("adab", fm(inp["ada_b"][l]))
    put("normw", fm(inp["norm_w"][l]))
    put("sink", rep(inp["attn_sink"][l]))
    put("mu0", fm(inp["rwkv_mu"][l, 0]))
    put("mu1", fm(inp["rwkv_mu"][l, 1]))
    for n, k in (("w0", "rwkv_w0"), ("a0", "rwkv_a0"), ("kk", "rwkv_k_k"), ("ka", "rwkv_k_a")):
        put(n, np.concatenate([fm(inp[k][l, 0]), fm(inp[k][l, 1])], 1))
    put("rk", fm(inp["rwkv_r_k"][l].reshape(-1)))
    put("lnw", fm(inp["rwkv_ln_w"][l]))
    put("lnb", fm(inp["rwkv_ln_b"][l]))
    cw = inp["ssm_conv_w"][l]
    put("convw", np.stack([fm(cw[0]), fm(cw[1]), fm(cw[2])], -1).reshape(128, 21))
    put("convb", fm(inp["ssm_conv_b"][l]))
    put("ssmd", fm(np.repeat(inp["ssm_d"][l], 64)))
    put("ssmnw", fm(inp["ssm_norm_w"][l]))
    return p


def make_pr(inp, l):
    p = np.zeros((128, NPR), np.float32)

    def put(n, a):
        o, w = PR[n]
        p[:, o:o + w] = a
    put("adabg", rep(inp["ada_b"][l, 2048:3072]))
    put("dtb", rep(inp["ssm_dt_bias"][l].reshape(-1)))
    put("alog", rep(inp["ssm_a_log"][l].reshape(-1)))
    put("fnw", rep(inp["final_norm_w"]))
    return p


def make_pr2(inp, l):
    p = np.zeros((128, NPR2), np.float32)

    def put(n, a):
        o, w = PR2[n]
        p[:, o:o + w] = a
    put("convr", rep(inp["ssm_conv_w"][l].reshape(-1)))
    mu = inp["rwkv_mu"][l]
    put("shr", rep(np.concatenate([mu[0], mu[1], mu[1]], 0)))
    return p


class MK:
    def __init__(self, nc, stop_after=None, dbg=False):
        self.nc = nc
        self.P = P = Prog(nc)
        self.dbg = dbg
        self.stop_after = stop_after
        ok = "ExternalOutput" if dbg else "Internal"
        self.xin = P.dram("xin", [S, DM], F32, kind="ExternalInput")
        self.cvec = P.dram("cvec", [128, 16], F32, kind="ExternalInput")
        self.ada_w = P.dram("ada_w", [DEPTH, DM, 3 * DM], F32, kind="ExternalInput")
        self.w_in = P.dram("w_in", [DEPTH, DM, NIN], F32, kind="ExternalInput")
        self.w_out = P.dram("w_out", [DEPTH, DM, DM], F32, kind="ExternalInput")
        self.wup = P.dram("wup", [DEPTH, 2, 64, 256], F32, kind="ExternalInput")
        self.aup = P.dram("aup", [DEPTH, 2, 64, 256], F32, kind="ExternalInput")
        self.pp = P.dram("pp", [DEPTH, 128, NPP], F32, kind="ExternalInput")
        self.pr = P.dram("pr", [DEPTH, 128, NPR], F32, kind="ExternalInput")
        self.pr2 = P.dram("pr2", [DEPTH, 128, NPR2], F32, kind="ExternalInput")
        self.cst = P.dram("cst", [128, NCS], F32, kind="ExternalInput")
        self.y = P.dram("y", [TLAT, DM], F32, kind="ExternalOutput")
        self.wbin = P.dram("wbin", [DEPTH, 128, 8, NIN], BF16)
        self.wbout = P.dram("wbout", [DEPTH, 128, 8, DM], BF16)
        self.hT = P.dram("hT", [128, 8, S], BF16, kind=ok)
        self.mix = P.dram("mixT", [DM, S], BF16, kind=ok)
        self.xres = P.dram("xres", [S, DM], F32, kind=ok)
        self.yf = P.dram("yfwd", [384, S], F32)
        self.psh = nc.alloc_psum_tensor("psum", [128, 8, 512], F32)
        self.bt = [Trk(ps=True) for _ in range(8)]
        self.c_ident = P.sb([128, 128], F32, "c_ident")
        self.c_identb = P.sb([128, 128], BF16, "c_identb")
        self.load_c(self.c_ident, "ident")
        P.copy("dve", self.c_identb[:], self.c_ident[:])

    def ps(self, b0, nb=1, dt=F32):
        ap = self.psh[:, b0:b0 + nb, :]
        if dt is not F32:
            ap = ap.bitcast(dt)
        return V(ap, tuple(self.bt[b0:b0 + nb]))

    def load_c(self, tile, name, q="sp", sub=None):
        o, w = CS[name]
        if sub is not None:
            o, w = o + sub[0], sub[1]
        self.P.dma(q, tile[:], self.cst.v("c", (slice(None), slice(o, o + w))))

    def phase_w(self):
        P = self.P
        st = [P.sb([128, 8, 512], F32, f"w_st{i}") for i in range(2)]
        sb = [P.sb([128, 8, 512], BF16, f"w_sb{i}") for i in range(2)]
        it = 0
        for l in range(DEPTH):
            for (src, dst, n) in ((self.w_in, self.wbin, NIN), (self.w_out, self.wbout, DM)):
                for c0 in range(0, n, 512):
                    cw = min(512, n - c0)
                    a, b = st[it % 2], sb[it % 2]
                    q = "sp" if it % 2 == 0 else "act"
                    P.dma(q, a[:, :, 0:cw], src.v(("w", l), fn=lambda ap, l=l, c0=c0, cw=cw:
                                                   ap[l].rearrange("(kc p) n -> p kc n", p=128)[:, :, c0:c0 + cw]))
                    P.copy("dve" if it % 2 == 0 else "pool", b[:, :, 0:cw], a[:, :, 0:cw])
                    P.dma(q, dst.v(("wb", l, c0), fn=lambda ap, l=l, c0=c0, cw=cw: ap[l][:, :, c0:c0 + cw]),
                          b[:, :, 0:cw])
                    it += 1

    def wb(self, l, c0, cw):
        t0 = (c0 // 512) * 512
        trk = []
        ap = self.wbin.h.ap()[l][:, :, c0:c0 + cw]
        for t in range(t0, c0 + cw, 512):
            trk.append(self.wbin.reg.setdefault(("wb", l, t), Trk()))
        return V(ap, tuple(trk))

    def wbo(self, l, c0, cw):
        t0 = (c0 // 512) * 512
        trk = []
        ap = self.wbout.h.ap()[l][:, :, c0:c0 + cw]
        for t in range(t0, c0 + cw, 512):
            trk.append(self.wbout.reg.setdefault(("wb", l, t), Trk()))
        return V(ap, tuple(trk))

    def phase_mod(self, l):
        P = self.P
        if not hasattr(self, "ppt"):
            self.ppt = P.sb([128, NPP], F32, "ppt")
            self.prt = P.sb([128, NPR], F32, "prt")
            self.cact = P.sb([128, 8, 2], F32, "cact")
            self.modT = P.sb([128, 24, 2], F32, "modT")
            self.s1T = P.sb([128, 8, 2], F32, "s1T")
            self.gate_bc = P.sb([128, 2, 1024], F32, "gate_bc")
            craw = P.sb([128, 16], F32, "craw")
            P.dma("sp", craw[:], self.cvec.v(0))
            P.act(self.cact[:].re("p k w -> p w k"), craw[:].re("p (w k) -> p w k", w=2), AF.Silu)
        mk_ = P.mark()
        self.crep = P.sb([128, 8, 2, 128], F32, "crep")
        self.aw = [P.sb([128, 8, 512], F32, f"aw{i}") for i in range(2)]
        P.copy("dve", self.crep[:], self.cact[:].m(lambda a: a.unsqueeze(3)).bc([128, 8, 2, 128]))
        P.dma("sp", self.ppt[:], self.pp.v(l, fn=lambda a: a[l]))
        P.dma("act", self.prt[:], self.pr.v(l, fn=lambda a: a[l]))
        for t in range(6):
            a = self.aw[t % 2]
            P.dma("sp" if t % 2 == 0 else "act", a[:],
                  self.ada_w.v(("aw", l), fn=lambda ap, t=t: ap[l].rearrange("(kc p) n -> p kc n", p=128)[:, :, t * 512:(t + 1) * 512]))
            pm = self.ps(0)
            for j in range(4):
                for kc in range(8):
                    P.mm(pm[:, 0, j * 2:(j + 1) * 2], a[:, kc, j * 128:(j + 1) * 128], self.cact[:, kc, :],
                         start=(kc == 0), stop=(kc == 7))
            P.copy("dve", self.modT[:, t * 4:(t + 1) * 4, :], pm[:, 0, 0:8].re("p (j w) -> p j w", w=2))
            if t >= 4:
                for w in range(2):
                    pg = self.ps(1 + w)
                    for kc in range(8):
                        P.mm(pg[:, 0, :], self.crep[:, kc, w, :], a[:, kc, :], start=(kc == 0), stop=(kc == 7))
                    o = PR["adabg"][0] + (t - 4) * 512
                    P.tt("dve", self.gate_bc[:, w, (t - 4) * 512:(t - 3) * 512], pg[:, 0, :], self.prt[:, o:o + 512], ALU.add)
        o = PP["adab"][0]
        P.tt("dve", self.modT[:], self.modT[:], self.ppt[:, o:o + 24].m(lambda a: a.unsqueeze(2)).bc([128, 24, 2]), ALU.add)
        o = PP["normw"][0]
        P.stt("dve", self.s1T[:], self.modT[:, 8:16, :], 1.0,
              self.ppt[:, o:o + 8].m(lambda a: a.unsqueeze(2)).bc([128, 8, 2]), ALU.add, ALU.mult)
        P.release(mk_)

    def phase_norm(self, l):
        P = self.P
        if not hasattr(self, "n_x"):
            self.n_x = [P.sb([128, DM], F32, f"n_x{i}") for i in range(2)]
            self.n_junk = P.sb([128, DM], F32, "n_junk")
            self.n_xb = [P.sb([128, DM], BF16, f"n_xb{i}") for i in range(2)]
            self.n_ss = [P.sb([128, 4], F32, f"n_ss{i}") for i in range(2)]
            self.n_h = [P.sb([128, 8, 512], BF16, f"n_h{i}") for i in range(2)]
            self.n_t = [P.sb([128, 8, 128], F32, f"n_t{i}") for i in range(2)]
        src = self.xin if l == 0 else self.xres
        for ti, (t0, tn) in enumerate(TILES):
            w = 1 if ti == 0 else 0
            hb = self.n_h[ti % 2]
            for bi in range(tn // 128):
                blk = (t0 // 128) + bi
                xt, xb, ss = self.n_x[blk % 2], self.n_xb[blk % 2], self.n_ss[blk % 2]
                P.dma("sp" if blk % 2 == 0 else "act", xt[:], src.v(("x", blk), (slice(blk * 128, blk * 128 + 128), slice(None))))
                P.act(self.n_junk[:], xt[:], AF.Square, accum_out=ss[:, 0:1])
                P.act(ss[:, 1:2], ss[:, 0:1], AF.Sqrt, bias=1e-6, scale=1.0 / DM)
                P.recip(ss[:, 2:3], ss[:, 1:2])
                P.ts("dve", xb[:], xt[:], ss[:, 2:3], None, ALU.mult)
                pt = self.ps(2 + blk % 2, 1, BF16)
                for kc in range(8):
                    P.tr(pt[:, 0, kc * 128:(kc + 1) * 128], xb[:, kc * 128:(kc + 1) * 128], self.c_identb[:])
                tmp = self.n_t[blk % 2]
                P.tt("dve", tmp[:], pt[:, 0, :].re("p (k t) -> p k t", k=8),
                     self.s1T[:, :, w:w + 1].bc([128, 8, 128]), ALU.mult)
                P.tt("pool", hb[:, :, bi * 128:(bi + 1) * 128], tmp[:],
                     self.modT[:, 0:8, w:w + 1].bc([128, 8, 128]), ALU.add)
            P.dma("sp", self.hT.v(("h", ti), (slice(None), slice(None), slice(t0, t0 + tn))), hb[:, :, 0:tn])

    def phase_out(self, l):
        P = self.P
        last = (l == DEPTH - 1)
        if not hasattr(self, "o_w"):
            self.o_w = P.sb([128, 8, DM], BF16, "o_w")
            self.o_m = [P.sb([128, 8, 128], BF16, f"o_m{i}") for i in range(2)]
            self.o_x = [P.sb([128, DM], F32, f"o_x{i}") for i in range(2)]
            self.o_t = [P.sb([128, DM], F32, f"o_t{i}") for i in range(2)]
            self.o_ss = [P.sb([128, 4], F32, f"o_ss{i}") for i in range(2)]
        P.dma("sp", self.o_w[:], self.wbo(l, 0, DM))
        src = self.xin if l == 0 else self.xres
        for blk in range(NBLK):
            if last and blk < 2:
                continue
            w = 1 if blk < 2 else 0
            m, xt, tt_, ss = self.o_m[blk % 2], self.o_x[blk % 2], self.o_t[blk % 2], self.o_ss[blk % 2]
            tsl = slice(blk * 128, blk * 128 + 128)
            P.dma("sp", m[:], self.mix.v(("m", blk), fn=lambda ap, tsl=tsl: ap.rearrange("(kc p) t -> p kc t", p=128)[:, :, tsl]))
            P.dma("act", xt[:], src.v(("x", blk), (tsl, slice(None))))
            for hf in range(2):
                po = self.ps(4 + 2 * (blk % 2) + hf)
                for kc in range(8):
                    P.mm(po[:, 0, :], m[:, kc, :], self.o_w[:, kc, hf * 512:(hf + 1) * 512], start=(kc == 0), stop=(kc == 7))
                P.tt("dve", tt_[:, hf * 512:(hf + 1) * 512], po[:, 0, :], self.gate_bc[:, w, hf * 512:(hf + 1) * 512], ALU.mult)
            P.tt("pool", tt_[:], tt_[:], xt[:], ALU.add)
            if not last:
                P.dma("sp", self.xres.v(("x", blk), (tsl, slice(None))), tt_[:])
            else:
                P.act(xt[:], tt_[:], AF.Square, accum_out=ss[:, 0:1])
                P.act(ss[:, 1:2], ss[:, 0:1], AF.Sqrt, bias=1e-6, scale=1.0 / DM)
                P.recip(ss[:, 2:3], ss[:, 1:2])
                o = PR["fnw"][0]
                P.stt("dve", xt[:], tt_[:], ss[:, 2:3], self.prt[:, o:o + DM], ALU.mult, ALU.mult)
                P.dma("sp", self.y.v(("y", blk), (slice(blk * 128 - CTX, blk * 128 - CTX + 128), slice(None))), xt[:])

    def load_h(self, tile, ti, q="sp"):
        t0, tn = TILES[ti]
        self.P.dma(q, tile[:, :, 0:tn], self.hT.v(("h", ti), (slice(None), slice(None), slice(t0, t0 + tn))))

    def phase_att(self, l):
        P = self.P
        ctx_out = (l < DEPTH - 1)
        import os
        sm = int(os.environ.get("MK_SET", "63"))
        wq = P.sb([128, 8, 384], BF16, "a_wq")
        wg = P.sb([128, 8, 384], BF16, "a_wg")
        wk = P.sb([128, 8, 128], BF16, "a_wk")
        wv = P.sb([128, 8, 128], BF16, "a_wv")
        if sm & 1:
            for j in range(3):
                for hh in range(2):
                    h = j + 3 * hh
                    P.dma("sp", wq[:, :, j * 128 + hh * 64: j * 128 + hh * 64 + 64], self.wb(l, C_Q + h * 64, 64))
                    P.dma("act", wg[:, :, j * 128 + hh * 64: j * 128 + hh * 64 + 64], self.wb(l, C_G + h * 64, 64))
            P.dma("sp", wk[:], self.wb(l, C_K, 128))
            P.dma("act", wv[:], self.wb(l, C_V, 128))
        cos = P.sb([128, TLAT], F32, "a_cos")
        sin = P.sb([128, TLAT], F32, "a_sin")
        if sm & 2:
            self.load_c(cos, "cos", "sp")
            self.load_c(sin, "sin", "act")
        pf = P.sb([128, 128], F32, "a_pf")
        permb = P.sb([128, 128], BF16, "a_permb")
        if sm & 4:
            self.load_c(pf, "perm")
            P.copy("dve", permb[:], pf[:])
        mband = P.sb([128, 3, 384], F32, "a_mband")
        o, w_ = CS["mband"]
        if sm & 8:
            P.dma("sp", mband[:].re("p a b -> p (a b)"), self.cst.v("c", (slice(None), slice(o, o + w_))))
        sink8 = P.sb([128, 6], F32, "a_sink8")
        o = PP["sink"][0]
        if sm & 16:
            P.ts("dve", sink8[:], self.ppt[:, o:o + 6], 8.0, None, ALU.mult)
        KB = P.sb([128, 4608], BF16, "a_KB")
        Vtm = P.sb([128, 36, 128], BF16, "a_V")
        if sm & 32:
            P.memset("pool", KB[:, 256:384], 0.0)
            P.memset("pool", KB[:, 4480:4608], 0.0)
            P.memset("pool", Vtm[:, 2, :], 0.0)
            P.memset("pool", Vtm[:, 35, :], 0.0)
        hb = [P.sb([128, 8, 512], BF16, f"a_h{i}") for i in range(2)]
        kf = P.sb([128, 512], F32, "a_kf")
        t1 = [P.sb([128, 512], F32, f"a_t1{i}") for i in range(2)]
        t2 = [P.sb([128, 512], F32, f"a_t2{i}") for i in range(2)]

        def kcol(tok):
            return tok if tok < CTX else tok + 128

        def proj(dst, wt, c0, cw, hbt, tn, lat_t0, bank, func=None):
            pp_ = self.ps(bank)
            for kc in range(8):
                P.mm(pp_[:, 0, 0:tn], wt[:, kc, c0:c0 + cw], hbt[:, kc, 0:tn], start=(kc == 0), stop=(kc == 7))
            if func is not None:
                P.act(dst, pp_[:, 0, 0:tn], func)
                return
            if lat_t0 is None:
                P.copy("act", dst, pp_[:, 0, 0:tn])
                return
            P.copy("act", kf[:, 0:tn], pp_[:, 0, 0:tn])
            pr_ = self.ps(2)
            P.mm(pr_[:, 0, 0:tn], pf[:], kf[:, 0:tn])
            a, b = t1[bank % 2], t2[bank % 2]
            P.tt("pool", a[:, 0:tn], kf[:, 0:tn], cos[:, lat_t0:lat_t0 + tn], ALU.mult)
            P.tt("dve", b[:, 0:tn], pr_[:, 0, 0:tn], sin[:, lat_t0:lat_t0 + tn], ALU.mult)
            P.tt("pool", dst, a[:, 0:tn], b[:, 0:tn], ALU.add)

        stage = int(os.environ.get("MK_ATT", "9"))
        if stage <= 0:
            return
        for ti, (t0, tn) in enumerate(TILES):
            h = hb[ti % 2]
            self.load_h(h, ti, "sp" if ti % 2 == 0 else "act")
            kc0 = kcol(t0)
            ma = int(os.environ.get("MK_A", "3"))
            if ti >= int(os.environ.get("MK_AT", "9")):
                break
            if ma & 1:
                proj(KB[:, kc0:kc0 + tn], wk, 0, 128, h, tn, None if ti == 0 else t0 - CTX, ti % 2)
            for bi in range(tn // 128):
                if not (ma & 2):
                    break
                vb = (t0 // 128 + bi)
                vb = vb if vb < 2 else vb + 1
                pv = self.ps(3 + bi % 2)
                for kc in range(8):
                    P.mm(pv[:, 0, 0:128], h[:, kc, bi * 128:(bi + 1) * 128], wv[:, kc, :], start=(kc == 0), stop=(kc == 7))
                P.copy("dve" if bi % 2 == 0 else "act", Vtm[:, vb, :], pv[:, 0, 0:128])

        import os
        stage = int(os.environ.get("MK_ATT", "9"))
        if stage <= 1:
            return
        qT = [P.sb([128, 3, 512], BF16, f"a_q{i}") for i in range(2)]
        gT = [P.sb([128, 3, 512], BF16, f"a_g{i}") for i in range(2)]
        om = [P.sb([128, 3, 512], BF16, f"a_om{i}") for i in range(2)]
        sc = [P.sb([128, 3, 640], F32, f"a_sc{i}") for i in range(2)]
        pe = [P.sb([128, 3, 640], F32, f"a_pe{i}") for i in range(2)]
        pn = [P.sb([128, 3, 640], BF16, f"a_pn{i}") for i in range(2)]
        pTs = [P.sb([128, 5, 128], BF16, f"a_pT{i}") for i in range(3)]
        st = [P.sb([128, 8, 3], F32, f"a_st{i}") for i in range(2)]
        it = 0
        npt = 0
        for ti, (t0, tn) in enumerate(TILES):
            if ti == 0 and not ctx_out:
                continue
            h = hb[ti % 2]
            self.load_h(h, ti, "sp" if ti % 2 == 0 else "act")
            q_, g_, o_ = qT[ti % 2], gT[ti % 2], om[ti % 2]
            for j in range(3):
                proj(q_[:, j, 0:tn], wq, j * 128, 128, h, tn, None if ti == 0 else t0 - CTX, j % 2)
                proj(g_[:, j, 0:tn], wg, j * 128, 128, h, tn, None, (j + 1) % 2, func=AF.Silu)
            if stage <= 2:
                break
            for bi in range(tn // 128):
                if stage <= 5 and (bi > 0 or ti > 1):
                    break
                qs = slice(bi * 128, bi * 128 + 128)
                isctx = (ti == 0)
                n = (t0 - CTX) // 128 + bi if not isctx else None
                lo = 384 if isctx else 0
                po = self.ps(2)
                for kg in range(2):
                    ps_ = slice(kg * 64, kg * 64 + 64)
                    s_, e_, n_, stt_ = sc[it % 2], pe[it % 2], pn[it % 2], st[it % 2]
                    it += 1
                    for j in range(3):
                        pb = self.ps(3 + 2 * (j % 2))
                        pc = self.ps(4 + 2 * (j % 2))
                        if not isctx:
                            kb0 = 256 + n * 128
                            P.mm(pb[:, 0, 0:384], q_[ps_, j, qs], KB[ps_, kb0:kb0 + 384])
                            var = 0 if n == 0 else (2 if n == 31 else 1)
                            P.tt("dve", s_[:, j, 0:384], pb[:, 0, 0:384], mband[:, var, :], ALU.add)
                        P.mm(pc[:, 0, 0:256], q_[ps_, j, qs], KB[ps_, 0:256])
                        P.copy("act", s_[:, j, 384:640], pc[:, 0, 0:256])
                    if stage <= 3:
                        continue
                    P.reduce("dve", stt_[:, 0, :], s_[:, :, lo:640], ALU.max)
                    P.tt("dve", stt_[:, 1, :], stt_[:, 0, :], sink8[:, kg * 3:kg * 3 + 3], ALU.max)
                    P.ts("dve", stt_[:, 2, :], stt_[:, 1, :], -0.125, None, ALU.mult)
                    for j in range(3):
                        P.act(e_[:, j, lo:640], s_[:, j, lo:640], AF.Exp, bias=stt_[:, 2, j:j + 1], scale=0.125,
                              accum_out=stt_[:, 3, j:j + 1])
                    P.stt("dve", stt_[:, 4, :], sink8[:, kg * 3:kg * 3 + 3], 0.125, stt_[:, 2, :], ALU.mult, ALU.add)
                    P.act(stt_[:, 5, :], stt_[:, 4, :], AF.Exp)
                    P.tt("dve", stt_[:, 6, :], stt_[:, 5, :], stt_[:, 3, :], ALU.add)
                    P.recip(stt_[:, 7, :], stt_[:, 6, :])
                    P.tt("dve", n_[:, :, lo:640], e_[:, :, lo:640],
                         stt_[:, 7, :].m(lambda a: a.unsqueeze(2)).bc([128, 3, 640 - lo]), ALU.mult)
                    if stage <= 4:
                        continue
                    for j in range(3):
                        pt = self.ps(7, 1, BF16)
                        chunks = ([] if isctx else [0, 1, 2]) + [3, 4]
                        for c in chunks:
                            P.tr(pt[:, 0, c * 128:(c + 1) * 128], n_[:, j, c * 128:(c + 1) * 128], self.c_identb[:])
                        pts = pTs[npt % 3]
                        npt += 1
                        c0 = chunks[0]
                        if npt % 2 == 0:
                            P.copy("act", pts[:, c0:5, :], pt[:, 0, c0 * 128:640].re("p (c t) -> p c t", t=128))
                        else:
                            P.copy("dve", pts[:, c0:5, :], pt[:, 0, c0 * 128:640].re("p (c t) -> p c t", t=128))
                        for ci, c in enumerate(chunks):
                            vb = (2 + n + c) if c < 3 else (c - 3)
                            P.mm(po[ps_, 0, j * 128:(j + 1) * 128], Vtm[:, vb, kg * 64:kg * 64 + 64], pts[:, c, :],
                                 start=(ci == 0), stop=(ci == len(chunks) - 1))
                P.tt("dve", o_[:, :, qs], po[:, 0, 0:384].re("p (j t) -> p j t", j=3), g_[:, :, qs], ALU.mult)
            for j in range(3):
                for hh in range(2):
                    hd = j + 3 * hh
                    P.dma("sp" if hh == 0 else "act",
                          self.mix.v(("ma", ti, hd), (slice(hd * 64, hd * 64 + 64), slice(t0, t0 + tn))),
                          o_[hh * 64:hh * 64 + 64, j, 0:tn])

    def load_h_halo(self, tile, ti, q="sp"):
        P = self.P
        t0, tn = TILES[ti]
        lo, hi = t0 - 1, t0 + tn + 1
        if ti <= 1:
            P.memset("pool", tile[:, :, 0:1], 0.0)
            lo = t0
        if ti == 0 or ti == len(TILES) - 1:
            P.memset("pool", tile[:, :, tn + 1:tn + 2], 0.0)
            hi = t0 + tn
        P.dma(q, tile[:, :, lo - t0 + 1:hi - t0 + 1], self.hT.v(("h", ti), (slice(None), slice(None), slice(lo, hi))))

    def scaled_w(self, l, c0, ncol, rows, name, roff=0):
        P = self.P
        rows = [r[:, roff:roff + ncol] for r in rows]
        w3 = P.sb([128, 3, 8, ncol], BF16, name)
        mk_ = P.mark()
        half = ncol // 2
        stg = [P.sb([128, 8, half], F32, f"{name}_st{i}") for i in range(2)]
        for hf in range(2):
            st = stg[hf]
            P.dma("sp" if hf == 0 else "act", st[:],
                  self.w_in.v(("w", l), fn=lambda ap, hf=hf: ap[l].rearrange("(kc p) n -> p kc n", p=128)[:, :, c0 + hf * half:c0 + (hf + 1) * half]))
            for tap in range(3):
                P.tt("dve" if tap != 1 else "pool", w3[:, tap, :, hf * half:(hf + 1) * half], st[:],
                     rows[tap][:, hf * half:(hf + 1) * half].m(lambda a: a.unsqueeze(1)).bc([128, 8, half]), ALU.mult)
        P.release(mk_)
        return w3

    def phase_ssd(self, l):
        P = self.P
        ctx_out = (l < DEPTH - 1)
        xsT = P.sb([128, 3, S], F32, "s_xsT")
        BT = P.sb([128, 2, S], BF16, "s_BT")
        CT = P.sb([128, 2, S], BF16, "s_CT")
        dt_tm = P.sb([128, NBLK, 12], F32, "s_dt")
        dtA_tm = P.sb([128, NBLK, 12], F32, "s_dtA")
        aneg = P.sb([128, 12], F32, "s_aneg")
        o = PR["alog"][0]
        P.act(aneg[:], self.prt[:, o:o + 12], AF.Exp)
        P.ts("dve", aneg[:], aneg[:], -1.0, None, ALU.mult)
        mk1 = P.mark()
        o = PR2["convr"][0]
        crow = P.sb([128, 3 * 896], F32, "s_crow")
        P.dma("sp", crow[:], self.pr2.v(l, fn=lambda a: a[l][:, o:o + 3 * 896]))
        rows = [crow[:, k * 896:(k + 1) * 896] for k in range(3)]
        wx3 = self.scaled_w(l, C_XBC, 896, rows, "s_wx3")
        wdt = P.sb([128, 8, 12], BF16, "s_wdt")
        P.dma("sp", wdt[:], self.wb(l, C_DT, 12))
        hh = [P.sb([128, 8, 514], BF16, f"s_hh{i}") for i in range(2)]
        dtmp = [P.sb([128, 12], F32, f"s_dtmp{i}") for i in range(2)]
        ocb = PP["convb"][0]
        nb = 0
        for ti, (t0, tn) in enumerate(TILES):
            h = hh[ti % 2]
            self.load_h_halo(h, ti, "sp" if ti % 2 == 0 else "act")
            for c in range(7):
                pb = self.ps(c % 2)
                n = 0
                for tap in range(3):
                    for kc in range(8):
                        P.mm(pb[:, 0, 0:tn], wx3[:, tap, kc, c * 128:(c + 1) * 128], h[:, kc, tap:tap + tn],
                             start=(n == 0), stop=(n == 23))
                        n += 1
                if c < 3:
                    dst = xsT[:, c, t0:t0 + tn]
                elif c < 5:
                    dst = BT[:, c - 3, t0:t0 + tn]
                else:
                    dst = CT[:, c - 5, t0:t0 + tn]
                P.act(dst, pb[:, 0, 0:tn], AF.Silu, bias=self.ppt[:, ocb + c:ocb + c + 1])
            for bi in range(tn // 128):
                blk = t0 // 128 + bi
                pd = self.ps(2 + blk % 2)
                for kc in range(8):
                    P.mm(pd[:, 0, 0:12], h[:, kc, 1 + bi * 128:1 + (bi + 1) * 128], wdt[:, kc, :], start=(kc == 0), stop=(kc == 7))
                d_ = dtmp[blk % 2]
                o = PR["dtb"][0]
                P.tt("dve", d_[:], pd[:, 0, 0:12], self.prt[:, o:o + 12], ALU.add)
                P.act(d_[:], d_[:], AF.Exp)
                P.act(dt_tm[:, blk, :], d_[:], AF.Ln, bias=1.0)
                P.tt("dve", dtA_tm[:, blk, :], dt_tm[:, blk, :], aneg[:], ALU.mult)
        P.release(mk1)
        if int(os.environ.get("MK_SSD", "9")) <= 1:
            self.dbg_xsT = xsT
            return
        wz = P.sb([128, 8, 384], BF16, "s_wz")
        P.dma("sp", wz[:], self.wb(l, C_Z, 384))
        cf = {}
        for nme in ("snf", "snb", "trif", "trib", "ones", "hlo", "hhi"):
            cf[nme] = P.sb([128, 128], F32, "s_c_" + nme)
            self.load_c(cf[nme], nme, "act")
        St = P.sb([128, 6, 64], F32, "s_S")
        NB2 = 2
        xs_sb = [P.sb([128, 384], F32, f"s_xs{i}") for i in range(NB2)]
        Bt_sb = [P.sb([128, 2, 128], BF16, f"s_Bt{i}") for i in range(NB2)]
        Gs = [P.sb([128, 2, 128], F32, f"s_G{i}") for i in range(NB2)]
        cumc = [P.sb([128, 6], F32, f"s_cc{i}") for i in range(NB2)]
        dbc = [P.sb([128, 6, 128], F32, f"s_dbc{i}") for i in range(NB2)]
        Dm = [P.sb([128, 6, 128], F32, f"s_D{i}") for i in range(NB2)]
        Em = [P.sb([128, 6, 128], F32, f"s_E{i}") for i in range(NB2)]
        scT = [P.sb([128, 6, 128], BF16, f"s_sc{i}") for i in range(NB2)]
        Er = [P.sb([128, 6, 128], F32, f"s_Er{i}") for i in range(NB2)]
        Cd = [P.sb([128, 6, 128], F32, f"s_Cd{i}") for i in range(NB2)]
        xdt = [P.sb([128, 6, 64], BF16, f"s_xdt{i}") for i in range(NB2)]
        xdtt = [P.sb([128, 6, 64], BF16, f"s_xdtt{i}") for i in range(NB2)]
        ysb = [P.sb([128, 3, 128], F32, f"s_y{i}") for i in range(NB2)]
        yfl = [P.sb([128, 3, 128], F32, f"s_yf{i}") for i in range(NB2)]
        hz = [P.sb([128, 8, 128], BF16, f"s_hz{i}") for i in range(NB2)]
        zs = [P.sb([128, 3, 128], F32, f"s_z{i}") for i in range(NB2)]
        sq = [P.sb([128, 3, 128], F32, f"s_sq{i}") for i in range(NB2)]
        rs = [P.sb([128, 2, 128], F32, f"s_rs{i}") for i in range(NB2)]
        mo = [P.sb([128, 3, 128], BF16, f"s_mo{i}") for i in range(NB2)]
        osd, onw = PP["ssmd"][0], PP["ssmnw"][0]
        it = 0
        for d in range(2):
            order = list(range(NBLK)) if d == 0 else [1, 0] + list(range(NBLK - 1, 1, -1))
            tri = cf["trif"] if d == 0 else cf["trib"]
            sneg = cf["snf"] if d == 0 else cf["snb"]
            last = 127 if d == 0 else 0
            P.memset("dve", St[:], 0.0)
            for c in order:
                if d == 1 and c < 2 and not ctx_out:
                    pass
                k = it % NB2
                it += 1
                cs_ = slice(c * 128, (c + 1) * 128)
                pxs = self.ps(0)
                for j in range(3):
                    P.tr(pxs[:, 0, j * 128:(j + 1) * 128], xsT[:, j, cs_], self.c_ident[:])
                P.copy("act", xs_sb[k][:], pxs[:, 0, 0:384])
                pbt = self.ps(1, 1, BF16)
                for g in range(2):
                    P.tr(pbt[:, 0, g * 128:(g + 1) * 128], BT[:, g, cs_], self.c_identb[:])
                P.copy("dve", Bt_sb[k][:].re("p g n -> p (g n)"), pbt[:, 0, 0:256])
                pg = self.ps(2)
                for g in range(2):
                    P.mm(pg[:, 0, g * 128:(g + 1) * 128], BT[:, g, cs_], CT[:, g, cs_])
                P.copy("act", Gs[k][:].re("p g n -> p (g n)"), pg[:, 0, 0:256])
                dA = dtA_tm[:, c, d * 6:(d + 1) * 6]
                P.mm(pg[:, 0, 256:262], tri[:], dA)
                P.copy("dve", cumc[k][:], pg[:, 0, 256:262])
                P.copy("pool", dbc[k][:], dA.m(lambda a: a.unsqueeze(2)).bc([128, 6, 128]))
                pcr = self.ps(3, 2)
                for h in range(6):
                    P.mm(pcr[:, h // 4, (h % 4) * 128:(h % 4 + 1) * 128], dbc[k][:, h, :], tri[:])
                pcr_v = pcr.re("p b n -> p (b n)")[:, 0:768].re("p (h n) -> p h n", h=6)
                P.tt("dve", Dm[k][:], pcr_v, cumc[k][:].m(lambda a: a.unsqueeze(2)).bc([128, 6, 128]), ALU.subtract)
                P.act(Er[k][:], pcr_v, AF.Exp)
                P.tt("pool", Dm[k][:], Dm[k][:], sneg[:].m(lambda a: a.unsqueeze(1)).bc([128, 6, 128]), ALU.add)
                P.act(Em[k][:], Dm[k][:], AF.Exp)
                P.tt("pool", scT[k][:].re("p (g h) n -> p g h n", g=2), Em[k][:].re("p (g h) n -> p g h n", g=2),
                     Gs[k][:].m(lambda a: a.unsqueeze(2)).bc([128, 2, 3, 128]), ALU.mult)
                P.tt("dve", Cd[k][:].re("p (g h) n -> p g h n", g=2), Er[k][:].re("p (g h) n -> p g h n", g=2),
                     CT[:, :, cs_].m(lambda a: a.unsqueeze(2)).bc([128, 2, 3, 128]), ALU.mult)
                dtc = dt_tm[:, c, d * 6:(d + 1) * 6]
                P.tt("dve", xdt[k][:], xs_sb[k][:].re("p (h q) -> p h q", h=6),
                     dtc.m(lambda a: a.unsqueeze(2)).bc([128, 6, 64]), ALU.mult)
                P.tt("pool", xdtt[k][:], xdt[k][:], Em[k][:, :, last:last + 1].bc([128, 6, 64]), ALU.mult)
                py = self.ps(5)
                for h in range(6):
                    o_ = py[(h % 2) * 64:(h % 2) * 64 + 64, 0, (h // 2) * 128:(h // 2 + 1) * 128]
                    P.mm(o_, xdt[k][:, h, :], scT[k][:, h, :], start=True, stop=False)
                    P.mm(o_, St[:, h, :], Cd[k][:, h, :], start=False, stop=True)
                pcs = self.ps(6)
                for h in range(6):
                    P.mm(pcs[:, 0, h * 64:(h + 1) * 64], Bt_sb[k][:, h // 3, :], xdtt[k][:, h, :])
                P.tt("dve", St[:], St[:], Er[k][:, :, last:last + 1].bc([128, 6, 64]), ALU.mult)
                P.tt("dve", St[:], St[:], pcs[:, 0, 0:384].re("p (h q) -> p h q", h=6), ALU.add)
                if d == 0:
                    P.copy("act", ysb[k][:].re("p j n -> p (j n)"), py[:, 0, 0:384])
                    P.dma("sp", self.yf.v(("yf", c), fn=lambda ap, cs_=cs_: ap.rearrange("(j p) t -> p j t", p=128)[:, :, cs_]), ysb[k][:])
                    continue
                if c < 2 and not ctx_out:
                    continue
                P.dma("act", yfl[k][:], self.yf.v(("yf", c), fn=lambda ap, cs_=cs_: ap.rearrange("(j p) t -> p j t", p=128)[:, :, cs_]))
                P.dma("sp", hz[k][:], self.hT.v(("h", "z"), (slice(None), slice(None), cs_)))
                pz = self.ps(7)
                for j in range(3):
                    for kc in range(8):
                        P.mm(pz[:, 0, j * 128:(j + 1) * 128], wz[:, kc, j * 128:(j + 1) * 128], hz[k][:, kc, :],
                             start=(kc == 0), stop=(kc == 7))
                P.act(zs[k][:].re("p j n -> p (j n)"), pz[:, 0, 0:384], AF.Silu)
                y_ = ysb[k]
                P.tt("dve", y_[:].re("p j n -> p (j n)"), py[:, 0, 0:384], yfl[k][:].re("p j n -> p (j n)"), ALU.add)
                P.tt("pool", yfl[k][:], xsT[:, :, cs_], self.ppt[:, osd:osd + 3].m(lambda a: a.unsqueeze(2)).bc([128, 3, 128]), ALU.mult)
                P.tt("pool", y_[:], y_[:], yfl[k][:], ALU.add)
                P.tt("dve", y_[:], y_[:], zs[k][:], ALU.mult)
                P.act(sq[k][:], y_[:], AF.Square)
                pgs = self.ps(7)
                P.mm(pgs[:, 0, 384:512].m(lambda a: a), cf["ones"][:], sq[k][:, 0, :], start=True, stop=False)
                P.mm(pgs[:, 0, 384:512], cf["hlo"][:], sq[k][:, 1, :], start=False, stop=True)
                pgs2 = self.ps(6)
                P.mm(pgs2[:, 0, 384:512], cf["hhi"][:], sq[k][:, 1, :], start=True, stop=False)
                P.mm(pgs2[:, 0, 384:512], cf["ones"][:], sq[k][:, 2, :], start=False, stop=True)
                P.act(rs[k][:, 0, :], pgs[:, 0, 384:512], AF.Sqrt, bias=1e-5, scale=1.0 / 192)
                P.act(rs[k][:, 1, :], pgs2[:, 0, 384:512], AF.Sqrt, bias=1e-5, scale=1.0 / 192)
                P.recip(rs[k][:], rs[k][:])
                m_ = mo[k]
                P.stt("dve", m_[:, 0, :], y_[:, 0, :], self.ppt[:, onw:onw + 1], rs[k][:, 0, :], ALU.mult, ALU.mult)
                P.stt("dve", m_[0:64, 1, :], y_[0:64, 1, :], self.ppt[0:64, onw + 1:onw + 2], rs[k][0:64, 0, :], ALU.mult, ALU.mult)
                P.stt("dve", m_[64:128, 1, :], y_[64:128, 1, :], self.ppt[64:128, onw + 1:onw + 2], rs[k][64:128, 1, :], ALU.mult, ALU.mult)
                P.stt("dve", m_[:, 2, :], y_[:, 2, :], self.ppt[:, onw + 2:onw + 3], rs[k][:, 1, :], ALU.mult, ALU.mult)
                P.dma("sp", self.mix.v(("ms", c), fn=lambda ap, cs_=cs_: ap[640:1024].rearrange("(j p) t -> p j t", p=128)[:, :, cs_]), m_[:])

    def phase_rwkv(self, l):
        P = self.P
        ctx_out = (l < DEPTH - 1)
        LWC = -0.6065306597126334
        cf = {}
        for nme in ("blk", "lowf", "lowb"):
            cf[nme] = P.sb([128, 128], F32, "r_c_" + nme)
            self.load_c(cf[nme], nme, "act")
        m4 = {nme: P.sb([128, 2, 256], F32, "r_m4_" + nme) for nme in ("rmf", "rmb")}
        mk0 = P.mark()
        for nme in ("rmf", "rmb"):
            t_ = P.sb([128, 256], F32, "r_c_" + nme)
            self.load_c(t_, nme, "act")
            P.copy("dve", m4[nme][:], t_[:].m(lambda a: a.unsqueeze(1)).bc([128, 2, 256]))
        P.release(mk0)
        ones64 = P.sb([128, 64], F32, "r_ones")
        P.memset("pool", ones64[:], 1.0)
        omka = P.sb([128, 4], F32, "r_omka")
        oka = PP["ka"][0]
        P.ts("dve", omka[:], self.ppt[:, oka:oka + 4], -1.0, 1.0, ALU.mult, ALU.add)
        wupf = P.sb([128, 256], F32, "r_wupf")
        aupf = P.sb([128, 256], F32, "r_aupf")
        wupb = P.sb([128, 256], BF16, "r_wupb")
        aupb = P.sb([128, 256], BF16, "r_aupb")
        for d in range(2):
            P.dma("sp", wupf[d * 64:(d + 1) * 64, :], self.wup.v(("u", l, d), fn=lambda ap, d=d: ap[l][d]))
            P.dma("act", aupf[d * 64:(d + 1) * 64, :], self.aup.v(("u", l, d), fn=lambda ap, d=d: ap[l][d]))
        P.copy("dve", wupb[:], wupf[:])
        P.copy("pool", aupb[:], aupf[:])
        wdT = P.sb([128, S], BF16, "r_wdT")
        adT = P.sb([128, S], BF16, "r_adT")
        mk1 = P.mark()
        rows = self.shift_rows(l)
        hh = [P.sb([128, 8, 514], BF16, f"r_hh{i}") for i in range(2)]
        wl3 = self.scaled_w(l, C_RW + 768, 256, rows, "r_wl3", roff=768)
        for ti, (t0, tn) in enumerate(TILES):
            h = hh[ti % 2]
            self.load_h_halo(h, ti, "sp" if ti % 2 == 0 else "act")
            for c in range(2):
                pb = self.ps(c)
                n = 0
                for tap in range(3):
                    for kc in range(8):
                        P.mm(pb[:, 0, 0:tn], wl3[:, tap, kc, c * 128:(c + 1) * 128], h[:, kc, tap:tap + tn],
                             start=(n == 0), stop=(n == 23))
                        n += 1
                if c == 0:
                    P.act(wdT[:, t0:t0 + tn], pb[:, 0, 0:tn], AF.Tanh)
                else:
                    P.copy("act", adT[:, t0:t0 + tn], pb[:, 0, 0:tn])
        P.release(mk1)
        RW = int(os.environ.get("MK_RW", "99"))
        if RW <= 1:
            return
        mk_hp = P.mark()
        for hp in range(2):
            rT = P.sb([128, S], F32, "r_rT")
            kT = P.sb([128, S], F32, "r_kT")
            vT = P.sb([128, S], F32, "r_vT")
            ysum = P.sb([128, S], F32, "r_ysum")
            mk2 = P.mark()
            rows = self.shift_rows(l)
            hh = [P.sb([128, 8, 514], BF16, f"r_hh{i}") for i in range(2)]
            w3 = [self.scaled_w(l, C_RW + j * 256 + hp * 128, 128, rows, f"r_w3{j}", roff=j * 256 + hp * 128) for j in range(3)]
            for ti, (t0, tn) in enumerate(TILES):
                h = hh[ti % 2]
                self.load_h_halo(h, ti, "sp" if ti % 2 == 0 else "act")
                for j, dst in enumerate((rT, kT, vT)):
                    pb = self.ps(j % 2)
                    n = 0
                    for tap in range(3):
                        for kc in range(8):
                            P.mm(pb[:, 0, 0:tn], w3[j][:, tap, kc, :], h[:, kc, tap:tap + tn], start=(n == 0), stop=(n == 23))
                            n += 1
                    P.copy("act", dst[:, t0:t0 + tn], pb[:, 0, 0:tn])
            P.release(mk2)
            if RW <= 2:
                return
            mk3 = P.mark()
            self.rwkv_scan(l, hp, rT, kT, vT, wdT, adT, ysum, wupb, aupb, cf, m4, ones64, omka, LWC)
            P.release(mk3)
            self.rwkv_finish(l, hp, rT, kT, vT, ysum, cf, ctx_out)
            P.release(mk_hp)

    def shift_rows(self, l):
        P = self.P
        o = PR2["shr"][0]
        sr = P.sb([128, 3 * 1024], F32, "r_shr")
        P.dma("sp", sr[:], self.pr2.v(l, fn=lambda a: a[l][:, o:o + 3 * 1024]))
        r0, r2, r1 = sr[:, 0:1024], sr[:, 1024:2048], sr[:, 2048:3072]
        P.tt("dve", r1, r0, r2, ALU.add)
        P.ts("dve", r1, r1, -1.0, 1.0, ALU.mult, ALU.add)
        return [r0, r1, r2]

    def rwkv_scan(self, l, hp, rT, kT, vT, wdT, adT, ysum, wupb, aupb, cf, m4, ones64, omka, LWC):
        P = self.P
        NB = 2
        f32t = lambda n, w=512: [P.sb([128, w], F32, f"r_{n}")] * NB
        sg, aa, cum = f32t("sg"), f32t("aa"), f32t("cum")
        lw = sg
        cumx = [P.sb([128, 8], F32, "r_tot")] * NB
        e1, e2, e3, e4 = f32t("e1"), f32t("e2"), f32t("e3"), f32t("e4")
        kkr, sqk, nrm, kd, bq = f32t("kkr"), f32t("sqk"), f32t("nrm"), f32t("kd"), f32t("bq")
        kk, tmpa = kkr, nrm
        WC = [P.sb([128, 8], F32, f"r_WC{i}") for i in range(NB)]
        arb = [P.sb([128, 4, 2, 128], F32, "r_arb")] * NB
        kb = [P.sb([128, 512], F32, "r_kb")] * NB
        bb = [P.sb([128, 512], F32, "r_bb")] * NB
        kt = [P.sb([128, 512], F32, "r_kt")] * NB
        btl = [P.sb([128, 512], F32, "r_btl")] * NB
        G4 = 4
        TM = [P.sb([128, 4, 128], F32, f"r_TM{i}") for i in range(G4)]
        AT = [P.sb([128, 2, 4, 128], F32, f"r_AT{i}") for i in range(G4)]
        X1 = [P.sb([128, 2, 128], F32, f"r_X1{i}") for i in range(G4)]
        XX = [P.sb([128, 2, 2, 128], F32, f"r_XX{i}") for i in range(G4)]
        Qf = [P.sb([128, 2, 128], F32, f"r_Qf{i}") for i in range(G4)]
        W1 = [P.sb([128, 128], F32, f"r_W1{i}") for i in range(G4)]
        AU = [P.sb([128, 256], F32, f"r_AU{i}") for i in range(G4)]
        Y0T = [P.sb([128, 128], F32, f"r_Y0T{i}") for i in range(G4)]
        RhT = [P.sb([128, 128], F32, f"r_RhT{i}") for i in range(G4)]
        MTt = [P.sb([128, 128], F32, "r_MTt")] * 2
        MT = [[P.sb([128, 128], F32, f"r_MT{i}_{c}") for c in range(2)] for i in range(G4)]
        Ns = [[P.sb([128, 64], F32, f"r_Ns{i}_{c}") for c in range(2)] for i in range(G4)]
        ytmp = [P.sb([128, 64], F32, f"r_yt{i}") for i in range(2)]
        Sseq = P.sb([128, 4, 64], F32, "r_Sseq")
        ow0, oa0, okk, oka = PP["w0"][0], PP["a0"][0], PP["kk"][0], PP["ka"][0]
        if os.environ.get("MK_MEM"):
            print("rwkv_scan arena used", P._aoff, "of", P.ARENA)
        nblk_done = 0
        for d in range(2):
            if d == 1 and hp == 0 and os.environ.get("MK_RWDBG"):
                if not hasattr(self, "dbgy"):
                    self.dbgy = P.dram("dbgy", [128, S], F32, kind="ExternalOutput")
                P.dma("sp", self.dbgy.v(0), ysum[:])
            col = d * 2 + hp
            mask4 = m4["rmf"] if d == 0 else m4["rmb"]
            low = cf["lowf"] if d == 0 else cf["lowb"]
            P.memset("dve", Sseq[:, 0, :], 0.0)
            si = 0
            tiles = list(range(len(TILES))) if d == 0 else [0] + list(range(len(TILES) - 1, 0, -1))
            def prepA(tix):
                ti = tiles[tix]
                t0, tn = TILES[ti]
                nch = tn // 64
                nbk = tn // 128
                k_ = tix % NB
                ts_ = slice(t0, t0 + tn)
                ds_ = slice(d * 64, (d + 1) * 64)
                pxw, pxa = self.ps(0), self.ps(1)
                P.mm(pxw[:, 0, 0:tn], wupb[ds_, hp * 128:(hp + 1) * 128], wdT[ds_, ts_])
                yield
                P.mm(pxa[:, 0, 0:tn], aupb[ds_, hp * 128:(hp + 1) * 128], adT[ds_, ts_])
                yield
                P.act(sg[k_][:, 0:tn], pxw[:, 0, 0:tn], AF.Sigmoid, bias=self.ppt[:, ow0 + col:ow0 + col + 1])
                yield
                P.act(aa[k_][:, 0:tn], pxa[:, 0, 0:tn], AF.Sigmoid, bias=self.ppt[:, oa0 + col:oa0 + col + 1])
                yield
                P.ts("dve", lw[k_][:, 0:tn], sg[k_][:, 0:tn], LWC, None, ALU.mult)
                yield
                for c in range(nch):
                    cs = slice(c * 64, (c + 1) * 64)
                    P.scan("dve", cum[k_][:, cs], ones64[:], lw[k_][:, cs], 0.0, ALU.mult, ALU.add)
                    yield
                c3 = lambda t_: t_[:, 0:tn].re("p (c n) -> p c n", n=64)
                P.act(WC[k_][:, 0:nch].m(lambda a: a.unsqueeze(2)), c3(cum[k_])[:, :, 63:64], AF.Exp)
                yield
                if d == 1:
                    P.copy("pool", cumx[k_][:, 0:nch].m(lambda a: a.unsqueeze(2)), c3(cum[k_])[:, :, 63:64])
                    yield
                    P.tt("dve", cum[k_][:, 0:tn], lw[k_][:, 0:tn], cum[k_][:, 0:tn], ALU.subtract)
                    yield
                    P.tt("dve", c3(cum[k_]), c3(cum[k_]), cumx[k_][:, 0:nch].m(lambda a: a.unsqueeze(2)).bc([128, nch, 64]), ALU.add)
                    yield
                    tot = None
                P.tt("pool", e4[k_][:, 0:tn], cum[k_][:, 0:tn], lw[k_][:, 0:tn], ALU.subtract)
                yield
                P.act(e1[k_][:, 0:tn], cum[k_][:, 0:tn], AF.Exp)
                yield
                P.act(e2[k_][:, 0:tn], cum[k_][:, 0:tn], AF.Exp, scale=-1.0)
                yield
                P.act(e3[k_][:, 0:tn], e4[k_][:, 0:tn], AF.Exp)
                yield
                P.tt("dve", c3(e4[k_]), c3(e2[k_]), WC[k_][:, 0:nch].m(lambda a: a.unsqueeze(2)).bc([128, nch, 64]), ALU.mult)
                yield
                P.ts("dve", kkr[k_][:, 0:tn], kT[:, ts_], self.ppt[:, okk + col:okk + col + 1], None, ALU.mult)
                yield
                P.act(sqk[k_][:, 0:tn], kkr[k_][:, 0:tn], AF.Square)
                yield
                pss = self.ps(1)
                P.mm(pss[:, 0, 0:tn], cf["blk"][:], sqk[k_][:, 0:tn])
                yield
                P.act(nrm[k_][:, 0:tn], pss[:, 0, 0:tn], AF.Sqrt)
                yield
                P.ts("dve", nrm[k_][:, 0:tn], nrm[k_][:, 0:tn], 1e-12, None, ALU.max)
                yield
                P.recip(nrm[k_][:, 0:tn], nrm[k_][:, 0:tn])
                yield
                P.tt("dve", kk[k_][:, 0:tn], kkr[k_][:, 0:tn], nrm[k_][:, 0:tn], ALU.mult)
                yield
                P.ts("pool", tmpa[k_][:, 0:tn], aa[k_][:, 0:tn], self.ppt[:, oka + col:oka + col + 1], omka[:, col:col + 1], ALU.mult, ALU.add)
                yield
                P.tt("pool", kd[k_][:, 0:tn], kT[:, ts_], tmpa[k_][:, 0:tn], ALU.mult)
                yield
                P.tt("dve", bq[k_][:, 0:tn], kk[k_][:, 0:tn], aa[k_][:, 0:tn], ALU.mult)
                yield
            def prepB(tix):
                ti = tiles[tix]
                t0, tn = TILES[ti]
                nch = tn // 64
                nbk = tn // 128
                k_ = tix % NB
                ts_ = slice(t0, t0 + tn)
                ds_ = slice(d * 64, (d + 1) * 64)
                c3 = lambda t_: t_[:, 0:tn].re("p (c n) -> p c n", n=64)
                b3 = lambda t_: t_[:, 0:tn].re("p (b n) -> p b n", n=128)
                P.ts("pool", sqk[k_][:, 0:tn], kk[k_][:, 0:tn], -1.0, None, ALU.mult)
                yield
                P.tt("dve", arb[k_][:, 0:nbk, 0, :], b3(sqk[k_]), b3(e3[k_]), ALU.mult)
                yield
                P.tt("pool", arb[k_][:, 0:nbk, 1, :], rT[:, ts_].re("p (b n) -> p b n", n=128), b3(e1[k_]), ALU.mult)
                yield
                P.tt("dve", kb[k_][:, 0:tn], kd[k_][:, 0:tn], e2[k_][:, 0:tn], ALU.mult)
                yield
                P.tt("pool", bb[k_][:, 0:tn], bq[k_][:, 0:tn], e2[k_][:, 0:tn], ALU.mult)
                yield
                P.tt("dve", kt[k_][:, 0:tn], kd[k_][:, 0:tn], e4[k_][:, 0:tn], ALU.mult)
                yield
                P.tt("pool", btl[k_][:, 0:tn], bq[k_][:, 0:tn], e4[k_][:, 0:tn], ALU.mult)
                yield
            def step(gen, n=10 ** 9):
                for _ in range(n):
                    if next(gen, 'done') == 'done':
                        return
            step(prepA(0))
            step(prepB(0))
            for tix, ti in enumerate(tiles):
                t0, tn = TILES[ti]
                nch = tn // 64
                nbk = tn // 128
                k_ = tix % NB
                ts_ = slice(t0, t0 + tn)
                ds_ = slice(d * 64, (d + 1) * 64)
                nxtA = prepA(tix + 1) if tix + 1 < len(tiles) else iter(())
                blocks = list(range(nbk)) if d == 0 else list(range(nbk - 1, -1, -1))
                G = len(blocks)
                idm = lambda t_: t_[:].m(lambda a: a.unsqueeze(1)).bc([128, 2, 128])
                ev = 0
                for g, bi in enumerate(blocks):
                    bs = slice(bi * 128, (bi + 1) * 128)
                    pt = self.ps(2 + g)
                    P.tr(pt[:, 0, 0:128], arb[k_][:, bi, 0, :], self.c_ident[:])
                    P.tr(pt[:, 0, 128:256], btl[k_][:, bs], self.c_ident[:])
                    P.tr(pt[:, 0, 256:384], kt[k_][:, bs], self.c_ident[:])
                    P.tr(pt[:, 0, 384:512], vT[:, t0 + bi * 128:t0 + (bi + 1) * 128], self.c_ident[:])
                for g, bi in enumerate(blocks):
                    P.copy("act" if g % 2 == 0 else "dve", TM[g][:].re("p a n -> p (a n)"), self.ps(2 + g)[:, 0, 0:512])
                px2 = self.ps(6, 2)
                for half in range(0, G, 2):
                    gs = list(range(half, min(half + 2, G)))
                    for g in gs:
                        bi = blocks[g]
                        bs = slice(bi * 128, (bi + 1) * 128)
                        pa = self.ps(2 + 2 * (g % 2), 2)
                        for h in range(2):
                            hs = slice(h * 64, (h + 1) * 64)
                            ar_ = arb[k_][hs, bi, :, :].re("p a n -> p (a n)")
                            P.mm(pa[:, h, 0:256], bb[k_][hs, bs], ar_)
                            P.mm(pa[:, h, 256:512], kb[k_][hs, bs], ar_)
                            P.mm(px2[:, h, g * 128:(g + 1) * 128], arb[k_][hs, bi, 0, :], bb[k_][hs, bs])
                    for g in gs:
                        pa = self.ps(2 + 2 * (g % 2), 2)
                        P.tt("dve", AT[g][:].re("p h a n -> p h (a n)"), pa,
                             mask4[:].re("p a n -> p (a n)").m(lambda a: a.unsqueeze(1)).bc([128, 2, 512]), ALU.mult)
                for g in range(G):
                    P.tt("dve", X1[g][:], px2[:, :, g * 128:(g + 1) * 128], idm(low), ALU.mult)
                    P.tt("pool", Qf[g][:], AT[g][:, :, 0, :], idm(self.c_ident), ALU.add)
                xk = [[X1[g][:, h, :] for h in range(2)] for g in range(G)]
                xtk = [[AT[g][:, h, 0, :] for h in range(2)] for g in range(G)]
                for lev in range(5):
                    for g in range(G):
                        pn = self.ps(2 + g)
                        for h in range(2):
                            P.mm(pn[:, 0, h * 256:h * 256 + 128], xtk[g][h], xk[g][h])
                            if lev < 4:
                                P.mm(pn[:, 0, h * 256 + 128:h * 256 + 256], xk[g][h], xtk[g][h])
                    for g in range(G):
                        P.copy("act", XX[g][:].re("p h a n -> p (h a n)"), self.ps(2 + g)[:, 0, :])
                        xk[g] = [XX[g][:, h, 0, :] for h in range(2)]
                        xtk[g] = [XX[g][:, h, 1, :] for h in range(2)]
                    for g in range(G):
                        pq = self.ps(6 + g // 2)
                        for h in range(2):
                            c0 = (g % 2) * 256 + h * 128
                            P.mm(pq[:, 0, c0:c0 + 128], xk[g][h], Qf[g][:, h, :])
                    for g in range(G):
                        pq = self.ps(6 + g // 2)
                        c0 = (g % 2) * 256
                        P.tt("dve", Qf[g][:], pq[:, 0, c0:c0 + 256].re("p (h n) -> p h n", h=2), Qf[g][:], ALU.add)
                    step(nxtA, 7)
                pw = self.ps(0)
                for g in range(G):
                    for h in range(2):
                        P.mm(pw[:, 0, g * 128 + h * 64:g * 128 + (h + 1) * 64], AT[g][:, h, 2, :], TM[g][:, 3, h * 64:(h + 1) * 64])
                for g in range(G):
                    P.copy("act", W1[g][:], pw[:, 0, g * 128:(g + 1) * 128])
                for g in range(G):
                    pau = self.ps(2 + g // 2)
                    c0 = (g % 2) * 256
                    for h in range(2):
                        P.mm(pau[:, 0, c0 + h * 64:c0 + (h + 1) * 64], Qf[g][:, h, :], TM[g][:, 0, h * 64:(h + 1) * 64])
                        P.mm(pau[:, 0, c0 + 128 + h * 64:c0 + 128 + (h + 1) * 64], Qf[g][:, h, :], W1[g][:, h * 64:(h + 1) * 64])
                for g in range(G):
                    pau = self.ps(2 + g // 2)
                    c0 = (g % 2) * 256
                    P.copy("act" if g % 2 == 0 else "dve", AU[g][:], pau[:, 0, c0:c0 + 256])
                for g, bi in enumerate(blocks):
                    py = self.ps(4 + g // 2)
                    c0 = (g % 2) * 256
                    for h in range(2):
                        hs = slice(h * 64, (h + 1) * 64)
                        P.mm(py[hs, 0, c0:c0 + 128], AU[g][:, 128 + h * 64:128 + (h + 1) * 64], AT[g][:, h, 1, :], start=True, stop=False)
                        P.mm(py[hs, 0, c0:c0 + 128], TM[g][:, 3, h * 64:(h + 1) * 64], AT[g][:, h, 3, :], start=False, stop=True)
                        P.mm(py[hs, 0, c0 + 128:c0 + 256], AU[g][:, h * 64:(h + 1) * 64], AT[g][:, h, 1, :])
                for g, bi in enumerate(blocks):
                    py = self.ps(4 + g // 2)
                    c0 = (g % 2) * 256
                    P.copy("act", Y0T[g][:], py[:, 0, c0:c0 + 128])
                    P.tt("dve", RhT[g][:], py[:, 0, c0 + 128:c0 + 256], arb[k_][:, bi, 1, :], ALU.add)
                step(nxtA)
                if tix + 1 < len(tiles):
                    step(prepB(tix + 1))
                for g, bi in enumerate(blocks):
                    for cc in range(2):
                        cs = slice(cc * 64, (cc + 1) * 64)
                        pmn = self.ps((6 if g < 2 else 2) + cc)
                        c0 = (g % 2) * 192
                        P.mm(pmn[:, 0, c0:c0 + 128], AU[g][cs, 0:128], TM[g][cs, 1, :])
                        for h in range(2):
                            hs = slice(h * 64, (h + 1) * 64)
                            P.mm(pmn[hs, 0, c0 + 128:c0 + 192], TM[g][cs, 1, hs], AU[g][cs, 128 + h * 64:128 + (h + 1) * 64], start=True, stop=False)
                            P.mm(pmn[hs, 0, c0 + 128:c0 + 192], TM[g][cs, 2, hs], TM[g][cs, 3, hs], start=False, stop=True)
                for g, bi in enumerate(blocks):
                    for cc in range(2):
                        pmn = self.ps((6 if g < 2 else 2) + cc)
                        c0 = (g % 2) * 192
                        mt_ = MTt[cc]
                        P.tt("dve", mt_[:], pmn[:, 0, c0:c0 + 128], cf["blk"][:], ALU.mult)
                        wcol = bi * 2 + cc
                        P.stt("dve", MT[g][cc][:], self.c_ident[:], WC[k_][:, wcol:wcol + 1], mt_[:], ALU.mult, ALU.add)
                        P.copy("act", Ns[g][cc][:], pmn[:, 0, c0 + 128:c0 + 192])
                for g, bi in enumerate(blocks):
                    for cc in ([0, 1] if d == 0 else [1, 0]):
                        cs = slice(cc * 64, (cc + 1) * 64)
                        tok = slice(t0 + bi * 128 + cc * 64, t0 + bi * 128 + cc * 64 + 64)
                        pyh = [self.ps(4), self.ps(5)]
                        pss_ = self.ps(0)
                        for h in range(2):
                            hs = slice(h * 64, (h + 1) * 64)
                            P.mm(pyh[h][hs, 0, 0:64], Sseq[hs, si % 4, :], RhT[g][hs, cs])
                        P.mm(pss_[:, 0, 0:64], MT[g][cc][:], Sseq[:, si % 4, :])
                        P.tt("dve", Sseq[:, (si + 1) % 4, :], pss_[:, 0, 0:64], Ns[g][cc][:], ALU.add)
                        for h in range(2):
                            hs = slice(h * 64, (h + 1) * 64)
                            if d == 0:
                                P.tt("pool" if False else "dve", ysum[hs, tok], pyh[h][hs, 0, 0:64], Y0T[g][hs, cs], ALU.add)
                            else:
                                yt_ = ytmp[cc]
                                P.tt("dve", yt_[hs, :], pyh[h][hs, 0, 0:64], Y0T[g][hs, cs], ALU.add)
                                P.tt("pool", ysum[hs, tok], ysum[hs, tok], yt_[hs, :], ALU.add)
                        si += 1

    def rwkv_finish(self, l, hp, rT, kT, vT, ysum, cf, ctx_out):
        P = self.P
        wg = P.sb([128, 8, 128], BF16, "rf_wg")
        P.dma("sp", wg[:], self.wb(l, C_RG + hp * 128, 128))
        hb = [P.sb([128, 8, 512], BF16, f"rf_h{i}") for i in range(2)]
        f = lambda n: [P.sb([128, 512], F32, f"rf_{n}{i}") for i in range(2)]
        yc, sq, rstd, rk, gs = f("yc"), f("sq"), f("rstd"), f("rk"), f("gs")
        mo = [P.sb([128, 512], BF16, f"rf_mo{i}") for i in range(2)]
        olw, olb, ork = PP["lnw"][0] + hp, PP["lnb"][0] + hp, PP["rk"][0] + hp
        for ti, (t0, tn) in enumerate(TILES):
            if ti == 0 and not ctx_out:
                continue
            k_ = ti % 2
            ts_ = slice(t0, t0 + tn)
            self.load_h(hb[k_], ti, "sp" if k_ == 0 else "act")
            pg = self.ps(0)
            for kc in range(8):
                P.mm(pg[:, 0, 0:tn], wg[:, kc, :], hb[k_][:, kc, 0:tn], start=(kc == 0), stop=(kc == 7))
            P.act(gs[k_][:, 0:tn], pg[:, 0, 0:tn], AF.Silu)
            pm = self.ps(1)
            P.mm(pm[:, 0, 0:tn], cf["blk"][:], ysum[:, ts_])
            P.stt("dve", yc[k_][:, 0:tn], pm[:, 0, 0:tn], -1.0 / 64, ysum[:, ts_], ALU.mult, ALU.add)
            P.act(sq[k_][:, 0:tn], yc[k_][:, 0:tn], AF.Square)
            pv = self.ps(2)
            P.mm(pv[:, 0, 0:tn], cf["blk"][:], sq[k_][:, 0:tn])
            P.act(rstd[k_][:, 0:tn], pv[:, 0, 0:tn], AF.Sqrt, bias=64e-5, scale=1.0 / 64)
            P.recip(rstd[k_][:, 0:tn], rstd[k_][:, 0:tn])
            P.tt("dve", yc[k_][:, 0:tn], yc[k_][:, 0:tn], rstd[k_][:, 0:tn], ALU.mult)
            P.ts("dve", yc[k_][:, 0:tn], yc[k_][:, 0:tn], self.ppt[:, olw:olw + 1], self.ppt[:, olb:olb + 1], ALU.mult, ALU.add)
            P.stt("dve", rk[k_][:, 0:tn], rT[:, ts_], self.ppt[:, ork:ork + 1], kT[:, ts_], ALU.mult, ALU.mult)
            pb = self.ps(3)
            P.mm(pb[:, 0, 0:tn], cf["blk"][:], rk[k_][:, 0:tn])
            P.tt("dve", rk[k_][:, 0:tn], pb[:, 0, 0:tn], vT[:, ts_], ALU.mult)
            P.tt("pool", yc[k_][:, 0:tn], yc[k_][:, 0:tn], rk[k_][:, 0:tn], ALU.add)
            P.tt("dve", mo[k_][:, 0:tn], yc[k_][:, 0:tn], gs[k_][:, 0:tn], ALU.mult)
            r0_ = 384 + hp * 128
            P.dma("sp", self.mix.v(("mr", hp, ti), (slice(r0_, r0_ + 128), ts_)), mo[k_][:, 0:tn])

    def zero_mix(self, r0, r1):
        P = self.P
        z = P.sb([128, S], BF16, "zmix")
        P.memset("pool", z[:], 0.0)
        for kc in range(r0 // 128, r1 // 128):
            P.dma("sp", self.mix.v(("mz", kc), (slice(kc * 128, kc * 128 + 128), slice(None))), z[:])

    def build(self, layers=DEPTH):
        P = self.P
        base = P.mark()
        self.phase_w()
        P.release(base)
        for l in range(layers):
            self.phase_mod(l)
            keep = P.mark()
            self.phase_norm(l)
            P.release(keep)
            if self.stop_after == "norm":
                return
            if not os.environ.get("MK_SKIP_ATT"):
                self.phase_att(l)
                P.release(keep)
            if self.stop_after == "att":
                return
            if os.environ.get("MK_SKIP_RWKV"):
                self.zero_mix(384, 640)
            else:
                self.phase_rwkv(l)
            P.release(keep)
            if self.stop_after == "rwkv":
                return
            self.phase_ssd(l)
            P.release(keep)
            if self.stop_after == "ssd":
                return
            self.phase_out(l)
            P.release(keep)
            for a in ("n_x", "o_w"):
                if hasattr(self, a):
                    delattr(self, a)

def build_program(stop_after=None, dbg=False, layers=DEPTH):
    nc = bass.Bass("TRN2", target_bir_lowering=False)
    k = MK(nc, stop_after=stop_after, dbg=dbg)
    k.build(layers)
    k.P.emit()
    return nc, k


def prep_inputs(inp, b):
    inp = {k: np.asarray(v) for k, v in inp.items()}
    d = {}
    d["xin"] = np.ascontiguousarray(np.concatenate([inp["ctx"][b], inp["x"][b]], 0).astype(np.float32))
    d["cvec"] = np.ascontiguousarray(np.concatenate([fm(inp["c"][b]), fm(inp["c_ctx"])], 1))
    d["ada_w"] = np.ascontiguousarray(inp["ada_w"], np.float32)
    d["w_in"] = np.ascontiguousarray(inp["w_in"], np.float32)
    d["w_out"] = np.ascontiguousarray(inp["w_out"], np.float32)
    d["wup"] = np.ascontiguousarray(inp["rwkv_w_up"], np.float32)
    d["aup"] = np.ascontiguousarray(inp["rwkv_a_up"], np.float32)
    d["pp"] = np.stack([make_pp(inp, l) for l in range(DEPTH)])
    d["pr"] = np.stack([make_pr(inp, l) for l in range(DEPTH)])
    d["pr2"] = np.stack([make_pr2(inp, l) for l in range(DEPTH)])
    d["cst"] = make_consts()
    return d


def kernel(**inputs):
    nc, _ = build_program()
    maps = [prep_inputs(inputs, b % 4) for b in range(4)]
    in_maps = [maps[i % 4] for i in range(8)]
    res = run_bass_kernel_spmd(nc, in_maps, core_ids=list(range(8)))
    return np.stack([np.asarray(res.results[b]["y"], np.float32) for b in range(4)], 0)
```

```python
import numpy as np
import concourse.bass as bass
import concourse.mybir as mybir

F32 = mybir.dt.float32
BF16 = mybir.dt.bfloat16
ALU = mybir.AluOpType
AF = mybir.ActivationFunctionType
AX = mybir.AxisListType

import os
DUMP = os.environ.get("MK_DUMP", "")
NOSYNC = set(filter(None, os.environ.get("MK_NOSYNC", "").split(",")))
ENGS = ("pe", "dve", "act", "pool", "sp")
N_DMA_SEMS = 20


class Trk:
    __slots__ = ("lw", "rd", "ps")

    def __init__(self, ps=False):
        self.lw = None
        self.rd = []
        self.ps = ps


class V:
    __slots__ = ("ap", "trk")

    def __init__(self, ap, trk):
        self.ap = ap
        self.trk = trk

    def __getitem__(self, idx):
        return V(self.ap[idx], self.trk)

    def m(self, fn):
        return V(fn(self.ap), self.trk)

    def re(self, pat, **kw):
        return V(self.ap.rearrange(pat, **kw), self.trk)

    def bc(self, shape):
        return V(self.ap.broadcast_to(shape), self.trk)

    @property
    def shape(self):
        return self.ap.shape


class Tile:
    def __init__(self, handle):
        self.h = handle
        self.trk = Trk()

    def __getitem__(self, idx):
        return V(self.h[idx], (self.trk,))

    def ap(self):
        return V(self.h.ap() if hasattr(self.h, "ap") else self.h[:], (self.trk,))


class DTile:
    def __init__(self, handle):
        self.h = handle
        self.reg = {}

    def v(self, key, idx=None, fn=None):
        t = self.reg.setdefault(key, Trk())
        ap = self.h.ap()
        if fn is not None:
            ap = fn(ap)
        if idx is not None:
            ap = ap[idx]
        return V(ap, (t,))


class Op:
    __slots__ = ("eng", "fn", "reads", "writes", "idx", "deps", "signal", "sig_count",
                 "is_dma", "dma_sem", "dma_val", "dma_prev", "kind")


class Prog:
    def __init__(self, nc):
        self.nc = nc
        self.ops = {e: [] for e in ENGS}
        self.n_t = 0
        self.all_ops = []

    ARENA = 207 * 1024

    def sb(self, shape, dtype, name=None):
        self.n_t += 1
        if not hasattr(self, "_abase"):
            a = self.nc.alloc_sbuf_tensor("arena", [128, self.ARENA], mybir.dt.uint8)
            self._abase = self.nc.lookup_mloc(a).addr
            self._aoff = 0
        esz = 2 if dtype == BF16 else 4
        n = esz
        for d in shape[1:]:
            n *= d
        off = (self._aoff + 63) // 64 * 64
        assert off + n <= self.ARENA, f"SBUF arena overflow allocating {name} {shape}: {off + n}"
        self._aoff = off + n
        return Tile(self.nc.alloc_sbuf_tensor_at(f"{name or 'sb'}_{self.n_t}", list(shape), dtype, offset=self._abase + off))

    def mark(self):
        return getattr(self, "_aoff", 0)

    def release(self, mark):
        self.barrier()
        self._aoff = mark

    def barrier(self):
        lasts = []
        for e in ENGS:
            for op in reversed(self.ops[e]):
                if not op.is_dma and op.kind != "bar":
                    lasts.append(op)
                    break
        dmas = [op for op in self.all_ops[getattr(self, "_bar_pos", 0):] if op.is_dma]
        self._bar_pos = len(self.all_ops)
        for e in ENGS:
            op = self.rec(e, lambda eh: eh.nop(nofuse=True), [], [], kind="bar")
            op.deps = [d for d in lasts if d.eng != e] + dmas
            for d in op.deps:
                d.signal = True

    def ps(self, shape, dtype=F32, name=None):
        self.n_t += 1
        return Tile(self.nc.alloc_psum_tensor(name or f"ps{self.n_t}", list(shape), dtype))

    def dram(self, name, shape, dtype, kind="Internal"):
        return DTile(self.nc.dram_tensor(name, list(shape), dtype, kind=kind))

    def rec(self, eng, fn, reads, writes, is_dma=False, kind=""):
        op = Op()
        op.eng = eng
        op.fn = fn
        op.kind = kind
        op.is_dma = is_dma
        op.idx = len(self.ops[eng])
        op.signal = False
        op.deps = []
        rt = []
        for v in reads:
            if v is None or not isinstance(v, V):
                continue
            rt.extend(v.trk)
        wt = []
        for v in writes:
            if v is None or not isinstance(v, V):
                continue
            wt.extend(v.trk)
        deps = set()
        for t in rt:
            if t.lw is not None:
                deps.add(t.lw)
            if t.ps:
                for r in t.rd:
                    if r.eng != eng:
                        deps.add(r)
        for t in wt:
            if t.lw is not None:
                deps.add(t.lw)
            for r in t.rd:
                deps.add(r)
        deps.discard(op)
        for t in rt:
            t.rd.append(op)
        for t in wt:
            t.lw = op
            t.rd = []
        final = []
        for d in deps:
            if d.eng == "pe" and eng == "pe" and not d.is_dma and not is_dma:
                continue
            if d.eng == eng and eng in NOSYNC and not d.is_dma and not is_dma:
                continue
            final.append(d)
        op.deps = final
        for d in final:
            d.signal = True
        self.ops[eng].append(op)
        self.all_ops.append(op)
        return op

    def mm(self, out, lhsT, rhs, start=True, stop=True):
        return self.rec("pe", lambda e: e.matmul(out.ap, lhsT.ap, rhs.ap, start=start, stop=stop),
                        [lhsT, rhs], [out], kind="mm")

    def tr(self, out, in_, ident):
        return self.rec("pe", lambda e: e.transpose(out.ap, in_.ap, ident.ap), [in_, ident], [out], kind="tr")

    def act(self, out, in_, func, bias=None, scale=None, accum_out=None, eng="act"):
        kw = {}
        rd = [in_]
        if bias is not None:
            kw["bias"] = bias.ap if isinstance(bias, V) else bias
            rd.append(bias)
        if scale is not None:
            kw["scale"] = scale.ap if isinstance(scale, V) else scale
            rd.append(scale)
        wr = [out]
        if accum_out is not None:
            kw["accum_out"] = accum_out.ap
            wr.append(accum_out)
        return self.rec("act", lambda e: e.activation(out.ap, in_.ap, func, **kw), rd, wr, kind="act")

    def tt(self, eng, out, in0, in1, op):
        return self.rec(eng, lambda e: e.tensor_tensor(out.ap, in0.ap, in1.ap, op), [in0, in1], [out], kind="tt")

    def ts(self, eng, out, in0, s1, s2, op0, op1=None, accum_out=None):
        rd = [in0, s1, s2]
        a1 = s1.ap if isinstance(s1, V) else s1
        a2 = s2.ap if isinstance(s2, V) else s2
        kw = {}
        wr = [out]
        if accum_out is not None:
            kw["accum_out"] = accum_out.ap
            wr.append(accum_out)
        if op1 is None:
            return self.rec(eng, lambda e: e.tensor_scalar(out.ap, in0.ap, a1, None, op0, **kw), rd, wr, kind="ts")
        return self.rec(eng, lambda e: e.tensor_scalar(out.ap, in0.ap, a1, a2, op0, op1, **kw), rd, wr, kind="ts")

    def stt(self, eng, out, in0, scalar, in1, op0, op1):
        a = scalar.ap if isinstance(scalar, V) else scalar
        return self.rec(eng, lambda e: e.scalar_tensor_tensor(out.ap, in0.ap, a, in1.ap, op0, op1),
                        [in0, scalar, in1], [out], kind="stt")

    def copy(self, eng, out, in_):
        if eng == "act":
            return self.rec("act", lambda e: e.copy(out.ap, in_.ap), [in_], [out], kind="copy")
        return self.rec(eng, lambda e: e.tensor_copy(out.ap, in_.ap), [in_], [out], kind="copy")

    def memset(self, eng, out, val):
        return self.rec(eng, lambda e: e.memset(out.ap, val), [], [out], kind="memset")

    def reduce(self, eng, out, in_, op, axis=AX.X):
        return self.rec(eng, lambda e: e.tensor_reduce(out.ap, in_.ap, axis, op), [in_], [out], kind="red")

    def recip(self, out, in_):
        return self.rec("dve", lambda e: e.reciprocal(out.ap, in_.ap), [in_], [out], kind="recip")

    def scan(self, eng, out, d0, d1, initial, op0, op1):
        ini = initial.ap if isinstance(initial, V) else initial
        return self.rec(eng, lambda e: e.tensor_tensor_scan(out.ap, d0.ap, d1.ap, ini, op0, op1),
                        [d0, d1, initial], [out], kind="scan")

    def dma(self, q, out, in_, **kw):
        return self.rec(q, lambda e: e.dma_start(out.ap, in_.ap, **kw), [in_], [out], is_dma=True, kind="dma")

    def emit(self):
        nc = self.nc
        eng_sem = {e: nc.alloc_semaphore(f"s_{e}") for e in ENGS}
        dma_sems = {e: ([nc.alloc_semaphore(f"d_{e}_{i}") for i in range(N_DMA_SEMS)]
                        if any(o.is_dma for o in self.ops[e]) else []) for e in ENGS}
        for e in ENGS:
            cnt = 0
            nd = 0
            last_on_sem = {}
            for op in self.ops[e]:
                if op.is_dma:
                    j = nd % N_DMA_SEMS
                    nd += 1
                    op.dma_sem = dma_sems[e][j]
                    k = nd_k = (nd - 1) // N_DMA_SEMS + 1
                    op.dma_val = 16 * k
                    op.dma_prev = last_on_sem.get(j)
                    last_on_sem[j] = op
                else:
                    if op.signal:
                        cnt += 1
                        op.sig_count = cnt
        self._eng_sem = eng_sem
        handles = {"pe": "tensor", "dve": "vector", "act": "scalar", "pool": "gpsimd", "sp": "sync"}

        def run_engine(ename, eh):
            seen = {}

            def wait(sem, val):
                key = id(sem)
                if seen.get(key, 0) >= val:
                    return
                seen[key] = val
                eh.wait_ge(sem, val)

            for op in self.ops[ename]:
                for d in op.deps:
                    if d.is_dma:
                        wait(d.dma_sem, d.dma_val)
                    else:
                        wait(eng_sem[d.eng], d.sig_count)
                if op.is_dma:
                    if op.dma_prev is not None:
                        wait(op.dma_prev.dma_sem, op.dma_prev.dma_val)
                    ins = op.fn(eh)
                    ins.then_inc(op.dma_sem, 16)
                else:
                    ins = op.fn(eh)
                    if op.signal:
                        ins.then_inc(eng_sem[ename], 1)
                    if DUMP and ename in DUMP:
                        print(ename, op.idx, op.kind, ins.concise(), flush=True)
            last = {}
            for op in self.ops[ename]:
                if op.is_dma:
                    last[id(op.dma_sem)] = op
            for op in last.values():
                wait(op.dma_sem, op.dma_val)

        with nc.Block() as block:
            @block.tensor
            def _(eh):
                run_engine("pe", eh)

            @block.vector
            def _(eh):
                run_engine("dve", eh)

            @block.scalar
            def _(eh):
                run_engine("act", eh)

            @block.gpsimd
            def _(eh):
                run_engine("pool", eh)

            @block.sync
            def _(eh):
                run_engine("sp", eh)

    def stats(self):
        return {e: len(self.ops[e]) for e in ENGS}

from concourse.bass_utils import run_bass_kernel_spmd

S = 4352
CTX = 256
TLAT = 4096
DM = 1024
NIN = 3596
NBLK = 34
DEPTH = 2
TILES = [(0, 256)] + [(256 + 512 * i, 512) for i in range(8)]
NEG = -30000.0

C_Q, C_K, C_V, C_G = 0, 384, 512, 640
C_RW = 1024
C_RG = 2048
C_Z = 2304
C_XBC = 2688
C_DT = 3584


def _slots(spec):
    d, o = {}, 0
    for n, w in spec:
        d[n] = (o, w)
        o += w
    return d, o


PP, NPP = _slots([("adab", 24), ("normw", 8), ("sink", 6), ("mu0", 8), ("mu1", 8), ("w0", 4), ("a0", 4),
                  ("kk", 4), ("ka", 4), ("rk", 2), ("lnw", 2), ("lnb", 2), ("convw", 21), ("convb", 7),
                  ("ssmd", 3), ("ssmnw", 3)])
PR, NPR = _slots([("adabg", 1024), ("dtb", 12), ("alog", 12), ("fnw", 1024)])
PR2, NPR2 = _slots([("convr", 3 * 896), ("shr", 3 * 1024)])
CS, NCS = _slots([("ident", 128), ("mband", 3 * 384), ("perm", 128), ("cos", 4096), ("sin", 4096),
                  ("blk", 128), ("rmf", 256), ("rmb", 256), ("snf", 128), ("snb", 128), ("trif", 128),
                  ("trib", 128), ("ones", 128), ("hlo", 128), ("hhi", 128), ("lowf", 128), ("lowb", 128)])


def fm(v):
    v = np.asarray(v, np.float32)
    return np.ascontiguousarray(v.reshape(-1, 128).T)


def rep(v):
    v = np.asarray(v, np.float32).reshape(1, -1)
    return np.ascontiguousarray(np.broadcast_to(v, (128, v.shape[1])))


def make_consts():
    c = np.zeros((128, NCS), np.float32)

    def put(n, a):
        o, w = CS[n]
        c[:, o:o + w] = a
    put("ident", np.eye(128, dtype=np.float32))
    qi = np.arange(128)[:, None]
    kj = np.arange(384)[None, :]
    band = np.abs(kj - 128 - qi) <= 128
    mb = []
    for var in range(3):
        ok = band.copy()
        if var == 0:
            ok &= kj >= 128
        if var == 2:
            ok &= kj < 256
        mb.append(np.where(ok, 0.0, NEG))
    put("mband", np.concatenate(mb, 1))
    perm = np.zeros((128, 128), np.float32)
    for m in range(128):
        d = m % 64
        half = (d % 32) // 16
        partner = m + 16 if half == 0 else m - 16
        perm[partner, m] = 1.0
    put("perm", perm)
    rows = TLAT // 64
    row = np.repeat(np.arange(rows), 64).astype(np.float32)
    col = np.tile(np.arange(64), rows).astype(np.float32)
    pos = np.stack([row, col], -1)
    inv = (np.float32(10000.0) ** (-np.arange(16, dtype=np.float32) / np.float32(16))).astype(np.float32)
    ang = (pos[:, :, None] * inv).astype(np.float32)
    cosv, sinv = np.cos(ang).astype(np.float32), np.sin(ang).astype(np.float32)
    ct = np.zeros((128, TLAT), np.float32)
    st = np.zeros((128, TLAT), np.float32)
    for m in range(128):
        d = m % 64
        ax, half, f = d // 32, (d % 32) // 16, d % 16
        ct[m] = cosv[:, ax, f]
        st[m] = -sinv[:, ax, f] if half == 0 else sinv[:, ax, f]
    put("cos", ct)
    put("sin", st)
    blk = np.zeros((128, 128), np.float32)
    blk[:64, :64] = 1
    blk[64:, 64:] = 1
    put("blk", blk)
    s = np.arange(128)[:, None]
    t = np.arange(128)[None, :]
    same = (s // 64) == (t // 64)
    put("rmf", np.concatenate([(same & (s < t)), (same & (s <= t))], 1).astype(np.float32))
    put("rmb", np.concatenate([(same & (s > t)), (same & (s >= t))], 1).astype(np.float32))
    put("lowf", (same & (t < s)).astype(np.float32))
    put("lowb", (same & (t > s)).astype(np.float32))
    put("snf", np.where(s <= t, 0.0, NEG))
    put("snb", np.where(s >= t, 0.0, NEG))
    put("trif", (s <= t).astype(np.float32))
    put("trib", (s >= t).astype(np.float32))
    put("ones", np.ones((128, 128), np.float32))
    hlo = np.zeros((128, 128), np.float32)
    hlo[:64] = 1
    put("hlo", hlo)
    put("hhi", 1 - hlo)
    return c


def make_pp(inp, l):
    p = np.zeros((128, NPP), np.float32)

    def put(n, a):
        o, w = PP[n]
        assert a.shape == (128, w), (n, a.shape)
        p[:, o:o + w] = a
    put("adab", fm(inp["ada_b"][l]))
    put("normw", fm(inp["norm_w"][l]))
    put("sink", rep(inp["attn_sink"][l]))
    put("mu0", fm(inp["rwkv_mu"][l, 0]))
    put("mu1", fm(inp["rwkv_mu"][l, 1]))
    for n, k in (("w0", "rwkv_w0"), ("a0", "rwkv_a0"), ("kk", "rwkv_k_k"), ("ka", "rwkv_k_a")):
        put(n, np.concatenate([fm(inp[k][l, 0]), fm(inp[k][l, 1])], 1))
    put("rk", fm(inp["rwkv_r_k"][l].reshape(-1)))
    put("lnw", fm(inp["rwkv_ln_w"][l]))
    put("lnb", fm(inp["rwkv_ln_b"][l]))
    cw = inp["ssm_conv_w"][l]
    put("convw", np.stack([fm(cw[0]), fm(cw[1]), fm(cw[2])], -1).reshape(128, 21))
    put("convb", fm(inp["ssm_conv_b"][l]))
    put("ssmd", fm(np.repeat(inp["ssm_d"][l], 64)))
    put("ssmnw", fm(inp["ssm_norm_w"][l]))
    return p


def make_pr(inp, l):
    p = np.zeros((128, NPR), np.float32)

    def put(n, a):
        o, w = PR[n]
        p[:, o:o + w] = a
    put("adabg", rep(inp["ada_b"][l, 2048:3072]))
    put("dtb", rep(inp["ssm_dt_bias"][l].reshape(-1)))
    put("alog", rep(inp["ssm_a_log"][l].reshape(-1)))
    put("fnw", rep(inp["final_norm_w"]))
    return p


def make_pr2(inp, l):
    p = np.zeros((128, NPR2), np.float32)

    def put(n, a):
        o, w = PR2[n]
        p[:, o:o + w] = a
    put("convr", rep(inp["ssm_conv_w"][l].reshape(-1)))
    mu = inp["rwkv_mu"][l]
    put("shr", rep(np.concatenate([mu[0], mu[1], mu[1]], 0)))
    return p


class MK:
    def __init__(self, nc, stop_after=None, dbg=False):
        self.nc = nc
        self.P = P = Prog(nc)
        self.dbg = dbg
        self.stop_after = stop_after
        ok = "ExternalOutput" if dbg else "Internal"
        self.xin = P.dram("xin", [S, DM], F32, kind="ExternalInput")
        self.cvec = P.dram("cvec", [128, 16], F32, kind="ExternalInput")
        self.ada_w = P.dram("ada_w", [DEPTH, DM, 3 * DM], F32, kind="ExternalInput")
        self.w_in = P.dram("w_in", [DEPTH, DM, NIN], F32, kind="ExternalInput")
        self.w_out = P.dram("w_out", [DEPTH, DM, DM], F32, kind="ExternalInput")
        self.wup = P.dram("wup", [DEPTH, 2, 64, 256], F32, kind="ExternalInput")
        self.aup = P.dram("aup", [DEPTH, 2, 64, 256], F32, kind="ExternalInput")
        self.pp = P.dram("pp", [DEPTH, 128, NPP], F32, kind="ExternalInput")
        self.pr = P.dram("pr", [DEPTH, 128, NPR], F32, kind="ExternalInput")
        self.pr2 = P.dram("pr2", [DEPTH, 128, NPR2], F32, kind="ExternalInput")
        self.cst = P.dram("cst", [128, NCS], F32, kind="ExternalInput")
        self.y = P.dram("y", [TLAT, DM], F32, kind="ExternalOutput")
        self.wbin = P.dram("wbin", [DEPTH, 128, 8, NIN], BF16)
        self.wbout = P.dram("wbout", [DEPTH, 128, 8, DM], BF16)
        self.hT = P.dram("hT", [128, 8, S], BF16, kind=ok)
        self.mix = P.dram("mixT", [DM, S], BF16, kind=ok)
        self.xres = P.dram("xres", [S, DM], F32, kind=ok)
        self.yf = P.dram("yfwd", [384, S], F32)
        self.psh = nc.alloc_psum_tensor("psum", [128, 8, 512], F32)
        self.bt = [Trk(ps=True) for _ in range(8)]
        self.c_ident = P.sb([128, 128], F32, "c_ident")
        self.c_identb = P.sb([128, 128], BF16, "c_identb")
        self.load_c(self.c_ident, "ident")
        P.copy("dve", self.c_identb[:], self.c_ident[:])

    def ps(self, b0, nb=1, dt=F32):
        ap = self.psh[:, b0:b0 + nb, :]
        if dt is not F32:
            ap = ap.bitcast(dt)
        return V(ap, tuple(self.bt[b0:b0 + nb]))

    def load_c(self, tile, name, q="sp", sub=None):
        o, w = CS[name]
        if sub is not None:
            o, w = o + sub[0], sub[1]
        self.P.dma(q, tile[:], self.cst.v("c", (slice(None), slice(o, o + w))))

    def phase_w(self):
        P = self.P
        st = [P.sb([128, 8, 512], F32, f"w_st{i}") for i in range(2)]
        sb = [P.sb([128, 8, 512], BF16, f"w_sb{i}") for i in range(2)]
        it = 0
        for l in range(DEPTH):
            for (src, dst, n) in ((self.w_in, self.wbin, NIN), (self.w_out, self.wbout, DM)):
                for c0 in range(0, n, 512):
                    cw = min(512, n - c0)
                    a, b = st[it % 2], sb[it % 2]
                    q = "sp" if it % 2 == 0 else "act"
                    P.dma(q, a[:, :, 0:cw], src.v(("w", l), fn=lambda ap, l=l, c0=c0, cw=cw:
                                                   ap[l].rearrange("(kc p) n -> p kc n", p=128)[:, :, c0:c0 + cw]))
                    P.copy("dve" if it % 2 == 0 else "pool", b[:, :, 0:cw], a[:, :, 0:cw])
                    P.dma(q, dst.v(("wb", l, c0), fn=lambda ap, l=l, c0=c0, cw=cw: ap[l][:, :, c0:c0 + cw]),
                          b[:, :, 0:cw])
                    it += 1

    def wb(self, l, c0, cw):
        t0 = (c0 // 512) * 512
        trk = []
        ap = self.wbin.h.ap()[l][:, :, c0:c0 + cw]
        for t in range(t0, c0 + cw, 512):
            trk.append(self.wbin.reg.setdefault(("wb", l, t), Trk()))
        return V(ap, tuple(trk))

    def wbo(self, l, c0, cw):
        t0 = (c0 // 512) * 512
        trk = []
        ap = self.wbout.h.ap()[l][:, :, c0:c0 + cw]
        for t in range(t0, c0 + cw, 512):
            trk.append(self.wbout.reg.setdefault(("wb", l, t), Trk()))
        return V(ap, tuple(trk))

    def phase_mod(self, l):
        P = self.P
        if not hasattr(self, "ppt"):
            self.ppt = P.sb([128, NPP], F32, "ppt")
            self.prt = P.sb([128, NPR], F32, "prt")
            self.cact = P.sb([128, 8, 2], F32, "cact")
            self.modT = P.sb([128, 24, 2], F32, "modT")
            self.s1T = P.sb([128, 8, 2], F32, "s1T")
            self.gate_bc = P.sb([128, 2, 1024], F32, "gate_bc")
            craw = P.sb([128, 16], F32, "craw")
            P.dma("sp", craw[:], self.cvec.v(0))
            P.act(self.cact[:].re("p k w -> p w k"), craw[:].re("p (w k) -> p w k", w=2), AF.Silu)
        mk_ = P.mark()
        self.crep = P.sb([128, 8, 2, 128], F32, "crep")
        self.aw = [P.sb([128, 8, 512], F32, f"aw{i}") for i in range(2)]
        P.copy("dve", self.crep[:], self.cact[:].m(lambda a: a.unsqueeze(3)).bc([128, 8, 2, 128]))
        P.dma("sp", self.ppt[:], self.pp.v(l, fn=lambda a: a[l]))
        P.dma("act", self.prt[:], self.pr.v(l, fn=lambda a: a[l]))
        for t in range(6):
            a = self.aw[t % 2]
            P.dma("sp" if t % 2 == 0 else "act", a[:],
                  self.ada_w.v(("aw", l), fn=lambda ap, t=t: ap[l].rearrange("(kc p) n -> p kc n", p=128)[:, :, t * 512:(t + 1) * 512]))
            pm = self.ps(0)
            for j in range(4):
                for kc in range(8):
                    P.mm(pm[:, 0, j * 2:(j + 1) * 2], a[:, kc, j * 128:(j + 1) * 128], self.cact[:, kc, :],
                         start=(kc == 0), stop=(kc == 7))
            P.copy("dve", self.modT[:, t * 4:(t + 1) * 4, :], pm[:, 0, 0:8].re("p (j w) -> p j w", w=2))
            if t >= 4:
                for w in range(2):
                    pg = self.ps(1 + w)
                    for kc in range(8):
                        P.mm(pg[:, 0, :], self.crep[:, kc, w, :], a[:, kc, :], start=(kc == 0), stop=(kc == 7))
                    o = PR["adabg"][0] + (t - 4) * 512
                    P.tt("dve", self.gate_bc[:, w, (t - 4) * 512:(t - 3) * 512], pg[:, 0, :], self.prt[:, o:o + 512], ALU.add)
        o = PP["adab"][0]
        P.tt("dve", self.modT[:], self.modT[:], self.ppt[:, o:o + 24].m(lambda a: a.unsqueeze(2)).bc([128, 24, 2]), ALU.add)
        o = PP["normw"][0]
        P.stt("dve", self.s1T[:], self.modT[:, 8:16, :], 1.0,
              self.ppt[:, o:o + 8].m(lambda a: a.unsqueeze(2)).bc([128, 8, 2]), ALU.add, ALU.mult)
        P.release(mk_)

    def phase_norm(self, l):
        P = self.P
        if not hasattr(self, "n_x"):
            self.n_x = [P.sb([128, DM], F32, f"n_x{i}") for i in range(2)]
            self.n_junk = P.sb([128, DM], F32, "n_junk")
            self.n_xb = [P.sb([128, DM], BF16, f"n_xb{i}") for i in range(2)]
            self.n_ss = [P.sb([128, 4], F32, f"n_ss{i}") for i in range(2)]
            self.n_h = [P.sb([128, 8, 512], BF16, f"n_h{i}") for i in range(2)]
            self.n_t = [P.sb([128, 8, 128], F32, f"n_t{i}") for i in range(2)]
        src = self.xin if l == 0 else self.xres
        for ti, (t0, tn) in enumerate(TILES):
            w = 1 if ti == 0 else 0
            hb = self.n_h[ti % 2]
            for bi in range(tn // 128):
                blk = (t0 // 128) + bi
                xt, xb, ss = self.n_x[blk % 2], self.n_xb[blk % 2], self.n_ss[blk % 2]
                P.dma("sp" if blk % 2 == 0 else "act", xt[:], src.v(("x", blk), (slice(blk * 128, blk * 128 + 128), slice(None))))
                P.act(self.n_junk[:], xt[:], AF.Square, accum_out=ss[:, 0:1])
                P.act(ss[:, 1:2], ss[:, 0:1], AF.Sqrt, bias=1e-6, scale=1.0 / DM)
                P.recip(ss[:, 2:3], ss[:, 1:2])
                P.ts("dve", xb[:], xt[:], ss[:, 2:3], None, ALU.mult)
                pt = self.ps(2 + blk % 2, 1, BF16)
                for kc in range(8):
                    P.tr(pt[:, 0, kc * 128:(kc + 1) * 128], xb[:, kc * 128:(kc + 1) * 128], self.c_identb[:])
                tmp = self.n_t[blk % 2]
                P.tt("dve", tmp[:], pt[:, 0, :].re("p (k t) -> p k t", k=8),
                     self.s1T[:, :, w:w + 1].bc([128, 8, 128]), ALU.mult)
                P.tt("pool", hb[:, :, bi * 128:(bi + 1) * 128], tmp[:],
                     self.modT[:, 0:8, w:w + 1].bc([128, 8, 128]), ALU.add)
            P.dma("sp", self.hT.v(("h", ti), (slice(None), slice(None), slice(t0, t0 + tn))), hb[:, :, 0:tn])

    def phase_out(self, l):
        P = self.P
        last = (l == DEPTH - 1)
        if not hasattr(self, "o_w"):
            self.o_w = P.sb([128, 8, DM], BF16, "o_w")
            self.o_m = [P.sb([128, 8, 128], BF16, f"o_m{i}") for i in range(2)]
            self.o_x = [P.sb([128, DM], F32, f"o_x{i}") for i in range(2)]
            self.o_t = [P.sb([128, DM], F32, f"o_t{i}") for i in range(2)]
            self.o_ss = [P.sb([128, 4], F32, f"o_ss{i}") for i in range(2)]
        P.dma("sp", self.o_w[:], self.wbo(l, 0, DM))
        src = self.xin if l == 0 else self.xres
        for blk in range(NBLK):
            if last and blk < 2:
                continue
            w = 1 if blk < 2 else 0
            m, xt, tt_, ss = self.o_m[blk % 2], self.o_x[blk % 2], self.o_t[blk % 2], self.o_ss[blk % 2]
            tsl = slice(blk * 128, blk * 128 + 128)
            P.dma("sp", m[:], self.mix.v(("m", blk), fn=lambda ap, tsl=tsl: ap.rearrange("(kc p) t -> p kc t", p=128)[:, :, tsl]))
            P.dma("act", xt[:], src.v(("x", blk), (tsl, slice(None))))
            for hf in range(2):
                po = self.ps(4 + 2 * (blk % 2) + hf)
                for kc in range(8):
                    P.mm(po[:, 0, :], m[:, kc, :], self.o_w[:, kc, hf * 512:(hf + 1) * 512], start=(kc == 0), stop=(kc == 7))
                P.tt("dve", tt_[:, hf * 512:(hf + 1) * 512], po[:, 0, :], self.gate_bc[:, w, hf * 512:(hf + 1) * 512], ALU.mult)
            P.tt("pool", tt_[:], tt_[:], xt[:], ALU.add)
            if not last:
                P.dma("sp", self.xres.v(("x", blk), (tsl, slice(None))), tt_[:])
            else:
                P.act(xt[:], tt_[:], AF.Square, accum_out=ss[:, 0:1])
                P.act(ss[:, 1:2], ss[:, 0:1], AF.Sqrt, bias=1e-6, scale=1.0 / DM)
                P.recip(ss[:, 2:3], ss[:, 1:2])
                o = PR["fnw"][0]
                P.stt("dve", xt[:], tt_[:], ss[:, 2:3], self.prt[:, o:o + DM], ALU.mult, ALU.mult)
                P.dma("sp", self.y.v(("y", blk), (slice(blk * 128 - CTX, blk * 128 - CTX + 128), slice(None))), xt[:])

    def load_h(self, tile, ti, q="sp"):
        t0, tn = TILES[ti]
        self.P.dma(q, tile[:, :, 0:tn], self.hT.v(("h", ti), (slice(None), slice(None), slice(t0, t0 + tn))))

    def phase_att(self, l):
        P = self.P
        ctx_out = (l < DEPTH - 1)
        import os
        sm = int(os.environ.get("MK_SET", "63"))
        wq = P.sb([128, 8, 384], BF16, "a_wq")
        wg = P.sb([128, 8, 384], BF16, "a_wg")
        wk = P.sb([128, 8, 128], BF16, "a_wk")
        wv = P.sb([128, 8, 128], BF16, "a_wv")
        if sm & 1:
            for j in range(3):
                for hh in range(2):
                    h = j + 3 * hh
                    P.dma("sp", wq[:, :, j * 128 + hh * 64: j * 128 + hh * 64 + 64], self.wb(l, C_Q + h * 64, 64))
                    P.dma("act", wg[:, :, j * 128 + hh * 64: j * 128 + hh * 64 + 64], self.wb(l, C_G + h * 64, 64))
            P.dma("sp", wk[:], self.wb(l, C_K, 128))
            P.dma("act", wv[:], self.wb(l, C_V, 128))
        cos = P.sb([128, TLAT], F32, "a_cos")
        sin = P.sb([128, TLAT], F32, "a_sin")
        if sm & 2:
            self.load_c(cos, "cos", "sp")
            self.load_c(sin, "sin", "act")
        pf = P.sb([128, 128], F32, "a_pf")
        permb = P.sb([128, 128], BF16, "a_permb")
        if sm & 4:
            self.load_c(pf, "perm")
            P.copy("dve", permb[:], pf[:])
        mband = P.sb([128, 3, 384], F32, "a_mband")
        o, w_ = CS["mband"]
        if sm & 8:
            P.dma("sp", mband[:].re("p a b -> p (a b)"), self.cst.v("c", (slice(None), slice(o, o + w_))))
        sink8 = P.sb([128, 6], F32, "a_sink8")
        o = PP["sink"][0]
        if sm & 16:
            P.ts("dve", sink8[:], self.ppt[:, o:o + 6], 8.0, None, ALU.mult)
        KB = P.sb([128, 4608], BF16, "a_KB")
        Vtm = P.sb([128, 36, 128], BF16, "a_V")
        if sm & 32:
            P.memset("pool", KB[:, 256:384], 0.0)
            P.memset("pool", KB[:, 4480:4608], 0.0)
            P.memset("pool", Vtm[:, 2, :], 0.0)
            P.memset("pool", Vtm[:, 35, :], 0.0)
        hb = [P.sb([128, 8, 512], BF16, f"a_h{i}") for i in range(2)]
        kf = P.sb([128, 512], F32, "a_kf")
        t1 = [P.sb([128, 512], F32, f"a_t1{i}") for i in range(2)]
        t2 = [P.sb([128, 512], F32, f"a_t2{i}") for i in range(2)]

        def kcol(tok):
            return tok if tok < CTX else tok + 128

        def proj(dst, wt, c0, cw, hbt, tn, lat_t0, bank, func=None):
            pp_ = self.ps(bank)
            for kc in range(8):
                P.mm(pp_[:, 0, 0:tn], wt[:, kc, c0:c0 + cw], hbt[:, kc, 0:tn], start=(kc == 0), stop=(kc == 7))
            if func is not None:
                P.act(dst, pp_[:, 0, 0:tn], func)
                return
            if lat_t0 is None:
                P.copy("act", dst, pp_[:, 0, 0:tn])
                return
            P.copy("act", kf[:, 0:tn], pp_[:, 0, 0:tn])
            pr_ = self.ps(2)
            P.mm(pr_[:, 0, 0:tn], pf[:], kf[:, 0:tn])
            a, b = t1[bank % 2], t2[bank % 2]
            P.tt("pool", a[:, 0:tn], kf[:, 0:tn], cos[:, lat_t0:lat_t0 + tn], ALU.mult)
            P.tt("dve", b[:, 0:tn], pr_[:, 0, 0:tn], sin[:, lat_t0:lat_t0 + tn], ALU.mult)
            P.tt("pool", dst, a[:, 0:tn], b[:, 0:tn], ALU.add)

        stage = int(os.environ.get("MK_ATT", "9"))
        if stage <= 0:
            return
        for ti, (t0, tn) in enumerate(TILES):
            h = hb[ti % 2]
            self.load_h(h, ti, "sp" if ti % 2 == 0 else "act")
            kc0 = kcol(t0)
            ma = int(os.environ.get("MK_A", "3"))
            if ti >= int(os.environ.get("MK_AT", "9")):
                break
            if ma & 1:
                proj(KB[:, kc0:kc0 + tn], wk, 0, 128, h, tn, None if ti == 0 else t0 - CTX, ti % 2)
            for bi in range(tn // 128):
                if not (ma & 2):
                    break
                vb = (t0 // 128 + bi)
                vb = vb if vb < 2 else vb + 1
                pv = self.ps(3 + bi % 2)
                for kc in range(8):
                    P.mm(pv[:, 0, 0:128], h[:, kc, bi * 128:(bi + 1) * 128], wv[:, kc, :], start=(kc == 0), stop=(kc == 7))
                P.copy("dve" if bi % 2 == 0 else "act", Vtm[:, vb, :], pv[:, 0, 0:128])

        import os
        stage = int(os.environ.get("MK_ATT", "9"))
        if stage <= 1:
            return
        qT = [P.sb([128, 3, 512], BF16, f"a_q{i}") for i in range(2)]
        gT = [P.sb([128, 3, 512], BF16, f"a_g{i}") for i in range(2)]
        om = [P.sb([128, 3, 512], BF16, f"a_om{i}") for i in range(2)]
        sc = [P.sb([128, 3, 640], F32, f"a_sc{i}") for i in range(2)]
        pe = [P.sb([128, 3, 640], F32, f"a_pe{i}") for i in range(2)]
        pn = [P.sb([128, 3, 640], BF16, f"a_pn{i}") for i in range(2)]
        pTs = [P.sb([128, 5, 128], BF16, f"a_pT{i}") for i in range(3)]
        st = [P.sb([128, 8, 3], F32, f"a_st{i}") for i in range(2)]
        it = 0
        npt = 0
        for ti, (t0, tn) in enumerate(TILES):
            if ti == 0 and not ctx_out:
                continue
            h = hb[ti % 2]
            self.load_h(h, ti, "sp" if ti % 2 == 0 else "act")
            q_, g_, o_ = qT[ti % 2], gT[ti % 2], om[ti % 2]
            for j in range(3):
                proj(q_[:, j, 0:tn], wq, j * 128, 128, h, tn, None if ti == 0 else t0 - CTX, j % 2)
                proj(g_[:, j, 0:tn], wg, j * 128, 128, h, tn, None, (j + 1) % 2, func=AF.Silu)
            if stage <= 2:
                break
            for bi in range(tn // 128):
                if stage <= 5 and (bi > 0 or ti > 1):
                    break
                qs = slice(bi * 128, bi * 128 + 128)
                isctx = (ti == 0)
                n = (t0 - CTX) // 128 + bi if not isctx else None
                lo = 384 if isctx else 0
                po = self.ps(2)
                KG = [(kg, slice(kg * 64, kg * 64 + 64), sc[kg], pe[kg], pn[kg], st[kg]) for kg in range(2)]
                chunks = ([] if isctx else [0, 1, 2]) + [3, 4]
                for kg, ps_, s_, e_, n_, stt_ in KG:
                    for j in range(3):
                        pb = self.ps(3 + 2 * kg)
                        pc = self.ps(4 + 2 * kg)
                        if not isctx:
                            kb0 = 256 + n * 128
                            P.mm(pb[:, 0, 0:384], q_[ps_, j, qs], KB[ps_, kb0:kb0 + 384])
                            var = 0 if n == 0 else (2 if n == 31 else 1)
                            P.tt("dve", s_[:, j, 0:384], pb[:, 0, 0:384], mband[:, var, :], ALU.add)
                        P.mm(pc[:, 0, 0:256], q_[ps_, j, qs], KB[ps_, 0:256])
                        P.copy("act", s_[:, j, 384:640], pc[:, 0, 0:256])
                for kg, ps_, s_, e_, n_, stt_ in KG:
                    P.reduce("dve", stt_[:, 0, :], s_[:, :, lo:640], ALU.max)
                    P.tt("dve", stt_[:, 1, :], stt_[:, 0, :], sink8[:, kg * 3:kg * 3 + 3], ALU.max)
                    P.ts("dve", stt_[:, 2, :], stt_[:, 1, :], -0.125, None, ALU.mult)
                for kg, ps_, s_, e_, n_, stt_ in KG:
                    for j in range(3):
                        P.act(e_[:, j, lo:640], s_[:, j, lo:640], AF.Exp, bias=stt_[:, 2, j:j + 1], scale=0.125,
                              accum_out=stt_[:, 3, j:j + 1])
                    P.stt("dve", stt_[:, 4, :], sink8[:, kg * 3:kg * 3 + 3], 0.125, stt_[:, 2, :], ALU.mult, ALU.add)
                for kg, ps_, s_, e_, n_, stt_ in KG:
                    P.act(stt_[:, 5, :], stt_[:, 4, :], AF.Exp)
                for kg, ps_, s_, e_, n_, stt_ in KG:
                    P.tt("dve", stt_[:, 6, :], stt_[:, 5, :], stt_[:, 3, :], ALU.add)
                    P.recip(stt_[:, 7, :], stt_[:, 6, :])
                    P.tt("dve" if kg == 0 else "pool", n_[:, :, lo:640], e_[:, :, lo:640],
                         stt_[:, 7, :].m(lambda a: a.unsqueeze(2)).bc([128, 3, 640 - lo]), ALU.mult)
                for j in range(3):
                    for kg, ps_, s_, e_, n_, stt_ in KG:
                        pt = self.ps([7, 0, 1][npt % 3], 1, BF16)
                        for c in chunks:
                            P.tr(pt[:, 0, c * 128:(c + 1) * 128], n_[:, j, c * 128:(c + 1) * 128], self.c_identb[:])
                        pts = pTs[npt % 3]
                        npt += 1
                        c0 = chunks[0]
                        P.copy("act" if npt % 2 == 0 else "dve", pts[:, c0:5, :], pt[:, 0, c0 * 128:640].re("p (c t) -> p c t", t=128))
                        for ci, c in enumerate(chunks):
                            vb = (2 + n + c) if c < 3 else (c - 3)
                            P.mm(po[ps_, 0, j * 128:(j + 1) * 128], Vtm[:, vb, kg * 64:kg * 64 + 64], pts[:, c, :],
                                 start=(ci == 0), stop=(ci == len(chunks) - 1))
                P.tt("dve", o_[:, :, qs], po[:, 0, 0:384].re("p (j t) -> p j t", j=3), g_[:, :, qs], ALU.mult)
            for j in range(3):
                for hh in range(2):
                    hd = j + 3 * hh
                    P.dma("sp" if hh == 0 else "act",
                          self.mix.v(("ma", ti, hd), (slice(hd * 64, hd * 64 + 64), slice(t0, t0 + tn))),
                          o_[hh * 64:hh * 64 + 64, j, 0:tn])

    def load_h_halo(self, tile, ti, q="sp"):
        P = self.P
        t0, tn = TILES[ti]
        lo, hi = t0 - 1, t0 + tn + 1
        if ti <= 1:
            P.memset("pool", tile[:, :, 0:1], 0.0)
            lo = t0
        if ti == 0 or ti == len(TILES) - 1:
            P.memset("pool", tile[:, :, tn + 1:tn + 2], 0.0)
            hi = t0 + tn
        P.dma(q, tile[:, :, lo - t0 + 1:hi - t0 + 1], self.hT.v(("h", ti), (slice(None), slice(None), slice(lo, hi))))

    def scaled_w(self, l, c0, ncol, rows, name, roff=0):
        P = self.P
        rows = [r[:, roff:roff + ncol] for r in rows]
        w3 = P.sb([128, 3, 8, ncol], BF16, name)
        mk_ = P.mark()
        half = ncol // 2
        stg = [P.sb([128, 8, half], F32, f"{name}_st{i}") for i in range(2)]
        for hf in range(2):
            st = stg[hf]
            P.dma("sp" if hf == 0 else "act", st[:],
                  self.w_in.v(("w", l), fn=lambda ap, hf=hf: ap[l].rearrange("(kc p) n -> p kc n", p=128)[:, :, c0 + hf * half:c0 + (hf + 1) * half]))
            for tap in range(3):
                P.tt("dve" if tap != 1 else "pool", w3[:, tap, :, hf * half:(hf + 1) * half], st[:],
                     rows[tap][:, hf * half:(hf + 1) * half].m(lambda a: a.unsqueeze(1)).bc([128, 8, half]), ALU.mult)
        P.release(mk_)
        return w3

    def phase_ssd(self, l):
        P = self.P
        ctx_out = (l < DEPTH - 1)
        xsT = P.sb([128, 3, S], F32, "s_xsT")
        BT = P.sb([128, 2, S], BF16, "s_BT")
        CT = P.sb([128, 2, S], BF16, "s_CT")
        dt_tm = P.sb([128, NBLK, 12], F32, "s_dt")
        dtA_tm = P.sb([128, NBLK, 12], F32, "s_dtA")
        aneg = P.sb([128, 12], F32, "s_aneg")
        o = PR["alog"][0]
        P.act(aneg[:], self.prt[:, o:o + 12], AF.Exp)
        P.ts("dve", aneg[:], aneg[:], -1.0, None, ALU.mult)
        mk1 = P.mark()
        o = PR2["convr"][0]
        crow = P.sb([128, 3 * 896], F32, "s_crow")
        P.dma("sp", crow[:], self.pr2.v(l, fn=lambda a: a[l][:, o:o + 3 * 896]))
        rows = [crow[:, k * 896:(k + 1) * 896] for k in range(3)]
        wx3 = self.scaled_w(l, C_XBC, 896, rows, "s_wx3")
        wdt = P.sb([128, 8, 12], BF16, "s_wdt")
        P.dma("sp", wdt[:], self.wb(l, C_DT, 12))
        hh = [P.sb([128, 8, 514], BF16, f"s_hh{i}") for i in range(2)]
        dtmp = [P.sb([128, 12], F32, f"s_dtmp{i}") for i in range(2)]
        ocb = PP["convb"][0]
        nb = 0
        for ti, (t0, tn) in enumerate(TILES):
            h = hh[ti % 2]
            self.load_h_halo(h, ti, "sp" if ti % 2 == 0 else "act")
            for c in range(7):
                pb = self.ps(c % 2)
                n = 0
                for tap in range(3):
                    for kc in range(8):
                        P.mm(pb[:, 0, 0:tn], wx3[:, tap, kc, c * 128:(c + 1) * 128], h[:, kc, tap:tap + tn],
                             start=(n == 0), stop=(n == 23))
                        n += 1
                if c < 3:
                    dst = xsT[:, c, t0:t0 + tn]
                elif c < 5:
                    dst = BT[:, c - 3, t0:t0 + tn]
                else:
                    dst = CT[:, c - 5, t0:t0 + tn]
                P.act(dst, pb[:, 0, 0:tn], AF.Silu, bias=self.ppt[:, ocb + c:ocb + c + 1])
            for bi in range(tn // 128):
                blk = t0 // 128 + bi
                pd = self.ps(2 + blk % 2)
                for kc in range(8):
                    P.mm(pd[:, 0, 0:12], h[:, kc, 1 + bi * 128:1 + (bi + 1) * 128], wdt[:, kc, :], start=(kc == 0), stop=(kc == 7))
                d_ = dtmp[blk % 2]
                o = PR["dtb"][0]
                P.tt("dve", d_[:], pd[:, 0, 0:12], self.prt[:, o:o + 12], ALU.add)
                P.act(d_[:], d_[:], AF.Exp)
                P.act(dt_tm[:, blk, :], d_[:], AF.Ln, bias=1.0)
                P.tt("dve", dtA_tm[:, blk, :], dt_tm[:, blk, :], aneg[:], ALU.mult)
        P.release(mk1)
        if int(os.environ.get("MK_SSD", "9")) <= 1:
            self.dbg_xsT = xsT
            return
        wz = P.sb([128, 8, 384], BF16, "s_wz")
        P.dma("sp", wz[:], self.wb(l, C_Z, 384))
        cf = {}
        for nme in ("snf", "snb", "trif", "trib", "ones", "hlo", "hhi"):
            cf[nme] = P.sb([128, 128], F32, "s_c_" + nme)
            self.load_c(cf[nme], nme, "act")
        St = P.sb([128, 6, 64], F32, "s_S")
        NB2 = 2
        xs_sb = [P.sb([128, 384], F32, f"s_xs{i}") for i in range(NB2)]
        Bt_sb = [P.sb([128, 2, 128], BF16, f"s_Bt{i}") for i in range(NB2)]
        Gs = [P.sb([128, 2, 128], F32, f"s_G{i}") for i in range(NB2)]
        cumc = [P.sb([128, 6], F32, f"s_cc{i}") for i in range(NB2)]
        dbc = [P.sb([128, 6, 128], F32, f"s_dbc{i}") for i in range(NB2)]
        Dm = [P.sb([128, 6, 128], F32, f"s_D{i}") for i in range(NB2)]
        Em = [P.sb([128, 6, 128], F32, f"s_E{i}") for i in range(NB2)]
        scT = [P.sb([128, 6, 128], BF16, f"s_sc{i}") for i in range(NB2)]
        Er = [P.sb([128, 6, 128], F32, f"s_Er{i}") for i in range(NB2)]
        Cd = [P.sb([128, 6, 128], F32, f"s_Cd{i}") for i in range(NB2)]
        xdt = [P.sb([128, 6, 64], BF16, f"s_xdt{i}") for i in range(NB2)]
        xdtt = [P.sb([128, 6, 64], BF16, f"s_xdtt{i}") for i in range(NB2)]
        ysb = [P.sb([128, 3, 128], F32, f"s_y{i}") for i in range(NB2)]
        yfl = [P.sb([128, 3, 128], F32, f"s_yf{i}") for i in range(NB2)]
        hz = [P.sb([128, 8, 128], BF16, f"s_hz{i}") for i in range(NB2)]
        zs = [P.sb([128, 3, 128], F32, f"s_z{i}") for i in range(NB2)]
        sq = [P.sb([128, 3, 128], F32, f"s_sq{i}") for i in range(NB2)]
        rs = [P.sb([128, 2, 128], F32, f"s_rs{i}") for i in range(NB2)]
        mo = [P.sb([128, 3, 128], BF16, f"s_mo{i}") for i in range(NB2)]
        osd, onw = PP["ssmd"][0], PP["ssmnw"][0]
        it = 0
        for d in range(2):
            order = list(range(NBLK)) if d == 0 else [1, 0] + list(range(NBLK - 1, 1, -1))
            tri = cf["trif"] if d == 0 else cf["trib"]
            sneg = cf["snf"] if d == 0 else cf["snb"]
            last = 127 if d == 0 else 0
            P.memset("dve", St[:], 0.0)
            pending = None
            for c in order:
                if d == 1 and c < 2 and not ctx_out:
                    pass
                k = it % NB2
                it += 1
                cs_ = slice(c * 128, (c + 1) * 128)
                pxs = self.ps(0)
                for j in range(3):
                    P.tr(pxs[:, 0, j * 128:(j + 1) * 128], xsT[:, j, cs_], self.c_ident[:])
                P.copy("act", xs_sb[k][:], pxs[:, 0, 0:384])
                pbt = self.ps(1, 1, BF16)
                for g in range(2):
                    P.tr(pbt[:, 0, g * 128:(g + 1) * 128], BT[:, g, cs_], self.c_identb[:])
                P.copy("dve", Bt_sb[k][:].re("p g n -> p (g n)"), pbt[:, 0, 0:256])
                pg = self.ps(2)
                for g in range(2):
                    P.mm(pg[:, 0, g * 128:(g + 1) * 128], BT[:, g, cs_], CT[:, g, cs_])
                P.copy("act", Gs[k][:].re("p g n -> p (g n)"), pg[:, 0, 0:256])
                dA = dtA_tm[:, c, d * 6:(d + 1) * 6]
                P.mm(pg[:, 0, 256:262], tri[:], dA)
                P.copy("dve", cumc[k][:], pg[:, 0, 256:262])
                P.copy("pool", dbc[k][:], dA.m(lambda a: a.unsqueeze(2)).bc([128, 6, 128]))
                pcr = self.ps(3, 2)
                for h in range(6):
                    P.mm(pcr[:, h // 4, (h % 4) * 128:(h % 4 + 1) * 128], dbc[k][:, h, :], tri[:])
                pcr_v = pcr.re("p b n -> p (b n)")[:, 0:768].re("p (h n) -> p h n", h=6)
                P.tt("dve", Dm[k][:], pcr_v, cumc[k][:].m(lambda a: a.unsqueeze(2)).bc([128, 6, 128]), ALU.subtract)
                P.act(Er[k][:], pcr_v, AF.Exp)
                P.tt("pool", Dm[k][:], Dm[k][:], sneg[:].m(lambda a: a.unsqueeze(1)).bc([128, 6, 128]), ALU.add)
                P.act(Em[k][:], Dm[k][:], AF.Exp)
                P.tt("pool", scT[k][:].re("p (g h) n -> p g h n", g=2), Em[k][:].re("p (g h) n -> p g h n", g=2),
                     Gs[k][:].m(lambda a: a.unsqueeze(2)).bc([128, 2, 3, 128]), ALU.mult)
                P.tt("dve", Cd[k][:].re("p (g h) n -> p g h n", g=2), Er[k][:].re("p (g h) n -> p g h n", g=2),
                     CT[:, :, cs_].m(lambda a: a.unsqueeze(2)).bc([128, 2, 3, 128]), ALU.mult)
                dtc = dt_tm[:, c, d * 6:(d + 1) * 6]
                P.tt("dve", xdt[k][:], xs_sb[k][:].re("p (h q) -> p h q", h=6),
                     dtc.m(lambda a: a.unsqueeze(2)).bc([128, 6, 64]), ALU.mult)
                P.tt("pool", xdtt[k][:], xdt[k][:], Em[k][:, :, last:last + 1].bc([128, 6, 64]), ALU.mult)
                def back(c=c, k=k, cs_=cs_, d=d, last=last):
                    py = self.ps(5)
                    for h in range(6):
                        o_ = py[(h % 2) * 64:(h % 2) * 64 + 64, 0, (h // 2) * 128:(h // 2 + 1) * 128]
                        P.mm(o_, xdt[k][:, h, :], scT[k][:, h, :], start=True, stop=False)
                        P.mm(o_, St[:, h, :], Cd[k][:, h, :], start=False, stop=True)
                    pcs = self.ps(6)
                    for h in range(6):
                        P.mm(pcs[:, 0, h * 64:(h + 1) * 64], Bt_sb[k][:, h // 3, :], xdtt[k][:, h, :])
                    P.tt("dve", St[:], St[:], Er[k][:, :, last:last + 1].bc([128, 6, 64]), ALU.mult)
                    P.tt("dve", St[:], St[:], pcs[:, 0, 0:384].re("p (h q) -> p h q", h=6), ALU.add)
                    if d == 0:
                        P.copy("act", ysb[k][:].re("p j n -> p (j n)"), py[:, 0, 0:384])
                        P.dma("sp", self.yf.v(("yf", c), fn=lambda ap, cs_=cs_: ap.rearrange("(j p) t -> p j t", p=128)[:, :, cs_]), ysb[k][:])
                        return
                    if c < 2 and not ctx_out:
                        return
                    P.dma("act", yfl[k][:], self.yf.v(("yf", c), fn=lambda ap, cs_=cs_: ap.rearrange("(j p) t -> p j t", p=128)[:, :, cs_]))
                    P.dma("sp", hz[k][:], self.hT.v(("h", "z"), (slice(None), slice(None), cs_)))
                    pz = self.ps(7)
                    for j in range(3):
                        for kc in range(8):
                            P.mm(pz[:, 0, j * 128:(j + 1) * 128], wz[:, kc, j * 128:(j + 1) * 128], hz[k][:, kc, :],
                                 start=(kc == 0), stop=(kc == 7))
                    P.act(zs[k][:].re("p j n -> p (j n)"), pz[:, 0, 0:384], AF.Silu)
                    y_ = ysb[k]
                    P.tt("dve", y_[:].re("p j n -> p (j n)"), py[:, 0, 0:384], yfl[k][:].re("p j n -> p (j n)"), ALU.add)
                    P.tt("pool", yfl[k][:], xsT[:, :, cs_], self.ppt[:, osd:osd + 3].m(lambda a: a.unsqueeze(2)).bc([128, 3, 128]), ALU.mult)
                    P.tt("pool", y_[:], y_[:], yfl[k][:], ALU.add)
                    P.tt("dve", y_[:], y_[:], zs[k][:], ALU.mult)
                    P.act(sq[k][:], y_[:], AF.Square)
                    pgs = self.ps(7)
                    P.mm(pgs[:, 0, 384:512].m(lambda a: a), cf["ones"][:], sq[k][:, 0, :], start=True, stop=False)
                    P.mm(pgs[:, 0, 384:512], cf["hlo"][:], sq[k][:, 1, :], start=False, stop=True)
                    pgs2 = self.ps(6)
                    P.mm(pgs2[:, 0, 384:512], cf["hhi"][:], sq[k][:, 1, :], start=True, stop=False)
                    P.mm(pgs2[:, 0, 384:512], cf["ones"][:], sq[k][:, 2, :], start=False, stop=True)
                    P.act(rs[k][:, 0, :], pgs[:, 0, 384:512], AF.Sqrt, bias=1e-5, scale=1.0 / 192)
                    P.act(rs[k][:, 1, :], pgs2[:, 0, 384:512], AF.Sqrt, bias=1e-5, scale=1.0 / 192)
                    P.recip(rs[k][:], rs[k][:])
                    m_ = mo[k]
                    P.stt("dve", m_[:, 0, :], y_[:, 0, :], self.ppt[:, onw:onw + 1], rs[k][:, 0, :], ALU.mult, ALU.mult)
                    P.stt("dve", m_[0:64, 1, :], y_[0:64, 1, :], self.ppt[0:64, onw + 1:onw + 2], rs[k][0:64, 0, :], ALU.mult, ALU.mult)
                    P.stt("dve", m_[64:128, 1, :], y_[64:128, 1, :], self.ppt[64:128, onw + 1:onw + 2], rs[k][64:128, 1, :], ALU.mult, ALU.mult)
                    P.stt("dve", m_[:, 2, :], y_[:, 2, :], self.ppt[:, onw + 2:onw + 3], rs[k][:, 1, :], ALU.mult, ALU.mult)
                    P.dma("sp", self.mix.v(("ms", c), fn=lambda ap, cs_=cs_: ap[640:1024].rearrange("(j p) t -> p j t", p=128)[:, :, cs_]), m_[:])
                if pending is not None:
                    pending()
                pending = back
            if pending is not None:
                pending()

    def phase_rwkv(self, l):
        P = self.P
        ctx_out = (l < DEPTH - 1)
        LWC = -0.6065306597126334
        cf = {}
        for nme in ("blk", "lowf", "lowb"):
            cf[nme] = P.sb([128, 128], F32, "r_c_" + nme)
            self.load_c(cf[nme], nme, "act")
        m4 = {nme: P.sb([128, 2, 256], F32, "r_m4_" + nme) for nme in ("rmf", "rmb")}
        mk0 = P.mark()
        for nme in ("rmf", "rmb"):
            t_ = P.sb([128, 256], F32, "r_c_" + nme)
            self.load_c(t_, nme, "act")
            P.copy("dve", m4[nme][:], t_[:].m(lambda a: a.unsqueeze(1)).bc([128, 2, 256]))
        P.release(mk0)
        ones64 = P.sb([128, 64], F32, "r_ones")
        P.memset("pool", ones64[:], 1.0)
        omka = P.sb([128, 4], F32, "r_omka")
        oka = PP["ka"][0]
        P.ts("dve", omka[:], self.ppt[:, oka:oka + 4], -1.0, 1.0, ALU.mult, ALU.add)
        wupf = P.sb([128, 256], F32, "r_wupf")
        aupf = P.sb([128, 256], F32, "r_aupf")
        wupb = P.sb([128, 256], BF16, "r_wupb")
        aupb = P.sb([128, 256], BF16, "r_aupb")
        for d in range(2):
            P.dma("sp", wupf[d * 64:(d + 1) * 64, :], self.wup.v(("u", l, d), fn=lambda ap, d=d: ap[l][d]))
            P.dma("act", aupf[d * 64:(d + 1) * 64, :], self.aup.v(("u", l, d), fn=lambda ap, d=d: ap[l][d]))
        P.copy("dve", wupb[:], wupf[:])
        P.copy("pool", aupb[:], aupf[:])
        wdT = P.sb([128, S], BF16, "r_wdT")
        adT = P.sb([128, S], BF16, "r_adT")
        mk1 = P.mark()
        rows = self.shift_rows(l)
        hh = [P.sb([128, 8, 514], BF16, f"r_hh{i}") for i in range(2)]
        wl3 = self.scaled_w(l, C_RW + 768, 256, rows, "r_wl3", roff=768)
        for ti, (t0, tn) in enumerate(TILES):
            h = hh[ti % 2]
            self.load_h_halo(h, ti, "sp" if ti % 2 == 0 else "act")
            for c in range(2):
                pb = self.ps(c)
                n = 0
                for tap in range(3):
                    for kc in range(8):
                        P.mm(pb[:, 0, 0:tn], wl3[:, tap, kc, c * 128:(c + 1) * 128], h[:, kc, tap:tap + tn],
                             start=(n == 0), stop=(n == 23))
                        n += 1
                if c == 0:
                    P.act(wdT[:, t0:t0 + tn], pb[:, 0, 0:tn], AF.Tanh)
                else:
                    P.copy("act", adT[:, t0:t0 + tn], pb[:, 0, 0:tn])
        P.release(mk1)
        RW = int(os.environ.get("MK_RW", "99"))
        if RW <= 1:
            return
        mk_hp = P.mark()
        for hp in range(2):
            rT = P.sb([128, S], F32, "r_rT")
            kT = P.sb([128, S], F32, "r_kT")
            vT = P.sb([128, S], F32, "r_vT")
            ysum = P.sb([128, S], F32, "r_ysum")
            mk2 = P.mark()
            rows = self.shift_rows(l)
            hh = [P.sb([128, 8, 514], BF16, f"r_hh{i}") for i in range(2)]
            w3 = [self.scaled_w(l, C_RW + j * 256 + hp * 128, 128, rows, f"r_w3{j}", roff=j * 256 + hp * 128) for j in range(3)]
            for ti, (t0, tn) in enumerate(TILES):
                h = hh[ti % 2]
                self.load_h_halo(h, ti, "sp" if ti % 2 == 0 else "act")
                for j, dst in enumerate((rT, kT, vT)):
                    pb = self.ps(j % 2)
                    n = 0
                    for tap in range(3):
                        for kc in range(8):
                            P.mm(pb[:, 0, 0:tn], w3[j][:, tap, kc, :], h[:, kc, tap:tap + tn], start=(n == 0), stop=(n == 23))
                            n += 1
                    P.copy("act", dst[:, t0:t0 + tn], pb[:, 0, 0:tn])
            P.release(mk2)
            if RW <= 2:
                return
            mk3 = P.mark()
            self.rwkv_scan(l, hp, rT, kT, vT, wdT, adT, ysum, wupb, aupb, cf, m4, ones64, omka, LWC)
            P.release(mk3)
            self.rwkv_finish(l, hp, rT, kT, vT, ysum, cf, ctx_out)
            P.release(mk_hp)

    def shift_rows(self, l):
        P = self.P
        o = PR2["shr"][0]
        sr = P.sb([128, 3 * 1024], F32, "r_shr")
        P.dma("sp", sr[:], self.pr2.v(l, fn=lambda a: a[l][:, o:o + 3 * 1024]))
        r0, r2, r1 = sr[:, 0:1024], sr[:, 1024:2048], sr[:, 2048:3072]
        P.tt("dve", r1, r0, r2, ALU.add)
        P.ts("dve", r1, r1, -1.0, 1.0, ALU.mult, ALU.add)
        return [r0, r1, r2]

    def rwkv_scan(self, l, hp, rT, kT, vT, wdT, adT, ysum, wupb, aupb, cf, m4, ones64, omka, LWC):
        P = self.P
        NB = 2
        f32t = lambda n, w=512: [P.sb([128, w], F32, f"r_{n}")] * NB
        sg, aa, cum = f32t("sg"), f32t("aa"), f32t("cum")
        lw = sg
        cumx = [P.sb([128, 8], F32, "r_tot")] * NB
        e1, e2, e3, e4 = f32t("e1"), f32t("e2"), f32t("e3"), f32t("e4")
        kkr, sqk, nrm, kd, bq = f32t("kkr"), f32t("sqk"), f32t("nrm"), f32t("kd"), f32t("bq")
        kk, tmpa = kkr, nrm
        WC = [P.sb([128, 8], F32, f"r_WC{i}") for i in range(NB)]
        arb = [P.sb([128, 4, 2, 128], F32, "r_arb")] * NB
        kb = [P.sb([128, 512], F32, "r_kb")] * NB
        bb = [P.sb([128, 512], F32, "r_bb")] * NB
        kt = [P.sb([128, 512], F32, "r_kt")] * NB
        btl = [P.sb([128, 512], F32, "r_btl")] * NB
        G4 = 4
        TM = [P.sb([128, 4, 128], F32, f"r_TM{i}") for i in range(G4)]
        AT = [P.sb([128, 2, 4, 128], F32, f"r_AT{i}") for i in range(G4)]
        X1 = [P.sb([128, 2, 128], F32, f"r_X1{i}") for i in range(G4)]
        XX = [P.sb([128, 2, 2, 128], F32, f"r_XX{i}") for i in range(G4)]
        Qf = [P.sb([128, 2, 128], F32, f"r_Qf{i}") for i in range(G4)]
        W1 = [P.sb([128, 128], F32, f"r_W1{i}") for i in range(G4)]
        AU = [P.sb([128, 256], F32, f"r_AU{i}") for i in range(G4)]
        Y0T = [P.sb([128, 128], F32, f"r_Y0T{i}") for i in range(G4)]
        RhT = [P.sb([128, 128], F32, f"r_RhT{i}") for i in range(G4)]
        MTt = [P.sb([128, 128], F32, "r_MTt")] * 2
        MT = [[P.sb([128, 128], F32, f"r_MT{i}_{c}") for c in range(2)] for i in range(G4)]
        Ns = [[P.sb([128, 64], F32, f"r_Ns{i}_{c}") for c in range(2)] for i in range(G4)]
        ytmp = [P.sb([128, 64], F32, f"r_yt{i}") for i in range(2)]
        Sseq = P.sb([128, 4, 64], F32, "r_Sseq")
        ow0, oa0, okk, oka = PP["w0"][0], PP["a0"][0], PP["kk"][0], PP["ka"][0]
        if os.environ.get("MK_MEM"):
            print("rwkv_scan arena used", P._aoff, "of", P.ARENA)
        nblk_done = 0
        for d in range(2):
            if d == 1 and hp == 0 and os.environ.get("MK_RWDBG"):
                if not hasattr(self, "dbgy"):
                    self.dbgy = P.dram("dbgy", [128, S], F32, kind="ExternalOutput")
                P.dma("sp", self.dbgy.v(0), ysum[:])
            col = d * 2 + hp
            mask4 = m4["rmf"] if d == 0 else m4["rmb"]
            low = cf["lowf"] if d == 0 else cf["lowb"]
            P.memset("dve", Sseq[:, 0, :], 0.0)
            si = 0
            tiles = list(range(len(TILES))) if d == 0 else [0] + list(range(len(TILES) - 1, 0, -1))
            def prepA(tix):
                ti = tiles[tix]
                t0, tn = TILES[ti]
                nch = tn // 64
                nbk = tn // 128
                k_ = tix % NB
                ts_ = slice(t0, t0 + tn)
                ds_ = slice(d * 64, (d + 1) * 64)
                pxw, pxa = self.ps(0), self.ps(1)
                P.mm(pxw[:, 0, 0:tn], wupb[ds_, hp * 128:(hp + 1) * 128], wdT[ds_, ts_])
                P.act(sg[k_][:, 0:tn], pxw[:, 0, 0:tn], AF.Sigmoid, bias=self.ppt[:, ow0 + col:ow0 + col + 1])
                P.mm(pxa[:, 0, 0:tn], aupb[ds_, hp * 128:(hp + 1) * 128], adT[ds_, ts_])
                P.act(aa[k_][:, 0:tn], pxa[:, 0, 0:tn], AF.Sigmoid, bias=self.ppt[:, oa0 + col:oa0 + col + 1])
                yield
                P.ts("dve", lw[k_][:, 0:tn], sg[k_][:, 0:tn], LWC, None, ALU.mult)
                yield
                for c in range(nch):
                    cs = slice(c * 64, (c + 1) * 64)
                    P.scan("dve", cum[k_][:, cs], ones64[:], lw[k_][:, cs], 0.0, ALU.mult, ALU.add)
                    yield
                c3 = lambda t_: t_[:, 0:tn].re("p (c n) -> p c n", n=64)
                P.act(WC[k_][:, 0:nch].m(lambda a: a.unsqueeze(2)), c3(cum[k_])[:, :, 63:64], AF.Exp)
                yield
                if d == 1:
                    P.copy("pool", cumx[k_][:, 0:nch].m(lambda a: a.unsqueeze(2)), c3(cum[k_])[:, :, 63:64])
                    yield
                    P.tt("dve", cum[k_][:, 0:tn], lw[k_][:, 0:tn], cum[k_][:, 0:tn], ALU.subtract)
                    yield
                    P.tt("dve", c3(cum[k_]), c3(cum[k_]), cumx[k_][:, 0:nch].m(lambda a: a.unsqueeze(2)).bc([128, nch, 64]), ALU.add)
                    yield
                    tot = None
                P.tt("pool", e4[k_][:, 0:tn], cum[k_][:, 0:tn], lw[k_][:, 0:tn], ALU.subtract)
                yield
                P.act(e1[k_][:, 0:tn], cum[k_][:, 0:tn], AF.Exp)
                yield
                P.act(e2[k_][:, 0:tn], cum[k_][:, 0:tn], AF.Exp, scale=-1.0)
                yield
                P.act(e3[k_][:, 0:tn], e4[k_][:, 0:tn], AF.Exp)
                yield
                P.tt("dve", c3(e4[k_]), c3(e2[k_]), WC[k_][:, 0:nch].m(lambda a: a.unsqueeze(2)).bc([128, nch, 64]), ALU.mult)
                yield
                P.ts("dve", kkr[k_][:, 0:tn], kT[:, ts_], self.ppt[:, okk + col:okk + col + 1], None, ALU.mult)
                yield
                P.act(sqk[k_][:, 0:tn], kkr[k_][:, 0:tn], AF.Square)
                yield
                pss = self.ps(1)
                P.mm(pss[:, 0, 0:tn], cf["blk"][:], sqk[k_][:, 0:tn])
                P.act(nrm[k_][:, 0:tn], pss[:, 0, 0:tn], AF.Sqrt)
                yield
                P.ts("dve", nrm[k_][:, 0:tn], nrm[k_][:, 0:tn], 1e-12, None, ALU.max)
                yield
                P.recip(nrm[k_][:, 0:tn], nrm[k_][:, 0:tn])
                yield
                P.tt("dve", kk[k_][:, 0:tn], kkr[k_][:, 0:tn], nrm[k_][:, 0:tn], ALU.mult)
                yield
                P.ts("pool", tmpa[k_][:, 0:tn], aa[k_][:, 0:tn], self.ppt[:, oka + col:oka + col + 1], omka[:, col:col + 1], ALU.mult, ALU.add)
                yield
                P.tt("pool", kd[k_][:, 0:tn], kT[:, ts_], tmpa[k_][:, 0:tn], ALU.mult)
                yield
                P.tt("dve", bq[k_][:, 0:tn], kk[k_][:, 0:tn], aa[k_][:, 0:tn], ALU.mult)
                yield
            def prepB(tix):
                ti = tiles[tix]
                t0, tn = TILES[ti]
                nch = tn // 64
                nbk = tn // 128
                k_ = tix % NB
                ts_ = slice(t0, t0 + tn)
                ds_ = slice(d * 64, (d + 1) * 64)
                c3 = lambda t_: t_[:, 0:tn].re("p (c n) -> p c n", n=64)
                b3 = lambda t_: t_[:, 0:tn].re("p (b n) -> p b n", n=128)
                P.ts("pool", sqk[k_][:, 0:tn], kk[k_][:, 0:tn], -1.0, None, ALU.mult)
                yield
                P.tt("dve", arb[k_][:, 0:nbk, 0, :], b3(sqk[k_]), b3(e3[k_]), ALU.mult)
                yield
                P.tt("pool", arb[k_][:, 0:nbk, 1, :], rT[:, ts_].re("p (b n) -> p b n", n=128), b3(e1[k_]), ALU.mult)
                yield
                P.tt("dve", kb[k_][:, 0:tn], kd[k_][:, 0:tn], e2[k_][:, 0:tn], ALU.mult)
                yield
                P.tt("pool", bb[k_][:, 0:tn], bq[k_][:, 0:tn], e2[k_][:, 0:tn], ALU.mult)
                yield
                P.tt("dve", kt[k_][:, 0:tn], kd[k_][:, 0:tn], e4[k_][:, 0:tn], ALU.mult)
                yield
                P.tt("pool", btl[k_][:, 0:tn], bq[k_][:, 0:tn], e4[k_][:, 0:tn], ALU.mult)
                yield
            def step(gen, n=10 ** 9):
                for _ in range(n):
                    if next(gen, 'done') == 'done':
                        return
            step(prepA(0))
            step(prepB(0))
            for tix, ti in enumerate(tiles):
                t0, tn = TILES[ti]
                nch = tn // 64
                nbk = tn // 128
                k_ = tix % NB
                ts_ = slice(t0, t0 + tn)
                ds_ = slice(d * 64, (d + 1) * 64)
                nxtA = prepA(tix + 1) if tix + 1 < len(tiles) else iter(())
                blocks = list(range(nbk)) if d == 0 else list(range(nbk - 1, -1, -1))
                G = len(blocks)
                idm = lambda t_: t_[:].m(lambda a: a.unsqueeze(1)).bc([128, 2, 128])
                ev = 0
                for g, bi in enumerate(blocks):
                    bs = slice(bi * 128, (bi + 1) * 128)
                    pt = self.ps(2 + g)
                    P.tr(pt[:, 0, 0:128], arb[k_][:, bi, 0, :], self.c_ident[:])
                    P.tr(pt[:, 0, 128:256], btl[k_][:, bs], self.c_ident[:])
                    P.tr(pt[:, 0, 256:384], kt[k_][:, bs], self.c_ident[:])
                    P.tr(pt[:, 0, 384:512], vT[:, t0 + bi * 128:t0 + (bi + 1) * 128], self.c_ident[:])
                for g, bi in enumerate(blocks):
                    P.copy("act" if g % 2 == 0 else "dve", TM[g][:].re("p a n -> p (a n)"), self.ps(2 + g)[:, 0, 0:512])
                px2 = self.ps(6, 2)
                for half in range(0, G, 2):
                    gs = list(range(half, min(half + 2, G)))
                    for g in gs:
                        bi = blocks[g]
                        bs = slice(bi * 128, (bi + 1) * 128)
                        pa = self.ps(2 + 2 * (g % 2), 2)
                        for h in range(2):
                            hs = slice(h * 64, (h + 1) * 64)
                            ar_ = arb[k_][hs, bi, :, :].re("p a n -> p (a n)")
                            P.mm(pa[:, h, 0:256], bb[k_][hs, bs], ar_)
                            P.mm(pa[:, h, 256:512], kb[k_][hs, bs], ar_)
                            P.mm(px2[:, h, g * 128:(g + 1) * 128], arb[k_][hs, bi, 0, :], bb[k_][hs, bs])
                    for g in gs:
                        pa = self.ps(2 + 2 * (g % 2), 2)
                        P.tt("dve", AT[g][:].re("p h a n -> p h (a n)"), pa,
                             mask4[:].re("p a n -> p (a n)").m(lambda a: a.unsqueeze(1)).bc([128, 2, 512]), ALU.mult)
                for g in range(G):
                    P.tt("dve", X1[g][:], px2[:, :, g * 128:(g + 1) * 128], idm(low), ALU.mult)
                    P.tt("pool", Qf[g][:], AT[g][:, :, 0, :], idm(self.c_ident), ALU.add)
                xk = [[X1[g][:, h, :] for h in range(2)] for g in range(G)]
                xtk = [[AT[g][:, h, 0, :] for h in range(2)] for g in range(G)]
                for lev in range(5):
                    for g in range(G):
                        pn = self.ps(2 + g)
                        for h in range(2):
                            P.mm(pn[:, 0, h * 256:h * 256 + 128], xtk[g][h], xk[g][h])
                            if lev < 4:
                                P.mm(pn[:, 0, h * 256 + 128:h * 256 + 256], xk[g][h], xtk[g][h])
                    for g in range(G):
                        P.copy("act", XX[g][:].re("p h a n -> p (h a n)"), self.ps(2 + g)[:, 0, :])
                        xk[g] = [XX[g][:, h, 0, :] for h in range(2)]
                        xtk[g] = [XX[g][:, h, 1, :] for h in range(2)]
                    for g in range(G):
                        pq = self.ps(6 + g // 2)
                        for h in range(2):
                            c0 = (g % 2) * 256 + h * 128
                            P.mm(pq[:, 0, c0:c0 + 128], xk[g][h], Qf[g][:, h, :])
                    for g in range(G):
                        pq = self.ps(6 + g // 2)
                        c0 = (g % 2) * 256
                        P.tt("dve", Qf[g][:], pq[:, 0, c0:c0 + 256].re("p (h n) -> p h n", h=2), Qf[g][:], ALU.add)
                    step(nxtA, 7)
                pw = self.ps(0)
                for g in range(G):
                    for h in range(2):
                        P.mm(pw[:, 0, g * 128 + h * 64:g * 128 + (h + 1) * 64], AT[g][:, h, 2, :], TM[g][:, 3, h * 64:(h + 1) * 64])
                for g in range(G):
                    P.copy("act", W1[g][:], pw[:, 0, g * 128:(g + 1) * 128])
                for g in range(G):
                    pau = self.ps(2 + g // 2)
                    c0 = (g % 2) * 256
                    for h in range(2):
                        P.mm(pau[:, 0, c0 + h * 64:c0 + (h + 1) * 64], Qf[g][:, h, :], TM[g][:, 0, h * 64:(h + 1) * 64])
                        P.mm(pau[:, 0, c0 + 128 + h * 64:c0 + 128 + (h + 1) * 64], Qf[g][:, h, :], W1[g][:, h * 64:(h + 1) * 64])
                for g in range(G):
                    pau = self.ps(2 + g // 2)
                    c0 = (g % 2) * 256
                    P.copy("act" if g % 2 == 0 else "dve", AU[g][:], pau[:, 0, c0:c0 + 256])
                for g, bi in enumerate(blocks):
                    py = self.ps(4 + g // 2)
                    c0 = (g % 2) * 256
                    for h in range(2):
                        hs = slice(h * 64, (h + 1) * 64)
                        P.mm(py[hs, 0, c0:c0 + 128], AU[g][:, 128 + h * 64:128 + (h + 1) * 64], AT[g][:, h, 1, :], start=True, stop=False)
                        P.mm(py[hs, 0, c0:c0 + 128], TM[g][:, 3, h * 64:(h + 1) * 64], AT[g][:, h, 3, :], start=False, stop=True)
                        P.mm(py[hs, 0, c0 + 128:c0 + 256], AU[g][:, h * 64:(h + 1) * 64], AT[g][:, h, 1, :])
                for g, bi in enumerate(blocks):
                    py = self.ps(4 + g // 2)
                    c0 = (g % 2) * 256
                    P.copy("act", Y0T[g][:], py[:, 0, c0:c0 + 128])
                    P.tt("dve", RhT[g][:], py[:, 0, c0 + 128:c0 + 256], arb[k_][:, bi, 1, :], ALU.add)
                step(nxtA)
                if tix + 1 < len(tiles):
                    step(prepB(tix + 1))
                for g, bi in enumerate(blocks):
                    for cc in range(2):
                        cs = slice(cc * 64, (cc + 1) * 64)
                        pmn = self.ps((6 if g < 2 else 2) + cc)
                        c0 = (g % 2) * 192
                        P.mm(pmn[:, 0, c0:c0 + 128], AU[g][cs, 0:128], TM[g][cs, 1, :])
                        for h in range(2):
                            hs = slice(h * 64, (h + 1) * 64)
                            P.mm(pmn[hs, 0, c0 + 128:c0 + 192], TM[g][cs, 1, hs], AU[g][cs, 128 + h * 64:128 + (h + 1) * 64], start=True, stop=False)
                            P.mm(pmn[hs, 0, c0 + 128:c0 + 192], TM[g][cs, 2, hs], TM[g][cs, 3, hs], start=False, stop=True)
                for g, bi in enumerate(blocks):
                    for cc in range(2):
                        pmn = self.ps((6 if g < 2 else 2) + cc)
                        c0 = (g % 2) * 192
                        mt_ = MTt[cc]
                        P.tt("dve", mt_[:], pmn[:, 0, c0:c0 + 128], cf["blk"][:], ALU.mult)
                        wcol = bi * 2 + cc
                        P.stt("dve", MT[g][cc][:], self.c_ident[:], WC[k_][:, wcol:wcol + 1], mt_[:], ALU.mult, ALU.add)
                        P.copy("act", Ns[g][cc][:], pmn[:, 0, c0 + 128:c0 + 192])
                for g, bi in enumerate(blocks):
                    for cc in ([0, 1] if d == 0 else [1, 0]):
                        cs = slice(cc * 64, (cc + 1) * 64)
                        tok = slice(t0 + bi * 128 + cc * 64, t0 + bi * 128 + cc * 64 + 64)
                        pyh = [self.ps(4), self.ps(5)]
                        pss_ = self.ps(0)
                        for h in range(2):
                            hs = slice(h * 64, (h + 1) * 64)
                            P.mm(pyh[h][hs, 0, 0:64], Sseq[hs, si % 4, :], RhT[g][hs, cs])
                        P.mm(pss_[:, 0, 0:64], MT[g][cc][:], Sseq[:, si % 4, :])
                        P.tt("dve", Sseq[:, (si + 1) % 4, :], pss_[:, 0, 0:64], Ns[g][cc][:], ALU.add)
                        for h in range(2):
                            hs = slice(h * 64, (h + 1) * 64)
                            if d == 0:
                                P.tt("pool" if False else "dve", ysum[hs, tok], pyh[h][hs, 0, 0:64], Y0T[g][hs, cs], ALU.add)
                            else:
                                yt_ = ytmp[cc]
                                P.tt("dve", yt_[hs, :], pyh[h][hs, 0, 0:64], Y0T[g][hs, cs], ALU.add)
                                P.tt("pool", ysum[hs, tok], ysum[hs, tok], yt_[hs, :], ALU.add)
                        si += 1

    def rwkv_finish(self, l, hp, rT, kT, vT, ysum, cf, ctx_out):
        P = self.P
        wg = P.sb([128, 8, 128], BF16, "rf_wg")
        P.dma("sp", wg[:], self.wb(l, C_RG + hp * 128, 128))
        hb = [P.sb([128, 8, 512], BF16, f"rf_h{i}") for i in range(2)]
        f = lambda n: [P.sb([128, 512], F32, f"rf_{n}{i}") for i in range(2)]
        yc, sq, rstd, rk, gs = f("yc"), f("sq"), f("rstd"), f("rk"), f("gs")
        mo = [P.sb([128, 512], BF16, f"rf_mo{i}") for i in range(2)]
        olw, olb, ork = PP["lnw"][0] + hp, PP["lnb"][0] + hp, PP["rk"][0] + hp
        for ti, (t0, tn) in enumerate(TILES):
            if ti == 0 and not ctx_out:
                continue
            k_ = ti % 2
            ts_ = slice(t0, t0 + tn)
            self.load_h(hb[k_], ti, "sp" if k_ == 0 else "act")
            pg = self.ps(0)
            for kc in range(8):
                P.mm(pg[:, 0, 0:tn], wg[:, kc, :], hb[k_][:, kc, 0:tn], start=(kc == 0), stop=(kc == 7))
            P.act(gs[k_][:, 0:tn], pg[:, 0, 0:tn], AF.Silu)
            pm = self.ps(1)
            P.mm(pm[:, 0, 0:tn], cf["blk"][:], ysum[:, ts_])
            P.stt("dve", yc[k_][:, 0:tn], pm[:, 0, 0:tn], -1.0 / 64, ysum[:, ts_], ALU.mult, ALU.add)
            P.act(sq[k_][:, 0:tn], yc[k_][:, 0:tn], AF.Square)
            pv = self.ps(2)
            P.mm(pv[:, 0, 0:tn], cf["blk"][:], sq[k_][:, 0:tn])
            P.act(rstd[k_][:, 0:tn], pv[:, 0, 0:tn], AF.Sqrt, bias=64e-5, scale=1.0 / 64)
            P.recip(rstd[k_][:, 0:tn], rstd[k_][:, 0:tn])
            P.tt("dve", yc[k_][:, 0:tn], yc[k_][:, 0:tn], rstd[k_][:, 0:tn], ALU.mult)
            P.ts("dve", yc[k_][:, 0:tn], yc[k_][:, 0:tn], self.ppt[:, olw:olw + 1], self.ppt[:, olb:olb + 1], ALU.mult, ALU.add)
            P.stt("dve", rk[k_][:, 0:tn], rT[:, ts_], self.ppt[:, ork:ork + 1], kT[:, ts_], ALU.mult, ALU.mult)
            pb = self.ps(3)
            P.mm(pb[:, 0, 0:tn], cf["blk"][:], rk[k_][:, 0:tn])
            P.tt("dve", rk[k_][:, 0:tn], pb[:, 0, 0:tn], vT[:, ts_], ALU.mult)
            P.tt("pool", yc[k_][:, 0:tn], yc[k_][:, 0:tn], rk[k_][:, 0:tn], ALU.add)
            P.tt("dve", mo[k_][:, 0:tn], yc[k_][:, 0:tn], gs[k_][:, 0:tn], ALU.mult)
            r0_ = 384 + hp * 128
            P.dma("sp", self.mix.v(("mr", hp, ti), (slice(r0_, r0_ + 128), ts_)), mo[k_][:, 0:tn])

    def zero_mix(self, r0, r1):
        P = self.P
        z = P.sb([128, S], BF16, "zmix")
        P.memset("pool", z[:], 0.0)
        for kc in range(r0 // 128, r1 // 128):
            P.dma("sp", self.mix.v(("mz", kc), (slice(kc * 128, kc * 128 + 128), slice(None))), z[:])

    def build(self, layers=DEPTH):
        P = self.P
        base = P.mark()
        self.phase_w()
        P.release(base)
        for l in range(layers):
            self.phase_mod(l)
            keep = P.mark()
            self.phase_norm(l)
            P.release(keep)
            if self.stop_after == "norm":
                return
            if not os.environ.get("MK_SKIP_ATT"):
                self.phase_att(l)
                P.release(keep)
            if self.stop_after == "att":
                return
            if os.environ.get("MK_SKIP_RWKV"):
                self.zero_mix(384, 640)
            else:
                self.phase_rwkv(l)
            P.release(keep)
            if self.stop_after == "rwkv":
                return
            self.phase_ssd(l)
            P.release(keep)
            if self.stop_after == "ssd":
                return
            self.phase_out(l)
            P.release(keep)
            for a in ("n_x", "o_w"):
                if hasattr(self, a):
                    delattr(self, a)

def build_program(stop_after=None, dbg=False, layers=DEPTH):
    nc = bass.Bass("TRN2", target_bir_lowering=False)
    k = MK(nc, stop_after=stop_after, dbg=dbg)
    k.build(layers)
    k.P.emit()
    return nc, k


def prep_inputs(inp, b):
    inp = {k: np.asarray(v) for k, v in inp.items()}
    d = {}
    d["xin"] = np.ascontiguousarray(np.concatenate([inp["ctx"][b], inp["x"][b]], 0).astype(np.float32))
    d["cvec"] = np.ascontiguousarray(np.concatenate([fm(inp["c"][b]), fm(inp["c_ctx"])], 1))
    d["ada_w"] = np.ascontiguousarray(inp["ada_w"], np.float32)
    d["w_in"] = np.ascontiguousarray(inp["w_in"], np.float32)
    d["w_out"] = np.ascontiguousarray(inp["w_out"], np.float32)
    d["wup"] = np.ascontiguousarray(inp["rwkv_w_up"], np.float32)
    d["aup"] = np.ascontiguousarray(inp["rwkv_a_up"], np.float32)
    d["pp"] = np.stack([make_pp(inp, l) for l in range(DEPTH)])
    d["pr"] = np.stack([make_pr(inp, l) for l in range(DEPTH)])
    d["pr2"] = np.stack([make_pr2(inp, l) for l in range(DEPTH)])
    d["cst"] = make_consts()
    return d


def kernel(**inputs):
    nc, _ = build_program()
    maps = [prep_inputs(inputs, b % 4) for b in range(4)]
    in_maps = [maps[i % 4] for i in range(8)]
    res = run_bass_kernel_spmd(nc, in_maps, core_ids=list(range(8)))
    return np.stack([np.asarray(res.results[b]["y"], np.float32) for b in range(4)], 0)
```

```python
import numpy as np
import concourse.bass as bass
import concourse.mybir as mybir

F32 = mybir.dt.float32
BF16 = mybir.dt.bfloat16
ALU = mybir.AluOpType
AF = mybir.ActivationFunctionType
AX = mybir.AxisListType

import os
DUMP = os.environ.get("MK_DUMP", "")
NOSYNC = set(filter(None, os.environ.get("MK_NOSYNC", "").split(",")))
ENGS = ("pe", "dve", "act", "pool", "sp")
N_DMA_SEMS = 20


class Trk:
    __slots__ = ("lw", "rd", "ps")

    def __init__(self, ps=False):
        self.lw = None
        self.rd = []
        self.ps = ps


class V:
    __slots__ = ("ap", "trk")

    def __init__(self, ap, trk):
        self.ap = ap
        self.trk = trk

    def __getitem__(self, idx):
        return V(self.ap[idx], self.trk)

    def m(self, fn):
        return V(fn(self.ap), self.trk)

    def re(self, pat, **kw):
        return V(self.ap.rearrange(pat, **kw), self.trk)

    def bc(self, shape):
        return V(self.ap.broadcast_to(shape), self.trk)

    @property
    def shape(self):
        return self.ap.shape


class Tile:
    def __init__(self, handle):
        self.h = handle
        self.trk = Trk()

    def __getitem__(self, idx):
        return V(self.h[idx], (self.trk,))

    def ap(self):
        return V(self.h.ap() if hasattr(self.h, "ap") else self.h[:], (self.trk,))


class DTile:
    def __init__(self, handle):
        self.h = handle
        self.reg = {}

    def v(self, key, idx=None, fn=None):
        t = self.reg.setdefault(key, Trk())
        ap = self.h.ap()
        if fn is not None:
            ap = fn(ap)
        if idx is not None:
            ap = ap[idx]
        return V(ap, (t,))


class Op:
    __slots__ = ("eng", "fn", "reads", "writes", "idx", "deps", "signal", "sig_count",
                 "is_dma", "dma_sem", "dma_val", "dma_prev", "kind")


class Prog:
    def __init__(self, nc):
        self.nc = nc
        self.ops = {e: [] for e in ENGS}
        self.n_t = 0
        self.all_ops = []

    ARENA = 207 * 1024

    def sb(self, shape, dtype, name=None):
        self.n_t += 1
        if not hasattr(self, "_abase"):
            a = self.nc.alloc_sbuf_tensor("arena", [128, self.ARENA], mybir.dt.uint8)
            self._abase = self.nc.lookup_mloc(a).addr
            self._aoff = 0
        esz = 2 if dtype == BF16 else 4
        n = esz
        for d in shape[1:]:
            n *= d
        off = (self._aoff + 63) // 64 * 64
        assert off + n <= self.ARENA, f"SBUF arena overflow allocating {name} {shape}: {off + n}"
        self._aoff = off + n
        return Tile(self.nc.alloc_sbuf_tensor_at(f"{name or 'sb'}_{self.n_t}", list(shape), dtype, offset=self._abase + off))

    def mark(self):
        return getattr(self, "_aoff", 0)

    def release(self, mark):
        self.barrier()
        self._aoff = mark

    def barrier(self):
        lasts = []
        for e in ENGS:
            for op in reversed(self.ops[e]):
                if not op.is_dma and op.kind != "bar":
                    lasts.append(op)
                    break
        dmas = [op for op in self.all_ops[getattr(self, "_bar_pos", 0):] if op.is_dma]
        self._bar_pos = len(self.all_ops)
        for e in ENGS:
            op = self.rec(e, lambda eh: eh.nop(nofuse=True), [], [], kind="bar")
            op.deps = [d for d in lasts if d.eng != e] + dmas
            for d in op.deps:
                d.signal = True

    def ps(self, shape, dtype=F32, name=None):
        self.n_t += 1
        return Tile(self.nc.alloc_psum_tensor(name or f"ps{self.n_t}", list(shape), dtype))

    def dram(self, name, shape, dtype, kind="Internal"):
        return DTile(self.nc.dram_tensor(name, list(shape), dtype, kind=kind))

    def rec(self, eng, fn, reads, writes, is_dma=False, kind=""):
        op = Op()
        op.eng = eng
        op.fn = fn
        op.kind = kind
        op.is_dma = is_dma
        op.idx = len(self.ops[eng])
        op.signal = False
        op.deps = []
        rt = []
        for v in reads:
            if v is None or not isinstance(v, V):
                continue
            rt.extend(v.trk)
        wt = []
        for v in writes:
            if v is None or not isinstance(v, V):
                continue
            wt.extend(v.trk)
        deps = set()
        for t in rt:
            if t.lw is not None:
                deps.add(t.lw)
            if t.ps:
                for r in t.rd:
                    if r.eng != eng:
                        deps.add(r)
        for t in wt:
            if t.lw is not None:
                deps.add(t.lw)
            for r in t.rd:
                deps.add(r)
        deps.discard(op)
        for t in rt:
            t.rd.append(op)
        for t in wt:
            t.lw = op
            t.rd = []
        final = []
        for d in deps:
            if d.eng == "pe" and eng == "pe" and not d.is_dma and not is_dma:
                continue
            if d.eng == eng and eng in NOSYNC and not d.is_dma and not is_dma:
                continue
            final.append(d)
        op.deps = final
        for d in final:
            d.signal = True
        self.ops[eng].append(op)
        self.all_ops.append(op)
        return op

    def mm(self, out, lhsT, rhs, start=True, stop=True):
        return self.rec("pe", lambda e: e.matmul(out.ap, lhsT.ap, rhs.ap, start=start, stop=stop),
                        [lhsT, rhs], [out], kind="mm")

    def tr(self, out, in_, ident):
        return self.rec("pe", lambda e: e.transpose(out.ap, in_.ap, ident.ap), [in_, ident], [out], kind="tr")

    def act(self, out, in_, func, bias=None, scale=None, accum_out=None, eng="act"):
        kw = {}
        rd = [in_]
        if bias is not None:
            kw["bias"] = bias.ap if isinstance(bias, V) else bias
            rd.append(bias)
        if scale is not None:
            kw["scale"] = scale.ap if isinstance(scale, V) else scale
            rd.append(scale)
        wr = [out]
        if accum_out is not None:
            kw["accum_out"] = accum_out.ap
            wr.append(accum_out)
        return self.rec("act", lambda e: e.activation(out.ap, in_.ap, func, **kw), rd, wr, kind="act")

    def tt(self, eng, out, in0, in1, op):
        return self.rec(eng, lambda e: e.tensor_tensor(out.ap, in0.ap, in1.ap, op), [in0, in1], [out], kind="tt")

    def ts(self, eng, out, in0, s1, s2, op0, op1=None, accum_out=None):
        rd = [in0, s1, s2]
        a1 = s1.ap if isinstance(s1, V) else s1
        a2 = s2.ap if isinstance(s2, V) else s2
        kw = {}
        wr = [out]
        if accum_out is not None:
            kw["accum_out"] = accum_out.ap
            wr.append(accum_out)
        if op1 is None:
            return self.rec(eng, lambda e: e.tensor_scalar(out.ap, in0.ap, a1, None, op0, **kw), rd, wr, kind="ts")
        return self.rec(eng, lambda e: e.tensor_scalar(out.ap, in0.ap, a1, a2, op0, op1, **kw), rd, wr, kind="ts")

    def stt(self, eng, out, in0, scalar, in1, op0, op1):
        a = scalar.ap if isinstance(scalar, V) else scalar
        return self.rec(eng, lambda e: e.scalar_tensor_tensor(out.ap, in0.ap, a, in1.ap, op0, op1),
                        [in0, scalar, in1], [out], kind="stt")

    def copy(self, eng, out, in_):
        if eng == "act":
            return self.rec("act", lambda e: e.copy(out.ap, in_.ap), [in_], [out], kind="copy")
        return self.rec(eng, lambda e: e.tensor_copy(out.ap, in_.ap), [in_], [out], kind="copy")

    def memset(self, eng, out, val):
        return self.rec(eng, lambda e: e.memset(out.ap, val), [], [out], kind="memset")

    def reduce(self, eng, out, in_, op, axis=AX.X):
        return self.rec(eng, lambda e: e.tensor_reduce(out.ap, in_.ap, axis, op), [in_], [out], kind="red")

    def recip(self, out, in_):
        return self.rec("dve", lambda e: e.reciprocal(out.ap, in_.ap), [in_], [out], kind="recip")

    def scan(self, eng, out, d0, d1, initial, op0, op1):
        ini = initial.ap if isinstance(initial, V) else initial
        return self.rec(eng, lambda e: e.tensor_tensor_scan(out.ap, d0.ap, d1.ap, ini, op0, op1),
                        [d0, d1, initial], [out], kind="scan")

    def dma(self, q, out, in_, **kw):
        return self.rec(q, lambda e: e.dma_start(out.ap, in_.ap, **kw), [in_], [out], is_dma=True, kind="dma")

    def emit(self):
        nc = self.nc
        eng_sem = {e: nc.alloc_semaphore(f"s_{e}") for e in ENGS}
        dma_sems = {e: ([nc.alloc_semaphore(f"d_{e}_{i}") for i in range(N_DMA_SEMS)]
                        if any(o.is_dma for o in self.ops[e]) else []) for e in ENGS}
        for e in ENGS:
            cnt = 0
            nd = 0
            last_on_sem = {}
            for op in self.ops[e]:
                if op.is_dma:
                    j = nd % N_DMA_SEMS
                    nd += 1
                    op.dma_sem = dma_sems[e][j]
                    k = nd_k = (nd - 1) // N_DMA_SEMS + 1
                    op.dma_val = 16 * k
                    op.dma_prev = last_on_sem.get(j)
                    last_on_sem[j] = op
                else:
                    if op.signal:
                        cnt += 1
                        op.sig_count = cnt
        self._eng_sem = eng_sem
        handles = {"pe": "tensor", "dve": "vector", "act": "scalar", "pool": "gpsimd", "sp": "sync"}

        def run_engine(ename, eh):
            seen = {}

            def wait(sem, val):
                key = id(sem)
                if seen.get(key, 0) >= val:
                    return
                seen[key] = val
                eh.wait_ge(sem, val)

            for op in self.ops[ename]:
                for d in op.deps:
                    if d.is_dma:
                        wait(d.dma_sem, d.dma_val)
                    else:
                        wait(eng_sem[d.eng], d.sig_count)
                if op.is_dma:
                    if op.dma_prev is not None:
                        wait(op.dma_prev.dma_sem, op.dma_prev.dma_val)
                    ins = op.fn(eh)
                    ins.then_inc(op.dma_sem, 16)
                else:
                    ins = op.fn(eh)
                    if op.signal:
                        ins.then_inc(eng_sem[ename], 1)
                    if DUMP and ename in DUMP:
                        print(ename, op.idx, op.kind, ins.concise(), flush=True)
            last = {}
            for op in self.ops[ename]:
                if op.is_dma:
                    last[id(op.dma_sem)] = op
            for op in last.values():
                wait(op.dma_sem, op.dma_val)

        with nc.Block() as block:
            @block.tensor
            def _(eh):
                run_engine("pe", eh)

            @block.vector
            def _(eh):
                run_engine("dve", eh)

            @block.scalar
            def _(eh):
                run_engine("act", eh)

            @block.gpsimd
            def _(eh):
                run_engine("pool", eh)

            @block.sync
            def _(eh):
                run_engine("sp", eh)

    def stats(self):
        return {e: len(self.ops[e]) for e in ENGS}

from concourse.bass_utils import run_bass_kernel_spmd

S = 4352
CTX = 256
TLAT = 4096
DM = 1024
NIN = 3596
NBLK = 34
DEPTH = 2
TILES = [(0, 256)] + [(256 + 512 * i, 512) for i in range(8)]
NEG = -30000.0

C_Q, C_K, C_V, C_G = 0, 384, 512, 640
C_RW = 1024
C_RG = 2048
C_Z = 2304
C_XBC = 2688
C_DT = 3584


def _slots(spec):
    d, o = {}, 0
    for n, w in spec:
        d[n] = (o, w)
        o += w
    return d, o


PP, NPP = _slots([("adab", 24), ("normw", 8), ("sink", 6), ("mu0", 8), ("mu1", 8), ("w0", 4), ("a0", 4),
                  ("kk", 4), ("ka", 4), ("rk", 2), ("lnw", 2), ("lnb", 2), ("convw", 21), ("convb", 7),
                  ("ssmd", 3), ("ssmnw", 3)])
PR, NPR = _slots([("adabg", 1024), ("dtb", 12), ("alog", 12), ("fnw", 1024)])
PR2, NPR2 = _slots([("convr", 3 * 896), ("shr", 3 * 1024)])
CS, NCS = _slots([("ident", 128), ("mband", 3 * 384), ("perm", 128), ("cos", 4096), ("sin", 4096),
                  ("blk", 128), ("rmf", 256), ("rmb", 256), ("snf", 128), ("snb", 128), ("trif", 128),
                  ("trib", 128), ("ones", 128), ("hlo", 128), ("hhi", 128), ("lowf", 128), ("lowb", 128)])


def fm(v):
    v = np.asarray(v, np.float32)
    return np.ascontiguousarray(v.reshape(-1, 128).T)


def rep(v):
    v = np.asarray(v, np.float32).reshape(1, -1)
    return np.ascontiguousarray(np.broadcast_to(v, (128, v.shape[1])))


def make_consts():
    c = np.zeros((128, NCS), np.float32)

    def put(n, a):
        o, w = CS[n]
        c[:, o:o + w] = a
    put("ident", np.eye(128, dtype=np.float32))
    qi = np.arange(128)[:, None]
    kj = np.arange(384)[None, :]
    band = np.abs(kj - 128 - qi) <= 128
    mb = []
    for var in range(3):
        ok = band.copy()
        if var == 0:
            ok &= kj >= 128
        if var == 2:
            ok &= kj < 256
        mb.append(np.where(ok, 0.0, NEG))
    put("mband", np.concatenate(mb, 1))
    perm = np.zeros((128, 128), np.float32)
    for m in range(128):
        d = m % 64
        half = (d % 32) // 16
        partner = m + 16 if half == 0 else m - 16
        perm[partner, m] = 1.0
    put("perm", perm)
    rows = TLAT // 64
    row = np.repeat(np.arange(rows), 64).astype(np.float32)
    col = np.tile(np.arange(64), rows).astype(np.float32)
    pos = np.stack([row, col], -1)
    inv = (np.float32(10000.0) ** (-np.arange(16, dtype=np.float32) / np.float32(16))).astype(np.float32)
    ang = (pos[:, :, None] * inv).astype(np.float32)
    cosv, sinv = np.cos(ang).astype(np.float32), np.sin(ang).astype(np.float32)
    ct = np.zeros((128, TLAT), np.float32)
    st = np.zeros((128, TLAT), np.float32)
    for m in range(128):
        d = m % 64
        ax, half, f = d // 32, (d % 32) // 16, d % 16
        ct[m] = cosv[:, ax, f]
        st[m] = -sinv[:, ax, f] if half == 0 else sinv[:, ax, f]
    put("cos", ct)
    put("sin", st)
    blk = np.zeros((128, 128), np.float32)
    blk[:64, :64] = 1
    blk[64:, 64:] = 1
    put("blk", blk)
    s = np.arange(128)[:, None]
    t = np.arange(128)[None, :]
    same = (s // 64) == (t // 64)
    put("rmf", np.concatenate([(same & (s < t)), (same & (s <= t))], 1).astype(np.float32))
    put("rmb", np.concatenate([(same & (s > t)), (same & (s >= t))], 1).astype(np.float32))
    put("lowf", (same & (t < s)).astype(np.float32))
    put("lowb", (same & (t > s)).astype(np.float32))
    put("snf", np.where(s <= t, 0.0, NEG))
    put("snb", np.where(s >= t, 0.0, NEG))
    put("trif", (s <= t).astype(np.float32))
    put("trib", (s >= t).astype(np.float32))
    put("ones", np.ones((128, 128), np.float32))
    hlo = np.zeros((128, 128), np.float32)
    hlo[:64] = 1
    put("hlo", hlo)
    put("hhi", 1 - hlo)
    return c


def make_pp(inp, l):
    p = np.zeros((128, NPP), np.float32)

    def put(n, a):
        o, w = PP[n]
        assert a.shape == (128, w), (n, a.shape)
        p[:, o:o + w] = a
    put("adab", fm(inp["ada_b"][l]))
    put("normw", fm(inp["norm_w"][l]))
    put("sink", rep(inp["attn_sink"][l]))
    put("mu0", fm(inp["rwkv_mu"][l, 0]))
    put("mu1", fm(inp["rwkv_mu"][l, 1]))
    for n, k in (("w0", "rwkv_w0"), ("a0", "rwkv_a0"), ("kk", "rwkv_k_k"), ("ka", "rwkv_k_a")):
        put(n, np.concatenate([fm(inp[k][l, 0]), fm(inp[k][l, 1])], 1))
    put("rk", fm(inp["rwkv_r_k"][l].reshape(-1)))
    put("lnw", fm(inp["rwkv_ln_w"][l]))
    put("lnb", fm(inp["rwkv_ln_b"][l]))
    cw = inp["ssm_conv_w"][l]
    put("convw", np.stack([fm(cw[0]), fm(cw[1]), fm(cw[2])], -1).reshape(128, 21))
    put("convb", fm(inp["ssm_conv_b"][l]))
    put("ssmd", fm(np.repeat(inp["ssm_d"][l], 64)))
    put("ssmnw", fm(inp["ssm_norm_w"][l]))
    return p


def make_pr(inp, l):
    p = np.zeros((128, NPR), np.float32)

    def put(n, a):
        o, w = PR[n]
        p[:, o:o + w] = a
    put("adabg", rep(inp["ada_b"][l, 2048:3072]))
    put("dtb", rep(inp["ssm_dt_bias"][l].reshape(-1)))
    put("alog", rep(inp["ssm_a_log"][l].reshape(-1)))
    put("fnw", rep(inp["final_norm_w"]))
    return p


def make_pr2(inp, l):
    p = np.zeros((128, NPR2), np.float32)

    def put(n, a):
        o, w = PR2[n]
        p[:, o:o + w] = a
    put("convr", rep(inp["ssm_conv_w"][l].reshape(-1)))
    mu = inp["rwkv_mu"][l]
    put("shr", rep(np.concatenate([mu[0], mu[1], mu[1]], 0)))
    return p


class MK:
    def __init__(self, nc, stop_after=None, dbg=False):
        self.nc = nc
        self.P = P = Prog(nc)
        self.dbg = dbg
        self.stop_after = stop_after
        ok = "ExternalOutput" if dbg else "Internal"
        self.xin = P.dram("xin", [S, DM], F32, kind="ExternalInput")
        self.cvec = P.dram("cvec", [128, 16], F32, kind="ExternalInput")
        self.ada_w = P.dram("ada_w", [DEPTH, DM, 3 * DM], F32, kind="ExternalInput")
        self.w_in = P.dram("w_in", [DEPTH, DM, NIN], F32, kind="ExternalInput")
        self.w_out = P.dram("w_out", [DEPTH, DM, DM], F32, kind="ExternalInput")
        self.wup = P.dram("wup", [DEPTH, 2, 64, 256], F32, kind="ExternalInput")
        self.aup = P.dram("aup", [DEPTH, 2, 64, 256], F32, kind="ExternalInput")
        self.pp = P.dram("pp", [DEPTH, 128, NPP], F32, kind="ExternalInput")
        self.pr = P.dram("pr", [DEPTH, 128, NPR], F32, kind="ExternalInput")
        self.pr2 = P.dram("pr2", [DEPTH, 128, NPR2], F32, kind="ExternalInput")
        self.cst = P.dram("cst", [128, NCS], F32, kind="ExternalInput")
        self.y = P.dram("y", [TLAT, DM], F32, kind="ExternalOutput")
        self.wbin = P.dram("wbin", [DEPTH, 128, 8, NIN], BF16)
        self.wbout = P.dram("wbout", [DEPTH, 128, 8, DM], BF16)
        self.hT = P.dram("hT", [128, 8, S], BF16, kind=ok)
        self.mix = P.dram("mixT", [DM, S], BF16, kind=ok)
        self.xres = P.dram("xres", [S, DM], F32, kind=ok)
        self.yf = P.dram("yfwd", [384, S], F32)
        self.psh = nc.alloc_psum_tensor("psum", [128, 8, 512], F32)
        self.bt = [Trk(ps=True) for _ in range(8)]
        self.c_ident = P.sb([128, 128], F32, "c_ident")
        self.c_identb = P.sb([128, 128], BF16, "c_identb")
        self.load_c(self.c_ident, "ident")
        P.copy("dve", self.c_identb[:], self.c_ident[:])

    def ps(self, b0, nb=1, dt=F32):
        ap = self.psh[:, b0:b0 + nb, :]
        if dt is not F32:
            ap = ap.bitcast(dt)
        return V(ap, tuple(self.bt[b0:b0 + nb]))

    def load_c(self, tile, name, q="sp", sub=None):
        o, w = CS[name]
        if sub is not None:
            o, w = o + sub[0], sub[1]
        self.P.dma(q, tile[:], self.cst.v("c", (slice(None), slice(o, o + w))))

    def phase_w(self):
        P = self.P
        st = [P.sb([128, 8, 512], F32, f"w_st{i}") for i in range(2)]
        sb = [P.sb([128, 8, 512], BF16, f"w_sb{i}") for i in range(2)]
        it = 0
        for l in range(DEPTH):
            for (src, dst, n) in ((self.w_in, self.wbin, NIN), (self.w_out, self.wbout, DM)):
                for c0 in range(0, n, 512):
                    cw = min(512, n - c0)
                    a, b = st[it % 2], sb[it % 2]
                    q = "sp" if it % 2 == 0 else "act"
                    P.dma(q, a[:, :, 0:cw], src.v(("w", l), fn=lambda ap, l=l, c0=c0, cw=cw:
                                                   ap[l].rearrange("(kc p) n -> p kc n", p=128)[:, :, c0:c0 + cw]))
                    P.copy("dve" if it % 2 == 0 else "pool", b[:, :, 0:cw], a[:, :, 0:cw])
                    P.dma(q, dst.v(("wb", l, c0), fn=lambda ap, l=l, c0=c0, cw=cw: ap[l][:, :, c0:c0 + cw]),
                          b[:, :, 0:cw])
                    it += 1

    def wb(self, l, c0, cw):
        t0 = (c0 // 512) * 512
        trk = []
        ap = self.wbin.h.ap()[l][:, :, c0:c0 + cw]
        for t in range(t0, c0 + cw, 512):
            trk.append(self.wbin.reg.setdefault(("wb", l, t), Trk()))
        return V(ap, tuple(trk))

    def wbo(self, l, c0, cw):
        t0 = (c0 // 512) * 512
        trk = []
        ap = self.wbout.h.ap()[l][:, :, c0:c0 + cw]
        for t in range(t0, c0 + cw, 512):
            trk.append(self.wbout.reg.setdefault(("wb", l, t), Trk()))
        return V(ap, tuple(trk))

    def phase_mod(self, l):
        P = self.P
        if not hasattr(self, "ppt"):
            self.ppt = P.sb([128, NPP], F32, "ppt")
            self.prt = P.sb([128, NPR], F32, "prt")
            self.cact = P.sb([128, 8, 2], F32, "cact")
            self.modT = P.sb([128, 24, 2], F32, "modT")
            self.s1T = P.sb([128, 8, 2], F32, "s1T")
            self.gate_bc = P.sb([128, 2, 1024], F32, "gate_bc")
            craw = P.sb([128, 16], F32, "craw")
            P.dma("sp", craw[:], self.cvec.v(0))
            P.act(self.cact[:].re("p k w -> p w k"), craw[:].re("p (w k) -> p w k", w=2), AF.Silu)
        mk_ = P.mark()
        self.crep = P.sb([128, 8, 2, 128], F32, "crep")
        self.aw = [P.sb([128, 8, 512], F32, f"aw{i}") for i in range(2)]
        P.copy("dve", self.crep[:], self.cact[:].m(lambda a: a.unsqueeze(3)).bc([128, 8, 2, 128]))
        P.dma("sp", self.ppt[:], self.pp.v(l, fn=lambda a: a[l]))
        P.dma("act", self.prt[:], self.pr.v(l, fn=lambda a: a[l]))
        for t in range(6):
            a = self.aw[t % 2]
            P.dma("sp" if t % 2 == 0 else "act", a[:],
                  self.ada_w.v(("aw", l), fn=lambda ap, t=t: ap[l].rearrange("(kc p) n -> p kc n", p=128)[:, :, t * 512:(t + 1) * 512]))
            pm = self.ps(0)
            for j in range(4):
                for kc in range(8):
                    P.mm(pm[:, 0, j * 2:(j + 1) * 2], a[:, kc, j * 128:(j + 1) * 128], self.cact[:, kc, :],
                         start=(kc == 0), stop=(kc == 7))
            P.copy("dve", self.modT[:, t * 4:(t + 1) * 4, :], pm[:, 0, 0:8].re("p (j w) -> p j w", w=2))
            if t >= 4:
                for w in range(2):
                    pg = self.ps(1 + w)
                    for kc in range(8):
                        P.mm(pg[:, 0, :], self.crep[:, kc, w, :], a[:, kc, :], start=(kc == 0), stop=(kc == 7))
                    o = PR["adabg"][0] + (t - 4) * 512
                    P.tt("dve", self.gate_bc[:, w, (t - 4) * 512:(t - 3) * 512], pg[:, 0, :], self.prt[:, o:o + 512], ALU.add)
        o = PP["adab"][0]
        P.tt("dve", self.modT[:], self.modT[:], self.ppt[:, o:o + 24].m(lambda a: a.unsqueeze(2)).bc([128, 24, 2]), ALU.add)
        o = PP["normw"][0]
        P.stt("dve", self.s1T[:], self.modT[:, 8:16, :], 1.0,
              self.ppt[:, o:o + 8].m(lambda a: a.unsqueeze(2)).bc([128, 8, 2]), ALU.add, ALU.mult)
        P.release(mk_)

    def phase_norm(self, l):
        P = self.P
        if not hasattr(self, "n_x"):
            self.n_x = [P.sb([128, DM], F32, f"n_x{i}") for i in range(2)]
            self.n_junk = P.sb([128, DM], F32, "n_junk")
            self.n_xb = [P.sb([128, DM], BF16, f"n_xb{i}") for i in range(2)]
            self.n_ss = [P.sb([128, 4], F32, f"n_ss{i}") for i in range(2)]
            self.n_h = [P.sb([128, 8, 512], BF16, f"n_h{i}") for i in range(2)]
            self.n_t = [P.sb([128, 8, 128], F32, f"n_t{i}") for i in range(2)]
        src = self.xin if l == 0 else self.xres
        for ti, (t0, tn) in enumerate(TILES):
            w = 1 if ti == 0 else 0
            hb = self.n_h[ti % 2]
            for bi in range(tn // 128):
                blk = (t0 // 128) + bi
                xt, xb, ss = self.n_x[blk % 2], self.n_xb[blk % 2], self.n_ss[blk % 2]
                P.dma("sp" if blk % 2 == 0 else "act", xt[:], src.v(("x", blk), (slice(blk * 128, blk * 128 + 128), slice(None))))
                P.act(self.n_junk[:], xt[:], AF.Square, accum_out=ss[:, 0:1])
                P.act(ss[:, 1:2], ss[:, 0:1], AF.Sqrt, bias=1e-6, scale=1.0 / DM)
                P.recip(ss[:, 2:3], ss[:, 1:2])
                P.ts("dve", xb[:], xt[:], ss[:, 2:3], None, ALU.mult)
                pt = self.ps(2 + blk % 2, 1, BF16)
                for kc in range(8):
                    P.tr(pt[:, 0, kc * 128:(kc + 1) * 128], xb[:, kc * 128:(kc + 1) * 128], self.c_identb[:])
                tmp = self.n_t[blk % 2]
                P.tt("dve", tmp[:], pt[:, 0, :].re("p (k t) -> p k t", k=8),
                     self.s1T[:, :, w:w + 1].bc([128, 8, 128]), ALU.mult)
                P.tt("pool", hb[:, :, bi * 128:(bi + 1) * 128], tmp[:],
                     self.modT[:, 0:8, w:w + 1].bc([128, 8, 128]), ALU.add)
            P.dma("sp", self.hT.v(("h", ti), (slice(None), slice(None), slice(t0, t0 + tn))), hb[:, :, 0:tn])

    def phase_out(self, l):
        P = self.P
        last = (l == DEPTH - 1)
        if not hasattr(self, "o_w"):
            self.o_w = P.sb([128, 8, DM], BF16, "o_w")
            self.o_m = [P.sb([128, 8, 128], BF16, f"o_m{i}") for i in range(2)]
            self.o_x = [P.sb([128, DM], F32, f"o_x{i}") for i in range(2)]
            self.o_t = [P.sb([128, DM], F32, f"o_t{i}") for i in range(2)]
            self.o_ss = [P.sb([128, 4], F32, f"o_ss{i}") for i in range(2)]
        P.dma("sp", self.o_w[:], self.wbo(l, 0, DM))
        src = self.xin if l == 0 else self.xres
        for blk in range(NBLK):
            if last and blk < 2:
                continue
            w = 1 if blk < 2 else 0
            m, xt, tt_, ss = self.o_m[blk % 2], self.o_x[blk % 2], self.o_t[blk % 2], self.o_ss[blk % 2]
            tsl = slice(blk * 128, blk * 128 + 128)
            P.dma("sp", m[:], self.mix.v(("m", blk), fn=lambda ap, tsl=tsl: ap.rearrange("(kc p) t -> p kc t", p=128)[:, :, tsl]))
            P.dma("act", xt[:], src.v(("x", blk), (tsl, slice(None))))
            for hf in range(2):
                po = self.ps(4 + 2 * (blk % 2) + hf)
                for kc in range(8):
                    P.mm(po[:, 0, :], m[:, kc, :], self.o_w[:, kc, hf * 512:(hf + 1) * 512], start=(kc == 0), stop=(kc == 7))
                P.tt("dve", tt_[:, hf * 512:(hf + 1) * 512], po[:, 0, :], self.gate_bc[:, w, hf * 512:(hf + 1) * 512], ALU.mult)
            P.tt("pool", tt_[:], tt_[:], xt[:], ALU.add)
            if not last:
                P.dma("sp", self.xres.v(("x", blk), (tsl, slice(None))), tt_[:])
            else:
                P.act(xt[:], tt_[:], AF.Square, accum_out=ss[:, 0:1])
                P.act(ss[:, 1:2], ss[:, 0:1], AF.Sqrt, bias=1e-6, scale=1.0 / DM)
                P.recip(ss[:, 2:3], ss[:, 1:2])
                o = PR["fnw"][0]
                P.stt("dve", xt[:], tt_[:], ss[:, 2:3], self.prt[:, o:o + DM], ALU.mult, ALU.mult)
                P.dma("sp", self.y.v(("y", blk), (slice(blk * 128 - CTX, blk * 128 - CTX + 128), slice(None))), xt[:])

    def load_h(self, tile, ti, q="sp"):
        t0, tn = TILES[ti]
        self.P.dma(q, tile[:, :, 0:tn], self.hT.v(("h", ti), (slice(None), slice(None), slice(t0, t0 + tn))))

    def phase_att(self, l):
        P = self.P
        ctx_out = (l < DEPTH - 1)
        import os
        sm = int(os.environ.get("MK_SET", "63"))
        wq = P.sb([128, 8, 384], BF16, "a_wq")
        wg = P.sb([128, 8, 384], BF16, "a_wg")
        wk = P.sb([128, 8, 128], BF16, "a_wk")
        wv = P.sb([128, 8, 128], BF16, "a_wv")
        if sm & 1:
            for j in range(3):
                for hh in range(2):
                    h = j + 3 * hh
                    P.dma("sp", wq[:, :, j * 128 + hh * 64: j * 128 + hh * 64 + 64], self.wb(l, C_Q + h * 64, 64))
                    P.dma("act", wg[:, :, j * 128 + hh * 64: j * 128 + hh * 64 + 64], self.wb(l, C_G + h * 64, 64))
            P.dma("sp", wk[:], self.wb(l, C_K, 128))
            P.dma("act", wv[:], self.wb(l, C_V, 128))
        cos = P.sb([128, TLAT], F32, "a_cos")
        sin = P.sb([128, TLAT], F32, "a_sin")
        if sm & 2:
            self.load_c(cos, "cos", "sp")
            self.load_c(sin, "sin", "act")
        pf = P.sb([128, 128], F32, "a_pf")
        permb = P.sb([128, 128], BF16, "a_permb")
        if sm & 4:
            self.load_c(pf, "perm")
            P.copy("dve", permb[:], pf[:])
        mband = P.sb([128, 3, 384], F32, "a_mband")
        o, w_ = CS["mband"]
        if sm & 8:
            P.dma("sp", mband[:].re("p a b -> p (a b)"), self.cst.v("c", (slice(None), slice(o, o + w_))))
        sink8 = P.sb([128, 6], F32, "a_sink8")
        o = PP["sink"][0]
        if sm & 16:
            P.ts("dve", sink8[:], self.ppt[:, o:o + 6], 8.0, None, ALU.mult)
        KB = P.sb([128, 4608], BF16, "a_KB")
        Vtm = P.sb([128, 36, 128], BF16, "a_V")
        if sm & 32:
            P.memset("pool", KB[:, 256:384], 0.0)
            P.memset("pool", KB[:, 4480:4608], 0.0)
            P.memset("pool", Vtm[:, 2, :], 0.0)
            P.memset("pool", Vtm[:, 35, :], 0.0)
        hb = [P.sb([128, 8, 512], BF16, f"a_h{i}") for i in range(2)]
        kf = P.sb([128, 512], F32, "a_kf")
        t1 = [P.sb([128, 512], F32, f"a_t1{i}") for i in range(2)]
        t2 = [P.sb([128, 512], F32, f"a_t2{i}") for i in range(2)]

        def kcol(tok):
            return tok if tok < CTX else tok + 128

        def proj(dst, wt, c0, cw, hbt, tn, lat_t0, bank, func=None):
            pp_ = self.ps(bank)
            for kc in range(8):
                P.mm(pp_[:, 0, 0:tn], wt[:, kc, c0:c0 + cw], hbt[:, kc, 0:tn], start=(kc == 0), stop=(kc == 7))
            if func is not None:
                P.act(dst, pp_[:, 0, 0:tn], func)
                return
            if lat_t0 is None:
                P.copy("act", dst, pp_[:, 0, 0:tn])
                return
            P.copy("act", kf[:, 0:tn], pp_[:, 0, 0:tn])
            pr_ = self.ps(2)
            P.mm(pr_[:, 0, 0:tn], pf[:], kf[:, 0:tn])
            a, b = t1[bank % 2], t2[bank % 2]
            P.tt("pool", a[:, 0:tn], kf[:, 0:tn], cos[:, lat_t0:lat_t0 + tn], ALU.mult)
            P.tt("dve", b[:, 0:tn], pr_[:, 0, 0:tn], sin[:, lat_t0:lat_t0 + tn], ALU.mult)
            P.tt("pool", dst, a[:, 0:tn], b[:, 0:tn], ALU.add)

        stage = int(os.environ.get("MK_ATT", "9"))
        if stage <= 0:
            return
        for ti, (t0, tn) in enumerate(TILES):
            h = hb[ti % 2]
            self.load_h(h, ti, "sp" if ti % 2 == 0 else "act")
            kc0 = kcol(t0)
            ma = int(os.environ.get("MK_A", "3"))
            if ti >= int(os.environ.get("MK_AT", "9")):
                break
            if ma & 1:
                proj(KB[:, kc0:kc0 + tn], wk, 0, 128, h, tn, None if ti == 0 else t0 - CTX, ti % 2)
            for bi in range(tn // 128):
                if not (ma & 2):
                    break
                vb = (t0 // 128 + bi)
                vb = vb if vb < 2 else vb + 1
                pv = self.ps(3 + bi % 2)
                for kc in range(8):
                    P.mm(pv[:, 0, 0:128], h[:, kc, bi * 128:(bi + 1) * 128], wv[:, kc, :], start=(kc == 0), stop=(kc == 7))
                P.copy("dve" if bi % 2 == 0 else "act", Vtm[:, vb, :], pv[:, 0, 0:128])

        import os
        stage = int(os.environ.get("MK_ATT", "9"))
        if stage <= 1:
            return
        qT = [P.sb([128, 3, 512], BF16, f"a_q{i}") for i in range(2)]
        gT = [P.sb([128, 3, 512], BF16, f"a_g{i}") for i in range(2)]
        om = [P.sb([128, 3, 512], BF16, f"a_om{i}") for i in range(2)]
        sc = [P.sb([128, 3, 641], F32, f"a_sc{i}") for i in range(2)]
        pe = [P.sb([128, 3, 641], F32, f"a_pe{i}") for i in range(2)]
        for kg in range(2):
            P.copy("dve", sc[kg][:, :, 640:641], sink8[:, kg * 3:kg * 3 + 3].m(lambda a: a.unsqueeze(2)))
        pn = [P.sb([128, 3, 640], BF16, f"a_pn{i}") for i in range(2)]
        pTs = [P.sb([128, 5, 128], BF16, f"a_pT{i}") for i in range(3)]
        st = [P.sb([128, 8, 3], F32, f"a_st{i}") for i in range(2)]
        it = 0
        npt = 0
        for ti, (t0, tn) in enumerate(TILES):
            if ti == 0 and not ctx_out:
                continue
            h = hb[ti % 2]
            self.load_h(h, ti, "sp" if ti % 2 == 0 else "act")
            q_, g_, o_ = qT[ti % 2], gT[ti % 2], om[ti % 2]
            for j in range(3):
                proj(q_[:, j, 0:tn], wq, j * 128, 128, h, tn, None if ti == 0 else t0 - CTX, j % 2)
                proj(g_[:, j, 0:tn], wg, j * 128, 128, h, tn, None, (j + 1) % 2, func=AF.Silu)
            if stage <= 2:
                break
            for bi in range(tn // 128):
                if stage <= 5 and (bi > 0 or ti > 1):
                    break
                qs = slice(bi * 128, bi * 128 + 128)
                isctx = (ti == 0)
                n = (t0 - CTX) // 128 + bi if not isctx else None
                lo = 384 if isctx else 0
                po = self.ps(2)
                KG = [(kg, slice(kg * 64, kg * 64 + 64), sc[kg], pe[kg], pn[kg], st[kg]) for kg in range(2)]
                chunks = ([] if isctx else [0, 1, 2]) + [3, 4]
                for kg, ps_, s_, e_, n_, stt_ in KG:
                    for j in range(3):
                        pb = self.ps(3 + 2 * kg)
                        pc = self.ps(4 + 2 * kg)
                        if not isctx:
                            kb0 = 256 + n * 128
                            P.mm(pb[:, 0, 0:384], q_[ps_, j, qs], KB[ps_, kb0:kb0 + 384])
                            var = 0 if n == 0 else (2 if n == 31 else 1)
                            P.tt("dve", s_[:, j, 0:384], pb[:, 0, 0:384], mband[:, var, :], ALU.add)
                        P.mm(pc[:, 0, 0:256], q_[ps_, j, qs], KB[ps_, 0:256])
                        P.copy("act", s_[:, j, 384:640], pc[:, 0, 0:256])
                for kg, ps_, s_, e_, n_, stt_ in KG:
                    P.reduce("dve", stt_[:, 0, :], s_[:, :, lo:641], ALU.max)
                    P.ts("dve", stt_[:, 2, :], stt_[:, 0, :], -0.125, None, ALU.mult)
                for kg, ps_, s_, e_, n_, stt_ in KG:
                    for j in range(3):
                        P.act(e_[:, j, lo:641], s_[:, j, lo:641], AF.Exp, bias=stt_[:, 2, j:j + 1], scale=0.125,
                              accum_out=stt_[:, 3, j:j + 1])
                for kg, ps_, s_, e_, n_, stt_ in KG:
                    P.recip(stt_[:, 7, :], stt_[:, 3, :])
                    P.tt("dve" if kg == 0 else "pool", n_[:, :, lo:640], e_[:, :, lo:640],
                         stt_[:, 7, :].m(lambda a: a.unsqueeze(2)).bc([128, 3, 640 - lo]), ALU.mult)
                for j in range(3):
                    for kg, ps_, s_, e_, n_, stt_ in KG:
                        pt = self.ps([7, 0, 1][npt % 3], 1, BF16)
                        for c in chunks:
                            P.tr(pt[:, 0, c * 128:(c + 1) * 128], n_[:, j, c * 128:(c + 1) * 128], self.c_identb[:])
                        pts = pTs[npt % 3]
                        npt += 1
                        c0 = chunks[0]
                        P.copy("act" if npt % 2 == 0 else "dve", pts[:, c0:5, :], pt[:, 0, c0 * 128:640].re("p (c t) -> p c t", t=128))
                        for ci, c in enumerate(chunks):
                            vb = (2 + n + c) if c < 3 else (c - 3)
                            P.mm(po[ps_, 0, j * 128:(j + 1) * 128], Vtm[:, vb, kg * 64:kg * 64 + 64], pts[:, c, :],
                                 start=(ci == 0), stop=(ci == len(chunks) - 1))
                P.tt("dve", o_[:, :, qs], po[:, 0, 0:384].re("p (j t) -> p j t", j=3), g_[:, :, qs], ALU.mult)
            for j in range(3):
                for hh in range(2):
                    hd = j + 3 * hh
                    P.dma("sp" if hh == 0 else "act",
                          self.mix.v(("ma", ti, hd), (slice(hd * 64, hd * 64 + 64), slice(t0, t0 + tn))),
                          o_[hh * 64:hh * 64 + 64, j, 0:tn])

    def load_h_halo(self, tile, ti, q="sp"):
        P = self.P
        t0, tn = TILES[ti]
        lo, hi = t0 - 1, t0 + tn + 1
        if ti <= 1:
            P.memset("pool", tile[:, :, 0:1], 0.0)
            lo = t0
        if ti == 0 or ti == len(TILES) - 1:
            P.memset("pool", tile[:, :, tn + 1:tn + 2], 0.0)
            hi = t0 + tn
        P.dma(q, tile[:, :, lo - t0 + 1:hi - t0 + 1], self.hT.v(("h", ti), (slice(None), slice(None), slice(lo, hi))))

    def scaled_w(self, l, c0, ncol, rows, name, roff=0):
        P = self.P
        rows = [r[:, roff:roff + ncol] for r in rows]
        w3 = P.sb([128, 3, 8, ncol], BF16, name)
        mk_ = P.mark()
        half = ncol // 2
        stg = [P.sb([128, 8, half], F32, f"{name}_st{i}") for i in range(2)]
        for hf in range(2):
            st = stg[hf]
            P.dma("sp" if hf == 0 else "act", st[:],
                  self.w_in.v(("w", l), fn=lambda ap, hf=hf: ap[l].rearrange("(kc p) n -> p kc n", p=128)[:, :, c0 + hf * half:c0 + (hf + 1) * half]))
            for tap in range(3):
                P.tt("dve" if tap != 1 else "pool", w3[:, tap, :, hf * half:(hf + 1) * half], st[:],
                     rows[tap][:, hf * half:(hf + 1) * half].m(lambda a: a.unsqueeze(1)).bc([128, 8, half]), ALU.mult)
        P.release(mk_)
        return w3

    def phase_ssd(self, l):
        P = self.P
        ctx_out = (l < DEPTH - 1)
        xsT = P.sb([128, 3, S], F32, "s_xsT")
        BT = P.sb([128, 2, S], BF16, "s_BT")
        CT = P.sb([128, 2, S], BF16, "s_CT")
        dt_tm = P.sb([128, NBLK, 12], F32, "s_dt")
        dtA_tm = P.sb([128, NBLK, 12], F32, "s_dtA")
        aneg = P.sb([128, 12], F32, "s_aneg")
        o = PR["alog"][0]
        P.act(aneg[:], self.prt[:, o:o + 12], AF.Exp)
        P.ts("dve", aneg[:], aneg[:], -1.0, None, ALU.mult)
        mk1 = P.mark()
        o = PR2["convr"][0]
        crow = P.sb([128, 3 * 896], F32, "s_crow")
        P.dma("sp", crow[:], self.pr2.v(l, fn=lambda a: a[l][:, o:o + 3 * 896]))
        rows = [crow[:, k * 896:(k + 1) * 896] for k in range(3)]
        wx3 = self.scaled_w(l, C_XBC, 896, rows, "s_wx3")
        wdt = P.sb([128, 8, 12], BF16, "s_wdt")
        P.dma("sp", wdt[:], self.wb(l, C_DT, 12))
        hh = [P.sb([128, 8, 514], BF16, f"s_hh{i}") for i in range(2)]
        dtmp = [P.sb([128, 12], F32, f"s_dtmp{i}") for i in range(2)]
        ocb = PP["convb"][0]
        nb = 0
        for ti, (t0, tn) in enumerate(TILES):
            h = hh[ti % 2]
            self.load_h_halo(h, ti, "sp" if ti % 2 == 0 else "act")
            for c in range(7):
                pb = self.ps(c % 2)
                n = 0
                for tap in range(3):
                    for kc in range(8):
                        P.mm(pb[:, 0, 0:tn], wx3[:, tap, kc, c * 128:(c + 1) * 128], h[:, kc, tap:tap + tn],
                             start=(n == 0), stop=(n == 23))
                        n += 1
                if c < 3:
                    dst = xsT[:, c, t0:t0 + tn]
                elif c < 5:
                    dst = BT[:, c - 3, t0:t0 + tn]
                else:
                    dst = CT[:, c - 5, t0:t0 + tn]
                P.act(dst, pb[:, 0, 0:tn], AF.Silu, bias=self.ppt[:, ocb + c:ocb + c + 1])
            for bi in range(tn // 128):
                blk = t0 // 128 + bi
                pd = self.ps(2 + blk % 2)
                for kc in range(8):
                    P.mm(pd[:, 0, 0:12], h[:, kc, 1 + bi * 128:1 + (bi + 1) * 128], wdt[:, kc, :], start=(kc == 0), stop=(kc == 7))
                d_ = dtmp[blk % 2]
                o = PR["dtb"][0]
                P.tt("dve", d_[:], pd[:, 0, 0:12], self.prt[:, o:o + 12], ALU.add)
                P.act(d_[:], d_[:], AF.Exp)
                P.act(dt_tm[:, blk, :], d_[:], AF.Ln, bias=1.0)
                P.tt("dve", dtA_tm[:, blk, :], dt_tm[:, blk, :], aneg[:], ALU.mult)
        P.release(mk1)
        if int(os.environ.get("MK_SSD", "9")) <= 1:
            self.dbg_xsT = xsT
            return
        wz = P.sb([128, 8, 384], BF16, "s_wz")
        P.dma("sp", wz[:], self.wb(l, C_Z, 384))
        cf = {}
        for nme in ("snf", "snb", "trif", "trib", "ones", "hlo", "hhi"):
            cf[nme] = P.sb([128, 128], F32, "s_c_" + nme)
            self.load_c(cf[nme], nme, "act")
        St = P.sb([128, 6, 64], F32, "s_S")
        NB2 = 2
        xs_sb = [P.sb([128, 384], F32, f"s_xs{i}") for i in range(NB2)]
        Bt_sb = [P.sb([128, 2, 128], BF16, f"s_Bt{i}") for i in range(NB2)]
        Gs = [P.sb([128, 2, 128], F32, f"s_G{i}") for i in range(NB2)]
        cumc = [P.sb([128, 6], F32, f"s_cc{i}") for i in range(NB2)]
        dbc = [P.sb([128, 6, 128], F32, f"s_dbc{i}") for i in range(NB2)]
        Dm = [P.sb([128, 6, 128], F32, f"s_D{i}") for i in range(NB2)]
        Em = [P.sb([128, 6, 128], F32, f"s_E{i}") for i in range(NB2)]
        scT = [P.sb([128, 6, 128], BF16, f"s_sc{i}") for i in range(NB2)]
        Er = [P.sb([128, 6, 128], F32, f"s_Er{i}") for i in range(NB2)]
        Cd = [P.sb([128, 6, 128], F32, f"s_Cd{i}") for i in range(NB2)]
        xdt = [P.sb([128, 6, 64], BF16, f"s_xdt{i}") for i in range(NB2)]
        xdtt = [P.sb([128, 6, 64], BF16, f"s_xdtt{i}") for i in range(NB2)]
        ysb = [P.sb([128, 3, 128], F32, f"s_y{i}") for i in range(NB2)]
        yfl = [P.sb([128, 3, 128], F32, f"s_yf{i}") for i in range(NB2)]
        hz = [P.sb([128, 8, 128], BF16, f"s_hz{i}") for i in range(NB2)]
        zs = [P.sb([128, 3, 128], F32, f"s_z{i}") for i in range(NB2)]
        sq = [P.sb([128, 3, 128], F32, f"s_sq{i}") for i in range(NB2)]
        rs = [P.sb([128, 2, 128], F32, f"s_rs{i}") for i in range(NB2)]
        mo = [P.sb([128, 3, 128], BF16, f"s_mo{i}") for i in range(NB2)]
        osd, onw = PP["ssmd"][0], PP["ssmnw"][0]
        it = 0
        for d in range(2):
            order = list(range(NBLK)) if d == 0 else [1, 0] + list(range(NBLK - 1, 1, -1))
            tri = cf["trif"] if d == 0 else cf["trib"]
            sneg = cf["snf"] if d == 0 else cf["snb"]
            last = 127 if d == 0 else 0
            P.memset("dve", St[:], 0.0)
            pending = None
            for c in order:
                if d == 1 and c < 2 and not ctx_out:
                    pass
                k = it % NB2
                it += 1
                cs_ = slice(c * 128, (c + 1) * 128)
                pxs = self.ps(0)
                for j in range(3):
                    P.tr(pxs[:, 0, j * 128:(j + 1) * 128], xsT[:, j, cs_], self.c_ident[:])
                P.copy("act", xs_sb[k][:], pxs[:, 0, 0:384])
                pbt = self.ps(1, 1, BF16)
                for g in range(2):
                    P.tr(pbt[:, 0, g * 128:(g + 1) * 128], BT[:, g, cs_], self.c_identb[:])
                P.copy("dve", Bt_sb[k][:].re("p g n -> p (g n)"), pbt[:, 0, 0:256])
                pg = self.ps(2)
                for g in range(2):
                    P.mm(pg[:, 0, g * 128:(g + 1) * 128], BT[:, g, cs_], CT[:, g, cs_])
                P.copy("act", Gs[k][:].re("p g n -> p (g n)"), pg[:, 0, 0:256])
                dA = dtA_tm[:, c, d * 6:(d + 1) * 6]
                P.mm(pg[:, 0, 256:262], tri[:], dA)
                P.copy("dve", cumc[k][:], pg[:, 0, 256:262])
                P.copy("pool", dbc[k][:], dA.m(lambda a: a.unsqueeze(2)).bc([128, 6, 128]))
                pcr = self.ps(3, 2)
                for h in range(6):
                    P.mm(pcr[:, h // 4, (h % 4) * 128:(h % 4 + 1) * 128], dbc[k][:, h, :], tri[:])
                pcr_v = pcr.re("p b n -> p (b n)")[:, 0:768].re("p (h n) -> p h n", h=6)
                P.tt("dve", Dm[k][:], pcr_v, cumc[k][:].m(lambda a: a.unsqueeze(2)).bc([128, 6, 128]), ALU.subtract)
                P.act(Er[k][:], pcr_v, AF.Exp)
                P.tt("pool", Dm[k][:], Dm[k][:], sneg[:].m(lambda a: a.unsqueeze(1)).bc([128, 6, 128]), ALU.add)
                P.act(Em[k][:], Dm[k][:], AF.Exp)
                P.tt("pool", scT[k][:].re("p (g h) n -> p g h n", g=2), Em[k][:].re("p (g h) n -> p g h n", g=2),
                     Gs[k][:].m(lambda a: a.unsqueeze(2)).bc([128, 2, 3, 128]), ALU.mult)
                P.tt("dve", Cd[k][:].re("p (g h) n -> p g h n", g=2), Er[k][:].re("p (g h) n -> p g h n", g=2),
                     CT[:, :, cs_].m(lambda a: a.unsqueeze(2)).bc([128, 2, 3, 128]), ALU.mult)
                dtc = dt_tm[:, c, d * 6:(d + 1) * 6]
                P.tt("dve", xdt[k][:], xs_sb[k][:].re("p (h q) -> p h q", h=6),
                     dtc.m(lambda a: a.unsqueeze(2)).bc([128, 6, 64]), ALU.mult)
                P.tt("pool", xdtt[k][:], xdt[k][:], Em[k][:, :, last:last + 1].bc([128, 6, 64]), ALU.mult)
                def back(c=c, k=k, cs_=cs_, d=d, last=last):
                    py = self.ps(5)
                    for h in range(6):
                        o_ = py[(h % 2) * 64:(h % 2) * 64 + 64, 0, (h // 2) * 128:(h // 2 + 1) * 128]
                        P.mm(o_, xdt[k][:, h, :], scT[k][:, h, :], start=True, stop=False)
                        P.mm(o_, St[:, h, :], Cd[k][:, h, :], start=False, stop=True)
                    pcs = self.ps(6)
                    for h in range(6):
                        P.mm(pcs[:, 0, h * 64:(h + 1) * 64], Bt_sb[k][:, h // 3, :], xdtt[k][:, h, :])
                    P.tt("dve", St[:], St[:], Er[k][:, :, last:last + 1].bc([128, 6, 64]), ALU.mult)
                    P.tt("dve", St[:], St[:], pcs[:, 0, 0:384].re("p (h q) -> p h q", h=6), ALU.add)
                    if d == 0:
                        P.copy("act", ysb[k][:].re("p j n -> p (j n)"), py[:, 0, 0:384])
                        P.dma("sp", self.yf.v(("yf", c), fn=lambda ap, cs_=cs_: ap.rearrange("(j p) t -> p j t", p=128)[:, :, cs_]), ysb[k][:])
                        return
                    if c < 2 and not ctx_out:
                        return
                    P.dma("act", yfl[k][:], self.yf.v(("yf", c), fn=lambda ap, cs_=cs_: ap.rearrange("(j p) t -> p j t", p=128)[:, :, cs_]))
                    P.dma("sp", hz[k][:], self.hT.v(("h", "z"), (slice(None), slice(None), cs_)))
                    pz = self.ps(7)
                    for j in range(3):
                        for kc in range(8):
                            P.mm(pz[:, 0, j * 128:(j + 1) * 128], wz[:, kc, j * 128:(j + 1) * 128], hz[k][:, kc, :],
                                 start=(kc == 0), stop=(kc == 7))
                    P.act(zs[k][:].re("p j n -> p (j n)"), pz[:, 0, 0:384], AF.Silu)
                    y_ = ysb[k]
                    P.tt("dve", y_[:].re("p j n -> p (j n)"), py[:, 0, 0:384], yfl[k][:].re("p j n -> p (j n)"), ALU.add)
                    P.tt("pool", yfl[k][:], xsT[:, :, cs_], self.ppt[:, osd:osd + 3].m(lambda a: a.unsqueeze(2)).bc([128, 3, 128]), ALU.mult)
                    P.tt("pool", y_[:], y_[:], yfl[k][:], ALU.add)
                    P.tt("dve", y_[:], y_[:], zs[k][:], ALU.mult)
                    P.act(sq[k][:], y_[:], AF.Square)
                    pgs = self.ps(7)
                    P.mm(pgs[:, 0, 384:512].m(lambda a: a), cf["ones"][:], sq[k][:, 0, :], start=True, stop=False)
                    P.mm(pgs[:, 0, 384:512], cf["hlo"][:], sq[k][:, 1, :], start=False, stop=True)
                    pgs2 = self.ps(6)
                    P.mm(pgs2[:, 0, 384:512], cf["hhi"][:], sq[k][:, 1, :], start=True, stop=False)
                    P.mm(pgs2[:, 0, 384:512], cf["ones"][:], sq[k][:, 2, :], start=False, stop=True)
                    P.act(rs[k][:, 0, :], pgs[:, 0, 384:512], AF.Sqrt, bias=1e-5, scale=1.0 / 192)
                    P.act(rs[k][:, 1, :], pgs2[:, 0, 384:512], AF.Sqrt, bias=1e-5, scale=1.0 / 192)
                    P.recip(rs[k][:], rs[k][:])
                    m_ = mo[k]
                    P.stt("dve", m_[:, 0, :], y_[:, 0, :], self.ppt[:, onw:onw + 1], rs[k][:, 0, :], ALU.mult, ALU.mult)
                    P.stt("dve", m_[0:64, 1, :], y_[0:64, 1, :], self.ppt[0:64, onw + 1:onw + 2], rs[k][0:64, 0, :], ALU.mult, ALU.mult)
                    P.stt("dve", m_[64:128, 1, :], y_[64:128, 1, :], self.ppt[64:128, onw + 1:onw + 2], rs[k][64:128, 1, :], ALU.mult, ALU.mult)
                    P.stt("dve", m_[:, 2, :], y_[:, 2, :], self.ppt[:, onw + 2:onw + 3], rs[k][:, 1, :], ALU.mult, ALU.mult)
                    P.dma("sp", self.mix.v(("ms", c), fn=lambda ap, cs_=cs_: ap[640:1024].rearrange("(j p) t -> p j t", p=128)[:, :, cs_]), m_[:])
                if pending is not None:
                    pending()
                pending = back
            if pending is not None:
                pending()

    def phase_rwkv(self, l):
        P = self.P
        ctx_out = (l < DEPTH - 1)
        LWC = -0.6065306597126334
        cf = {}
        for nme in ("blk", "lowf", "lowb"):
            cf[nme] = P.sb([128, 128], F32, "r_c_" + nme)
            self.load_c(cf[nme], nme, "act")
        m4 = {nme: P.sb([128, 2, 256], F32, "r_m4_" + nme) for nme in ("rmf", "rmb")}
        mk0 = P.mark()
        for nme in ("rmf", "rmb"):
            t_ = P.sb([128, 256], F32, "r_c_" + nme)
            self.load_c(t_, nme, "act")
            P.copy("dve", m4[nme][:], t_[:].m(lambda a: a.unsqueeze(1)).bc([128, 2, 256]))
        P.release(mk0)
        ones64 = P.sb([128, 64], F32, "r_ones")
        P.memset("pool", ones64[:], 1.0)
        omka = P.sb([128, 4], F32, "r_omka")
        oka = PP["ka"][0]
        P.ts("dve", omka[:], self.ppt[:, oka:oka + 4], -1.0, 1.0, ALU.mult, ALU.add)
        wupf = P.sb([128, 256], F32, "r_wupf")
        aupf = P.sb([128, 256], F32, "r_aupf")
        wupb = P.sb([128, 256], BF16, "r_wupb")
        aupb = P.sb([128, 256], BF16, "r_aupb")
        for d in range(2):
            P.dma("sp", wupf[d * 64:(d + 1) * 64, :], self.wup.v(("u", l, d), fn=lambda ap, d=d: ap[l][d]))
            P.dma("act", aupf[d * 64:(d + 1) * 64, :], self.aup.v(("u", l, d), fn=lambda ap, d=d: ap[l][d]))
        P.copy("dve", wupb[:], wupf[:])
        P.copy("pool", aupb[:], aupf[:])
        wdT = P.sb([128, S], BF16, "r_wdT")
        adT = P.sb([128, S], BF16, "r_adT")
        mk1 = P.mark()
        rows = self.shift_rows(l)
        hh = [P.sb([128, 8, 514], BF16, f"r_hh{i}") for i in range(2)]
        wl3 = self.scaled_w(l, C_RW + 768, 256, rows, "r_wl3", roff=768)
        for ti, (t0, tn) in enumerate(TILES):
            h = hh[ti % 2]
            self.load_h_halo(h, ti, "sp" if ti % 2 == 0 else "act")
            for c in range(2):
                pb = self.ps(c)
                n = 0
                for tap in range(3):
                    for kc in range(8):
                        P.mm(pb[:, 0, 0:tn], wl3[:, tap, kc, c * 128:(c + 1) * 128], h[:, kc, tap:tap + tn],
                             start=(n == 0), stop=(n == 23))
                        n += 1
                if c == 0:
                    P.act(wdT[:, t0:t0 + tn], pb[:, 0, 0:tn], AF.Tanh)
                else:
                    P.copy("act", adT[:, t0:t0 + tn], pb[:, 0, 0:tn])
        P.release(mk1)
        RW = int(os.environ.get("MK_RW", "99"))
        if RW <= 1:
            return
        mk_hp = P.mark()
        for hp in range(2):
            rT = P.sb([128, S], F32, "r_rT")
            kT = P.sb([128, S], F32, "r_kT")
            vT = P.sb([128, S], F32, "r_vT")
            ysum = P.sb([128, S], F32, "r_ysum")
            mk2 = P.mark()
            rows = self.shift_rows(l)
            hh = [P.sb([128, 8, 514], BF16, f"r_hh{i}") for i in range(2)]
            w3 = [self.scaled_w(l, C_RW + j * 256 + hp * 128, 128, rows, f"r_w3{j}", roff=j * 256 + hp * 128) for j in range(3)]
            for ti, (t0, tn) in enumerate(TILES):
                h = hh[ti % 2]
                self.load_h_halo(h, ti, "sp" if ti % 2 == 0 else "act")
                for j, dst in enumerate((rT, kT, vT)):
                    pb = self.ps(j % 2)
                    n = 0
                    for tap in range(3):
                        for kc in range(8):
                            P.mm(pb[:, 0, 0:tn], w3[j][:, tap, kc, :], h[:, kc, tap:tap + tn], start=(n == 0), stop=(n == 23))
                            n += 1
                    P.copy("act", dst[:, t0:t0 + tn], pb[:, 0, 0:tn])
            P.release(mk2)
            if RW <= 2:
                return
            mk3 = P.mark()
            self.rwkv_scan(l, hp, rT, kT, vT, wdT, adT, ysum, wupb, aupb, cf, m4, ones64, omka, LWC)
            P.release(mk3)
            self.rwkv_finish(l, hp, rT, kT, vT, ysum, cf, ctx_out)
            P.release(mk_hp)

    def shift_rows(self, l):
        P = self.P
        o = PR2["shr"][0]
        sr = P.sb([128, 3 * 1024], F32, "r_shr")
        P.dma("sp", sr[:], self.pr2.v(l, fn=lambda a: a[l][:, o:o + 3 * 1024]))
        r0, r2, r1 = sr[:, 0:1024], sr[:, 1024:2048], sr[:, 2048:3072]
        P.tt("dve", r1, r0, r2, ALU.add)
        P.ts("dve", r1, r1, -1.0, 1.0, ALU.mult, ALU.add)
        return [r0, r1, r2]

    def rwkv_scan(self, l, hp, rT, kT, vT, wdT, adT, ysum, wupb, aupb, cf, m4, ones64, omka, LWC):
        P = self.P
        NB = 2
        f32t = lambda n, w=512: [P.sb([128, w], F32, f"r_{n}")] * NB
        sg, aa, cum = f32t("sg"), f32t("aa"), f32t("cum")
        lw = sg
        cumx = [P.sb([128, 8], F32, "r_tot")] * NB
        e1, e2, e3, e4 = f32t("e1"), f32t("e2"), f32t("e3"), f32t("e4")
        kkr, sqk, nrm, kd, bq = f32t("kkr"), f32t("sqk"), f32t("nrm"), f32t("kd"), f32t("bq")
        kk, tmpa = kkr, nrm
        WC = [P.sb([128, 8], F32, f"r_WC{i}") for i in range(NB)]
        arb = [P.sb([128, 4, 2, 128], F32, "r_arb")] * NB
        kb = [P.sb([128, 512], F32, "r_kb")] * NB
        bb = [P.sb([128, 512], F32, "r_bb")] * NB
        kt = [P.sb([128, 512], F32, "r_kt")] * NB
        btl = [P.sb([128, 512], F32, "r_btl")] * NB
        G4 = 4
        TM = [P.sb([128, 4, 128], F32, f"r_TM{i}") for i in range(G4)]
        AT = [P.sb([128, 2, 4, 128], F32, f"r_AT{i}") for i in range(G4)]
        X1 = [P.sb([128, 2, 128], F32, f"r_X1{i}") for i in range(G4)]
        XX = [P.sb([128, 2, 2, 128], F32, f"r_XX{i}") for i in range(G4)]
        Qf = [P.sb([128, 2, 128], F32, f"r_Qf{i}") for i in range(G4)]
        W1 = [P.sb([128, 128], F32, f"r_W1{i}") for i in range(G4)]
        AU = [P.sb([128, 256], F32, f"r_AU{i}") for i in range(G4)]
        Y0T = [P.sb([128, 128], F32, f"r_Y0T{i}") for i in range(G4)]
        RhT = [P.sb([128, 128], F32, f"r_RhT{i}") for i in range(G4)]
        MTt = [P.sb([128, 128], F32, "r_MTt")] * 2
        MT = [[P.sb([128, 128], F32, f"r_MT{i}_{c}") for c in range(2)] for i in range(G4)]
        Ns = [[P.sb([128, 64], F32, f"r_Ns{i}_{c}") for c in range(2)] for i in range(G4)]
        ytmp = [P.sb([128, 64], F32, f"r_yt{i}") for i in range(2)]
        Sseq = P.sb([128, 4, 64], F32, "r_Sseq")
        ow0, oa0, okk, oka = PP["w0"][0], PP["a0"][0], PP["kk"][0], PP["ka"][0]
        if os.environ.get("MK_MEM"):
            print("rwkv_scan arena used", P._aoff, "of", P.ARENA)
        nblk_done = 0
        for d in range(2):
            if d == 1 and hp == 0 and os.environ.get("MK_RWDBG"):
                if not hasattr(self, "dbgy"):
                    self.dbgy = P.dram("dbgy", [128, S], F32, kind="ExternalOutput")
                P.dma("sp", self.dbgy.v(0), ysum[:])
            col = d * 2 + hp
            mask4 = m4["rmf"] if d == 0 else m4["rmb"]
            low = cf["lowf"] if d == 0 else cf["lowb"]
            P.memset("dve", Sseq[:, 0, :], 0.0)
            si = 0
            tiles = list(range(len(TILES))) if d == 0 else [0] + list(range(len(TILES) - 1, 0, -1))
            def prepA(tix):
                ti = tiles[tix]
                t0, tn = TILES[ti]
                nch = tn // 64
                nbk = tn // 128
                k_ = tix % NB
                ts_ = slice(t0, t0 + tn)
                ds_ = slice(d * 64, (d + 1) * 64)
                pxw, pxa = self.ps(0), self.ps(1)
                P.mm(pxw[:, 0, 0:tn], wupb[ds_, hp * 128:(hp + 1) * 128], wdT[ds_, ts_])
                P.act(sg[k_][:, 0:tn], pxw[:, 0, 0:tn], AF.Sigmoid, bias=self.ppt[:, ow0 + col:ow0 + col + 1])
                P.mm(pxa[:, 0, 0:tn], aupb[ds_, hp * 128:(hp + 1) * 128], adT[ds_, ts_])
                P.act(aa[k_][:, 0:tn], pxa[:, 0, 0:tn], AF.Sigmoid, bias=self.ppt[:, oa0 + col:oa0 + col + 1])
                yield
                P.ts("dve", lw[k_][:, 0:tn], sg[k_][:, 0:tn], LWC, None, ALU.mult)
                yield
                for c in range(nch):
                    cs = slice(c * 64, (c + 1) * 64)
                    P.scan("dve", cum[k_][:, cs], ones64[:], lw[k_][:, cs], 0.0, ALU.mult, ALU.add)
                    yield
                c3 = lambda t_: t_[:, 0:tn].re("p (c n) -> p c n", n=64)
                P.act(WC[k_][:, 0:nch].m(lambda a: a.unsqueeze(2)), c3(cum[k_])[:, :, 63:64], AF.Exp)
                yield
                if d == 1:
                    P.copy("pool", cumx[k_][:, 0:nch].m(lambda a: a.unsqueeze(2)), c3(cum[k_])[:, :, 63:64])
                    yield
                    P.tt("dve", cum[k_][:, 0:tn], lw[k_][:, 0:tn], cum[k_][:, 0:tn], ALU.subtract)
                    yield
                    P.tt("dve", c3(cum[k_]), c3(cum[k_]), cumx[k_][:, 0:nch].m(lambda a: a.unsqueeze(2)).bc([128, nch, 64]), ALU.add)
                    yield
                    tot = None
                P.tt("pool", e4[k_][:, 0:tn], cum[k_][:, 0:tn], lw[k_][:, 0:tn], ALU.subtract)
                yield
                P.act(e1[k_][:, 0:tn], cum[k_][:, 0:tn], AF.Exp)
                yield
                P.act(e2[k_][:, 0:tn], cum[k_][:, 0:tn], AF.Exp, scale=-1.0)
                yield
                P.act(e3[k_][:, 0:tn], e4[k_][:, 0:tn], AF.Exp)
                yield
                P.tt("dve", c3(e4[k_]), c3(e2[k_]), WC[k_][:, 0:nch].m(lambda a: a.unsqueeze(2)).bc([128, nch, 64]), ALU.mult)
                yield
                P.ts("dve", kkr[k_][:, 0:tn], kT[:, ts_], self.ppt[:, okk + col:okk + col + 1], None, ALU.mult)
                yield
                P.act(sqk[k_][:, 0:tn], kkr[k_][:, 0:tn], AF.Square)
                yield
                pss = self.ps(1)
                P.mm(pss[:, 0, 0:tn], cf["blk"][:], sqk[k_][:, 0:tn])
                P.act(nrm[k_][:, 0:tn], pss[:, 0, 0:tn], AF.Sqrt)
                yield
                P.ts("dve", nrm[k_][:, 0:tn], nrm[k_][:, 0:tn], 1e-12, None, ALU.max)
                yield
                P.recip(nrm[k_][:, 0:tn], nrm[k_][:, 0:tn])
                yield
                P.tt("dve", kk[k_][:, 0:tn], kkr[k_][:, 0:tn], nrm[k_][:, 0:tn], ALU.mult)
                yield
                P.ts("pool", tmpa[k_][:, 0:tn], aa[k_][:, 0:tn], self.ppt[:, oka + col:oka + col + 1], omka[:, col:col + 1], ALU.mult, ALU.add)
                yield
                P.tt("pool", kd[k_][:, 0:tn], kT[:, ts_], tmpa[k_][:, 0:tn], ALU.mult)
                yield
                P.tt("dve", bq[k_][:, 0:tn], kk[k_][:, 0:tn], aa[k_][:, 0:tn], ALU.mult)
                yield
            def prepB(tix):
                ti = tiles[tix]
                t0, tn = TILES[ti]
                nch = tn // 64
                nbk = tn // 128
                k_ = tix % NB
                ts_ = slice(t0, t0 + tn)
                ds_ = slice(d * 64, (d + 1) * 64)
                c3 = lambda t_: t_[:, 0:tn].re("p (c n) -> p c n", n=64)
                b3 = lambda t_: t_[:, 0:tn].re("p (b n) -> p b n", n=128)
                P.ts("pool", sqk[k_][:, 0:tn], kk[k_][:, 0:tn], -1.0, None, ALU.mult)
                yield
                P.tt("dve", arb[k_][:, 0:nbk, 0, :], b3(sqk[k_]), b3(e3[k_]), ALU.mult)
                yield
                P.tt("pool", arb[k_][:, 0:nbk, 1, :], rT[:, ts_].re("p (b n) -> p b n", n=128), b3(e1[k_]), ALU.mult)
                yield
                P.tt("dve", kb[k_][:, 0:tn], kd[k_][:, 0:tn], e2[k_][:, 0:tn], ALU.mult)
                yield
                P.tt("pool", bb[k_][:, 0:tn], bq[k_][:, 0:tn], e2[k_][:, 0:tn], ALU.mult)
                yield
                P.tt("dve", kt[k_][:, 0:tn], kd[k_][:, 0:tn], e4[k_][:, 0:tn], ALU.mult)
                yield
                P.tt("pool", btl[k_][:, 0:tn], bq[k_][:, 0:tn], e4[k_][:, 0:tn], ALU.mult)
                yield
            def step(gen, n=10 ** 9):
                for _ in range(n):
                    if next(gen, 'done') == 'done':
                        return
            step(prepA(0))
            step(prepB(0))
            for tix, ti in enumerate(tiles):
                t0, tn = TILES[ti]
                nch = tn // 64
                nbk = tn // 128
                k_ = tix % NB
                ts_ = slice(t0, t0 + tn)
                ds_ = slice(d * 64, (d + 1) * 64)
                nxtA = prepA(tix + 1) if tix + 1 < len(tiles) else iter(())
                blocks = list(range(nbk)) if d == 0 else list(range(nbk - 1, -1, -1))
                G = len(blocks)
                idm = lambda t_: t_[:].m(lambda a: a.unsqueeze(1)).bc([128, 2, 128])
                ev = 0
                for g, bi in enumerate(blocks):
                    bs = slice(bi * 128, (bi + 1) * 128)
                    pt = self.ps(2 + g)
                    P.tr(pt[:, 0, 0:128], arb[k_][:, bi, 0, :], self.c_ident[:])
                    P.tr(pt[:, 0, 128:256], btl[k_][:, bs], self.c_ident[:])
                    P.tr(pt[:, 0, 256:384], kt[k_][:, bs], self.c_ident[:])
                    P.tr(pt[:, 0, 384:512], vT[:, t0 + bi * 128:t0 + (bi + 1) * 128], self.c_ident[:])
                for g, bi in enumerate(blocks):
                    P.copy("act" if g % 2 == 0 else "dve", TM[g][:].re("p a n -> p (a n)"), self.ps(2 + g)[:, 0, 0:512])
                px2 = self.ps(6, 2)
                for half in range(0, G, 2):
                    gs = list(range(half, min(half + 2, G)))
                    for g in gs:
                        bi = blocks[g]
                        bs = slice(bi * 128, (bi + 1) * 128)
                        pa = self.ps(2 + 2 * (g % 2), 2)
                        for h in range(2):
                            hs = slice(h * 64, (h + 1) * 64)
                            ar_ = arb[k_][hs, bi, :, :].re("p a n -> p (a n)")
                            P.mm(pa[:, h, 0:256], bb[k_][hs, bs], ar_)
                            P.mm(pa[:, h, 256:512], kb[k_][hs, bs], ar_)
                            P.mm(px2[:, h, g * 128:(g + 1) * 128], arb[k_][hs, bi, 0, :], bb[k_][hs, bs])
                    for g in gs:
                        pa = self.ps(2 + 2 * (g % 2), 2)
                        P.tt("dve", AT[g][:].re("p h a n -> p h (a n)"), pa,
                             mask4[:].re("p a n -> p (a n)").m(lambda a: a.unsqueeze(1)).bc([128, 2, 512]), ALU.mult)
                for g in range(G):
                    P.tt("dve", X1[g][:], px2[:, :, g * 128:(g + 1) * 128], idm(low), ALU.mult)
                    P.tt("pool", Qf[g][:], AT[g][:, :, 0, :], idm(self.c_ident), ALU.add)
                xk = [[X1[g][:, h, :] for h in range(2)] for g in range(G)]
                xtk = [[AT[g][:, h, 0, :] for h in range(2)] for g in range(G)]
                for lev in range(int(os.environ.get('MK_NLEV', '5'))):
                    for g in range(G):
                        pn = self.ps(2 + g)
                        for h in range(2):
                            P.mm(pn[:, 0, h * 256:h * 256 + 128], xtk[g][h], xk[g][h])
                            if lev < 4:
                                P.mm(pn[:, 0, h * 256 + 128:h * 256 + 256], xk[g][h], xtk[g][h])
                    for g in range(G):
                        P.copy("act", XX[g][:].re("p h a n -> p (h a n)"), self.ps(2 + g)[:, 0, :])
                        xk[g] = [XX[g][:, h, 0, :] for h in range(2)]
                        xtk[g] = [XX[g][:, h, 1, :] for h in range(2)]
                    for g in range(G):
                        pq = self.ps(6 + g // 2)
                        for h in range(2):
                            c0 = (g % 2) * 256 + h * 128
                            P.mm(pq[:, 0, c0:c0 + 128], xk[g][h], Qf[g][:, h, :])
                    for g in range(G):
                        pq = self.ps(6 + g // 2)
                        c0 = (g % 2) * 256
                        P.tt("dve", Qf[g][:], pq[:, 0, c0:c0 + 256].re("p (h n) -> p h n", h=2), Qf[g][:], ALU.add)
                    step(nxtA, 7)
                pw = self.ps(0)
                for g in range(G):
                    for h in range(2):
                        P.mm(pw[:, 0, g * 128 + h * 64:g * 128 + (h + 1) * 64], AT[g][:, h, 2, :], TM[g][:, 3, h * 64:(h + 1) * 64])
                for g in range(G):
                    P.copy("act", W1[g][:], pw[:, 0, g * 128:(g + 1) * 128])
                for g in range(G):
                    pau = self.ps(2 + g // 2)
                    c0 = (g % 2) * 256
                    for h in range(2):
                        P.mm(pau[:, 0, c0 + h * 64:c0 + (h + 1) * 64], Qf[g][:, h, :], TM[g][:, 0, h * 64:(h + 1) * 64])
                        P.mm(pau[:, 0, c0 + 128 + h * 64:c0 + 128 + (h + 1) * 64], Qf[g][:, h, :], W1[g][:, h * 64:(h + 1) * 64])
                for g in range(G):
                    pau = self.ps(2 + g // 2)
                    c0 = (g % 2) * 256
                    P.copy("act" if g % 2 == 0 else "dve", AU[g][:], pau[:, 0, c0:c0 + 256])
                for g, bi in enumerate(blocks):
                    py = self.ps(4 + g // 2)
                    c0 = (g % 2) * 256
                    for h in range(2):
                        hs = slice(h * 64, (h + 1) * 64)
                        P.mm(py[hs, 0, c0:c0 + 128], AU[g][:, 128 + h * 64:128 + (h + 1) * 64], AT[g][:, h, 1, :], start=True, stop=False)
                        P.mm(py[hs, 0, c0:c0 + 128], TM[g][:, 3, h * 64:(h + 1) * 64], AT[g][:, h, 3, :], start=False, stop=True)
                        P.mm(py[hs, 0, c0 + 128:c0 + 256], AU[g][:, h * 64:(h + 1) * 64], AT[g][:, h, 1, :])
                for g, bi in enumerate(blocks):
                    py = self.ps(4 + g // 2)
                    c0 = (g % 2) * 256
                    P.copy("act", Y0T[g][:], py[:, 0, c0:c0 + 128])
                    P.tt("dve", RhT[g][:], py[:, 0, c0 + 128:c0 + 256], arb[k_][:, bi, 1, :], ALU.add)
                step(nxtA)
                if tix + 1 < len(tiles):
                    step(prepB(tix + 1))
                for g, bi in enumerate(blocks):
                    for cc in range(2):
                        cs = slice(cc * 64, (cc + 1) * 64)
                        pmn = self.ps((6 if g < 2 else 2) + cc)
                        c0 = (g % 2) * 192
                        P.mm(pmn[:, 0, c0:c0 + 128], AU[g][cs, 0:128], TM[g][cs, 1, :])
                        for h in range(2):
                            hs = slice(h * 64, (h + 1) * 64)
                            P.mm(pmn[hs, 0, c0 + 128:c0 + 192], TM[g][cs, 1, hs], AU[g][cs, 128 + h * 64:128 + (h + 1) * 64], start=True, stop=False)
                            P.mm(pmn[hs, 0, c0 + 128:c0 + 192], TM[g][cs, 2, hs], TM[g][cs, 3, hs], start=False, stop=True)
                for g, bi in enumerate(blocks):
                    for cc in range(2):
                        pmn = self.ps((6 if g < 2 else 2) + cc)
                        c0 = (g % 2) * 192
                        mt_ = MTt[cc]
                        P.tt("dve", mt_[:], pmn[:, 0, c0:c0 + 128], cf["blk"][:], ALU.mult)
                        wcol = bi * 2 + cc
                        P.stt("dve", MT[g][cc][:], self.c_ident[:], WC[k_][:, wcol:wcol + 1], mt_[:], ALU.mult, ALU.add)
                        P.copy("act", Ns[g][cc][:], pmn[:, 0, c0 + 128:c0 + 192])
                for g, bi in enumerate(blocks if not os.environ.get('MK_NOSEQ') else []):
                    for cc in ([0, 1] if d == 0 else [1, 0]):
                        cs = slice(cc * 64, (cc + 1) * 64)
                        tok = slice(t0 + bi * 128 + cc * 64, t0 + bi * 128 + cc * 64 + 64)
                        pyh = [self.ps(4), self.ps(5)]
                        pss_ = self.ps(0)
                        for h in range(2):
                            hs = slice(h * 64, (h + 1) * 64)
                            P.mm(pyh[h][hs, 0, 0:64], Sseq[hs, si % 4, :], RhT[g][hs, cs])
                        P.mm(pss_[:, 0, 0:64], MT[g][cc][:], Sseq[:, si % 4, :])
                        P.tt("dve", Sseq[:, (si + 1) % 4, :], pss_[:, 0, 0:64], Ns[g][cc][:], ALU.add)
                        for h in range(2):
                            hs = slice(h * 64, (h + 1) * 64)
                            if d == 0:
                                P.tt("pool" if False else "dve", ysum[hs, tok], pyh[h][hs, 0, 0:64], Y0T[g][hs, cs], ALU.add)
                            else:
                                yt_ = ytmp[cc]
                                P.tt("dve", yt_[hs, :], pyh[h][hs, 0, 0:64], Y0T[g][hs, cs], ALU.add)
                                P.tt("pool", ysum[hs, tok], ysum[hs, tok], yt_[hs, :], ALU.add)
                        si += 1

    def rwkv_finish(self, l, hp, rT, kT, vT, ysum, cf, ctx_out):
        P = self.P
        wg = P.sb([128, 8, 128], BF16, "rf_wg")
        P.dma("sp", wg[:], self.wb(l, C_RG + hp * 128, 128))
        hb = [P.sb([128, 8, 512], BF16, f"rf_h{i}") for i in range(2)]
        f = lambda n: [P.sb([128, 512], F32, f"rf_{n}{i}") for i in range(2)]
        yc, sq, rstd, rk, gs = f("yc"), f("sq"), f("rstd"), f("rk"), f("gs")
        mo = [P.sb([128, 512], BF16, f"rf_mo{i}") for i in range(2)]
        olw, olb, ork = PP["lnw"][0] + hp, PP["lnb"][0] + hp, PP["rk"][0] + hp
        for ti, (t0, tn) in enumerate(TILES):
            if ti == 0 and not ctx_out:
                continue
            k_ = ti % 2
            ts_ = slice(t0, t0 + tn)
            self.load_h(hb[k_], ti, "sp" if k_ == 0 else "act")
            pg = self.ps(0)
            for kc in range(8):
                P.mm(pg[:, 0, 0:tn], wg[:, kc, :], hb[k_][:, kc, 0:tn], start=(kc == 0), stop=(kc == 7))
            P.act(gs[k_][:, 0:tn], pg[:, 0, 0:tn], AF.Silu)
            pm = self.ps(1)
            P.mm(pm[:, 0, 0:tn], cf["blk"][:], ysum[:, ts_])
            P.stt("dve", yc[k_][:, 0:tn], pm[:, 0, 0:tn], -1.0 / 64, ysum[:, ts_], ALU.mult, ALU.add)
            P.act(sq[k_][:, 0:tn], yc[k_][:, 0:tn], AF.Square)
            pv = self.ps(2)
            P.mm(pv[:, 0, 0:tn], cf["blk"][:], sq[k_][:, 0:tn])
            P.act(rstd[k_][:, 0:tn], pv[:, 0, 0:tn], AF.Sqrt, bias=64e-5, scale=1.0 / 64)
            P.recip(rstd[k_][:, 0:tn], rstd[k_][:, 0:tn])
            P.tt("dve", yc[k_][:, 0:tn], yc[k_][:, 0:tn], rstd[k_][:, 0:tn], ALU.mult)
            P.ts("dve", yc[k_][:, 0:tn], yc[k_][:, 0:tn], self.ppt[:, olw:olw + 1], self.ppt[:, olb:olb + 1], ALU.mult, ALU.add)
            P.stt("dve", rk[k_][:, 0:tn], rT[:, ts_], self.ppt[:, ork:ork + 1], kT[:, ts_], ALU.mult, ALU.mult)
            pb = self.ps(3)
            P.mm(pb[:, 0, 0:tn], cf["blk"][:], rk[k_][:, 0:tn])
            P.tt("dve", rk[k_][:, 0:tn], pb[:, 0, 0:tn], vT[:, ts_], ALU.mult)
            P.tt("pool", yc[k_][:, 0:tn], yc[k_][:, 0:tn], rk[k_][:, 0:tn], ALU.add)
            P.tt("dve", mo[k_][:, 0:tn], yc[k_][:, 0:tn], gs[k_][:, 0:tn], ALU.mult)
            r0_ = 384 + hp * 128
            P.dma("sp", self.mix.v(("mr", hp, ti), (slice(r0_, r0_ + 128), ts_)), mo[k_][:, 0:tn])

    def zero_mix(self, r0, r1):
        P = self.P
        z = P.sb([128, S], BF16, "zmix")
        P.memset("pool", z[:], 0.0)
        for kc in range(r0 // 128, r1 // 128):
            P.dma("sp", self.mix.v(("mz", kc), (slice(kc * 128, kc * 128 + 128), slice(None))), z[:])

    def build(self, layers=DEPTH):
        P = self.P
        base = P.mark()
        self.phase_w()
        P.release(base)
        for l in range(layers):
            self.phase_mod(l)
            keep = P.mark()
            self.phase_norm(l)
            P.release(keep)
            if self.stop_after == "norm":
                return
            if not os.environ.get("MK_SKIP_ATT"):
                self.phase_att(l)
                P.release(keep)
            if self.stop_after == "att":
                return
            if os.environ.get("MK_SKIP_RWKV"):
                self.zero_mix(384, 640)
            else:
                self.phase_rwkv(l)
            P.release(keep)
            if self.stop_after == "rwkv":
                return
            self.phase_ssd(l)
            P.release(keep)
            if self.stop_after == "ssd":
                return
            self.phase_out(l)
            P.release(keep)
            for a in ("n_x", "o_w"):
                if hasattr(self, a):
                    delattr(self, a)

def build_program(stop_after=None, dbg=False, layers=DEPTH):
    nc = bass.Bass("TRN2", target_bir_lowering=False)
    k = MK(nc, stop_after=stop_after, dbg=dbg)
    k.build(layers)
    k.P.emit()
    return nc, k


def prep_inputs(inp, b):
    inp = {k: np.asarray(v) for k, v in inp.items()}
    d = {}
    d["xin"] = np.ascontiguousarray(np.concatenate([inp["ctx"][b], inp["x"][b]], 0).astype(np.float32))
    d["cvec"] = np.ascontiguousarray(np.concatenate([fm(inp["c"][b]), fm(inp["c_ctx"])], 1))
    d["ada_w"] = np.ascontiguousarray(inp["ada_w"], np.float32)
    d["w_in"] = np.ascontiguousarray(inp["w_in"], np.float32)
    d["w_out"] = np.ascontiguousarray(inp["w_out"], np.float32)
    d["wup"] = np.ascontiguousarray(inp["rwkv_w_up"], np.float32)
    d["aup"] = np.ascontiguousarray(inp["rwkv_a_up"], np.float32)
    d["pp"] = np.stack([make_pp(inp, l) for l in range(DEPTH)])
    d["pr"] = np.stack([make_pr(inp, l) for l in range(DEPTH)])
    d["pr2"] = np.stack([make_pr2(inp, l) for l in range(DEPTH)])
    d["cst"] = make_consts()
    return d


def kernel(**inputs):
    nc, _ = build_program()
    maps = [prep_inputs(inputs, b % 4) for b in range(4)]
    in_maps = [maps[i % 4] for i in range(8)]
    res = run_bass_kernel_spmd(nc, in_maps, core_ids=list(range(8)))
    return np.stack([np.asarray(res.results[b]["y"], np.float32) for b in range(4)], 0)
```

```python
import numpy as np
import concourse.bass as bass
import concourse.mybir as mybir

F32 = mybir.dt.float32
BF16 = mybir.dt.bfloat16
ALU = mybir.AluOpType
AF = mybir.ActivationFunctionType
AX = mybir.AxisListType

import os
DUMP = os.environ.get("MK_DUMP", "")
ATTACH = os.environ.get("MK_ATTACH", "1") != "0"
NOSYNC = set(filter(None, os.environ.get("MK_NOSYNC", "").split(",")))
ENGS = ("pe", "dve", "act", "pool", "sp")
N_DMA_SEMS = 20


class Trk:
    __slots__ = ("lw", "rd", "ps")

    def __init__(self, ps=False):
        self.lw = None
        self.rd = []
        self.ps = ps


class V:
    __slots__ = ("ap", "trk")

    def __init__(self, ap, trk):
        self.ap = ap
        self.trk = trk

    def __getitem__(self, idx):
        return V(self.ap[idx], self.trk)

    def m(self, fn):
        return V(fn(self.ap), self.trk)

    def re(self, pat, **kw):
        return V(self.ap.rearrange(pat, **kw), self.trk)

    def bc(self, shape):
        return V(self.ap.broadcast_to(shape), self.trk)

    @property
    def shape(self):
        return self.ap.shape


class Tile:
    def __init__(self, handle):
        self.h = handle
        self.trk = Trk()

    def __getitem__(self, idx):
        return V(self.h[idx], (self.trk,))

    def ap(self):
        return V(self.h.ap() if hasattr(self.h, "ap") else self.h[:], (self.trk,))


class DTile:
    def __init__(self, handle):
        self.h = handle
        self.reg = {}

    def v(self, key, idx=None, fn=None):
        t = self.reg.setdefault(key, Trk())
        ap = self.h.ap()
        if fn is not None:
            ap = fn(ap)
        if idx is not None:
            ap = ap[idx]
        return V(ap, (t,))


class Op:
    __slots__ = ("eng", "fn", "reads", "writes", "idx", "deps", "signal", "sig_count",
                 "is_dma", "dma_sem", "dma_val", "dma_prev", "kind")


class Prog:
    def __init__(self, nc):
        self.nc = nc
        self.ops = {e: [] for e in ENGS}
        self.n_t = 0
        self.all_ops = []

    ARENA = 207 * 1024

    def sb(self, shape, dtype, name=None):
        self.n_t += 1
        if not hasattr(self, "_abase"):
            a = self.nc.alloc_sbuf_tensor("arena", [128, self.ARENA], mybir.dt.uint8)
            self._abase = self.nc.lookup_mloc(a).addr
            self._aoff = 0
        esz = 2 if dtype == BF16 else 4
        n = esz
        for d in shape[1:]:
            n *= d
        off = (self._aoff + 63) // 64 * 64
        assert off + n <= self.ARENA, f"SBUF arena overflow allocating {name} {shape}: {off + n}"
        self._aoff = off + n
        return Tile(self.nc.alloc_sbuf_tensor_at(f"{name or 'sb'}_{self.n_t}", list(shape), dtype, offset=self._abase + off))

    def mark(self):
        return getattr(self, "_aoff", 0)

    def release(self, mark):
        self.barrier()
        self._aoff = mark

    def barrier(self):
        lasts = []
        for e in ENGS:
            for op in reversed(self.ops[e]):
                if not op.is_dma and op.kind != "bar":
                    lasts.append(op)
                    break
        dmas = [op for op in self.all_ops[getattr(self, "_bar_pos", 0):] if op.is_dma]
        self._bar_pos = len(self.all_ops)
        for e in ENGS:
            op = self.rec(e, lambda eh: eh.nop(nofuse=True), [], [], kind="bar")
            op.deps = [d for d in lasts if d.eng != e] + dmas
            for d in op.deps:
                d.signal = True

    def ps(self, shape, dtype=F32, name=None):
        self.n_t += 1
        return Tile(self.nc.alloc_psum_tensor(name or f"ps{self.n_t}", list(shape), dtype))

    def dram(self, name, shape, dtype, kind="Internal"):
        return DTile(self.nc.dram_tensor(name, list(shape), dtype, kind=kind))

    def rec(self, eng, fn, reads, writes, is_dma=False, kind=""):
        op = Op()
        op.eng = eng
        op.fn = fn
        op.kind = kind
        op.is_dma = is_dma
        op.idx = len(self.ops[eng])
        op.signal = False
        op.deps = []
        rt = []
        for v in reads:
            if v is None or not isinstance(v, V):
                continue
            rt.extend(v.trk)
        wt = []
        for v in writes:
            if v is None or not isinstance(v, V):
                continue
            wt.extend(v.trk)
        deps = set()
        for t in rt:
            if t.lw is not None:
                deps.add(t.lw)
            if t.ps:
                for r in t.rd:
                    if r.eng != eng:
                        deps.add(r)
        for t in wt:
            if t.lw is not None:
                deps.add(t.lw)
            for r in t.rd:
                deps.add(r)
        deps.discard(op)
        for t in rt:
            t.rd.append(op)
        for t in wt:
            t.lw = op
            t.rd = []
        final = []
        for d in deps:
            if d.eng == "pe" and eng == "pe" and not d.is_dma and not is_dma:
                continue
            if d.eng == eng and eng in NOSYNC and not d.is_dma and not is_dma:
                continue
            final.append(d)
        op.deps = final
        for d in final:
            d.signal = True
        self.ops[eng].append(op)
        self.all_ops.append(op)
        return op

    def mm(self, out, lhsT, rhs, start=True, stop=True):
        return self.rec("pe", lambda e: e.matmul(out.ap, lhsT.ap, rhs.ap, start=start, stop=stop),
                        [lhsT, rhs], [out], kind="mm")

    def tr(self, out, in_, ident):
        return self.rec("pe", lambda e: e.transpose(out.ap, in_.ap, ident.ap), [in_, ident], [out], kind="tr")

    def act(self, out, in_, func, bias=None, scale=None, accum_out=None, eng="act"):
        kw = {}
        rd = [in_]
        if bias is not None:
            kw["bias"] = bias.ap if isinstance(bias, V) else bias
            rd.append(bias)
        if scale is not None:
            kw["scale"] = scale.ap if isinstance(scale, V) else scale
            rd.append(scale)
        wr = [out]
        if accum_out is not None:
            kw["accum_out"] = accum_out.ap
            wr.append(accum_out)
        return self.rec("act", lambda e: e.activation(out.ap, in_.ap, func, **kw), rd, wr, kind="act")

    def tt(self, eng, out, in0, in1, op):
        return self.rec(eng, lambda e: e.tensor_tensor(out.ap, in0.ap, in1.ap, op), [in0, in1], [out], kind="tt")

    def ts(self, eng, out, in0, s1, s2, op0, op1=None, accum_out=None):
        rd = [in0, s1, s2]
        a1 = s1.ap if isinstance(s1, V) else s1
        a2 = s2.ap if isinstance(s2, V) else s2
        kw = {}
        wr = [out]
        if accum_out is not None:
            kw["accum_out"] = accum_out.ap
            wr.append(accum_out)
        if op1 is None:
            return self.rec(eng, lambda e: e.tensor_scalar(out.ap, in0.ap, a1, None, op0, **kw), rd, wr, kind="ts")
        return self.rec(eng, lambda e: e.tensor_scalar(out.ap, in0.ap, a1, a2, op0, op1, **kw), rd, wr, kind="ts")

    def stt(self, eng, out, in0, scalar, in1, op0, op1):
        a = scalar.ap if isinstance(scalar, V) else scalar
        return self.rec(eng, lambda e: e.scalar_tensor_tensor(out.ap, in0.ap, a, in1.ap, op0, op1),
                        [in0, scalar, in1], [out], kind="stt")

    def copy(self, eng, out, in_):
        if eng == "act":
            return self.rec("act", lambda e: e.copy(out.ap, in_.ap), [in_], [out], kind="copy")
        return self.rec(eng, lambda e: e.tensor_copy(out.ap, in_.ap), [in_], [out], kind="copy")

    def memset(self, eng, out, val):
        return self.rec(eng, lambda e: e.memset(out.ap, val), [], [out], kind="memset")

    def reduce(self, eng, out, in_, op, axis=AX.X):
        return self.rec(eng, lambda e: e.tensor_reduce(out.ap, in_.ap, axis, op), [in_], [out], kind="red")

    def recip(self, out, in_):
        return self.rec("dve", lambda e: e.reciprocal(out.ap, in_.ap), [in_], [out], kind="recip")

    def scan(self, eng, out, d0, d1, initial, op0, op1):
        ini = initial.ap if isinstance(initial, V) else initial
        return self.rec(eng, lambda e: e.tensor_tensor_scan(out.ap, d0.ap, d1.ap, ini, op0, op1),
                        [d0, d1, initial], [out], kind="scan")

    def dma(self, q, out, in_, **kw):
        return self.rec(q, lambda e: e.dma_start(out.ap, in_.ap, **kw), [in_], [out], is_dma=True, kind="dma")

    def emit(self):
        nc = self.nc
        eng_sem = {e: nc.alloc_semaphore(f"s_{e}") for e in ENGS}
        dma_sems = {e: ([nc.alloc_semaphore(f"d_{e}_{i}") for i in range(N_DMA_SEMS)]
                        if any(o.is_dma for o in self.ops[e]) else []) for e in ENGS}
        for e in ENGS:
            cnt = 0
            nd = 0
            last_on_sem = {}
            for op in self.ops[e]:
                if op.is_dma:
                    j = nd % N_DMA_SEMS
                    nd += 1
                    op.dma_sem = dma_sems[e][j]
                    k = nd_k = (nd - 1) // N_DMA_SEMS + 1
                    op.dma_val = 16 * k
                    op.dma_prev = last_on_sem.get(j)
                    last_on_sem[j] = op
                else:
                    if op.signal:
                        cnt += 1
                        op.sig_count = cnt
        self._eng_sem = eng_sem
        handles = {"pe": "tensor", "dve": "vector", "act": "scalar", "pool": "gpsimd", "sp": "sync"}

        def run_engine(ename, eh):
            seen = {}

            def wait(sem, val):
                key = id(sem)
                if seen.get(key, 0) >= val:
                    return
                seen[key] = val
                eh.wait_ge(sem, val)

            for op in self.ops[ename]:
                need = []
                for d in op.deps:
                    if d.is_dma:
                        need.append((d.dma_sem, d.dma_val))
                    else:
                        need.append((eng_sem[d.eng], d.sig_count))
                if op.is_dma and op.dma_prev is not None:
                    need.append((op.dma_prev.dma_sem, op.dma_prev.dma_val))
                todo = []
                for sem, val in need:
                    key = id(sem)
                    if seen.get(key, 0) >= val:
                        continue
                    seen[key] = val
                    todo = [(s_, v_) for (s_, v_) in todo if s_ is not sem] + [(sem, val)]
                attach = None
                if ATTACH and todo and not op.is_dma and op.kind != "bar":
                    attach = todo.pop()
                for sem, val in todo:
                    eh.wait_ge(sem, val)
                if op.is_dma:
                    ins = op.fn(eh)
                    ins.then_inc(op.dma_sem, 16)
                else:
                    ins = op.fn(eh)
                    if attach is not None:
                        ins._wait_ge(attach[0], attach[1])
                    if op.signal:
                        ins.then_inc(eng_sem[ename], 1)
                    if DUMP and ename in DUMP:
                        print(ename, op.idx, op.kind, ins.concise(), flush=True)
            last = {}
            for op in self.ops[ename]:
                if op.is_dma:
                    last[id(op.dma_sem)] = op
            for op in last.values():
                wait(op.dma_sem, op.dma_val)

        with nc.Block() as block:
            @block.tensor
            def _(eh):
                run_engine("pe", eh)

            @block.vector
            def _(eh):
                run_engine("dve", eh)

            @block.scalar
            def _(eh):
                run_engine("act", eh)

            @block.gpsimd
            def _(eh):
                run_engine("pool", eh)

            @block.sync
            def _(eh):
                run_engine("sp", eh)

    def stats(self):
        return {e: len(self.ops[e]) for e in ENGS}

from concourse.bass_utils import run_bass_kernel_spmd

S = 4352
CTX = 256
TLAT = 4096
DM = 1024
NIN = 3596
NBLK = 34
DEPTH = 2
TILES = [(0, 256)] + [(256 + 512 * i, 512) for i in range(8)]
NEG = -30000.0

C_Q, C_K, C_V, C_G = 0, 384, 512, 640
C_RW = 1024
C_RG = 2048
C_Z = 2304
C_XBC = 2688
C_DT = 3584


def _slots(spec):
    d, o = {}, 0
    for n, w in spec:
        d[n] = (o, w)
        o += w
    return d, o


PP, NPP = _slots([("adab", 24), ("normw", 8), ("sink", 6), ("mu0", 8), ("mu1", 8), ("w0", 4), ("a0", 4),
                  ("kk", 4), ("ka", 4), ("rk", 2), ("lnw", 2), ("lnb", 2), ("convw", 21), ("convb", 7),
                  ("ssmd", 3), ("ssmnw", 3)])
PR, NPR = _slots([("adabg", 1024), ("dtb", 12), ("alog", 12), ("fnw", 1024)])
PR2, NPR2 = _slots([("convr", 3 * 896), ("shr", 3 * 1024)])
CS, NCS = _slots([("ident", 128), ("mband", 3 * 384), ("perm", 128), ("cos", 4096), ("sin", 4096),
                  ("blk", 128), ("rmf", 256), ("rmb", 256), ("snf", 128), ("snb", 128), ("trif", 128),
                  ("trib", 128), ("ones", 128), ("hlo", 128), ("hhi", 128), ("lowf", 128), ("lowb", 128)])


def fm(v):
    v = np.asarray(v, np.float32)
    return np.ascontiguousarray(v.reshape(-1, 128).T)


def rep(v):
    v = np.asarray(v, np.float32).reshape(1, -1)
    return np.ascontiguousarray(np.broadcast_to(v, (128, v.shape[1])))


def make_consts():
    c = np.zeros((128, NCS), np.float32)

    def put(n, a):
        o, w = CS[n]
        c[:, o:o + w] = a
    put("ident", np.eye(128, dtype=np.float32))
    qi = np.arange(128)[:, None]
    kj = np.arange(384)[None, :]
    band = np.abs(kj - 128 - qi) <= 128
    mb = []
    for var in range(3):
        ok = band.copy()
        if var == 0:
            ok &= kj >= 128
        if var == 2:
            ok &= kj < 256
        mb.append(np.where(ok, 0.0, NEG))
    put("mband", np.concatenate(mb, 1))
    perm = np.zeros((128, 128), np.float32)
    for m in range(128):
        d = m % 64
        half = (d % 32) // 16
        partner = m + 16 if half == 0 else m - 16
        perm[partner, m] = 1.0
    put("perm", perm)
    rows = TLAT // 64
    row = np.repeat(np.arange(rows), 64).astype(np.float32)
    col = np.tile(np.arange(64), rows).astype(np.float32)
    pos = np.stack([row, col], -1)
    inv = (np.float32(10000.0) ** (-np.arange(16, dtype=np.float32) / np.float32(16))).astype(np.float32)
    ang = (pos[:, :, None] * inv).astype(np.float32)
    cosv, sinv = np.cos(ang).astype(np.float32), np.sin(ang).astype(np.float32)
    ct = np.zeros((128, TLAT), np.float32)
    st = np.zeros((128, TLAT), np.float32)
    for m in range(128):
        d = m % 64
        ax, half, f = d // 32, (d % 32) // 16, d % 16
        ct[m] = cosv[:, ax, f]
        st[m] = -sinv[:, ax, f] if half == 0 else sinv[:, ax, f]
    put("cos", ct)
    put("sin", st)
    blk = np.zeros((128, 128), np.float32)
    blk[:64, :64] = 1
    blk[64:, 64:] = 1
    put("blk", blk)
    s = np.arange(128)[:, None]
    t = np.arange(128)[None, :]
    same = (s // 64) == (t // 64)
    put("rmf", np.concatenate([(same & (s < t)), (same & (s <= t))], 1).astype(np.float32))
    put("rmb", np.concatenate([(same & (s > t)), (same & (s >= t))], 1).astype(np.float32))
    put("lowf", (same & (t < s)).astype(np.float32))
    put("lowb", (same & (t > s)).astype(np.float32))
    put("snf", np.where(s <= t, 0.0, NEG))
    put("snb", np.where(s >= t, 0.0, NEG))
    put("trif", (s <= t).astype(np.float32))
    put("trib", (s >= t).astype(np.float32))
    put("ones", np.ones((128, 128), np.float32))
    hlo = np.zeros((128, 128), np.float32)
    hlo[:64] = 1
    put("hlo", hlo)
    put("hhi", 1 - hlo)
    return c


def make_pp(inp, l):
    p = np.zeros((128, NPP), np.float32)

    def put(n, a):
        o, w = PP[n]
        assert a.shape == (128, w), (n, a.shape)
        p[:, o:o + w] = a
    put("adab", fm(inp["ada_b"][l]))
    put("normw", fm(inp["norm_w"][l]))
    put("sink", rep(inp["attn_sink"][l]))
    put("mu0", fm(inp["rwkv_mu"][l, 0]))
    put("mu1", fm(inp["rwkv_mu"][l, 1]))
    for n, k in (("w0", "rwkv_w0"), ("a0", "rwkv_a0"), ("kk", "rwkv_k_k"), ("ka", "rwkv_k_a")):
        put(n, np.concatenate([fm(inp[k][l, 0]), fm(inp[k][l, 1])], 1))
    put("rk", fm(inp["rwkv_r_k"][l].reshape(-1)))
    put("lnw", fm(inp["rwkv_ln_w"][l]))
    put("lnb", fm(inp["rwkv_ln_b"][l]))
    cw = inp["ssm_conv_w"][l]
    put("convw", np.stack([fm(cw[0]), fm(cw[1]), fm(cw[2])], -1).reshape(128, 21))
    put("convb", fm(inp["ssm_conv_b"][l]))
    put("ssmd", fm(np.repeat(inp["ssm_d"][l], 64)))
    put("ssmnw", fm(inp["ssm_norm_w"][l]))
    return p


def make_pr(inp, l):
    p = np.zeros((128, NPR), np.float32)

    def put(n, a):
        o, w = PR[n]
        p[:, o:o + w] = a
    put("adabg", rep(inp["ada_b"][l, 2048:3072]))
    put("dtb", rep(inp["ssm_dt_bias"][l].reshape(-1)))
    put("alog", rep(inp["ssm_a_log"][l].reshape(-1)))
    put("fnw", rep(inp["final_norm_w"]))
    return p


def make_pr2(inp, l):
    p = np.zeros((128, NPR2), np.float32)

    def put(n, a):
        o, w = PR2[n]
        p[:, o:o + w] = a
    put("convr", rep(inp["ssm_conv_w"][l].reshape(-1)))
    mu = inp["rwkv_mu"][l]
    put("shr", rep(np.concatenate([mu[0], mu[1], mu[1]], 0)))
    return p


class MK:
    def __init__(self, nc, stop_after=None, dbg=False):
        self.nc = nc
        self.P = P = Prog(nc)
        self.dbg = dbg
        self.stop_after = stop_after
        ok = "ExternalOutput" if dbg else "Internal"
        self.xin = P.dram("xin", [S, DM], F32, kind="ExternalInput")
        self.cvec = P.dram("cvec", [128, 16], F32, kind="ExternalInput")
        self.ada_w = P.dram("ada_w", [DEPTH, DM, 3 * DM], F32, kind="ExternalInput")
        self.w_in = P.dram("w_in", [DEPTH, DM, NIN], F32, kind="ExternalInput")
        self.w_out = P.dram("w_out", [DEPTH, DM, DM], F32, kind="ExternalInput")
        self.wup = P.dram("wup", [DEPTH, 2, 64, 256], F32, kind="ExternalInput")
        self.aup = P.dram("aup", [DEPTH, 2, 64, 256], F32, kind="ExternalInput")
        self.pp = P.dram("pp", [DEPTH, 128, NPP], F32, kind="ExternalInput")
        self.pr = P.dram("pr", [DEPTH, 128, NPR], F32, kind="ExternalInput")
        self.pr2 = P.dram("pr2", [DEPTH, 128, NPR2], F32, kind="ExternalInput")
        self.cst = P.dram("cst", [128, NCS], F32, kind="ExternalInput")
        self.y = P.dram("y", [TLAT, DM], F32, kind="ExternalOutput")
        self.wbin = P.dram("wbin", [DEPTH, 128, 8, NIN], BF16)
        self.wbout = P.dram("wbout", [DEPTH, 128, 8, DM], BF16)
        self.hT = P.dram("hT", [128, 8, S], BF16, kind=ok)
        self.mix = P.dram("mixT", [DM, S], BF16, kind=ok)
        self.xres = P.dram("xres", [S, DM], F32, kind=ok)
        self.yf = P.dram("yfwd", [384, S], F32)
        self.psh = nc.alloc_psum_tensor("psum", [128, 8, 512], F32)
        self.bt = [Trk(ps=True) for _ in range(8)]
        self.c_ident = P.sb([128, 128], F32, "c_ident")
        self.c_identb = P.sb([128, 128], BF16, "c_identb")
        self.load_c(self.c_ident, "ident")
        P.copy("dve", self.c_identb[:], self.c_ident[:])

    def ps(self, b0, nb=1, dt=F32):
        ap = self.psh[:, b0:b0 + nb, :]
        if dt is not F32:
            ap = ap.bitcast(dt)
        return V(ap, tuple(self.bt[b0:b0 + nb]))

    def load_c(self, tile, name, q="sp", sub=None):
        o, w = CS[name]
        if sub is not None:
            o, w = o + sub[0], sub[1]
        self.P.dma(q, tile[:], self.cst.v("c", (slice(None), slice(o, o + w))))

    def phase_w(self):
        P = self.P
        st = [P.sb([128, 8, 512], F32, f"w_st{i}") for i in range(2)]
        sb = [P.sb([128, 8, 512], BF16, f"w_sb{i}") for i in range(2)]
        it = 0
        for l in range(DEPTH):
            for (src, dst, n) in ((self.w_in, self.wbin, NIN), (self.w_out, self.wbout, DM)):
                for c0 in range(0, n, 512):
                    cw = min(512, n - c0)
                    a, b = st[it % 2], sb[it % 2]
                    q = "sp" if it % 2 == 0 else "act"
                    P.dma(q, a[:, :, 0:cw], src.v(("w", l), fn=lambda ap, l=l, c0=c0, cw=cw:
                                                   ap[l].rearrange("(kc p) n -> p kc n", p=128)[:, :, c0:c0 + cw]))
                    P.copy("dve" if it % 2 == 0 else "pool", b[:, :, 0:cw], a[:, :, 0:cw])
                    P.dma(q, dst.v(("wb", l, c0), fn=lambda ap, l=l, c0=c0, cw=cw: ap[l][:, :, c0:c0 + cw]),
                          b[:, :, 0:cw])
                    it += 1

    def wb(self, l, c0, cw):
        t0 = (c0 // 512) * 512
        trk = []
        ap = self.wbin.h.ap()[l][:, :, c0:c0 + cw]
        for t in range(t0, c0 + cw, 512):
            trk.append(self.wbin.reg.setdefault(("wb", l, t), Trk()))
        return V(ap, tuple(trk))

    def wbo(self, l, c0, cw):
        t0 = (c0 // 512) * 512
        trk = []
        ap = self.wbout.h.ap()[l][:, :, c0:c0 + cw]
        for t in range(t0, c0 + cw, 512):
            trk.append(self.wbout.reg.setdefault(("wb", l, t), Trk()))
        return V(ap, tuple(trk))

    def phase_mod(self, l):
        P = self.P
        if not hasattr(self, "ppt"):
            self.ppt = P.sb([128, NPP], F32, "ppt")
            self.prt = P.sb([128, NPR], F32, "prt")
            self.cact = P.sb([128, 8, 2], F32, "cact")
            self.modT = P.sb([128, 24, 2], F32, "modT")
            self.s1T = P.sb([128, 8, 2], F32, "s1T")
            self.gate_bc = P.sb([128, 2, 1024], F32, "gate_bc")
            craw = P.sb([128, 16], F32, "craw")
            P.dma("sp", craw[:], self.cvec.v(0))
            P.act(self.cact[:].re("p k w -> p w k"), craw[:].re("p (w k) -> p w k", w=2), AF.Silu)
        mk_ = P.mark()
        self.crep = P.sb([128, 8, 2, 128], F32, "crep")
        self.aw = [P.sb([128, 8, 512], F32, f"aw{i}") for i in range(2)]
        P.copy("dve", self.crep[:], self.cact[:].m(lambda a: a.unsqueeze(3)).bc([128, 8, 2, 128]))
        P.dma("sp", self.ppt[:], self.pp.v(l, fn=lambda a: a[l]))
        P.dma("act", self.prt[:], self.pr.v(l, fn=lambda a: a[l]))
        for t in range(6):
            a = self.aw[t % 2]
            P.dma("sp" if t % 2 == 0 else "act", a[:],
                  self.ada_w.v(("aw", l), fn=lambda ap, t=t: ap[l].rearrange("(kc p) n -> p kc n", p=128)[:, :, t * 512:(t + 1) * 512]))
            pm = self.ps(0)
            for j in range(4):
                for kc in range(8):
                    P.mm(pm[:, 0, j * 2:(j + 1) * 2], a[:, kc, j * 128:(j + 1) * 128], self.cact[:, kc, :],
                         start=(kc == 0), stop=(kc == 7))
            P.copy("dve", self.modT[:, t * 4:(t + 1) * 4, :], pm[:, 0, 0:8].re("p (j w) -> p j w", w=2))
            if t >= 4:
                for w in range(2):
                    pg = self.ps(1 + w)
                    for kc in range(8):
                        P.mm(pg[:, 0, :], self.crep[:, kc, w, :], a[:, kc, :], start=(kc == 0), stop=(kc == 7))
                    o = PR["adabg"][0] + (t - 4) * 512
                    P.tt("dve", self.gate_bc[:, w, (t - 4) * 512:(t - 3) * 512], pg[:, 0, :], self.prt[:, o:o + 512], ALU.add)
        o = PP["adab"][0]
        P.tt("dve", self.modT[:], self.modT[:], self.ppt[:, o:o + 24].m(lambda a: a.unsqueeze(2)).bc([128, 24, 2]), ALU.add)
        o = PP["normw"][0]
        P.stt("dve", self.s1T[:], self.modT[:, 8:16, :], 1.0,
              self.ppt[:, o:o + 8].m(lambda a: a.unsqueeze(2)).bc([128, 8, 2]), ALU.add, ALU.mult)
        P.release(mk_)

    def phase_norm(self, l):
        P = self.P
        if not hasattr(self, "n_x"):
            self.n_x = [P.sb([128, DM], F32, f"n_x{i}") for i in range(2)]
            self.n_junk = P.sb([128, DM], F32, "n_junk")
            self.n_xb = [P.sb([128, DM], BF16, f"n_xb{i}") for i in range(2)]
            self.n_ss = [P.sb([128, 4], F32, f"n_ss{i}") for i in range(2)]
            self.n_h = [P.sb([128, 8, 512], BF16, f"n_h{i}") for i in range(2)]
            self.n_t = [P.sb([128, 8, 128], F32, f"n_t{i}") for i in range(2)]
        src = self.xin if l == 0 else self.xres
        for ti, (t0, tn) in enumerate(TILES):
            w = 1 if ti == 0 else 0
            hb = self.n_h[ti % 2]
            for bi in range(tn // 128):
                blk = (t0 // 128) + bi
                xt, xb, ss = self.n_x[blk % 2], self.n_xb[blk % 2], self.n_ss[blk % 2]
                P.dma("sp" if blk % 2 == 0 else "act", xt[:], src.v(("x", blk), (slice(blk * 128, blk * 128 + 128), slice(None))))
                P.act(self.n_junk[:], xt[:], AF.Square, accum_out=ss[:, 0:1])
                P.act(ss[:, 1:2], ss[:, 0:1], AF.Sqrt, bias=1e-6, scale=1.0 / DM)
                P.recip(ss[:, 2:3], ss[:, 1:2])
                P.ts("dve", xb[:], xt[:], ss[:, 2:3], None, ALU.mult)
                pt = self.ps(2 + blk % 2, 1, BF16)
                for kc in range(8):
                    P.tr(pt[:, 0, kc * 128:(kc + 1) * 128], xb[:, kc * 128:(kc + 1) * 128], self.c_identb[:])
                tmp = self.n_t[blk % 2]
                P.tt("dve", tmp[:], pt[:, 0, :].re("p (k t) -> p k t", k=8),
                     self.s1T[:, :, w:w + 1].bc([128, 8, 128]), ALU.mult)
                P.tt("pool", hb[:, :, bi * 128:(bi + 1) * 128], tmp[:],
                     self.modT[:, 0:8, w:w + 1].bc([128, 8, 128]), ALU.add)
            P.dma("sp", self.hT.v(("h", ti), (slice(None), slice(None), slice(t0, t0 + tn))), hb[:, :, 0:tn])

    def phase_out(self, l):
        P = self.P
        last = (l == DEPTH - 1)
        if not hasattr(self, "o_w"):
            self.o_w = P.sb([128, 8, DM], BF16, "o_w")
            self.o_m = [P.sb([128, 8, 128], BF16, f"o_m{i}") for i in range(2)]
            self.o_x = [P.sb([128, DM], F32, f"o_x{i}") for i in range(2)]
            self.o_t = [P.sb([128, DM], F32, f"o_t{i}") for i in range(2)]
            self.o_ss = [P.sb([128, 4], F32, f"o_ss{i}") for i in range(2)]
        P.dma("sp", self.o_w[:], self.wbo(l, 0, DM))
        src = self.xin if l == 0 else self.xres
        for blk in range(NBLK):
            if last and blk < 2:
                continue
            w = 1 if blk < 2 else 0
            m, xt, tt_, ss = self.o_m[blk % 2], self.o_x[blk % 2], self.o_t[blk % 2], self.o_ss[blk % 2]
            tsl = slice(blk * 128, blk * 128 + 128)
            P.dma("sp", m[:], self.mix.v(("m", blk), fn=lambda ap, tsl=tsl: ap.rearrange("(kc p) t -> p kc t", p=128)[:, :, tsl]))
            P.dma("act", xt[:], src.v(("x", blk), (tsl, slice(None))))
            for hf in range(2):
                po = self.ps(4 + 2 * (blk % 2) + hf)
                for kc in range(8):
                    P.mm(po[:, 0, :], m[:, kc, :], self.o_w[:, kc, hf * 512:(hf + 1) * 512], start=(kc == 0), stop=(kc == 7))
                P.tt("dve", tt_[:, hf * 512:(hf + 1) * 512], po[:, 0, :], self.gate_bc[:, w, hf * 512:(hf + 1) * 512], ALU.mult)
            P.tt("pool", tt_[:], tt_[:], xt[:], ALU.add)
            if not last:
                P.dma("sp", self.xres.v(("x", blk), (tsl, slice(None))), tt_[:])
            else:
                P.act(xt[:], tt_[:], AF.Square, accum_out=ss[:, 0:1])
                P.act(ss[:, 1:2], ss[:, 0:1], AF.Sqrt, bias=1e-6, scale=1.0 / DM)
                P.recip(ss[:, 2:3], ss[:, 1:2])
                o = PR["fnw"][0]
                P.stt("dve", xt[:], tt_[:], ss[:, 2:3], self.prt[:, o:o + DM], ALU.mult, ALU.mult)
                P.dma("sp", self.y.v(("y", blk), (slice(blk * 128 - CTX, blk * 128 - CTX + 128), slice(None))), xt[:])

    def load_h(self, tile, ti, q="sp"):
        t0, tn = TILES[ti]
        self.P.dma(q, tile[:, :, 0:tn], self.hT.v(("h", ti), (slice(None), slice(None), slice(t0, t0 + tn))))

    def phase_att(self, l):
        P = self.P
        ctx_out = (l < DEPTH - 1)
        import os
        sm = int(os.environ.get("MK_SET", "63"))
        wq = P.sb([128, 8, 384], BF16, "a_wq")
        wg = P.sb([128, 8, 384], BF16, "a_wg")
        wk = P.sb([128, 8, 128], BF16, "a_wk")
        wv = P.sb([128, 8, 128], BF16, "a_wv")
        if sm & 1:
            for j in range(3):
                for hh in range(2):
                    h = j + 3 * hh
                    P.dma("sp", wq[:, :, j * 128 + hh * 64: j * 128 + hh * 64 + 64], self.wb(l, C_Q + h * 64, 64))
                    P.dma("act", wg[:, :, j * 128 + hh * 64: j * 128 + hh * 64 + 64], self.wb(l, C_G + h * 64, 64))
            P.dma("sp", wk[:], self.wb(l, C_K, 128))
            P.dma("act", wv[:], self.wb(l, C_V, 128))
        cos = P.sb([128, TLAT], F32, "a_cos")
        sin = P.sb([128, TLAT], F32, "a_sin")
        if sm & 2:
            self.load_c(cos, "cos", "sp")
            self.load_c(sin, "sin", "act")
        pf = P.sb([128, 128], F32, "a_pf")
        permb = P.sb([128, 128], BF16, "a_permb")
        if sm & 4:
            self.load_c(pf, "perm")
            P.copy("dve", permb[:], pf[:])
        mband = P.sb([128, 3, 384], F32, "a_mband")
        o, w_ = CS["mband"]
        if sm & 8:
            P.dma("sp", mband[:].re("p a b -> p (a b)"), self.cst.v("c", (slice(None), slice(o, o + w_))))
        sink8 = P.sb([128, 6], F32, "a_sink8")
        o = PP["sink"][0]
        if sm & 16:
            P.ts("dve", sink8[:], self.ppt[:, o:o + 6], 8.0, None, ALU.mult)
        KB = P.sb([128, 4608], BF16, "a_KB")
        Vtm = P.sb([128, 36, 128], BF16, "a_V")
        if sm & 32:
            P.memset("pool", KB[:, 256:384], 0.0)
            P.memset("pool", KB[:, 4480:4608], 0.0)
            P.memset("pool", Vtm[:, 2, :], 0.0)
            P.memset("pool", Vtm[:, 35, :], 0.0)
        hb = [P.sb([128, 8, 512], BF16, f"a_h{i}") for i in range(2)]
        kf = P.sb([128, 512], F32, "a_kf")
        t1 = [P.sb([128, 512], F32, f"a_t1{i}") for i in range(2)]
        t2 = [P.sb([128, 512], F32, f"a_t2{i}") for i in range(2)]

        def kcol(tok):
            return tok if tok < CTX else tok + 128

        def proj(dst, wt, c0, cw, hbt, tn, lat_t0, bank, func=None):
            pp_ = self.ps(bank)
            for kc in range(8):
                P.mm(pp_[:, 0, 0:tn], wt[:, kc, c0:c0 + cw], hbt[:, kc, 0:tn], start=(kc == 0), stop=(kc == 7))
            if func is not None:
                P.act(dst, pp_[:, 0, 0:tn], func)
                return
            if lat_t0 is None:
                P.copy("act", dst, pp_[:, 0, 0:tn])
                return
            P.copy("act", kf[:, 0:tn], pp_[:, 0, 0:tn])
            pr_ = self.ps(2)
            P.mm(pr_[:, 0, 0:tn], pf[:], kf[:, 0:tn])
            a, b = t1[bank % 2], t2[bank % 2]
            P.tt("pool", a[:, 0:tn], kf[:, 0:tn], cos[:, lat_t0:lat_t0 + tn], ALU.mult)
            P.tt("dve", b[:, 0:tn], pr_[:, 0, 0:tn], sin[:, lat_t0:lat_t0 + tn], ALU.mult)
            P.tt("pool", dst, a[:, 0:tn], b[:, 0:tn], ALU.add)

        stage = int(os.environ.get("MK_ATT", "9"))
        if stage <= 0:
            return
        for ti, (t0, tn) in enumerate(TILES):
            h = hb[ti % 2]
            self.load_h(h, ti, "sp" if ti % 2 == 0 else "act")
            kc0 = kcol(t0)
            ma = int(os.environ.get("MK_A", "3"))
            if ti >= int(os.environ.get("MK_AT", "9")):
                break
            if ma & 1:
                proj(KB[:, kc0:kc0 + tn], wk, 0, 128, h, tn, None if ti == 0 else t0 - CTX, ti % 2)
            for bi in range(tn // 128):
                if not (ma & 2):
                    break
                vb = (t0 // 128 + bi)
                vb = vb if vb < 2 else vb + 1
                pv = self.ps(3 + bi % 2)
                for kc in range(8):
                    P.mm(pv[:, 0, 0:128], h[:, kc, bi * 128:(bi + 1) * 128], wv[:, kc, :], start=(kc == 0), stop=(kc == 7))
                P.copy("dve" if bi % 2 == 0 else "act", Vtm[:, vb, :], pv[:, 0, 0:128])

        import os
        stage = int(os.environ.get("MK_ATT", "9"))
        if stage <= 1:
            return
        qT = [P.sb([128, 3, 512], BF16, f"a_q{i}") for i in range(2)]
        gT = [P.sb([128, 3, 512], BF16, f"a_g{i}") for i in range(2)]
        om = [P.sb([128, 3, 512], BF16, f"a_om{i}") for i in range(2)]
        sc = [P.sb([128, 3, 641], F32, f"a_sc{i}") for i in range(2)]
        pe = [P.sb([128, 3, 641], F32, f"a_pe{i}") for i in range(2)]
        for kg in range(2):
            P.copy("dve", sc[kg][:, :, 640:641], sink8[:, kg * 3:kg * 3 + 3].m(lambda a: a.unsqueeze(2)))
        pn = [P.sb([128, 3, 640], BF16, f"a_pn{i}") for i in range(2)]
        pTs = [P.sb([128, 5, 128], BF16, f"a_pT{i}") for i in range(3)]
        st = [P.sb([128, 8, 3], F32, f"a_st{i}") for i in range(2)]
        it = 0
        npt = 0
        for ti, (t0, tn) in enumerate(TILES):
            if ti == 0 and not ctx_out:
                continue
            h = hb[ti % 2]
            self.load_h(h, ti, "sp" if ti % 2 == 0 else "act")
            q_, g_, o_ = qT[ti % 2], gT[ti % 2], om[ti % 2]
            for j in range(3):
                proj(q_[:, j, 0:tn], wq, j * 128, 128, h, tn, None if ti == 0 else t0 - CTX, j % 2)
                proj(g_[:, j, 0:tn], wg, j * 128, 128, h, tn, None, (j + 1) % 2, func=AF.Silu)
            if stage <= 2:
                break
            for bi in range(tn // 128):
                if stage <= 5 and (bi > 0 or ti > 1):
                    break
                qs = slice(bi * 128, bi * 128 + 128)
                isctx = (ti == 0)
                n = (t0 - CTX) // 128 + bi if not isctx else None
                lo = 384 if isctx else 0
                po = self.ps(2)
                KG = [(kg, slice(kg * 64, kg * 64 + 64), sc[kg], pe[kg], pn[kg], st[kg]) for kg in range(2)]
                chunks = ([] if isctx else [0, 1, 2]) + [3, 4]
                for kg, ps_, s_, e_, n_, stt_ in KG:
                    for j in range(3):
                        pb = self.ps(3 + 2 * kg)
                        pc = self.ps(4 + 2 * kg)
                        if not isctx:
                            kb0 = 256 + n * 128
                            P.mm(pb[:, 0, 0:384], q_[ps_, j, qs], KB[ps_, kb0:kb0 + 384])
                            var = 0 if n == 0 else (2 if n == 31 else 1)
                            P.tt("dve", s_[:, j, 0:384], pb[:, 0, 0:384], mband[:, var, :], ALU.add)
                        P.mm(pc[:, 0, 0:256], q_[ps_, j, qs], KB[ps_, 0:256])
                        P.copy("act", s_[:, j, 384:640], pc[:, 0, 0:256])
                for kg, ps_, s_, e_, n_, stt_ in KG:
                    P.reduce("dve", stt_[:, 0, :], s_[:, :, lo:641], ALU.max)
                    P.ts("dve", stt_[:, 2, :], stt_[:, 0, :], -0.125, None, ALU.mult)
                for kg, ps_, s_, e_, n_, stt_ in KG:
                    for j in range(3):
                        P.act(e_[:, j, lo:641], s_[:, j, lo:641], AF.Exp, bias=stt_[:, 2, j:j + 1], scale=0.125,
                              accum_out=stt_[:, 3, j:j + 1])
                for kg, ps_, s_, e_, n_, stt_ in KG:
                    P.recip(stt_[:, 7, :], stt_[:, 3, :])
                    P.tt("dve" if kg == 0 else "pool", n_[:, :, lo:640], e_[:, :, lo:640],
                         stt_[:, 7, :].m(lambda a: a.unsqueeze(2)).bc([128, 3, 640 - lo]), ALU.mult)
                for j in range(3):
                    for kg, ps_, s_, e_, n_, stt_ in KG:
                        pt = self.ps([7, 0, 1][npt % 3], 1, BF16)
                        for c in chunks:
                            P.tr(pt[:, 0, c * 128:(c + 1) * 128], n_[:, j, c * 128:(c + 1) * 128], self.c_identb[:])
                        pts = pTs[npt % 3]
                        npt += 1
                        c0 = chunks[0]
                        P.copy("act" if npt % 2 == 0 else "dve", pts[:, c0:5, :], pt[:, 0, c0 * 128:640].re("p (c t) -> p c t", t=128))
                        for ci, c in enumerate(chunks):
                            vb = (2 + n + c) if c < 3 else (c - 3)
                            P.mm(po[ps_, 0, j * 128:(j + 1) * 128], Vtm[:, vb, kg * 64:kg * 64 + 64], pts[:, c, :],
                                 start=(ci == 0), stop=(ci == len(chunks) - 1))
                P.tt("dve", o_[:, :, qs], po[:, 0, 0:384].re("p (j t) -> p j t", j=3), g_[:, :, qs], ALU.mult)
            for j in range(3):
                for hh in range(2):
                    hd = j + 3 * hh
                    P.dma("sp" if hh == 0 else "act",
                          self.mix.v(("ma", ti, hd), (slice(hd * 64, hd * 64 + 64), slice(t0, t0 + tn))),
                          o_[hh * 64:hh * 64 + 64, j, 0:tn])

    def load_h_halo(self, tile, ti, q="sp"):
        P = self.P
        t0, tn = TILES[ti]
        lo, hi = t0 - 1, t0 + tn + 1
        if ti <= 1:
            P.memset("pool", tile[:, :, 0:1], 0.0)
            lo = t0
        if ti == 0 or ti == len(TILES) - 1:
            P.memset("pool", tile[:, :, tn + 1:tn + 2], 0.0)
            hi = t0 + tn
        P.dma(q, tile[:, :, lo - t0 + 1:hi - t0 + 1], self.hT.v(("h", ti), (slice(None), slice(None), slice(lo, hi))))

    def scaled_w(self, l, c0, ncol, rows, name, roff=0):
        P = self.P
        rows = [r[:, roff:roff + ncol] for r in rows]
        w3 = P.sb([128, 3, 8, ncol], BF16, name)
        mk_ = P.mark()
        half = ncol // 2
        stg = [P.sb([128, 8, half], F32, f"{name}_st{i}") for i in range(2)]
        for hf in range(2):
            st = stg[hf]
            P.dma("sp" if hf == 0 else "act", st[:],
                  self.w_in.v(("w", l), fn=lambda ap, hf=hf: ap[l].rearrange("(kc p) n -> p kc n", p=128)[:, :, c0 + hf * half:c0 + (hf + 1) * half]))
            for tap in range(3):
                P.tt("dve" if tap != 1 else "pool", w3[:, tap, :, hf * half:(hf + 1) * half], st[:],
                     rows[tap][:, hf * half:(hf + 1) * half].m(lambda a: a.unsqueeze(1)).bc([128, 8, half]), ALU.mult)
        P.release(mk_)
        return w3

    def phase_ssd(self, l):
        P = self.P
        ctx_out = (l < DEPTH - 1)
        xsT = P.sb([128, 3, S], F32, "s_xsT")
        BT = P.sb([128, 2, S], BF16, "s_BT")
        CT = P.sb([128, 2, S], BF16, "s_CT")
        dt_tm = P.sb([128, NBLK, 12], F32, "s_dt")
        dtA_tm = P.sb([128, NBLK, 12], F32, "s_dtA")
        aneg = P.sb([128, 12], F32, "s_aneg")
        o = PR["alog"][0]
        P.act(aneg[:], self.prt[:, o:o + 12], AF.Exp)
        P.ts("dve", aneg[:], aneg[:], -1.0, None, ALU.mult)
        mk1 = P.mark()
        o = PR2["convr"][0]
        crow = P.sb([128, 3 * 896], F32, "s_crow")
        P.dma("sp", crow[:], self.pr2.v(l, fn=lambda a: a[l][:, o:o + 3 * 896]))
        rows = [crow[:, k * 896:(k + 1) * 896] for k in range(3)]
        wx3 = self.scaled_w(l, C_XBC, 896, rows, "s_wx3")
        wdt = P.sb([128, 8, 12], BF16, "s_wdt")
        P.dma("sp", wdt[:], self.wb(l, C_DT, 12))
        hh = [P.sb([128, 8, 514], BF16, f"s_hh{i}") for i in range(2)]
        dtmp = [P.sb([128, 12], F32, f"s_dtmp{i}") for i in range(2)]
        ocb = PP["convb"][0]
        nb = 0
        for ti, (t0, tn) in enumerate(TILES):
            h = hh[ti % 2]
            self.load_h_halo(h, ti, "sp" if ti % 2 == 0 else "act")
            for c in range(7):
                pb = self.ps(c % 2)
                n = 0
                for tap in range(3):
                    for kc in range(8):
                        P.mm(pb[:, 0, 0:tn], wx3[:, tap, kc, c * 128:(c + 1) * 128], h[:, kc, tap:tap + tn],
                             start=(n == 0), stop=(n == 23))
                        n += 1
                if c < 3:
                    dst = xsT[:, c, t0:t0 + tn]
                elif c < 5:
                    dst = BT[:, c - 3, t0:t0 + tn]
                else:
                    dst = CT[:, c - 5, t0:t0 + tn]
                P.act(dst, pb[:, 0, 0:tn], AF.Silu, bias=self.ppt[:, ocb + c:ocb + c + 1])
            for bi in range(tn // 128):
                blk = t0 // 128 + bi
                pd = self.ps(2 + blk % 2)
                for kc in range(8):
                    P.mm(pd[:, 0, 0:12], h[:, kc, 1 + bi * 128:1 + (bi + 1) * 128], wdt[:, kc, :], start=(kc == 0), stop=(kc == 7))
                d_ = dtmp[blk % 2]
                o = PR["dtb"][0]
                P.tt("dve", d_[:], pd[:, 0, 0:12], self.prt[:, o:o + 12], ALU.add)
                P.act(d_[:], d_[:], AF.Exp)
                P.act(dt_tm[:, blk, :], d_[:], AF.Ln, bias=1.0)
                P.tt("dve", dtA_tm[:, blk, :], dt_tm[:, blk, :], aneg[:], ALU.mult)
        P.release(mk1)
        if int(os.environ.get("MK_SSD", "9")) <= 1:
            self.dbg_xsT = xsT
            return
        wz = P.sb([128, 8, 384], BF16, "s_wz")
        P.dma("sp", wz[:], self.wb(l, C_Z, 384))
        cf = {}
        for nme in ("snf", "snb", "trif", "trib", "ones", "hlo", "hhi"):
            cf[nme] = P.sb([128, 128], F32, "s_c_" + nme)
            self.load_c(cf[nme], nme, "act")
        St = P.sb([128, 6, 64], F32, "s_S")
        NB2 = 2
        xs_sb = [P.sb([128, 384], F32, f"s_xs{i}") for i in range(NB2)]
        Bt_sb = [P.sb([128, 2, 128], BF16, f"s_Bt{i}") for i in range(NB2)]
        Gs = [P.sb([128, 2, 128], F32, f"s_G{i}") for i in range(NB2)]
        cumc = [P.sb([128, 6], F32, f"s_cc{i}") for i in range(NB2)]
        dbc = [P.sb([128, 6, 128], F32, f"s_dbc{i}") for i in range(NB2)]
        Dm = [P.sb([128, 6, 128], F32, f"s_D{i}") for i in range(NB2)]
        Em = [P.sb([128, 6, 128], F32, f"s_E{i}") for i in range(NB2)]
        scT = [P.sb([128, 6, 128], BF16, f"s_sc{i}") for i in range(NB2)]
        Er = [P.sb([128, 6, 128], F32, f"s_Er{i}") for i in range(NB2)]
        Cd = [P.sb([128, 6, 128], F32, f"s_Cd{i}") for i in range(NB2)]
        xdt = [P.sb([128, 6, 64], BF16, f"s_xdt{i}") for i in range(NB2)]
        xdtt = [P.sb([128, 6, 64], BF16, f"s_xdtt{i}") for i in range(NB2)]
        ysb = [P.sb([128, 3, 128], F32, f"s_y{i}") for i in range(NB2)]
        yfl = [P.sb([128, 3, 128], F32, f"s_yf{i}") for i in range(NB2)]
        hz = [P.sb([128, 8, 128], BF16, f"s_hz{i}") for i in range(NB2)]
        zs = [P.sb([128, 3, 128], F32, f"s_z{i}") for i in range(NB2)]
        sq = [P.sb([128, 3, 128], F32, f"s_sq{i}") for i in range(NB2)]
        rs = [P.sb([128, 2, 128], F32, f"s_rs{i}") for i in range(NB2)]
        mo = [P.sb([128, 3, 128], BF16, f"s_mo{i}") for i in range(NB2)]
        osd, onw = PP["ssmd"][0], PP["ssmnw"][0]
        it = 0
        for d in range(2):
            order = list(range(NBLK)) if d == 0 else [1, 0] + list(range(NBLK - 1, 1, -1))
            tri = cf["trif"] if d == 0 else cf["trib"]
            sneg = cf["snf"] if d == 0 else cf["snb"]
            last = 127 if d == 0 else 0
            P.memset("dve", St[:], 0.0)
            pending = None
            for c in order:
                if d == 1 and c < 2 and not ctx_out:
                    pass
                k = it % NB2
                it += 1
                cs_ = slice(c * 128, (c + 1) * 128)
                pxs = self.ps(0)
                for j in range(3):
                    P.tr(pxs[:, 0, j * 128:(j + 1) * 128], xsT[:, j, cs_], self.c_ident[:])
                P.copy("act", xs_sb[k][:], pxs[:, 0, 0:384])
                pbt = self.ps(1, 1, BF16)
                for g in range(2):
                    P.tr(pbt[:, 0, g * 128:(g + 1) * 128], BT[:, g, cs_], self.c_identb[:])
                P.copy("dve", Bt_sb[k][:].re("p g n -> p (g n)"), pbt[:, 0, 0:256])
                pg = self.ps(2)
                for g in range(2):
                    P.mm(pg[:, 0, g * 128:(g + 1) * 128], BT[:, g, cs_], CT[:, g, cs_])
                P.copy("act", Gs[k][:].re("p g n -> p (g n)"), pg[:, 0, 0:256])
                dA = dtA_tm[:, c, d * 6:(d + 1) * 6]
                P.mm(pg[:, 0, 256:262], tri[:], dA)
                P.copy("dve", cumc[k][:], pg[:, 0, 256:262])
                P.copy("pool", dbc[k][:], dA.m(lambda a: a.unsqueeze(2)).bc([128, 6, 128]))
                pcr = self.ps(3, 2)
                for h in range(6):
                    P.mm(pcr[:, h // 4, (h % 4) * 128:(h % 4 + 1) * 128], dbc[k][:, h, :], tri[:])
                pcr_v = pcr.re("p b n -> p (b n)")[:, 0:768].re("p (h n) -> p h n", h=6)
                P.tt("dve", Dm[k][:], pcr_v, cumc[k][:].m(lambda a: a.unsqueeze(2)).bc([128, 6, 128]), ALU.subtract)
                P.act(Er[k][:], pcr_v, AF.Exp)
                P.tt("pool", Dm[k][:], Dm[k][:], sneg[:].m(lambda a: a.unsqueeze(1)).bc([128, 6, 128]), ALU.add)
                P.act(Em[k][:], Dm[k][:], AF.Exp)
                P.tt("pool", scT[k][:].re("p (g h) n -> p g h n", g=2), Em[k][:].re("p (g h) n -> p g h n", g=2),
                     Gs[k][:].m(lambda a: a.unsqueeze(2)).bc([128, 2, 3, 128]), ALU.mult)
                P.tt("dve", Cd[k][:].re("p (g h) n -> p g h n", g=2), Er[k][:].re("p (g h) n -> p g h n", g=2),
                     CT[:, :, cs_].m(lambda a: a.unsqueeze(2)).bc([128, 2, 3, 128]), ALU.mult)
                dtc = dt_tm[:, c, d * 6:(d + 1) * 6]
                P.tt("dve", xdt[k][:], xs_sb[k][:].re("p (h q) -> p h q", h=6),
                     dtc.m(lambda a: a.unsqueeze(2)).bc([128, 6, 64]), ALU.mult)
                P.tt("pool", xdtt[k][:], xdt[k][:], Em[k][:, :, last:last + 1].bc([128, 6, 64]), ALU.mult)
                def back(c=c, k=k, cs_=cs_, d=d, last=last):
                    py = self.ps(5)
                    for h in range(6):
                        o_ = py[(h % 2) * 64:(h % 2) * 64 + 64, 0, (h // 2) * 128:(h // 2 + 1) * 128]
                        P.mm(o_, xdt[k][:, h, :], scT[k][:, h, :], start=True, stop=False)
                        P.mm(o_, St[:, h, :], Cd[k][:, h, :], start=False, stop=True)
                    pcs = self.ps(6)
                    for h in range(6):
                        P.mm(pcs[:, 0, h * 64:(h + 1) * 64], Bt_sb[k][:, h // 3, :], xdtt[k][:, h, :])
                    P.tt("dve", St[:], St[:], Er[k][:, :, last:last + 1].bc([128, 6, 64]), ALU.mult)
                    P.tt("dve", St[:], St[:], pcs[:, 0, 0:384].re("p (h q) -> p h q", h=6), ALU.add)
                    if d == 0:
                        P.copy("act", ysb[k][:].re("p j n -> p (j n)"), py[:, 0, 0:384])
                        P.dma("sp", self.yf.v(("yf", c), fn=lambda ap, cs_=cs_: ap.rearrange("(j p) t -> p j t", p=128)[:, :, cs_]), ysb[k][:])
                        return
                    if c < 2 and not ctx_out:
                        return
                    P.dma("act", yfl[k][:], self.yf.v(("yf", c), fn=lambda ap, cs_=cs_: ap.rearrange("(j p) t -> p j t", p=128)[:, :, cs_]))
                    P.dma("sp", hz[k][:], self.hT.v(("h", "z"), (slice(None), slice(None), cs_)))
                    pz = self.ps(7)
                    for j in range(3):
                        for kc in range(8):
                            P.mm(pz[:, 0, j * 128:(j + 1) * 128], wz[:, kc, j * 128:(j + 1) * 128], hz[k][:, kc, :],
                                 start=(kc == 0), stop=(kc == 7))
                    P.act(zs[k][:].re("p j n -> p (j n)"), pz[:, 0, 0:384], AF.Silu)
                    y_ = ysb[k]
                    P.tt("dve", y_[:].re("p j n -> p (j n)"), py[:, 0, 0:384], yfl[k][:].re("p j n -> p (j n)"), ALU.add)
                    P.tt("pool", yfl[k][:], xsT[:, :, cs_], self.ppt[:, osd:osd + 3].m(lambda a: a.unsqueeze(2)).bc([128, 3, 128]), ALU.mult)
                    P.tt("pool", y_[:], y_[:], yfl[k][:], ALU.add)
                    P.tt("dve", y_[:], y_[:], zs[k][:], ALU.mult)
                    P.act(sq[k][:], y_[:], AF.Square)
                    pgs = self.ps(7)
                    P.mm(pgs[:, 0, 384:512].m(lambda a: a), cf["ones"][:], sq[k][:, 0, :], start=True, stop=False)
                    P.mm(pgs[:, 0, 384:512], cf["hlo"][:], sq[k][:, 1, :], start=False, stop=True)
                    pgs2 = self.ps(6)
                    P.mm(pgs2[:, 0, 384:512], cf["hhi"][:], sq[k][:, 1, :], start=True, stop=False)
                    P.mm(pgs2[:, 0, 384:512], cf["ones"][:], sq[k][:, 2, :], start=False, stop=True)
                    P.act(rs[k][:, 0, :], pgs[:, 0, 384:512], AF.Sqrt, bias=1e-5, scale=1.0 / 192)
                    P.act(rs[k][:, 1, :], pgs2[:, 0, 384:512], AF.Sqrt, bias=1e-5, scale=1.0 / 192)
                    P.recip(rs[k][:], rs[k][:])
                    m_ = mo[k]
                    P.stt("dve", m_[:, 0, :], y_[:, 0, :], self.ppt[:, onw:onw + 1], rs[k][:, 0, :], ALU.mult, ALU.mult)
                    P.stt("dve", m_[0:64, 1, :], y_[0:64, 1, :], self.ppt[0:64, onw + 1:onw + 2], rs[k][0:64, 0, :], ALU.mult, ALU.mult)
                    P.stt("dve", m_[64:128, 1, :], y_[64:128, 1, :], self.ppt[64:128, onw + 1:onw + 2], rs[k][64:128, 1, :], ALU.mult, ALU.mult)
                    P.stt("dve", m_[:, 2, :], y_[:, 2, :], self.ppt[:, onw + 2:onw + 3], rs[k][:, 1, :], ALU.mult, ALU.mult)
                    P.dma("sp", self.mix.v(("ms", c), fn=lambda ap, cs_=cs_: ap[640:1024].rearrange("(j p) t -> p j t", p=128)[:, :, cs_]), m_[:])
                if pending is not None:
                    pending()
                pending = back
            if pending is not None:
                pending()

    def phase_rwkv(self, l):
        P = self.P
        ctx_out = (l < DEPTH - 1)
        LWC = -0.6065306597126334
        cf = {}
        for nme in ("blk", "lowf", "lowb"):
            cf[nme] = P.sb([128, 128], F32, "r_c_" + nme)
            self.load_c(cf[nme], nme, "act")
        m4 = {nme: P.sb([128, 2, 256], F32, "r_m4_" + nme) for nme in ("rmf", "rmb")}
        mk0 = P.mark()
        for nme in ("rmf", "rmb"):
            t_ = P.sb([128, 256], F32, "r_c_" + nme)
            self.load_c(t_, nme, "act")
            P.copy("dve", m4[nme][:], t_[:].m(lambda a: a.unsqueeze(1)).bc([128, 2, 256]))
        P.release(mk0)
        ones64 = P.sb([128, 64], F32, "r_ones")
        P.memset("pool", ones64[:], 1.0)
        omka = P.sb([128, 4], F32, "r_omka")
        oka = PP["ka"][0]
        P.ts("dve", omka[:], self.ppt[:, oka:oka + 4], -1.0, 1.0, ALU.mult, ALU.add)
        wupf = P.sb([128, 256], F32, "r_wupf")
        aupf = P.sb([128, 256], F32, "r_aupf")
        wupb = P.sb([128, 256], BF16, "r_wupb")
        aupb = P.sb([128, 256], BF16, "r_aupb")
        for d in range(2):
            P.dma("sp", wupf[d * 64:(d + 1) * 64, :], self.wup.v(("u", l, d), fn=lambda ap, d=d: ap[l][d]))
            P.dma("act", aupf[d * 64:(d + 1) * 64, :], self.aup.v(("u", l, d), fn=lambda ap, d=d: ap[l][d]))
        P.copy("dve", wupb[:], wupf[:])
        P.copy("pool", aupb[:], aupf[:])
        wdT = P.sb([128, S], BF16, "r_wdT")
        adT = P.sb([128, S], BF16, "r_adT")
        mk1 = P.mark()
        rows = self.shift_rows(l)
        hh = [P.sb([128, 8, 514], BF16, f"r_hh{i}") for i in range(2)]
        wl3 = self.scaled_w(l, C_RW + 768, 256, rows, "r_wl3", roff=768)
        for ti, (t0, tn) in enumerate(TILES):
            h = hh[ti % 2]
            self.load_h_halo(h, ti, "sp" if ti % 2 == 0 else "act")
            for c in range(2):
                pb = self.ps(c)
                n = 0
                for tap in range(3):
                    for kc in range(8):
                        P.mm(pb[:, 0, 0:tn], wl3[:, tap, kc, c * 128:(c + 1) * 128], h[:, kc, tap:tap + tn],
                             start=(n == 0), stop=(n == 23))
                        n += 1
                if c == 0:
                    P.act(wdT[:, t0:t0 + tn], pb[:, 0, 0:tn], AF.Tanh)
                else:
                    P.copy("act", adT[:, t0:t0 + tn], pb[:, 0, 0:tn])
        P.release(mk1)
        RW = int(os.environ.get("MK_RW", "99"))
        if RW <= 1:
            return
        mk_hp = P.mark()
        for hp in range(2):
            rT = P.sb([128, S], F32, "r_rT")
            kT = P.sb([128, S], F32, "r_kT")
            vT = P.sb([128, S], F32, "r_vT")
            ysum = P.sb([128, S], F32, "r_ysum")
            mk2 = P.mark()
            rows = self.shift_rows(l)
            hh = [P.sb([128, 8, 514], BF16, f"r_hh{i}") for i in range(2)]
            w3 = [self.scaled_w(l, C_RW + j * 256 + hp * 128, 128, rows, f"r_w3{j}", roff=j * 256 + hp * 128) for j in range(3)]
            for ti, (t0, tn) in enumerate(TILES):
                h = hh[ti % 2]
                self.load_h_halo(h, ti, "sp" if ti % 2 == 0 else "act")
                for j, dst in enumerate((rT, kT, vT)):
                    pb = self.ps(j % 2)
                    n = 0
                    for tap in range(3):
                        for kc in range(8):
                            P.mm(pb[:, 0, 0:tn], w3[j][:, tap, kc, :], h[:, kc, tap:tap + tn], start=(n == 0), stop=(n == 23))
                            n += 1
                    P.copy("act", dst[:, t0:t0 + tn], pb[:, 0, 0:tn])
            P.release(mk2)
            if RW <= 2:
                return
            mk3 = P.mark()
            self.rwkv_scan(l, hp, rT, kT, vT, wdT, adT, ysum, wupb, aupb, cf, m4, ones64, omka, LWC)
            P.release(mk3)
            self.rwkv_finish(l, hp, rT, kT, vT, ysum, cf, ctx_out)
            P.release(mk_hp)

    def shift_rows(self, l):
        P = self.P
        o = PR2["shr"][0]
        sr = P.sb([128, 3 * 1024], F32, "r_shr")
        P.dma("sp", sr[:], self.pr2.v(l, fn=lambda a: a[l][:, o:o + 3 * 1024]))
        r0, r2, r1 = sr[:, 0:1024], sr[:, 1024:2048], sr[:, 2048:3072]
        P.tt("dve", r1, r0, r2, ALU.add)
        P.ts("dve", r1, r1, -1.0, 1.0, ALU.mult, ALU.add)
        return [r0, r1, r2]

    def rwkv_scan(self, l, hp, rT, kT, vT, wdT, adT, ysum, wupb, aupb, cf, m4, ones64, omka, LWC):
        P = self.P
        NB = 2
        f32t = lambda n, w=512: [P.sb([128, w], F32, f"r_{n}")] * NB
        sg, aa, cum = f32t("sg"), f32t("aa"), f32t("cum")
        lw = sg
        cumx = [P.sb([128, 8], F32, "r_tot")] * NB
        e1, e2, e3, e4 = f32t("e1"), f32t("e2"), f32t("e3"), f32t("e4")
        kkr, sqk, nrm, kd, bq = f32t("kkr"), f32t("sqk"), f32t("nrm"), f32t("kd"), f32t("bq")
        kk, tmpa = kkr, nrm
        WC = [P.sb([128, 8], F32, f"r_WC{i}") for i in range(NB)]
        arb = [P.sb([128, 4, 2, 128], F32, "r_arb")] * NB
        kb = [P.sb([128, 512], F32, "r_kb")] * NB
        bb = [P.sb([128, 512], F32, "r_bb")] * NB
        kt = [P.sb([128, 512], F32, "r_kt")] * NB
        btl = [P.sb([128, 512], F32, "r_btl")] * NB
        G4 = 4
        TM = [P.sb([128, 4, 128], F32, f"r_TM{i}") for i in range(G4)]
        AT = [P.sb([128, 2, 4, 128], F32, f"r_AT{i}") for i in range(G4)]
        X1 = [P.sb([128, 2, 128], F32, f"r_X1{i}") for i in range(G4)]
        XX = [P.sb([128, 2, 2, 128], F32, f"r_XX{i}") for i in range(G4)]
        Qf = [P.sb([128, 2, 128], F32, f"r_Qf{i}") for i in range(G4)]
        W1 = [P.sb([128, 128], F32, f"r_W1{i}") for i in range(G4)]
        AU = [P.sb([128, 256], F32, f"r_AU{i}") for i in range(G4)]
        Y0T = [P.sb([128, 128], F32, f"r_Y0T{i}") for i in range(G4)]
        RhT = [P.sb([128, 128], F32, f"r_RhT{i}") for i in range(G4)]
        MTt = [P.sb([128, 128], F32, "r_MTt")] * 2
        MT = [[P.sb([128, 128], F32, f"r_MT{i}_{c}") for c in range(2)] for i in range(G4)]
        Ns = [[P.sb([128, 64], F32, f"r_Ns{i}_{c}") for c in range(2)] for i in range(G4)]
        ytmp = [P.sb([128, 64], F32, f"r_yt{i}") for i in range(2)]
        Sseq = P.sb([128, 4, 64], F32, "r_Sseq")
        ow0, oa0, okk, oka = PP["w0"][0], PP["a0"][0], PP["kk"][0], PP["ka"][0]
        if os.environ.get("MK_MEM"):
            print("rwkv_scan arena used", P._aoff, "of", P.ARENA)
        nblk_done = 0
        for d in range(2):
            if d == 1 and hp == 0 and os.environ.get("MK_RWDBG"):
                if not hasattr(self, "dbgy"):
                    self.dbgy = P.dram("dbgy", [128, S], F32, kind="ExternalOutput")
                P.dma("sp", self.dbgy.v(0), ysum[:])
            col = d * 2 + hp
            mask4 = m4["rmf"] if d == 0 else m4["rmb"]
            low = cf["lowf"] if d == 0 else cf["lowb"]
            P.memset("dve", Sseq[:, 0, :], 0.0)
            si = 0
            tiles = list(range(len(TILES))) if d == 0 else [0] + list(range(len(TILES) - 1, 0, -1))
            def prepA(tix):
                ti = tiles[tix]
                t0, tn = TILES[ti]
                nch = tn // 64
                nbk = tn // 128
                k_ = tix % NB
                ts_ = slice(t0, t0 + tn)
                ds_ = slice(d * 64, (d + 1) * 64)
                pxw, pxa = self.ps(0), self.ps(1)
                P.mm(pxw[:, 0, 0:tn], wupb[ds_, hp * 128:(hp + 1) * 128], wdT[ds_, ts_])
                P.act(sg[k_][:, 0:tn], pxw[:, 0, 0:tn], AF.Sigmoid, bias=self.ppt[:, ow0 + col:ow0 + col + 1])
                P.mm(pxa[:, 0, 0:tn], aupb[ds_, hp * 128:(hp + 1) * 128], adT[ds_, ts_])
                P.act(aa[k_][:, 0:tn], pxa[:, 0, 0:tn], AF.Sigmoid, bias=self.ppt[:, oa0 + col:oa0 + col + 1])
                yield
                P.ts("dve", lw[k_][:, 0:tn], sg[k_][:, 0:tn], LWC, None, ALU.mult)
                yield
                for c in range(nch):
                    cs = slice(c * 64, (c + 1) * 64)
                    P.scan("dve", cum[k_][:, cs], ones64[:], lw[k_][:, cs], 0.0, ALU.mult, ALU.add)
                    yield
                c3 = lambda t_: t_[:, 0:tn].re("p (c n) -> p c n", n=64)
                P.act(WC[k_][:, 0:nch].m(lambda a: a.unsqueeze(2)), c3(cum[k_])[:, :, 63:64], AF.Exp)
                yield
                if d == 1:
                    P.copy("pool", cumx[k_][:, 0:nch].m(lambda a: a.unsqueeze(2)), c3(cum[k_])[:, :, 63:64])
                    yield
                    P.tt("dve", cum[k_][:, 0:tn], lw[k_][:, 0:tn], cum[k_][:, 0:tn], ALU.subtract)
                    yield
                    P.tt("dve", c3(cum[k_]), c3(cum[k_]), cumx[k_][:, 0:nch].m(lambda a: a.unsqueeze(2)).bc([128, nch, 64]), ALU.add)
                    yield
                    tot = None
                P.tt("pool", e4[k_][:, 0:tn], cum[k_][:, 0:tn], lw[k_][:, 0:tn], ALU.subtract)
                yield
                P.act(e1[k_][:, 0:tn], cum[k_][:, 0:tn], AF.Exp)
                yield
                P.act(e2[k_][:, 0:tn], cum[k_][:, 0:tn], AF.Exp, scale=-1.0)
                yield
                P.act(e3[k_][:, 0:tn], e4[k_][:, 0:tn], AF.Exp)
                yield
                P.tt("dve", c3(e4[k_]), c3(e2[k_]), WC[k_][:, 0:nch].m(lambda a: a.unsqueeze(2)).bc([128, nch, 64]), ALU.mult)
                yield
                P.ts("dve", kkr[k_][:, 0:tn], kT[:, ts_], self.ppt[:, okk + col:okk + col + 1], None, ALU.mult)
                yield
                P.act(sqk[k_][:, 0:tn], kkr[k_][:, 0:tn], AF.Square)
                yield
                pss = self.ps(1)
                P.mm(pss[:, 0, 0:tn], cf["blk"][:], sqk[k_][:, 0:tn])
                P.act(nrm[k_][:, 0:tn], pss[:, 0, 0:tn], AF.Sqrt)
                yield
                P.ts("dve", nrm[k_][:, 0:tn], nrm[k_][:, 0:tn], 1e-12, None, ALU.max)
                yield
                P.recip(nrm[k_][:, 0:tn], nrm[k_][:, 0:tn])
                yield
                P.tt("dve", kk[k_][:, 0:tn], kkr[k_][:, 0:tn], nrm[k_][:, 0:tn], ALU.mult)
                yield
                P.ts("pool", tmpa[k_][:, 0:tn], aa[k_][:, 0:tn], self.ppt[:, oka + col:oka + col + 1], omka[:, col:col + 1], ALU.mult, ALU.add)
                yield
                P.tt("pool", kd[k_][:, 0:tn], kT[:, ts_], tmpa[k_][:, 0:tn], ALU.mult)
                yield
                P.tt("dve", bq[k_][:, 0:tn], kk[k_][:, 0:tn], aa[k_][:, 0:tn], ALU.mult)
                yield
            def prepB(tix):
                ti = tiles[tix]
                t0, tn = TILES[ti]
                nch = tn // 64
                nbk = tn // 128
                k_ = tix % NB
                ts_ = slice(t0, t0 + tn)
                ds_ = slice(d * 64, (d + 1) * 64)
                c3 = lambda t_: t_[:, 0:tn].re("p (c n) -> p c n", n=64)
                b3 = lambda t_: t_[:, 0:tn].re("p (b n) -> p b n", n=128)
                P.ts("pool", sqk[k_][:, 0:tn], kk[k_][:, 0:tn], -1.0, None, ALU.mult)
                yield
                P.tt("dve", arb[k_][:, 0:nbk, 0, :], b3(sqk[k_]), b3(e3[k_]), ALU.mult)
                yield
                P.tt("pool", arb[k_][:, 0:nbk, 1, :], rT[:, ts_].re("p (b n) -> p b n", n=128), b3(e1[k_]), ALU.mult)
                yield
                P.tt("dve", kb[k_][:, 0:tn], kd[k_][:, 0:tn], e2[k_][:, 0:tn], ALU.mult)
                yield
                P.tt("pool", bb[k_][:, 0:tn], bq[k_][:, 0:tn], e2[k_][:, 0:tn], ALU.mult)
                yield
                P.tt("dve", kt[k_][:, 0:tn], kd[k_][:, 0:tn], e4[k_][:, 0:tn], ALU.mult)
                yield
                P.tt("pool", btl[k_][:, 0:tn], bq[k_][:, 0:tn], e4[k_][:, 0:tn], ALU.mult)
                yield
            def step(gen, n=10 ** 9):
                for _ in range(n):
                    if next(gen, 'done') == 'done':
                        return
            step(prepA(0))
            step(prepB(0))
            for tix, ti in enumerate(tiles):
                t0, tn = TILES[ti]
                nch = tn // 64
                nbk = tn // 128
                k_ = tix % NB
                ts_ = slice(t0, t0 + tn)
                ds_ = slice(d * 64, (d + 1) * 64)
                nxtA = prepA(tix + 1) if tix + 1 < len(tiles) else iter(())
                blocks = list(range(nbk)) if d == 0 else list(range(nbk - 1, -1, -1))
                G = len(blocks)
                idm = lambda t_: t_[:].m(lambda a: a.unsqueeze(1)).bc([128, 2, 128])
                ev = 0
                for g, bi in enumerate(blocks):
                    bs = slice(bi * 128, (bi + 1) * 128)
                    pt = self.ps(2 + g)
                    P.tr(pt[:, 0, 0:128], arb[k_][:, bi, 0, :], self.c_ident[:])
                    P.tr(pt[:, 0, 128:256], btl[k_][:, bs], self.c_ident[:])
                    P.tr(pt[:, 0, 256:384], kt[k_][:, bs], self.c_ident[:])
                    P.tr(pt[:, 0, 384:512], vT[:, t0 + bi * 128:t0 + (bi + 1) * 128], self.c_ident[:])
                for g, bi in enumerate(blocks):
                    P.copy("act" if g % 2 == 0 else "dve", TM[g][:].re("p a n -> p (a n)"), self.ps(2 + g)[:, 0, 0:512])
                px2 = self.ps(6, 2)
                for half in range(0, G, 2):
                    gs = list(range(half, min(half + 2, G)))
                    for g in gs:
                        bi = blocks[g]
                        bs = slice(bi * 128, (bi + 1) * 128)
                        pa = self.ps(2 + 2 * (g % 2), 2)
                        for h in range(2):
                            hs = slice(h * 64, (h + 1) * 64)
                            ar_ = arb[k_][hs, bi, :, :].re("p a n -> p (a n)")
                            P.mm(pa[:, h, 0:256], bb[k_][hs, bs], ar_)
                            P.mm(pa[:, h, 256:512], kb[k_][hs, bs], ar_)
                            P.mm(px2[:, h, g * 128:(g + 1) * 128], arb[k_][hs, bi, 0, :], bb[k_][hs, bs])
                    for g in gs:
                        pa = self.ps(2 + 2 * (g % 2), 2)
                        P.tt("dve", AT[g][:].re("p h a n -> p h (a n)"), pa,
                             mask4[:].re("p a n -> p (a n)").m(lambda a: a.unsqueeze(1)).bc([128, 2, 512]), ALU.mult)
                for g in range(G):
                    P.tt("dve", X1[g][:], px2[:, :, g * 128:(g + 1) * 128], idm(low), ALU.mult)
                    P.tt("pool", Qf[g][:], AT[g][:, :, 0, :], idm(self.c_ident), ALU.add)
                xk = [[X1[g][:, h, :] for h in range(2)] for g in range(G)]
                xtk = [[AT[g][:, h, 0, :] for h in range(2)] for g in range(G)]
                for lev in range(int(os.environ.get('MK_NLEV', '5'))):
                    for g in range(G):
                        pn = self.ps(2 + g)
                        for h in range(2):
                            P.mm(pn[:, 0, h * 256:h * 256 + 128], xtk[g][h], xk[g][h])
                            if lev < 4:
                                P.mm(pn[:, 0, h * 256 + 128:h * 256 + 256], xk[g][h], xtk[g][h])
                    for g in range(G):
                        P.copy("act", XX[g][:].re("p h a n -> p (h a n)"), self.ps(2 + g)[:, 0, :])
                        xk[g] = [XX[g][:, h, 0, :] for h in range(2)]
                        xtk[g] = [XX[g][:, h, 1, :] for h in range(2)]
                    for g in range(G):
                        pq = self.ps(6 + g // 2)
                        for h in range(2):
                            c0 = (g % 2) * 256 + h * 128
                            P.mm(pq[:, 0, c0:c0 + 128], xk[g][h], Qf[g][:, h, :])
                    for g in range(G):
                        pq = self.ps(6 + g // 2)
                        c0 = (g % 2) * 256
                        P.tt("dve", Qf[g][:], pq[:, 0, c0:c0 + 256].re("p (h n) -> p h n", h=2), Qf[g][:], ALU.add)
                    step(nxtA, 7)
                pw = self.ps(0)
                for g in range(G):
                    for h in range(2):
                        P.mm(pw[:, 0, g * 128 + h * 64:g * 128 + (h + 1) * 64], AT[g][:, h, 2, :], TM[g][:, 3, h * 64:(h + 1) * 64])
                for g in range(G):
                    P.copy("act", W1[g][:], pw[:, 0, g * 128:(g + 1) * 128])
                for g in range(G):
                    pau = self.ps(2 + g // 2)
                    c0 = (g % 2) * 256
                    for h in range(2):
                        P.mm(pau[:, 0, c0 + h * 64:c0 + (h + 1) * 64], Qf[g][:, h, :], TM[g][:, 0, h * 64:(h + 1) * 64])
                        P.mm(pau[:, 0, c0 + 128 + h * 64:c0 + 128 + (h + 1) * 64], Qf[g][:, h, :], W1[g][:, h * 64:(h + 1) * 64])
                for g in range(G):
                    pau = self.ps(2 + g // 2)
                    c0 = (g % 2) * 256
                    P.copy("act" if g % 2 == 0 else "dve", AU[g][:], pau[:, 0, c0:c0 + 256])
                for g, bi in enumerate(blocks):
                    py = self.ps(4 + g // 2)
                    c0 = (g % 2) * 256
                    for h in range(2):
                        hs = slice(h * 64, (h + 1) * 64)
                        P.mm(py[hs, 0, c0:c0 + 128], AU[g][:, 128 + h * 64:128 + (h + 1) * 64], AT[g][:, h, 1, :], start=True, stop=False)
                        P.mm(py[hs, 0, c0:c0 + 128], TM[g][:, 3, h * 64:(h + 1) * 64], AT[g][:, h, 3, :], start=False, stop=True)
                        P.mm(py[hs, 0, c0 + 128:c0 + 256], AU[g][:, h * 64:(h + 1) * 64], AT[g][:, h, 1, :])
                for g, bi in enumerate(blocks):
                    py = self.ps(4 + g // 2)
                    c0 = (g % 2) * 256
                    P.copy("act", Y0T[g][:], py[:, 0, c0:c0 + 128])
                    P.tt("dve", RhT[g][:], py[:, 0, c0 + 128:c0 + 256], arb[k_][:, bi, 1, :], ALU.add)
                step(nxtA)
                if tix + 1 < len(tiles):
                    step(prepB(tix + 1))
                for g, bi in enumerate(blocks):
                    for cc in range(2):
                        cs = slice(cc * 64, (cc + 1) * 64)
                        pmn = self.ps((6 if g < 2 else 2) + cc)
                        c0 = (g % 2) * 192
                        P.mm(pmn[:, 0, c0:c0 + 128], AU[g][cs, 0:128], TM[g][cs, 1, :])
                        for h in range(2):
                            hs = slice(h * 64, (h + 1) * 64)
                            P.mm(pmn[hs, 0, c0 + 128:c0 + 192], TM[g][cs, 1, hs], AU[g][cs, 128 + h * 64:128 + (h + 1) * 64], start=True, stop=False)
                            P.mm(pmn[hs, 0, c0 + 128:c0 + 192], TM[g][cs, 2, hs], TM[g][cs, 3, hs], start=False, stop=True)
                for g, bi in enumerate(blocks):
                    for cc in range(2):
                        pmn = self.ps((6 if g < 2 else 2) + cc)
                        c0 = (g % 2) * 192
                        mt_ = MTt[cc]
                        P.tt("dve", mt_[:], pmn[:, 0, c0:c0 + 128], cf["blk"][:], ALU.mult)
                        wcol = bi * 2 + cc
                        P.stt("dve", MT[g][cc][:], self.c_ident[:], WC[k_][:, wcol:wcol + 1], mt_[:], ALU.mult, ALU.add)
                        P.copy("act", Ns[g][cc][:], pmn[:, 0, c0 + 128:c0 + 192])
                for g, bi in enumerate(blocks if not os.environ.get('MK_NOSEQ') else []):
                    for cc in ([0, 1] if d == 0 else [1, 0]):
                        cs = slice(cc * 64, (cc + 1) * 64)
                        tok = slice(t0 + bi * 128 + cc * 64, t0 + bi * 128 + cc * 64 + 64)
                        pyh = [self.ps(4), self.ps(5)]
                        pss_ = self.ps(0)
                        for h in range(2):
                            hs = slice(h * 64, (h + 1) * 64)
                            P.mm(pyh[h][hs, 0, 0:64], Sseq[hs, si % 4, :], RhT[g][hs, cs])
                        P.mm(pss_[:, 0, 0:64], MT[g][cc][:], Sseq[:, si % 4, :])
                        P.tt("dve", Sseq[:, (si + 1) % 4, :], pss_[:, 0, 0:64], Ns[g][cc][:], ALU.add)
                        for h in range(2):
                            hs = slice(h * 64, (h + 1) * 64)
                            if d == 0:
                                P.tt("pool" if False else "dve", ysum[hs, tok], pyh[h][hs, 0, 0:64], Y0T[g][hs, cs], ALU.add)
                            else:
                                yt_ = ytmp[cc]
                                P.tt("dve", yt_[hs, :], pyh[h][hs, 0, 0:64], Y0T[g][hs, cs], ALU.add)
                                P.tt("pool", ysum[hs, tok], ysum[hs, tok], yt_[hs, :], ALU.add)
                        si += 1

    def rwkv_finish(self, l, hp, rT, kT, vT, ysum, cf, ctx_out):
        P = self.P
        wg = P.sb([128, 8, 128], BF16, "rf_wg")
        P.dma("sp", wg[:], self.wb(l, C_RG + hp * 128, 128))
        hb = [P.sb([128, 8, 512], BF16, f"rf_h{i}") for i in range(2)]
        f = lambda n: [P.sb([128, 512], F32, f"rf_{n}{i}") for i in range(2)]
        yc, sq, rstd, rk, gs = f("yc"), f("sq"), f("rstd"), f("rk"), f("gs")
        mo = [P.sb([128, 512], BF16, f"rf_mo{i}") for i in range(2)]
        olw, olb, ork = PP["lnw"][0] + hp, PP["lnb"][0] + hp, PP["rk"][0] + hp
        for ti, (t0, tn) in enumerate(TILES):
            if ti == 0 and not ctx_out:
                continue
            k_ = ti % 2
            ts_ = slice(t0, t0 + tn)
            self.load_h(hb[k_], ti, "sp" if k_ == 0 else "act")
            pg = self.ps(0)
            for kc in range(8):
                P.mm(pg[:, 0, 0:tn], wg[:, kc, :], hb[k_][:, kc, 0:tn], start=(kc == 0), stop=(kc == 7))
            P.act(gs[k_][:, 0:tn], pg[:, 0, 0:tn], AF.Silu)
            pm = self.ps(1)
            P.mm(pm[:, 0, 0:tn], cf["blk"][:], ysum[:, ts_])
            P.stt("dve", yc[k_][:, 0:tn], pm[:, 0, 0:tn], -1.0 / 64, ysum[:, ts_], ALU.mult, ALU.add)
            P.act(sq[k_][:, 0:tn], yc[k_][:, 0:tn], AF.Square)
            pv = self.ps(2)
            P.mm(pv[:, 0, 0:tn], cf["blk"][:], sq[k_][:, 0:tn])
            P.act(rstd[k_][:, 0:tn], pv[:, 0, 0:tn], AF.Sqrt, bias=64e-5, scale=1.0 / 64)
            P.recip(rstd[k_][:, 0:tn], rstd[k_][:, 0:tn])
            P.tt("dve", yc[k_][:, 0:tn], yc[k_][:, 0:tn], rstd[k_][:, 0:tn], ALU.mult)
            P.ts("dve", yc[k_][:, 0:tn], yc[k_][:, 0:tn], self.ppt[:, olw:olw + 1], self.ppt[:, olb:olb + 1], ALU.mult, ALU.add)
            P.stt("dve", rk[k_][:, 0:tn], rT[:, ts_], self.ppt[:, ork:ork + 1], kT[:, ts_], ALU.mult, ALU.mult)
            pb = self.ps(3)
            P.mm(pb[:, 0, 0:tn], cf["blk"][:], rk[k_][:, 0:tn])
            P.tt("dve", rk[k_][:, 0:tn], pb[:, 0, 0:tn], vT[:, ts_], ALU.mult)
            P.tt("pool", yc[k_][:, 0:tn], yc[k_][:, 0:tn], rk[k_][:, 0:tn], ALU.add)
            P.tt("dve", mo[k_][:, 0:tn], yc[k_][:, 0:tn], gs[k_][:, 0:tn], ALU.mult)
            r0_ = 384 + hp * 128
            P.dma("sp", self.mix.v(("mr", hp, ti), (slice(r0_, r0_ + 128), ts_)), mo[k_][:, 0:tn])

    def zero_mix(self, r0, r1):
        P = self.P
        z = P.sb([128, S], BF16, "zmix")
        P.memset("pool", z[:], 0.0)
        for kc in range(r0 // 128, r1 // 128):
            P.dma("sp", self.mix.v(("mz", kc), (slice(kc * 128, kc * 128 + 128), slice(None))), z[:])

    def build(self, layers=DEPTH):
        P = self.P
        base = P.mark()
        self.phase_w()
        P.release(base)
        for l in range(layers):
            self.phase_mod(l)
            keep = P.mark()
            self.phase_norm(l)
            P.release(keep)
            if self.stop_after == "norm":
                return
            if not os.environ.get("MK_SKIP_ATT"):
                self.phase_att(l)
                P.release(keep)
            if self.stop_after == "att":
                return
            if os.environ.get("MK_SKIP_RWKV"):
                self.zero_mix(384, 640)
            else:
                self.phase_rwkv(l)
            P.release(keep)
            if self.stop_after == "rwkv":
                return
            self.phase_ssd(l)
            P.release(keep)
            if self.stop_after == "ssd":
                return
            self.phase_out(l)
            P.release(keep)
            for a in ("n_x", "o_w"):
                if hasattr(self, a):
                    delattr(self, a)

def build_program(stop_after=None, dbg=False, layers=DEPTH):
    nc = bass.Bass("TRN2", target_bir_lowering=False)
    k = MK(nc, stop_after=stop_after, dbg=dbg)
    k.build(layers)
    k.P.emit()
    return nc, k


def prep_inputs(inp, b):
    inp = {k: np.asarray(v) for k, v in inp.items()}
    d = {}
    d["xin"] = np.ascontiguousarray(np.concatenate([inp["ctx"][b], inp["x"][b]], 0).astype(np.float32))
    d["cvec"] = np.ascontiguousarray(np.concatenate([fm(inp["c"][b]), fm(inp["c_ctx"])], 1))
    d["ada_w"] = np.ascontiguousarray(inp["ada_w"], np.float32)
    d["w_in"] = np.ascontiguousarray(inp["w_in"], np.float32)
    d["w_out"] = np.ascontiguousarray(inp["w_out"], np.float32)
    d["wup"] = np.ascontiguousarray(inp["rwkv_w_up"], np.float32)
    d["aup"] = np.ascontiguousarray(inp["rwkv_a_up"], np.float32)
    d["pp"] = np.stack([make_pp(inp, l) for l in range(DEPTH)])
    d["pr"] = np.stack([make_pr(inp, l) for l in range(DEPTH)])
    d["pr2"] = np.stack([make_pr2(inp, l) for l in range(DEPTH)])
    d["cst"] = make_consts()
    return d


def kernel(**inputs):
    nc, _ = build_program()
    maps = [prep_inputs(inputs, b % 4) for b in range(4)]
    in_maps = [maps[i % 4] for i in range(8)]
    res = run_bass_kernel_spmd(nc, in_maps, core_ids=list(range(8)))
    return np.stack([np.asarray(res.results[b]["y"], np.float32) for b in range(4)], 0)
```

```python
import numpy as np
import concourse.bass as bass
import concourse.mybir as mybir

F32 = mybir.dt.float32
BF16 = mybir.dt.bfloat16
ALU = mybir.AluOpType
AF = mybir.ActivationFunctionType
AX = mybir.AxisListType

import os
DUMP = os.environ.get("MK_DUMP", "")
DMAQ = dict(kv.split(":") for kv in filter(None, os.environ.get("MK_DMAQ", "act:sp").split(",")))
ATTACH = os.environ.get("MK_ATTACH", "1") != "0"
NOSYNC = set(filter(None, os.environ.get("MK_NOSYNC", "").split(",")))
ENGS = ("pe", "dve", "act", "pool", "sp")
N_DMA_SEMS = 20


class Trk:
    __slots__ = ("lw", "rd", "ps")

    def __init__(self, ps=False):
        self.lw = None
        self.rd = []
        self.ps = ps


class V:
    __slots__ = ("ap", "trk")

    def __init__(self, ap, trk):
        self.ap = ap
        self.trk = trk

    def __getitem__(self, idx):
        return V(self.ap[idx], self.trk)

    def m(self, fn):
        return V(fn(self.ap), self.trk)

    def re(self, pat, **kw):
        return V(self.ap.rearrange(pat, **kw), self.trk)

    def bc(self, shape):
        return V(self.ap.broadcast_to(shape), self.trk)

    @property
    def shape(self):
        return self.ap.shape


class Tile:
    def __init__(self, handle):
        self.h = handle
        self.trk = Trk()

    def __getitem__(self, idx):
        return V(self.h[idx], (self.trk,))

    def ap(self):
        return V(self.h.ap() if hasattr(self.h, "ap") else self.h[:], (self.trk,))


class DTile:
    def __init__(self, handle):
        self.h = handle
        self.reg = {}

    def v(self, key, idx=None, fn=None):
        t = self.reg.setdefault(key, Trk())
        ap = self.h.ap()
        if fn is not None:
            ap = fn(ap)
        if idx is not None:
            ap = ap[idx]
        return V(ap, (t,))


class Op:
    __slots__ = ("eng", "fn", "reads", "writes", "idx", "deps", "signal", "sig_count",
                 "is_dma", "dma_sem", "dma_val", "dma_prev", "kind")


class Prog:
    def __init__(self, nc):
        self.nc = nc
        self.ops = {e: [] for e in ENGS}
        self.n_t = 0
        self.all_ops = []

    ARENA = 207 * 1024

    def sb(self, shape, dtype, name=None):
        self.n_t += 1
        if not hasattr(self, "_abase"):
            a = self.nc.alloc_sbuf_tensor("arena", [128, self.ARENA], mybir.dt.uint8)
            self._abase = self.nc.lookup_mloc(a).addr
            self._aoff = 0
        esz = 2 if dtype == BF16 else 4
        n = esz
        for d in shape[1:]:
            n *= d
        off = (self._aoff + 63) // 64 * 64
        assert off + n <= self.ARENA, f"SBUF arena overflow allocating {name} {shape}: {off + n}"
        self._aoff = off + n
        return Tile(self.nc.alloc_sbuf_tensor_at(f"{name or 'sb'}_{self.n_t}", list(shape), dtype, offset=self._abase + off))

    def mark(self):
        return getattr(self, "_aoff", 0)

    def release(self, mark):
        self.barrier()
        self._aoff = mark

    def barrier(self):
        lasts = []
        for e in ENGS:
            for op in reversed(self.ops[e]):
                if not op.is_dma and op.kind != "bar":
                    lasts.append(op)
                    break
        dmas = [op for op in self.all_ops[getattr(self, "_bar_pos", 0):] if op.is_dma]
        self._bar_pos = len(self.all_ops)
        for e in ENGS:
            op = self.rec(e, lambda eh: eh.nop(nofuse=True), [], [], kind="bar")
            op.deps = [d for d in lasts if d.eng != e] + dmas
            for d in op.deps:
                d.signal = True

    def ps(self, shape, dtype=F32, name=None):
        self.n_t += 1
        return Tile(self.nc.alloc_psum_tensor(name or f"ps{self.n_t}", list(shape), dtype))

    def dram(self, name, shape, dtype, kind="Internal"):
        return DTile(self.nc.dram_tensor(name, list(shape), dtype, kind=kind))

    def rec(self, eng, fn, reads, writes, is_dma=False, kind=""):
        op = Op()
        op.eng = eng
        op.fn = fn
        op.kind = kind
        op.is_dma = is_dma
        op.idx = len(self.ops[eng])
        op.signal = False
        op.deps = []
        rt = []
        for v in reads:
            if v is None or not isinstance(v, V):
                continue
            rt.extend(v.trk)
        wt = []
        for v in writes:
            if v is None or not isinstance(v, V):
                continue
            wt.extend(v.trk)
        deps = set()
        for t in rt:
            if t.lw is not None:
                deps.add(t.lw)
            if t.ps:
                for r in t.rd:
                    if r.eng != eng:
                        deps.add(r)
        for t in wt:
            if t.lw is not None:
                deps.add(t.lw)
            for r in t.rd:
                deps.add(r)
        deps.discard(op)
        for t in rt:
            t.rd.append(op)
        for t in wt:
            t.lw = op
            t.rd = []
        final = []
        for d in deps:
            if d.eng == "pe" and eng == "pe" and not d.is_dma and not is_dma:
                continue
            if d.eng == eng and eng in NOSYNC and not d.is_dma and not is_dma:
                continue
            final.append(d)
        op.deps = final
        for d in final:
            d.signal = True
        self.ops[eng].append(op)
        self.all_ops.append(op)
        return op

    def mm(self, out, lhsT, rhs, start=True, stop=True):
        return self.rec("pe", lambda e: e.matmul(out.ap, lhsT.ap, rhs.ap, start=start, stop=stop),
                        [lhsT, rhs], [out], kind="mm")

    def tr(self, out, in_, ident):
        return self.rec("pe", lambda e: e.transpose(out.ap, in_.ap, ident.ap), [in_, ident], [out], kind="tr")

    def act(self, out, in_, func, bias=None, scale=None, accum_out=None, eng="act"):
        kw = {}
        rd = [in_]
        if bias is not None:
            kw["bias"] = bias.ap if isinstance(bias, V) else bias
            rd.append(bias)
        if scale is not None:
            kw["scale"] = scale.ap if isinstance(scale, V) else scale
            rd.append(scale)
        wr = [out]
        if accum_out is not None:
            kw["accum_out"] = accum_out.ap
            wr.append(accum_out)
        return self.rec("act", lambda e: e.activation(out.ap, in_.ap, func, **kw), rd, wr, kind="act")

    def tt(self, eng, out, in0, in1, op):
        return self.rec(eng, lambda e: e.tensor_tensor(out.ap, in0.ap, in1.ap, op), [in0, in1], [out], kind="tt")

    def ts(self, eng, out, in0, s1, s2, op0, op1=None, accum_out=None):
        rd = [in0, s1, s2]
        a1 = s1.ap if isinstance(s1, V) else s1
        a2 = s2.ap if isinstance(s2, V) else s2
        kw = {}
        wr = [out]
        if accum_out is not None:
            kw["accum_out"] = accum_out.ap
            wr.append(accum_out)
        if op1 is None:
            return self.rec(eng, lambda e: e.tensor_scalar(out.ap, in0.ap, a1, None, op0, **kw), rd, wr, kind="ts")
        return self.rec(eng, lambda e: e.tensor_scalar(out.ap, in0.ap, a1, a2, op0, op1, **kw), rd, wr, kind="ts")

    def stt(self, eng, out, in0, scalar, in1, op0, op1):
        a = scalar.ap if isinstance(scalar, V) else scalar
        return self.rec(eng, lambda e: e.scalar_tensor_tensor(out.ap, in0.ap, a, in1.ap, op0, op1),
                        [in0, scalar, in1], [out], kind="stt")

    def copy(self, eng, out, in_):
        if eng == "act":
            return self.rec("act", lambda e: e.copy(out.ap, in_.ap), [in_], [out], kind="copy")
        return self.rec(eng, lambda e: e.tensor_copy(out.ap, in_.ap), [in_], [out], kind="copy")

    def memset(self, eng, out, val):
        return self.rec(eng, lambda e: e.memset(out.ap, val), [], [out], kind="memset")

    def reduce(self, eng, out, in_, op, axis=AX.X):
        return self.rec(eng, lambda e: e.tensor_reduce(out.ap, in_.ap, axis, op), [in_], [out], kind="red")

    def recip(self, out, in_):
        return self.rec("dve", lambda e: e.reciprocal(out.ap, in_.ap), [in_], [out], kind="recip")

    def scan(self, eng, out, d0, d1, initial, op0, op1):
        ini = initial.ap if isinstance(initial, V) else initial
        return self.rec(eng, lambda e: e.tensor_tensor_scan(out.ap, d0.ap, d1.ap, ini, op0, op1),
                        [d0, d1, initial], [out], kind="scan")

    def dma(self, q, out, in_, **kw):
        q = DMAQ.get(q, q)
        return self.rec(q, lambda e: e.dma_start(out.ap, in_.ap, **kw), [in_], [out], is_dma=True, kind="dma")

    def emit(self):
        nc = self.nc
        eng_sem = {e: nc.alloc_semaphore(f"s_{e}") for e in ENGS}
        dma_sems = {e: ([nc.alloc_semaphore(f"d_{e}_{i}") for i in range(N_DMA_SEMS)]
                        if any(o.is_dma for o in self.ops[e]) else []) for e in ENGS}
        for e in ENGS:
            cnt = 0
            nd = 0
            last_on_sem = {}
            for op in self.ops[e]:
                if op.is_dma:
                    j = nd % N_DMA_SEMS
                    nd += 1
                    op.dma_sem = dma_sems[e][j]
                    k = nd_k = (nd - 1) // N_DMA_SEMS + 1
                    op.dma_val = 16 * k
                    op.dma_prev = last_on_sem.get(j)
                    last_on_sem[j] = op
                else:
                    if op.signal:
                        cnt += 1
                        op.sig_count = cnt
        self._eng_sem = eng_sem
        handles = {"pe": "tensor", "dve": "vector", "act": "scalar", "pool": "gpsimd", "sp": "sync"}

        def run_engine(ename, eh):
            seen = {}

            def wait(sem, val):
                key = id(sem)
                if seen.get(key, 0) >= val:
                    return
                seen[key] = val
                eh.wait_ge(sem, val)

            for op in self.ops[ename]:
                need = []
                for d in op.deps:
                    if d.is_dma:
                        need.append((d.dma_sem, d.dma_val))
                    else:
                        need.append((eng_sem[d.eng], d.sig_count))
                if op.is_dma and op.dma_prev is not None:
                    need.append((op.dma_prev.dma_sem, op.dma_prev.dma_val))
                todo = []
                for sem, val in need:
                    key = id(sem)
                    if seen.get(key, 0) >= val:
                        continue
                    seen[key] = val
                    todo = [(s_, v_) for (s_, v_) in todo if s_ is not sem] + [(sem, val)]
                attach = None
                if ATTACH and todo and not op.is_dma and op.kind != "bar":
                    attach = todo.pop()
                for sem, val in todo:
                    eh.wait_ge(sem, val)
                if op.is_dma:
                    ins = op.fn(eh)
                    ins.then_inc(op.dma_sem, 16)
                else:
                    ins = op.fn(eh)
                    if attach is not None:
                        ins._wait_ge(attach[0], attach[1])
                    if op.signal:
                        ins.then_inc(eng_sem[ename], 1)
                    if DUMP and ename in DUMP:
                        print(ename, op.idx, op.kind, ins.concise(), flush=True)
            last = {}
            for op in self.ops[ename]:
                if op.is_dma:
                    last[id(op.dma_sem)] = op
            for op in last.values():
                wait(op.dma_sem, op.dma_val)

        with nc.Block() as block:
            @block.tensor
            def _(eh):
                run_engine("pe", eh)

            @block.vector
            def _(eh):
                run_engine("dve", eh)

            @block.scalar
            def _(eh):
                run_engine("act", eh)

            @block.gpsimd
            def _(eh):
                run_engine("pool", eh)

            @block.sync
            def _(eh):
                run_engine("sp", eh)

    def stats(self):
        return {e: len(self.ops[e]) for e in ENGS}

from concourse.bass_utils import run_bass_kernel_spmd

S = 4352
CTX = 256
TLAT = 4096
DM = 1024
NIN = 3596
NBLK = 34
DEPTH = 2
TILES = [(0, 256)] + [(256 + 512 * i, 512) for i in range(8)]
NEG = -30000.0

C_Q, C_K, C_V, C_G = 0, 384, 512, 640
C_RW = 1024
C_RG = 2048
C_Z = 2304
C_XBC = 2688
C_DT = 3584


def _slots(spec):
    d, o = {}, 0
    for n, w in spec:
        d[n] = (o, w)
        o += w
    return d, o


PP, NPP = _slots([("adab", 24), ("normw", 8), ("sink", 6), ("mu0", 8), ("mu1", 8), ("w0", 4), ("a0", 4),
                  ("kk", 4), ("ka", 4), ("rk", 2), ("lnw", 2), ("lnb", 2), ("convw", 21), ("convb", 7),
                  ("ssmd", 3), ("ssmnw", 3)])
PR, NPR = _slots([("adabg", 1024), ("dtb", 12), ("alog", 12), ("fnw", 1024)])
PR2, NPR2 = _slots([("convr", 3 * 896), ("shr", 3 * 1024)])
CS, NCS = _slots([("ident", 128), ("mband", 3 * 384), ("perm", 128), ("cos", 4096), ("sin", 4096),
                  ("blk", 128), ("rmf", 256), ("rmb", 256), ("snf", 128), ("snb", 128), ("trif", 128),
                  ("trib", 128), ("ones", 128), ("hlo", 128), ("hhi", 128), ("lowf", 128), ("lowb", 128)])


def fm(v):
    v = np.asarray(v, np.float32)
    return np.ascontiguousarray(v.reshape(-1, 128).T)


def rep(v):
    v = np.asarray(v, np.float32).reshape(1, -1)
    return np.ascontiguousarray(np.broadcast_to(v, (128, v.shape[1])))


def make_consts():
    c = np.zeros((128, NCS), np.float32)

    def put(n, a):
        o, w = CS[n]
        c[:, o:o + w] = a
    put("ident", np.eye(128, dtype=np.float32))
    qi = np.arange(128)[:, None]
    kj = np.arange(384)[None, :]
    band = np.abs(kj - 128 - qi) <= 128
    mb = []
    for var in range(3):
        ok = band.copy()
        if var == 0:
            ok &= kj >= 128
        if var == 2:
            ok &= kj < 256
        mb.append(np.where(ok, 0.0, NEG))
    put("mband", np.concatenate(mb, 1))
    perm = np.zeros((128, 128), np.float32)
    for m in range(128):
        d = m % 64
        half = (d % 32) // 16
        partner = m + 16 if half == 0 else m - 16
        perm[partner, m] = 1.0
    put("perm", perm)
    rows = TLAT // 64
    row = np.repeat(np.arange(rows), 64).astype(np.float32)
    col = np.tile(np.arange(64), rows).astype(np.float32)
    pos = np.stack([row, col], -1)
    inv = (np.float32(10000.0) ** (-np.arange(16, dtype=np.float32) / np.float32(16))).astype(np.float32)
    ang = (pos[:, :, None] * inv).astype(np.float32)
    cosv, sinv = np.cos(ang).astype(np.float32), np.sin(ang).astype(np.float32)
    ct = np.zeros((128, TLAT), np.float32)
    st = np.zeros((128, TLAT), np.float32)
    for m in range(128):
        d = m % 64
        ax, half, f = d // 32, (d % 32) // 16, d % 16
        ct[m] = cosv[:, ax, f]
        st[m] = -sinv[:, ax, f] if half == 0 else sinv[:, ax, f]
    put("cos", ct)
    put("sin", st)
    blk = np.zeros((128, 128), np.float32)
    blk[:64, :64] = 1
    blk[64:, 64:] = 1
    put("blk", blk)
    s = np.arange(128)[:, None]
    t = np.arange(128)[None, :]
    same = (s // 64) == (t // 64)
    put("rmf", np.concatenate([(same & (s < t)), (same & (s <= t))], 1).astype(np.float32))
    put("rmb", np.concatenate([(same & (s > t)), (same & (s >= t))], 1).astype(np.float32))
    put("lowf", (same & (t < s)).astype(np.float32))
    put("lowb", (same & (t > s)).astype(np.float32))
    put("snf", np.where(s <= t, 0.0, NEG))
    put("snb", np.where(s >= t, 0.0, NEG))
    put("trif", (s <= t).astype(np.float32))
    put("trib", (s >= t).astype(np.float32))
    put("ones", np.ones((128, 128), np.float32))
    hlo = np.zeros((128, 128), np.float32)
    hlo[:64] = 1
    put("hlo", hlo)
    put("hhi", 1 - hlo)
    return c


def make_pp(inp, l):
    p = np.zeros((128, NPP), np.float32)

    def put(n, a):
        o, w = PP[n]
        assert a.shape == (128, w), (n, a.shape)
        p[:, o:o + w] = a
    put("adab", fm(inp["ada_b"][l]))
    put("normw", fm(inp["norm_w"][l]))
    put("sink", rep(inp["attn_sink"][l]))
    put("mu0", fm(inp["rwkv_mu"][l, 0]))
    put("mu1", fm(inp["rwkv_mu"][l, 1]))
    for n, k in (("w0", "rwkv_w0"), ("a0", "rwkv_a0"), ("kk", "rwkv_k_k"), ("ka", "rwkv_k_a")):
        put(n, np.concatenate([fm(inp[k][l, 0]), fm(inp[k][l, 1])], 1))
    put("rk", fm(inp["rwkv_r_k"][l].reshape(-1)))
    put("lnw", fm(inp["rwkv_ln_w"][l]))
    put("lnb", fm(inp["rwkv_ln_b"][l]))
    cw = inp["ssm_conv_w"][l]
    put("convw", np.stack([fm(cw[0]), fm(cw[1]), fm(cw[2])], -1).reshape(128, 21))
    put("convb", fm(inp["ssm_conv_b"][l]))
    put("ssmd", fm(np.repeat(inp["ssm_d"][l], 64)))
    put("ssmnw", fm(inp["ssm_norm_w"][l]))
    return p


def make_pr(inp, l):
    p = np.zeros((128, NPR), np.float32)

    def put(n, a):
        o, w = PR[n]
        p[:, o:o + w] = a
    put("adabg", rep(inp["ada_b"][l, 2048:3072]))
    put("dtb", rep(inp["ssm_dt_bias"][l].reshape(-1)))
    put("alog", rep(inp["ssm_a_log"][l].reshape(-1)))
    put("fnw", rep(inp["final_norm_w"]))
    return p


def make_pr2(inp, l):
    p = np.zeros((128, NPR2), np.float32)

    def put(n, a):
        o, w = PR2[n]
        p[:, o:o + w] = a
    put("convr", rep(inp["ssm_conv_w"][l].reshape(-1)))
    mu = inp["rwkv_mu"][l]
    put("shr", rep(np.concatenate([mu[0], mu[1], mu[1]], 0)))
    return p


class MK:
    def __init__(self, nc, stop_after=None, dbg=False):
        self.nc = nc
        self.P = P = Prog(nc)
        self.dbg = dbg
        self.stop_after = stop_after
        ok = "ExternalOutput" if dbg else "Internal"
        self.xin = P.dram("xin", [S, DM], F32, kind="ExternalInput")
        self.cvec = P.dram("cvec", [128, 16], F32, kind="ExternalInput")
        self.ada_w = P.dram("ada_w", [DEPTH, DM, 3 * DM], F32, kind="ExternalInput")
        self.w_in = P.dram("w_in", [DEPTH, DM, NIN], F32, kind="ExternalInput")
        self.w_out = P.dram("w_out", [DEPTH, DM, DM], F32, kind="ExternalInput")
        self.wup = P.dram("wup", [DEPTH, 2, 64, 256], F32, kind="ExternalInput")
        self.aup = P.dram("aup", [DEPTH, 2, 64, 256], F32, kind="ExternalInput")
        self.pp = P.dram("pp", [DEPTH, 128, NPP], F32, kind="ExternalInput")
        self.pr = P.dram("pr", [DEPTH, 128, NPR], F32, kind="ExternalInput")
        self.pr2 = P.dram("pr2", [DEPTH, 128, NPR2], F32, kind="ExternalInput")
        self.cst = P.dram("cst", [128, NCS], F32, kind="ExternalInput")
        self.y = P.dram("y", [TLAT, DM], F32, kind="ExternalOutput")
        self.wbin = P.dram("wbin", [DEPTH, 128, 8, NIN], BF16)
        self.wbout = P.dram("wbout", [DEPTH, 128, 8, DM], BF16)
        self.hT = P.dram("hT", [128, 8, S], BF16, kind=ok)
        self.mix = P.dram("mixT", [DM, S], BF16, kind=ok)
        self.xres = P.dram("xres", [S, DM], F32, kind=ok)
        self.yf = P.dram("yfwd", [384, S], F32)
        self.psh = nc.alloc_psum_tensor("psum", [128, 8, 512], F32)
        self.bt = [Trk(ps=True) for _ in range(8)]
        self.c_ident = P.sb([128, 128], F32, "c_ident")
        self.c_identb = P.sb([128, 128], BF16, "c_identb")
        self.load_c(self.c_ident, "ident")
        P.copy("dve", self.c_identb[:], self.c_ident[:])

    def ps(self, b0, nb=1, dt=F32):
        ap = self.psh[:, b0:b0 + nb, :]
        if dt is not F32:
            ap = ap.bitcast(dt)
        return V(ap, tuple(self.bt[b0:b0 + nb]))

    def load_c(self, tile, name, q="sp", sub=None):
        o, w = CS[name]
        if sub is not None:
            o, w = o + sub[0], sub[1]
        self.P.dma(q, tile[:], self.cst.v("c", (slice(None), slice(o, o + w))))

    def phase_w(self):
        P = self.P
        st = [P.sb([128, 8, 512], F32, f"w_st{i}") for i in range(2)]
        sb = [P.sb([128, 8, 512], BF16, f"w_sb{i}") for i in range(2)]
        it = 0
        for l in range(DEPTH):
            for (src, dst, n) in ((self.w_in, self.wbin, NIN), (self.w_out, self.wbout, DM)):
                for c0 in range(0, n, 512):
                    cw = min(512, n - c0)
                    a, b = st[it % 2], sb[it % 2]
                    q = "sp" if it % 2 == 0 else "act"
                    P.dma(q, a[:, :, 0:cw], src.v(("w", l), fn=lambda ap, l=l, c0=c0, cw=cw:
                                                   ap[l].rearrange("(kc p) n -> p kc n", p=128)[:, :, c0:c0 + cw]))
                    P.copy("dve" if it % 2 == 0 else "pool", b[:, :, 0:cw], a[:, :, 0:cw])
                    P.dma(q, dst.v(("wb", l, c0), fn=lambda ap, l=l, c0=c0, cw=cw: ap[l][:, :, c0:c0 + cw]),
                          b[:, :, 0:cw])
                    it += 1

    def wb(self, l, c0, cw):
        t0 = (c0 // 512) * 512
        trk = []
        ap = self.wbin.h.ap()[l][:, :, c0:c0 + cw]
        for t in range(t0, c0 + cw, 512):
            trk.append(self.wbin.reg.setdefault(("wb", l, t), Trk()))
        return V(ap, tuple(trk))

    def wbo(self, l, c0, cw):
        t0 = (c0 // 512) * 512
        trk = []
        ap = self.wbout.h.ap()[l][:, :, c0:c0 + cw]
        for t in range(t0, c0 + cw, 512):
            trk.append(self.wbout.reg.setdefault(("wb", l, t), Trk()))
        return V(ap, tuple(trk))

    def phase_mod(self, l):
        P = self.P
        if not hasattr(self, "ppt"):
            self.ppt = P.sb([128, NPP], F32, "ppt")
            self.prt = P.sb([128, NPR], F32, "prt")
            self.cact = P.sb([128, 8, 2], F32, "cact")
            self.modT = P.sb([128, 24, 2], F32, "modT")
            self.s1T = P.sb([128, 8, 2], F32, "s1T")
            self.gate_bc = P.sb([128, 2, 1024], F32, "gate_bc")
            craw = P.sb([128, 16], F32, "craw")
            P.dma("sp", craw[:], self.cvec.v(0))
            P.act(self.cact[:].re("p k w -> p w k"), craw[:].re("p (w k) -> p w k", w=2), AF.Silu)
        mk_ = P.mark()
        self.crep = P.sb([128, 8, 2, 128], F32, "crep")
        self.aw = [P.sb([128, 8, 512], F32, f"aw{i}") for i in range(2)]
        P.copy("dve", self.crep[:], self.cact[:].m(lambda a: a.unsqueeze(3)).bc([128, 8, 2, 128]))
        P.dma("sp", self.ppt[:], self.pp.v(l, fn=lambda a: a[l]))
        P.dma("act", self.prt[:], self.pr.v(l, fn=lambda a: a[l]))
        for t in range(6):
            a = self.aw[t % 2]
            P.dma("sp" if t % 2 == 0 else "act", a[:],
                  self.ada_w.v(("aw", l), fn=lambda ap, t=t: ap[l].rearrange("(kc p) n -> p kc n", p=128)[:, :, t * 512:(t + 1) * 512]))
            pm = self.ps(0)
            for j in range(4):
                for kc in range(8):
                    P.mm(pm[:, 0, j * 2:(j + 1) * 2], a[:, kc, j * 128:(j + 1) * 128], self.cact[:, kc, :],
                         start=(kc == 0), stop=(kc == 7))
            P.copy("dve", self.modT[:, t * 4:(t + 1) * 4, :], pm[:, 0, 0:8].re("p (j w) -> p j w", w=2))
            if t >= 4:
                for w in range(2):
                    pg = self.ps(1 + w)
                    for kc in range(8):
                        P.mm(pg[:, 0, :], self.crep[:, kc, w, :], a[:, kc, :], start=(kc == 0), stop=(kc == 7))
                    o = PR["adabg"][0] + (t - 4) * 512
                    P.tt("dve", self.gate_bc[:, w, (t - 4) * 512:(t - 3) * 512], pg[:, 0, :], self.prt[:, o:o + 512], ALU.add)
        o = PP["adab"][0]
        P.tt("dve", self.modT[:], self.modT[:], self.ppt[:, o:o + 24].m(lambda a: a.unsqueeze(2)).bc([128, 24, 2]), ALU.add)
        o = PP["normw"][0]
        P.stt("dve", self.s1T[:], self.modT[:, 8:16, :], 1.0,
              self.ppt[:, o:o + 8].m(lambda a: a.unsqueeze(2)).bc([128, 8, 2]), ALU.add, ALU.mult)
        P.release(mk_)

    def phase_norm(self, l):
        P = self.P
        if not hasattr(self, "n_x"):
            self.n_x = [P.sb([128, DM], F32, f"n_x{i}") for i in range(2)]
            self.n_junk = P.sb([128, DM], F32, "n_junk")
            self.n_xb = [P.sb([128, DM], BF16, f"n_xb{i}") for i in range(2)]
            self.n_ss = [P.sb([128, 4], F32, f"n_ss{i}") for i in range(2)]
            self.n_h = [P.sb([128, 8, 512], BF16, f"n_h{i}") for i in range(2)]
            self.n_t = [P.sb([128, 8, 128], F32, f"n_t{i}") for i in range(2)]
        src = self.xin if l == 0 else self.xres
        for ti, (t0, tn) in enumerate(TILES):
            w = 1 if ti == 0 else 0
            hb = self.n_h[ti % 2]
            for bi in range(tn // 128):
                blk = (t0 // 128) + bi
                xt, xb, ss = self.n_x[blk % 2], self.n_xb[blk % 2], self.n_ss[blk % 2]
                P.dma("sp" if blk % 2 == 0 else "act", xt[:], src.v(("x", blk), (slice(blk * 128, blk * 128 + 128), slice(None))))
                P.act(self.n_junk[:], xt[:], AF.Square, accum_out=ss[:, 0:1])
                P.act(ss[:, 1:2], ss[:, 0:1], AF.Sqrt, bias=1e-6, scale=1.0 / DM)
                P.recip(ss[:, 2:3], ss[:, 1:2])
                P.ts("dve", xb[:], xt[:], ss[:, 2:3], None, ALU.mult)
                pt = self.ps(2 + blk % 2, 1, BF16)
                for kc in range(8):
                    P.tr(pt[:, 0, kc * 128:(kc + 1) * 128], xb[:, kc * 128:(kc + 1) * 128], self.c_identb[:])
                tmp = self.n_t[blk % 2]
                P.tt("dve", tmp[:], pt[:, 0, :].re("p (k t) -> p k t", k=8),
                     self.s1T[:, :, w:w + 1].bc([128, 8, 128]), ALU.mult)
                P.tt("pool", hb[:, :, bi * 128:(bi + 1) * 128], tmp[:],
                     self.modT[:, 0:8, w:w + 1].bc([128, 8, 128]), ALU.add)
            P.dma("sp", self.hT.v(("h", ti), (slice(None), slice(None), slice(t0, t0 + tn))), hb[:, :, 0:tn])

    def phase_out(self, l):
        P = self.P
        last = (l == DEPTH - 1)
        if not hasattr(self, "o_w"):
            self.o_w = P.sb([128, 8, DM], BF16, "o_w")
            self.o_m = [P.sb([128, 8, 128], BF16, f"o_m{i}") for i in range(2)]
            self.o_x = [P.sb([128, DM], F32, f"o_x{i}") for i in range(2)]
            self.o_t = [P.sb([128, DM], F32, f"o_t{i}") for i in range(2)]
            self.o_ss = [P.sb([128, 4], F32, f"o_ss{i}") for i in range(2)]
        P.dma("sp", self.o_w[:], self.wbo(l, 0, DM))
        src = self.xin if l == 0 else self.xres
        for blk in range(NBLK):
            if last and blk < 2:
                continue
            w = 1 if blk < 2 else 0
            m, xt, tt_, ss = self.o_m[blk % 2], self.o_x[blk % 2], self.o_t[blk % 2], self.o_ss[blk % 2]
            tsl = slice(blk * 128, blk * 128 + 128)
            P.dma("sp", m[:], self.mix.v(("m", blk), fn=lambda ap, tsl=tsl: ap.rearrange("(kc p) t -> p kc t", p=128)[:, :, tsl]))
            P.dma("act", xt[:], src.v(("x", blk), (tsl, slice(None))))
            for hf in range(2):
                po = self.ps(4 + 2 * (blk % 2) + hf)
                for kc in range(8):
                    P.mm(po[:, 0, :], m[:, kc, :], self.o_w[:, kc, hf * 512:(hf + 1) * 512], start=(kc == 0), stop=(kc == 7))
                P.tt("dve", tt_[:, hf * 512:(hf + 1) * 512], po[:, 0, :], self.gate_bc[:, w, hf * 512:(hf + 1) * 512], ALU.mult)
            P.tt("pool", tt_[:], tt_[:], xt[:], ALU.add)
            if not last:
                P.dma("sp", self.xres.v(("x", blk), (tsl, slice(None))), tt_[:])
            else:
                P.act(xt[:], tt_[:], AF.Square, accum_out=ss[:, 0:1])
                P.act(ss[:, 1:2], ss[:, 0:1], AF.Sqrt, bias=1e-6, scale=1.0 / DM)
                P.recip(ss[:, 2:3], ss[:, 1:2])
                o = PR["fnw"][0]
                P.stt("dve", xt[:], tt_[:], ss[:, 2:3], self.prt[:, o:o + DM], ALU.mult, ALU.mult)
                P.dma("sp", self.y.v(("y", blk), (slice(blk * 128 - CTX, blk * 128 - CTX + 128), slice(None))), xt[:])

    def load_h(self, tile, ti, q="sp"):
        t0, tn = TILES[ti]
        self.P.dma(q, tile[:, :, 0:tn], self.hT.v(("h", ti), (slice(None), slice(None), slice(t0, t0 + tn))))

    def phase_att(self, l):
        P = self.P
        ctx_out = (l < DEPTH - 1)
        import os
        sm = int(os.environ.get("MK_SET", "63"))
        wq = P.sb([128, 8, 384], BF16, "a_wq")
        wg = P.sb([128, 8, 384], BF16, "a_wg")
        wk = P.sb([128, 8, 128], BF16, "a_wk")
        wv = P.sb([128, 8, 128], BF16, "a_wv")
        if sm & 1:
            for j in range(3):
                for hh in range(2):
                    h = j + 3 * hh
                    P.dma("sp", wq[:, :, j * 128 + hh * 64: j * 128 + hh * 64 + 64], self.wb(l, C_Q + h * 64, 64))
                    P.dma("act", wg[:, :, j * 128 + hh * 64: j * 128 + hh * 64 + 64], self.wb(l, C_G + h * 64, 64))
            P.dma("sp", wk[:], self.wb(l, C_K, 128))
            P.dma("act", wv[:], self.wb(l, C_V, 128))
        cos = P.sb([128, TLAT], F32, "a_cos")
        sin = P.sb([128, TLAT], F32, "a_sin")
        if sm & 2:
            self.load_c(cos, "cos", "sp")
            self.load_c(sin, "sin", "act")
        pf = P.sb([128, 128], F32, "a_pf")
        permb = P.sb([128, 128], BF16, "a_permb")
        if sm & 4:
            self.load_c(pf, "perm")
            P.copy("dve", permb[:], pf[:])
        mband = P.sb([128, 3, 384], F32, "a_mband")
        o, w_ = CS["mband"]
        if sm & 8:
            P.dma("sp", mband[:].re("p a b -> p (a b)"), self.cst.v("c", (slice(None), slice(o, o + w_))))
        sink8 = P.sb([128, 6], F32, "a_sink8")
        o = PP["sink"][0]
        if sm & 16:
            P.ts("dve", sink8[:], self.ppt[:, o:o + 6], 8.0, None, ALU.mult)
        KB = P.sb([128, 4608], BF16, "a_KB")
        Vtm = P.sb([128, 36, 128], BF16, "a_V")
        if sm & 32:
            P.memset("pool", KB[:, 256:384], 0.0)
            P.memset("pool", KB[:, 4480:4608], 0.0)
            P.memset("pool", Vtm[:, 2, :], 0.0)
            P.memset("pool", Vtm[:, 35, :], 0.0)
        hb = [P.sb([128, 8, 512], BF16, f"a_h{i}") for i in range(2)]
        kf = P.sb([128, 512], F32, "a_kf")
        t1 = [P.sb([128, 512], F32, f"a_t1{i}") for i in range(2)]
        t2 = [P.sb([128, 512], F32, f"a_t2{i}") for i in range(2)]

        def kcol(tok):
            return tok if tok < CTX else tok + 128

        def proj(dst, wt, c0, cw, hbt, tn, lat_t0, bank, func=None):
            pp_ = self.ps(bank)
            for kc in range(8):
                P.mm(pp_[:, 0, 0:tn], wt[:, kc, c0:c0 + cw], hbt[:, kc, 0:tn], start=(kc == 0), stop=(kc == 7))
            if func is not None:
                P.act(dst, pp_[:, 0, 0:tn], func)
                return
            if lat_t0 is None:
                P.copy("act", dst, pp_[:, 0, 0:tn])
                return
            P.copy("act", kf[:, 0:tn], pp_[:, 0, 0:tn])
            pr_ = self.ps(2)
            P.mm(pr_[:, 0, 0:tn], pf[:], kf[:, 0:tn])
            a, b = t1[bank % 2], t2[bank % 2]
            P.tt("pool", a[:, 0:tn], kf[:, 0:tn], cos[:, lat_t0:lat_t0 + tn], ALU.mult)
            P.tt("dve", b[:, 0:tn], pr_[:, 0, 0:tn], sin[:, lat_t0:lat_t0 + tn], ALU.mult)
            P.tt("pool", dst, a[:, 0:tn], b[:, 0:tn], ALU.add)

        stage = int(os.environ.get("MK_ATT", "9"))
        if stage <= 0:
            return
        for ti, (t0, tn) in enumerate(TILES):
            h = hb[ti % 2]
            self.load_h(h, ti, "sp" if ti % 2 == 0 else "act")
            kc0 = kcol(t0)
            ma = int(os.environ.get("MK_A", "3"))
            if ti >= int(os.environ.get("MK_AT", "9")):
                break
            if ma & 1:
                proj(KB[:, kc0:kc0 + tn], wk, 0, 128, h, tn, None if ti == 0 else t0 - CTX, ti % 2)
            for bi in range(tn // 128):
                if not (ma & 2):
                    break
                vb = (t0 // 128 + bi)
                vb = vb if vb < 2 else vb + 1
                pv = self.ps(3 + bi % 2)
                for kc in range(8):
                    P.mm(pv[:, 0, 0:128], h[:, kc, bi * 128:(bi + 1) * 128], wv[:, kc, :], start=(kc == 0), stop=(kc == 7))
                P.copy("dve" if bi % 2 == 0 else "act", Vtm[:, vb, :], pv[:, 0, 0:128])

        import os
        stage = int(os.environ.get("MK_ATT", "9"))
        if stage <= 1:
            return
        qT = [P.sb([128, 3, 512], BF16, f"a_q{i}") for i in range(2)]
        gT = [P.sb([128, 3, 512], BF16, f"a_g{i}") for i in range(2)]
        om = [P.sb([128, 3, 512], BF16, f"a_om{i}") for i in range(2)]
        sc = [P.sb([128, 3, 641], F32, f"a_sc{i}") for i in range(2)]
        pe = [P.sb([128, 3, 641], F32, f"a_pe{i}") for i in range(2)]
        for kg in range(2):
            P.copy("dve", sc[kg][:, :, 640:641], sink8[:, kg * 3:kg * 3 + 3].m(lambda a: a.unsqueeze(2)))
        pn = [P.sb([128, 3, 640], BF16, f"a_pn{i}") for i in range(2)]
        pTs = [P.sb([128, 5, 128], BF16, f"a_pT{i}") for i in range(3)]
        st = [P.sb([128, 8, 3], F32, f"a_st{i}") for i in range(2)]
        it = 0
        npt = 0
        for ti, (t0, tn) in enumerate(TILES):
            if ti == 0 and not ctx_out:
                continue
            h = hb[ti % 2]
            self.load_h(h, ti, "sp" if ti % 2 == 0 else "act")
            q_, g_, o_ = qT[ti % 2], gT[ti % 2], om[ti % 2]
            for j in range(3):
                proj(q_[:, j, 0:tn], wq, j * 128, 128, h, tn, None if ti == 0 else t0 - CTX, j % 2)
                proj(g_[:, j, 0:tn], wg, j * 128, 128, h, tn, None, (j + 1) % 2, func=AF.Silu)
            if stage <= 2:
                break
            for bi in range(tn // 128):
                if stage <= 5 and (bi > 0 or ti > 1):
                    break
                qs = slice(bi * 128, bi * 128 + 128)
                isctx = (ti == 0)
                n = (t0 - CTX) // 128 + bi if not isctx else None
                lo = 384 if isctx else 0
                po = self.ps(2)
                KG = [(kg, slice(kg * 64, kg * 64 + 64), sc[kg], pe[kg], pn[kg], st[kg]) for kg in range(2)]
                chunks = ([] if isctx else [0, 1, 2]) + [3, 4]
                for kg, ps_, s_, e_, n_, stt_ in KG:
                    for j in range(3):
                        pb = self.ps(3 + 2 * kg)
                        pc = self.ps(4 + 2 * kg)
                        if not isctx:
                            kb0 = 256 + n * 128
                            P.mm(pb[:, 0, 0:384], q_[ps_, j, qs], KB[ps_, kb0:kb0 + 384])
                            var = 0 if n == 0 else (2 if n == 31 else 1)
                            P.tt("dve", s_[:, j, 0:384], pb[:, 0, 0:384], mband[:, var, :], ALU.add)
                        P.mm(pc[:, 0, 0:256], q_[ps_, j, qs], KB[ps_, 0:256])
                        P.copy("act", s_[:, j, 384:640], pc[:, 0, 0:256])
                for kg, ps_, s_, e_, n_, stt_ in KG:
                    P.reduce("dve", stt_[:, 0, :], s_[:, :, lo:641], ALU.max)
                    P.ts("dve", stt_[:, 2, :], stt_[:, 0, :], -0.125, None, ALU.mult)
                for kg, ps_, s_, e_, n_, stt_ in KG:
                    for j in range(3):
                        P.act(e_[:, j, lo:641], s_[:, j, lo:641], AF.Exp, bias=stt_[:, 2, j:j + 1], scale=0.125,
                              accum_out=stt_[:, 3, j:j + 1])
                for kg, ps_, s_, e_, n_, stt_ in KG:
                    P.recip(stt_[:, 7, :], stt_[:, 3, :])
                    P.tt("dve" if kg == 0 else "pool", n_[:, :, lo:640], e_[:, :, lo:640],
                         stt_[:, 7, :].m(lambda a: a.unsqueeze(2)).bc([128, 3, 640 - lo]), ALU.mult)
                for j in range(3):
                    for kg, ps_, s_, e_, n_, stt_ in KG:
                        pt = self.ps([7, 0, 1][npt % 3], 1, BF16)
                        for c in chunks:
                            P.tr(pt[:, 0, c * 128:(c + 1) * 128], n_[:, j, c * 128:(c + 1) * 128], self.c_identb[:])
                        pts = pTs[npt % 3]
                        npt += 1
                        c0 = chunks[0]
                        P.copy("act" if npt % 2 == 0 else "dve", pts[:, c0:5, :], pt[:, 0, c0 * 128:640].re("p (c t) -> p c t", t=128))
                        for ci, c in enumerate(chunks):
                            vb = (2 + n + c) if c < 3 else (c - 3)
                            P.mm(po[ps_, 0, j * 128:(j + 1) * 128], Vtm[:, vb, kg * 64:kg * 64 + 64], pts[:, c, :],
                                 start=(ci == 0), stop=(ci == len(chunks) - 1))
                P.tt("dve", o_[:, :, qs], po[:, 0, 0:384].re("p (j t) -> p j t", j=3), g_[:, :, qs], ALU.mult)
            for j in range(3):
                for hh in range(2):
                    hd = j + 3 * hh
                    P.dma("sp" if hh == 0 else "act",
                          self.mix.v(("ma", ti, hd), (slice(hd * 64, hd * 64 + 64), slice(t0, t0 + tn))),
                          o_[hh * 64:hh * 64 + 64, j, 0:tn])

    def load_h_halo(self, tile, ti, q="sp"):
        P = self.P
        t0, tn = TILES[ti]
        lo, hi = t0 - 1, t0 + tn + 1
        if ti <= 1:
            P.memset("pool", tile[:, :, 0:1], 0.0)
            lo = t0
        if ti == 0 or ti == len(TILES) - 1:
            P.memset("pool", tile[:, :, tn + 1:tn + 2], 0.0)
            hi = t0 + tn
        P.dma(q, tile[:, :, lo - t0 + 1:hi - t0 + 1], self.hT.v(("h", ti), (slice(None), slice(None), slice(lo, hi))))

    def scaled_w(self, l, c0, ncol, rows, name, roff=0):
        P = self.P
        rows = [r[:, roff:roff + ncol] for r in rows]
        w3 = P.sb([128, 3, 8, ncol], BF16, name)
        mk_ = P.mark()
        half = ncol // 2
        stg = [P.sb([128, 8, half], F32, f"{name}_st{i}") for i in range(2)]
        for hf in range(2):
            st = stg[hf]
            P.dma("sp" if hf == 0 else "act", st[:],
                  self.w_in.v(("w", l), fn=lambda ap, hf=hf: ap[l].rearrange("(kc p) n -> p kc n", p=128)[:, :, c0 + hf * half:c0 + (hf + 1) * half]))
            for tap in range(3):
                P.tt("dve" if tap != 1 else "pool", w3[:, tap, :, hf * half:(hf + 1) * half], st[:],
                     rows[tap][:, hf * half:(hf + 1) * half].m(lambda a: a.unsqueeze(1)).bc([128, 8, half]), ALU.mult)
        P.release(mk_)
        return w3

    def phase_ssd(self, l):
        P = self.P
        ctx_out = (l < DEPTH - 1)
        xsT = P.sb([128, 3, S], F32, "s_xsT")
        BT = P.sb([128, 2, S], BF16, "s_BT")
        CT = P.sb([128, 2, S], BF16, "s_CT")
        dt_tm = P.sb([128, NBLK, 12], F32, "s_dt")
        dtA_tm = P.sb([128, NBLK, 12], F32, "s_dtA")
        aneg = P.sb([128, 12], F32, "s_aneg")
        o = PR["alog"][0]
        P.act(aneg[:], self.prt[:, o:o + 12], AF.Exp)
        P.ts("dve", aneg[:], aneg[:], -1.0, None, ALU.mult)
        mk1 = P.mark()
        o = PR2["convr"][0]
        crow = P.sb([128, 3 * 896], F32, "s_crow")
        P.dma("sp", crow[:], self.pr2.v(l, fn=lambda a: a[l][:, o:o + 3 * 896]))
        rows = [crow[:, k * 896:(k + 1) * 896] for k in range(3)]
        wx3 = self.scaled_w(l, C_XBC, 896, rows, "s_wx3")
        wdt = P.sb([128, 8, 12], BF16, "s_wdt")
        P.dma("sp", wdt[:], self.wb(l, C_DT, 12))
        hh = [P.sb([128, 8, 514], BF16, f"s_hh{i}") for i in range(2)]
        dtmp = [P.sb([128, 12], F32, f"s_dtmp{i}") for i in range(2)]
        ocb = PP["convb"][0]
        nb = 0
        for ti, (t0, tn) in enumerate(TILES):
            h = hh[ti % 2]
            self.load_h_halo(h, ti, "sp" if ti % 2 == 0 else "act")
            for c in range(7):
                pb = self.ps(c % 2)
                n = 0
                for tap in range(3):
                    for kc in range(8):
                        P.mm(pb[:, 0, 0:tn], wx3[:, tap, kc, c * 128:(c + 1) * 128], h[:, kc, tap:tap + tn],
                             start=(n == 0), stop=(n == 23))
                        n += 1
                if c < 3:
                    dst = xsT[:, c, t0:t0 + tn]
                elif c < 5:
                    dst = BT[:, c - 3, t0:t0 + tn]
                else:
                    dst = CT[:, c - 5, t0:t0 + tn]
                P.act(dst, pb[:, 0, 0:tn], AF.Silu, bias=self.ppt[:, ocb + c:ocb + c + 1])
            for bi in range(tn // 128):
                blk = t0 // 128 + bi
                pd = self.ps(2 + blk % 2)
                for kc in range(8):
                    P.mm(pd[:, 0, 0:12], h[:, kc, 1 + bi * 128:1 + (bi + 1) * 128], wdt[:, kc, :], start=(kc == 0), stop=(kc == 7))
                d_ = dtmp[blk % 2]
                o = PR["dtb"][0]
                P.tt("dve", d_[:], pd[:, 0, 0:12], self.prt[:, o:o + 12], ALU.add)
                P.act(d_[:], d_[:], AF.Exp)
                P.act(dt_tm[:, blk, :], d_[:], AF.Ln, bias=1.0)
                P.tt("dve", dtA_tm[:, blk, :], dt_tm[:, blk, :], aneg[:], ALU.mult)
        P.release(mk1)
        if int(os.environ.get("MK_SSD", "9")) <= 1:
            self.dbg_xsT = xsT
            return
        wz = P.sb([128, 8, 384], BF16, "s_wz")
        P.dma("sp", wz[:], self.wb(l, C_Z, 384))
        cf = {}
        for nme in ("snf", "snb", "trif", "trib", "ones", "hlo", "hhi"):
            cf[nme] = P.sb([128, 128], F32, "s_c_" + nme)
            self.load_c(cf[nme], nme, "act")
        St = P.sb([128, 6, 64], F32, "s_S")
        NB2 = 2
        xs_sb = [P.sb([128, 384], F32, f"s_xs{i}") for i in range(NB2)]
        Bt_sb = [P.sb([128, 2, 128], BF16, f"s_Bt{i}") for i in range(NB2)]
        Gs = [P.sb([128, 2, 128], F32, f"s_G{i}") for i in range(NB2)]
        cumc = [P.sb([128, 6], F32, f"s_cc{i}") for i in range(NB2)]
        dbc = [P.sb([128, 6, 128], F32, f"s_dbc{i}") for i in range(NB2)]
        Dm = [P.sb([128, 6, 128], F32, f"s_D{i}") for i in range(NB2)]
        Em = [P.sb([128, 6, 128], F32, f"s_E{i}") for i in range(NB2)]
        scT = [P.sb([128, 6, 128], BF16, f"s_sc{i}") for i in range(NB2)]
        Er = [P.sb([128, 6, 128], F32, f"s_Er{i}") for i in range(NB2)]
        Cd = [P.sb([128, 6, 128], F32, f"s_Cd{i}") for i in range(NB2)]
        xdt = [P.sb([128, 6, 64], BF16, f"s_xdt{i}") for i in range(NB2)]
        xdtt = [P.sb([128, 6, 64], BF16, f"s_xdtt{i}") for i in range(NB2)]
        ysb = [P.sb([128, 3, 128], F32, f"s_y{i}") for i in range(NB2)]
        yfl = [P.sb([128, 3, 128], F32, f"s_yf{i}") for i in range(NB2)]
        hz = [P.sb([128, 8, 128], BF16, f"s_hz{i}") for i in range(NB2)]
        zs = [P.sb([128, 3, 128], F32, f"s_z{i}") for i in range(NB2)]
        sq = [P.sb([128, 3, 128], F32, f"s_sq{i}") for i in range(NB2)]
        rs = [P.sb([128, 2, 128], F32, f"s_rs{i}") for i in range(NB2)]
        mo = [P.sb([128, 3, 128], BF16, f"s_mo{i}") for i in range(NB2)]
        osd, onw = PP["ssmd"][0], PP["ssmnw"][0]
        it = 0
        for d in range(2):
            order = list(range(NBLK)) if d == 0 else [1, 0] + list(range(NBLK - 1, 1, -1))
            tri = cf["trif"] if d == 0 else cf["trib"]
            sneg = cf["snf"] if d == 0 else cf["snb"]
            last = 127 if d == 0 else 0
            P.memset("dve", St[:], 0.0)
            pending = None
            for c in order:
                if d == 1 and c < 2 and not ctx_out:
                    pass
                k = it % NB2
                it += 1
                cs_ = slice(c * 128, (c + 1) * 128)
                pxs = self.ps(0)
                for j in range(3):
                    P.tr(pxs[:, 0, j * 128:(j + 1) * 128], xsT[:, j, cs_], self.c_ident[:])
                P.copy("act", xs_sb[k][:], pxs[:, 0, 0:384])
                pbt = self.ps(1, 1, BF16)
                for g in range(2):
                    P.tr(pbt[:, 0, g * 128:(g + 1) * 128], BT[:, g, cs_], self.c_identb[:])
                P.copy("dve", Bt_sb[k][:].re("p g n -> p (g n)"), pbt[:, 0, 0:256])
                pg = self.ps(2)
                for g in range(2):
                    P.mm(pg[:, 0, g * 128:(g + 1) * 128], BT[:, g, cs_], CT[:, g, cs_])
                P.copy("act", Gs[k][:].re("p g n -> p (g n)"), pg[:, 0, 0:256])
                dA = dtA_tm[:, c, d * 6:(d + 1) * 6]
                P.mm(pg[:, 0, 256:262], tri[:], dA)
                P.copy("dve", cumc[k][:], pg[:, 0, 256:262])
                P.copy("pool", dbc[k][:], dA.m(lambda a: a.unsqueeze(2)).bc([128, 6, 128]))
                pcr = self.ps(3, 2)
                for h in range(6):
                    P.mm(pcr[:, h // 4, (h % 4) * 128:(h % 4 + 1) * 128], dbc[k][:, h, :], tri[:])
                pcr_v = pcr.re("p b n -> p (b n)")[:, 0:768].re("p (h n) -> p h n", h=6)
                P.tt("dve", Dm[k][:], pcr_v, cumc[k][:].m(lambda a: a.unsqueeze(2)).bc([128, 6, 128]), ALU.subtract)
                P.act(Er[k][:], pcr_v, AF.Exp)
                P.tt("pool", Dm[k][:], Dm[k][:], sneg[:].m(lambda a: a.unsqueeze(1)).bc([128, 6, 128]), ALU.add)
                P.act(Em[k][:], Dm[k][:], AF.Exp)
                P.tt("pool", scT[k][:].re("p (g h) n -> p g h n", g=2), Em[k][:].re("p (g h) n -> p g h n", g=2),
                     Gs[k][:].m(lambda a: a.unsqueeze(2)).bc([128, 2, 3, 128]), ALU.mult)
                P.tt("dve", Cd[k][:].re("p (g h) n -> p g h n", g=2), Er[k][:].re("p (g h) n -> p g h n", g=2),
                     CT[:, :, cs_].m(lambda a: a.unsqueeze(2)).bc([128, 2, 3, 128]), ALU.mult)
                dtc = dt_tm[:, c, d * 6:(d + 1) * 6]
                P.tt("dve", xdt[k][:], xs_sb[k][:].re("p (h q) -> p h q", h=6),
                     dtc.m(lambda a: a.unsqueeze(2)).bc([128, 6, 64]), ALU.mult)
                P.tt("pool", xdtt[k][:], xdt[k][:], Em[k][:, :, last:last + 1].bc([128, 6, 64]), ALU.mult)
                def back(c=c, k=k, cs_=cs_, d=d, last=last):
                    py = self.ps(5)
                    for h in range(6):
                        o_ = py[(h % 2) * 64:(h % 2) * 64 + 64, 0, (h // 2) * 128:(h // 2 + 1) * 128]
                        P.mm(o_, xdt[k][:, h, :], scT[k][:, h, :], start=True, stop=False)
                        P.mm(o_, St[:, h, :], Cd[k][:, h, :], start=False, stop=True)
                    pcs = self.ps(6)
                    for h in range(6):
                        P.mm(pcs[:, 0, h * 64:(h + 1) * 64], Bt_sb[k][:, h // 3, :], xdtt[k][:, h, :])
                    P.tt("dve", St[:], St[:], Er[k][:, :, last:last + 1].bc([128, 6, 64]), ALU.mult)
                    P.tt("dve", St[:], St[:], pcs[:, 0, 0:384].re("p (h q) -> p h q", h=6), ALU.add)
                    if d == 0:
                        P.copy("act", ysb[k][:].re("p j n -> p (j n)"), py[:, 0, 0:384])
                        P.dma("sp", self.yf.v(("yf", c), fn=lambda ap, cs_=cs_: ap.rearrange("(j p) t -> p j t", p=128)[:, :, cs_]), ysb[k][:])
                        return
                    if c < 2 and not ctx_out:
                        return
                    P.dma("act", yfl[k][:], self.yf.v(("yf", c), fn=lambda ap, cs_=cs_: ap.rearrange("(j p) t -> p j t", p=128)[:, :, cs_]))
                    P.dma("sp", hz[k][:], self.hT.v(("h", "z"), (slice(None), slice(None), cs_)))
                    pz = self.ps(7)
                    for j in range(3):
                        for kc in range(8):
                            P.mm(pz[:, 0, j * 128:(j + 1) * 128], wz[:, kc, j * 128:(j + 1) * 128], hz[k][:, kc, :],
                                 start=(kc == 0), stop=(kc == 7))
                    P.act(zs[k][:].re("p j n -> p (j n)"), pz[:, 0, 0:384], AF.Silu)
                    y_ = ysb[k]
                    P.tt("dve", y_[:].re("p j n -> p (j n)"), py[:, 0, 0:384], yfl[k][:].re("p j n -> p (j n)"), ALU.add)
                    P.tt("pool", yfl[k][:], xsT[:, :, cs_], self.ppt[:, osd:osd + 3].m(lambda a: a.unsqueeze(2)).bc([128, 3, 128]), ALU.mult)
                    P.tt("pool", y_[:], y_[:], yfl[k][:], ALU.add)
                    P.tt("dve", y_[:], y_[:], zs[k][:], ALU.mult)
                    P.act(sq[k][:], y_[:], AF.Square)
                    pgs = self.ps(7)
                    P.mm(pgs[:, 0, 384:512].m(lambda a: a), cf["ones"][:], sq[k][:, 0, :], start=True, stop=False)
                    P.mm(pgs[:, 0, 384:512], cf["hlo"][:], sq[k][:, 1, :], start=False, stop=True)
                    pgs2 = self.ps(6)
                    P.mm(pgs2[:, 0, 384:512], cf["hhi"][:], sq[k][:, 1, :], start=True, stop=False)
                    P.mm(pgs2[:, 0, 384:512], cf["ones"][:], sq[k][:, 2, :], start=False, stop=True)
                    P.act(rs[k][:, 0, :], pgs[:, 0, 384:512], AF.Sqrt, bias=1e-5, scale=1.0 / 192)
                    P.act(rs[k][:, 1, :], pgs2[:, 0, 384:512], AF.Sqrt, bias=1e-5, scale=1.0 / 192)
                    P.recip(rs[k][:], rs[k][:])
                    m_ = mo[k]
                    P.stt("dve", m_[:, 0, :], y_[:, 0, :], self.ppt[:, onw:onw + 1], rs[k][:, 0, :], ALU.mult, ALU.mult)
                    P.stt("dve", m_[0:64, 1, :], y_[0:64, 1, :], self.ppt[0:64, onw + 1:onw + 2], rs[k][0:64, 0, :], ALU.mult, ALU.mult)
                    P.stt("dve", m_[64:128, 1, :], y_[64:128, 1, :], self.ppt[64:128, onw + 1:onw + 2], rs[k][64:128, 1, :], ALU.mult, ALU.mult)
                    P.stt("dve", m_[:, 2, :], y_[:, 2, :], self.ppt[:, onw + 2:onw + 3], rs[k][:, 1, :], ALU.mult, ALU.mult)
                    P.dma("sp", self.mix.v(("ms", c), fn=lambda ap, cs_=cs_: ap[640:1024].rearrange("(j p) t -> p j t", p=128)[:, :, cs_]), m_[:])
                if pending is not None:
                    pending()
                pending = back
            if pending is not None:
                pending()

    def phase_rwkv(self, l):
        P = self.P
        ctx_out = (l < DEPTH - 1)
        LWC = -0.6065306597126334
        cf = {}
        for nme in ("blk", "lowf", "lowb"):
            cf[nme] = P.sb([128, 128], F32, "r_c_" + nme)
            self.load_c(cf[nme], nme, "act")
        m4 = {nme: P.sb([128, 2, 256], F32, "r_m4_" + nme) for nme in ("rmf", "rmb")}
        mk0 = P.mark()
        for nme in ("rmf", "rmb"):
            t_ = P.sb([128, 256], F32, "r_c_" + nme)
            self.load_c(t_, nme, "act")
            P.copy("dve", m4[nme][:], t_[:].m(lambda a: a.unsqueeze(1)).bc([128, 2, 256]))
        P.release(mk0)
        ones64 = P.sb([128, 64], F32, "r_ones")
        P.memset("pool", ones64[:], 1.0)
        omka = P.sb([128, 4], F32, "r_omka")
        oka = PP["ka"][0]
        P.ts("dve", omka[:], self.ppt[:, oka:oka + 4], -1.0, 1.0, ALU.mult, ALU.add)
        wupf = P.sb([128, 256], F32, "r_wupf")
        aupf = P.sb([128, 256], F32, "r_aupf")
        wupb = P.sb([128, 256], BF16, "r_wupb")
        aupb = P.sb([128, 256], BF16, "r_aupb")
        for d in range(2):
            P.dma("sp", wupf[d * 64:(d + 1) * 64, :], self.wup.v(("u", l, d), fn=lambda ap, d=d: ap[l][d]))
            P.dma("act", aupf[d * 64:(d + 1) * 64, :], self.aup.v(("u", l, d), fn=lambda ap, d=d: ap[l][d]))
        P.copy("dve", wupb[:], wupf[:])
        P.copy("pool", aupb[:], aupf[:])
        wdT = P.sb([128, S], BF16, "r_wdT")
        adT = P.sb([128, S], BF16, "r_adT")
        mk1 = P.mark()
        rows = self.shift_rows(l)
        hh = [P.sb([128, 8, 514], BF16, f"r_hh{i}") for i in range(2)]
        wl3 = self.scaled_w(l, C_RW + 768, 256, rows, "r_wl3", roff=768)
        for ti, (t0, tn) in enumerate(TILES):
            h = hh[ti % 2]
            self.load_h_halo(h, ti, "sp" if ti % 2 == 0 else "act")
            for c in range(2):
                pb = self.ps(c)
                n = 0
                for tap in range(3):
                    for kc in range(8):
                        P.mm(pb[:, 0, 0:tn], wl3[:, tap, kc, c * 128:(c + 1) * 128], h[:, kc, tap:tap + tn],
                             start=(n == 0), stop=(n == 23))
                        n += 1
                if c == 0:
                    P.act(wdT[:, t0:t0 + tn], pb[:, 0, 0:tn], AF.Tanh)
                else:
                    P.copy("act", adT[:, t0:t0 + tn], pb[:, 0, 0:tn])
        P.release(mk1)
        RW = int(os.environ.get("MK_RW", "99"))
        if RW <= 1:
            return
        mk_hp = P.mark()
        for hp in range(2):
            rT = P.sb([128, S], F32, "r_rT")
            kT = P.sb([128, S], F32, "r_kT")
            vT = P.sb([128, S], F32, "r_vT")
            ysum = P.sb([128, S], F32, "r_ysum")
            mk2 = P.mark()
            rows = self.shift_rows(l)
            hh = [P.sb([128, 8, 514], BF16, f"r_hh{i}") for i in range(2)]
            w3 = [self.scaled_w(l, C_RW + j * 256 + hp * 128, 128, rows, f"r_w3{j}", roff=j * 256 + hp * 128) for j in range(3)]
            for ti, (t0, tn) in enumerate(TILES):
                h = hh[ti % 2]
                self.load_h_halo(h, ti, "sp" if ti % 2 == 0 else "act")
                for j, dst in enumerate((rT, kT, vT)):
                    pb = self.ps(j % 2)
                    n = 0
                    for tap in range(3):
                        for kc in range(8):
                            P.mm(pb[:, 0, 0:tn], w3[j][:, tap, kc, :], h[:, kc, tap:tap + tn], start=(n == 0), stop=(n == 23))
                            n += 1
                    P.copy("act", dst[:, t0:t0 + tn], pb[:, 0, 0:tn])
            P.release(mk2)
            if RW <= 2:
                return
            mk3 = P.mark()
            self.rwkv_scan(l, hp, rT, kT, vT, wdT, adT, ysum, wupb, aupb, cf, m4, ones64, omka, LWC)
            P.release(mk3)
            self.rwkv_finish(l, hp, rT, kT, vT, ysum, cf, ctx_out)
            P.release(mk_hp)

    def shift_rows(self, l):
        P = self.P
        o = PR2["shr"][0]
        sr = P.sb([128, 3 * 1024], F32, "r_shr")
        P.dma("sp", sr[:], self.pr2.v(l, fn=lambda a: a[l][:, o:o + 3 * 1024]))
        r0, r2, r1 = sr[:, 0:1024], sr[:, 1024:2048], sr[:, 2048:3072]
        P.tt("dve", r1, r0, r2, ALU.add)
        P.ts("dve", r1, r1, -1.0, 1.0, ALU.mult, ALU.add)
        return [r0, r1, r2]

    def rwkv_scan(self, l, hp, rT, kT, vT, wdT, adT, ysum, wupb, aupb, cf, m4, ones64, omka, LWC):
        P = self.P
        NB = 2
        f32t = lambda n, w=512: [P.sb([128, w], F32, f"r_{n}")] * NB
        sg, aa, cum = f32t("sg"), f32t("aa"), f32t("cum")
        lw = sg
        cumx = [P.sb([128, 8], F32, "r_tot")] * NB
        e1, e2, e3, e4 = f32t("e1"), f32t("e2"), f32t("e3"), f32t("e4")
        kkr, sqk, nrm, kd, bq = f32t("kkr"), f32t("sqk"), f32t("nrm"), f32t("kd"), f32t("bq")
        kk, tmpa = kkr, nrm
        WC = [P.sb([128, 8], F32, f"r_WC{i}") for i in range(NB)]
        arb = [P.sb([128, 4, 2, 128], F32, "r_arb")] * NB
        kb = [P.sb([128, 512], F32, "r_kb")] * NB
        bb = [P.sb([128, 512], F32, "r_bb")] * NB
        kt = [P.sb([128, 512], F32, "r_kt")] * NB
        btl = [P.sb([128, 512], F32, "r_btl")] * NB
        G4 = 4
        TM = [P.sb([128, 4, 128], F32, f"r_TM{i}") for i in range(G4)]
        AT = [P.sb([128, 2, 4, 128], F32, f"r_AT{i}") for i in range(G4)]
        X1 = [P.sb([128, 2, 128], F32, f"r_X1{i}") for i in range(G4)]
        XX = [P.sb([128, 2, 2, 128], F32, f"r_XX{i}") for i in range(G4)]
        Qf = [P.sb([128, 2, 128], F32, f"r_Qf{i}") for i in range(G4)]
        W1 = [P.sb([128, 128], F32, f"r_W1{i}") for i in range(G4)]
        AU = [P.sb([128, 256], F32, f"r_AU{i}") for i in range(G4)]
        Y0T = [P.sb([128, 128], F32, f"r_Y0T{i}") for i in range(G4)]
        RhT = [P.sb([128, 128], F32, f"r_RhT{i}") for i in range(G4)]
        MTt = [P.sb([128, 128], F32, "r_MTt")] * 2
        MT = [[P.sb([128, 128], F32, f"r_MT{i}_{c}") for c in range(2)] for i in range(G4)]
        Ns = [[P.sb([128, 64], F32, f"r_Ns{i}_{c}") for c in range(2)] for i in range(G4)]
        ytmp = [P.sb([128, 64], F32, f"r_yt{i}") for i in range(2)]
        Sseq = P.sb([128, 4, 64], F32, "r_Sseq")
        ow0, oa0, okk, oka = PP["w0"][0], PP["a0"][0], PP["kk"][0], PP["ka"][0]
        if os.environ.get("MK_MEM"):
            print("rwkv_scan arena used", P._aoff, "of", P.ARENA)
        nblk_done = 0
        for d in range(2):
            if d == 1 and hp == 0 and os.environ.get("MK_RWDBG"):
                if not hasattr(self, "dbgy"):
                    self.dbgy = P.dram("dbgy", [128, S], F32, kind="ExternalOutput")
                P.dma("sp", self.dbgy.v(0), ysum[:])
            col = d * 2 + hp
            mask4 = m4["rmf"] if d == 0 else m4["rmb"]
            low = cf["lowf"] if d == 0 else cf["lowb"]
            P.memset("dve", Sseq[:, 0, :], 0.0)
            si = 0
            tiles = list(range(len(TILES))) if d == 0 else [0] + list(range(len(TILES) - 1, 0, -1))
            def prepA(tix):
                ti = tiles[tix]
                t0, tn = TILES[ti]
                nch = tn // 64
                nbk = tn // 128
                k_ = tix % NB
                ts_ = slice(t0, t0 + tn)
                ds_ = slice(d * 64, (d + 1) * 64)
                pxw, pxa = self.ps(0), self.ps(1)
                P.mm(pxw[:, 0, 0:tn], wupb[ds_, hp * 128:(hp + 1) * 128], wdT[ds_, ts_])
                P.act(sg[k_][:, 0:tn], pxw[:, 0, 0:tn], AF.Sigmoid, bias=self.ppt[:, ow0 + col:ow0 + col + 1])
                P.mm(pxa[:, 0, 0:tn], aupb[ds_, hp * 128:(hp + 1) * 128], adT[ds_, ts_])
                P.act(aa[k_][:, 0:tn], pxa[:, 0, 0:tn], AF.Sigmoid, bias=self.ppt[:, oa0 + col:oa0 + col + 1])
                yield
                P.ts("dve", lw[k_][:, 0:tn], sg[k_][:, 0:tn], LWC, None, ALU.mult)
                yield
                for c in range(nch):
                    cs = slice(c * 64, (c + 1) * 64)
                    P.scan("dve", cum[k_][:, cs], ones64[:], lw[k_][:, cs], 0.0, ALU.mult, ALU.add)
                    yield
                c3 = lambda t_: t_[:, 0:tn].re("p (c n) -> p c n", n=64)
                P.act(WC[k_][:, 0:nch].m(lambda a: a.unsqueeze(2)), c3(cum[k_])[:, :, 63:64], AF.Exp)
                yield
                if d == 1:
                    P.copy("pool", cumx[k_][:, 0:nch].m(lambda a: a.unsqueeze(2)), c3(cum[k_])[:, :, 63:64])
                    yield
                    P.tt("dve", cum[k_][:, 0:tn], lw[k_][:, 0:tn], cum[k_][:, 0:tn], ALU.subtract)
                    yield
                    P.tt("dve", c3(cum[k_]), c3(cum[k_]), cumx[k_][:, 0:nch].m(lambda a: a.unsqueeze(2)).bc([128, nch, 64]), ALU.add)
                    yield
                    tot = None
                P.tt("pool", e4[k_][:, 0:tn], cum[k_][:, 0:tn], lw[k_][:, 0:tn], ALU.subtract)
                yield
                P.act(e1[k_][:, 0:tn], cum[k_][:, 0:tn], AF.Exp)
                yield
                P.act(e2[k_][:, 0:tn], cum[k_][:, 0:tn], AF.Exp, scale=-1.0)
                yield
                P.act(e3[k_][:, 0:tn], e4[k_][:, 0:tn], AF.Exp)
                yield
                P.tt("dve", c3(e4[k_]), c3(e2[k_]), WC[k_][:, 0:nch].m(lambda a: a.unsqueeze(2)).bc([128, nch, 64]), ALU.mult)
                yield
                P.ts("dve", kkr[k_][:, 0:tn], kT[:, ts_], self.ppt[:, okk + col:okk + col + 1], None, ALU.mult)
                yield
                P.act(sqk[k_][:, 0:tn], kkr[k_][:, 0:tn], AF.Square)
                yield
                pss = self.ps(1)
                P.mm(pss[:, 0, 0:tn], cf["blk"][:], sqk[k_][:, 0:tn])
                P.act(nrm[k_][:, 0:tn], pss[:, 0, 0:tn], AF.Sqrt)
                yield
                P.ts("dve", nrm[k_][:, 0:tn], nrm[k_][:, 0:tn], 1e-12, None, ALU.max)
                yield
                P.recip(nrm[k_][:, 0:tn], nrm[k_][:, 0:tn])
                yield
                P.tt("dve", kk[k_][:, 0:tn], kkr[k_][:, 0:tn], nrm[k_][:, 0:tn], ALU.mult)
                yield
                P.ts("pool", tmpa[k_][:, 0:tn], aa[k_][:, 0:tn], self.ppt[:, oka + col:oka + col + 1], omka[:, col:col + 1], ALU.mult, ALU.add)
                yield
                P.tt("pool", kd[k_][:, 0:tn], kT[:, ts_], tmpa[k_][:, 0:tn], ALU.mult)
                yield
                P.tt("dve", bq[k_][:, 0:tn], kk[k_][:, 0:tn], aa[k_][:, 0:tn], ALU.mult)
                yield
            def prepB(tix):
                ti = tiles[tix]
                t0, tn = TILES[ti]
                nch = tn // 64
                nbk = tn // 128
                k_ = tix % NB
                ts_ = slice(t0, t0 + tn)
                ds_ = slice(d * 64, (d + 1) * 64)
                c3 = lambda t_: t_[:, 0:tn].re("p (c n) -> p c n", n=64)
                b3 = lambda t_: t_[:, 0:tn].re("p (b n) -> p b n", n=128)
                P.ts("pool", sqk[k_][:, 0:tn], kk[k_][:, 0:tn], -1.0, None, ALU.mult)
                yield
                P.tt("dve", arb[k_][:, 0:nbk, 0, :], b3(sqk[k_]), b3(e3[k_]), ALU.mult)
                yield
                P.tt("pool", arb[k_][:, 0:nbk, 1, :], rT[:, ts_].re("p (b n) -> p b n", n=128), b3(e1[k_]), ALU.mult)
                yield
                P.tt("dve", kb[k_][:, 0:tn], kd[k_][:, 0:tn], e2[k_][:, 0:tn], ALU.mult)
                yield
                P.tt("pool", bb[k_][:, 0:tn], bq[k_][:, 0:tn], e2[k_][:, 0:tn], ALU.mult)
                yield
                P.tt("dve", kt[k_][:, 0:tn], kd[k_][:, 0:tn], e4[k_][:, 0:tn], ALU.mult)
                yield
                P.tt("pool", btl[k_][:, 0:tn], bq[k_][:, 0:tn], e4[k_][:, 0:tn], ALU.mult)
                yield
            def step(gen, n=10 ** 9):
                for _ in range(n):
                    if next(gen, 'done') == 'done':
                        return
            step(prepA(0))
            step(prepB(0))
            for tix, ti in enumerate(tiles):
                t0, tn = TILES[ti]
                nch = tn // 64
                nbk = tn // 128
                k_ = tix % NB
                ts_ = slice(t0, t0 + tn)
                ds_ = slice(d * 64, (d + 1) * 64)
                nxtA = prepA(tix + 1) if tix + 1 < len(tiles) else iter(())
                blocks = list(range(nbk)) if d == 0 else list(range(nbk - 1, -1, -1))
                G = len(blocks)
                idm = lambda t_: t_[:].m(lambda a: a.unsqueeze(1)).bc([128, 2, 128])
                ev = 0
                for g, bi in enumerate(blocks):
                    bs = slice(bi * 128, (bi + 1) * 128)
                    pt = self.ps(2 + g)
                    P.tr(pt[:, 0, 0:128], arb[k_][:, bi, 0, :], self.c_ident[:])
                    P.tr(pt[:, 0, 128:256], btl[k_][:, bs], self.c_ident[:])
                    P.tr(pt[:, 0, 256:384], kt[k_][:, bs], self.c_ident[:])
                    P.tr(pt[:, 0, 384:512], vT[:, t0 + bi * 128:t0 + (bi + 1) * 128], self.c_ident[:])
                for g, bi in enumerate(blocks):
                    P.copy("act" if g % 2 == 0 else "dve", TM[g][:].re("p a n -> p (a n)"), self.ps(2 + g)[:, 0, 0:512])
                px2 = self.ps(6, 2)
                for half in range(0, G, 2):
                    gs = list(range(half, min(half + 2, G)))
                    for g in gs:
                        bi = blocks[g]
                        bs = slice(bi * 128, (bi + 1) * 128)
                        pa = self.ps(2 + 2 * (g % 2), 2)
                        for h in range(2):
                            hs = slice(h * 64, (h + 1) * 64)
                            ar_ = arb[k_][hs, bi, :, :].re("p a n -> p (a n)")
                            P.mm(pa[:, h, 0:256], bb[k_][hs, bs], ar_)
                            P.mm(pa[:, h, 256:512], kb[k_][hs, bs], ar_)
                            P.mm(px2[:, h, g * 128:(g + 1) * 128], arb[k_][hs, bi, 0, :], bb[k_][hs, bs])
                    for g in gs:
                        pa = self.ps(2 + 2 * (g % 2), 2)
                        P.tt("dve", AT[g][:].re("p h a n -> p h (a n)"), pa,
                             mask4[:].re("p a n -> p (a n)").m(lambda a: a.unsqueeze(1)).bc([128, 2, 512]), ALU.mult)
                for g in range(G):
                    P.tt("dve", X1[g][:], px2[:, :, g * 128:(g + 1) * 128], idm(low), ALU.mult)
                    P.tt("pool", Qf[g][:], AT[g][:, :, 0, :], idm(self.c_ident), ALU.add)
                xk = [[X1[g][:, h, :] for h in range(2)] for g in range(G)]
                xtk = [[AT[g][:, h, 0, :] for h in range(2)] for g in range(G)]
                for lev in range(int(os.environ.get('MK_NLEV', '5'))):
                    for g in range(G):
                        pn = self.ps(2 + g)
                        for h in range(2):
                            P.mm(pn[:, 0, h * 256:h * 256 + 128], xtk[g][h], xk[g][h])
                            if lev < 4:
                                P.mm(pn[:, 0, h * 256 + 128:h * 256 + 256], xk[g][h], xtk[g][h])
                    for g in range(G):
                        P.copy("act", XX[g][:].re("p h a n -> p (h a n)"), self.ps(2 + g)[:, 0, :])
                        xk[g] = [XX[g][:, h, 0, :] for h in range(2)]
                        xtk[g] = [XX[g][:, h, 1, :] for h in range(2)]
                    for g in range(G):
                        pq = self.ps(6 + g // 2)
                        for h in range(2):
                            c0 = (g % 2) * 256 + h * 128
                            P.mm(pq[:, 0, c0:c0 + 128], xk[g][h], Qf[g][:, h, :])
                    for g in range(G):
                        pq = self.ps(6 + g // 2)
                        c0 = (g % 2) * 256
                        P.tt("dve", Qf[g][:], pq[:, 0, c0:c0 + 256].re("p (h n) -> p h n", h=2), Qf[g][:], ALU.add)
                    step(nxtA, 7)
                pw = self.ps(0)
                for g in range(G):
                    for h in range(2):
                        P.mm(pw[:, 0, g * 128 + h * 64:g * 128 + (h + 1) * 64], AT[g][:, h, 2, :], TM[g][:, 3, h * 64:(h + 1) * 64])
                for g in range(G):
                    P.copy("act", W1[g][:], pw[:, 0, g * 128:(g + 1) * 128])
                for g in range(G):
                    pau = self.ps(2 + g // 2)
                    c0 = (g % 2) * 256
                    for h in range(2):
                        P.mm(pau[:, 0, c0 + h * 64:c0 + (h + 1) * 64], Qf[g][:, h, :], TM[g][:, 0, h * 64:(h + 1) * 64])
                        P.mm(pau[:, 0, c0 + 128 + h * 64:c0 + 128 + (h + 1) * 64], Qf[g][:, h, :], W1[g][:, h * 64:(h + 1) * 64])
                for g in range(G):
                    pau = self.ps(2 + g // 2)
                    c0 = (g % 2) * 256
                    P.copy("act" if g % 2 == 0 else "dve", AU[g][:], pau[:, 0, c0:c0 + 256])
                for g, bi in enumerate(blocks):
                    py = self.ps(4 + g // 2)
                    c0 = (g % 2) * 256
                    for h in range(2):
                        hs = slice(h * 64, (h + 1) * 64)
                        P.mm(py[hs, 0, c0:c0 + 128], AU[g][:, 128 + h * 64:128 + (h + 1) * 64], AT[g][:, h, 1, :], start=True, stop=False)
                        P.mm(py[hs, 0, c0:c0 + 128], TM[g][:, 3, h * 64:(h + 1) * 64], AT[g][:, h, 3, :], start=False, stop=True)
                        P.mm(py[hs, 0, c0 + 128:c0 + 256], AU[g][:, h * 64:(h + 1) * 64], AT[g][:, h, 1, :])
                for g, bi in enumerate(blocks):
                    py = self.ps(4 + g // 2)
                    c0 = (g % 2) * 256
                    P.copy("act", Y0T[g][:], py[:, 0, c0:c0 + 128])
                    P.tt("dve", RhT[g][:], py[:, 0, c0 + 128:c0 + 256], arb[k_][:, bi, 1, :], ALU.add)
                step(nxtA)
                if tix + 1 < len(tiles):
                    step(prepB(tix + 1))
                for g, bi in enumerate(blocks):
                    for cc in range(2):
                        cs = slice(cc * 64, (cc + 1) * 64)
                        pmn = self.ps((6 if g < 2 else 2) + cc)
                        c0 = (g % 2) * 192
                        P.mm(pmn[:, 0, c0:c0 + 128], AU[g][cs, 0:128], TM[g][cs, 1, :])
                        for h in range(2):
                            hs = slice(h * 64, (h + 1) * 64)
                            P.mm(pmn[hs, 0, c0 + 128:c0 + 192], TM[g][cs, 1, hs], AU[g][cs, 128 + h * 64:128 + (h + 1) * 64], start=True, stop=False)
                            P.mm(pmn[hs, 0, c0 + 128:c0 + 192], TM[g][cs, 2, hs], TM[g][cs, 3, hs], start=False, stop=True)
                for g, bi in enumerate(blocks):
                    for cc in range(2):
                        pmn = self.ps((6 if g < 2 else 2) + cc)
                        c0 = (g % 2) * 192
                        mt_ = MTt[cc]
                        P.tt("dve", mt_[:], pmn[:, 0, c0:c0 + 128], cf["blk"][:], ALU.mult)
                        wcol = bi * 2 + cc
                        P.stt("dve", MT[g][cc][:], self.c_ident[:], WC[k_][:, wcol:wcol + 1], mt_[:], ALU.mult, ALU.add)
                        P.copy("act", Ns[g][cc][:], pmn[:, 0, c0 + 128:c0 + 192])
                for g, bi in enumerate(blocks if not os.environ.get('MK_NOSEQ') else []):
                    for cc in ([0, 1] if d == 0 else [1, 0]):
                        cs = slice(cc * 64, (cc + 1) * 64)
                        tok = slice(t0 + bi * 128 + cc * 64, t0 + bi * 128 + cc * 64 + 64)
                        pyh = [self.ps(4), self.ps(5)]
                        pss_ = self.ps(0)
                        for h in range(2):
                            hs = slice(h * 64, (h + 1) * 64)
                            P.mm(pyh[h][hs, 0, 0:64], Sseq[hs, si % 4, :], RhT[g][hs, cs])
                        P.mm(pss_[:, 0, 0:64], MT[g][cc][:], Sseq[:, si % 4, :])
                        P.tt("dve", Sseq[:, (si + 1) % 4, :], pss_[:, 0, 0:64], Ns[g][cc][:], ALU.add)
                        for h in range(2):
                            hs = slice(h * 64, (h + 1) * 64)
                            if d == 0:
                                P.tt("pool" if False else "dve", ysum[hs, tok], pyh[h][hs, 0, 0:64], Y0T[g][hs, cs], ALU.add)
                            else:
                                yt_ = ytmp[cc]
                                P.tt("dve", yt_[hs, :], pyh[h][hs, 0, 0:64], Y0T[g][hs, cs], ALU.add)
                                P.tt("pool", ysum[hs, tok], ysum[hs, tok], yt_[hs, :], ALU.add)
                        si += 1

    def rwkv_finish(self, l, hp, rT, kT, vT, ysum, cf, ctx_out):
        P = self.P
        wg = P.sb([128, 8, 128], BF16, "rf_wg")
        P.dma("sp", wg[:], self.wb(l, C_RG + hp * 128, 128))
        hb = [P.sb([128, 8, 512], BF16, f"rf_h{i}") for i in range(2)]
        f = lambda n: [P.sb([128, 512], F32, f"rf_{n}{i}") for i in range(2)]
        yc, sq, rstd, rk, gs = f("yc"), f("sq"), f("rstd"), f("rk"), f("gs")
        mo = [P.sb([128, 512], BF16, f"rf_mo{i}") for i in range(2)]
        olw, olb, ork = PP["lnw"][0] + hp, PP["lnb"][0] + hp, PP["rk"][0] + hp
        for ti, (t0, tn) in enumerate(TILES):
            if ti == 0 and not ctx_out:
                continue
            k_ = ti % 2
            ts_ = slice(t0, t0 + tn)
            self.load_h(hb[k_], ti, "sp" if k_ == 0 else "act")
            pg = self.ps(0)
            for kc in range(8):
                P.mm(pg[:, 0, 0:tn], wg[:, kc, :], hb[k_][:, kc, 0:tn], start=(kc == 0), stop=(kc == 7))
            P.act(gs[k_][:, 0:tn], pg[:, 0, 0:tn], AF.Silu)
            pm = self.ps(1)
            P.mm(pm[:, 0, 0:tn], cf["blk"][:], ysum[:, ts_])
            P.stt("dve", yc[k_][:, 0:tn], pm[:, 0, 0:tn], -1.0 / 64, ysum[:, ts_], ALU.mult, ALU.add)
            P.act(sq[k_][:, 0:tn], yc[k_][:, 0:tn], AF.Square)
            pv = self.ps(2)
            P.mm(pv[:, 0, 0:tn], cf["blk"][:], sq[k_][:, 0:tn])
            P.act(rstd[k_][:, 0:tn], pv[:, 0, 0:tn], AF.Sqrt, bias=64e-5, scale=1.0 / 64)
            P.recip(rstd[k_][:, 0:tn], rstd[k_][:, 0:tn])
            P.tt("dve", yc[k_][:, 0:tn], yc[k_][:, 0:tn], rstd[k_][:, 0:tn], ALU.mult)
            P.ts("dve", yc[k_][:, 0:tn], yc[k_][:, 0:tn], self.ppt[:, olw:olw + 1], self.ppt[:, olb:olb + 1], ALU.mult, ALU.add)
            P.stt("dve", rk[k_][:, 0:tn], rT[:, ts_], self.ppt[:, ork:ork + 1], kT[:, ts_], ALU.mult, ALU.mult)
            pb = self.ps(3)
            P.mm(pb[:, 0, 0:tn], cf["blk"][:], rk[k_][:, 0:tn])
            P.tt("dve", rk[k_][:, 0:tn], pb[:, 0, 0:tn], vT[:, ts_], ALU.mult)
            P.tt("pool", yc[k_][:, 0:tn], yc[k_][:, 0:tn], rk[k_][:, 0:tn], ALU.add)
            P.tt("dve", mo[k_][:, 0:tn], yc[k_][:, 0:tn], gs[k_][:, 0:tn], ALU.mult)
            r0_ = 384 + hp * 128
            P.dma("sp", self.mix.v(("mr", hp, ti), (slice(r0_, r0_ + 128), ts_)), mo[k_][:, 0:tn])

    def zero_mix(self, r0, r1):
        P = self.P
        z = P.sb([128, S], BF16, "zmix")
        P.memset("pool", z[:], 0.0)
        for kc in range(r0 // 128, r1 // 128):
            P.dma("sp", self.mix.v(("mz", kc), (slice(kc * 128, kc * 128 + 128), slice(None))), z[:])

    def build(self, layers=DEPTH):
        P = self.P
        base = P.mark()
        self.phase_w()
        P.release(base)
        for l in range(layers):
            self.phase_mod(l)
            keep = P.mark()
            self.phase_norm(l)
            P.release(keep)
            if self.stop_after == "norm":
                return
            if not os.environ.get("MK_SKIP_ATT"):
                self.phase_att(l)
                P.release(keep)
            if self.stop_after == "att":
                return
            if os.environ.get("MK_SKIP_RWKV"):
                self.zero_mix(384, 640)
            else:
                self.phase_rwkv(l)
            P.release(keep)
            if self.stop_after == "rwkv":
                return
            self.phase_ssd(l)
            P.release(keep)
            if self.stop_after == "ssd":
                return
            self.phase_out(l)
            P.release(keep)
            for a in ("n_x", "o_w"):
                if hasattr(self, a):
                    delattr(self, a)

def build_program(stop_after=None, dbg=False, layers=DEPTH):
    nc = bass.Bass("TRN2", target_bir_lowering=False)
    k = MK(nc, stop_after=stop_after, dbg=dbg)
    k.build(layers)
    k.P.emit()
    return nc, k


def prep_inputs(inp, b):
    inp = {k: np.asarray(v) for k, v in inp.items()}
    d = {}
    d["xin"] = np.ascontiguousarray(np.concatenate([inp["ctx"][b], inp["x"][b]], 0).astype(np.float32))
    d["cvec"] = np.ascontiguousarray(np.concatenate([fm(inp["c"][b]), fm(inp["c_ctx"])], 1))
    d["ada_w"] = np.ascontiguousarray(inp["ada_w"], np.float32)
    d["w_in"] = np.ascontiguousarray(inp["w_in"], np.float32)
    d["w_out"] = np.ascontiguousarray(inp["w_out"], np.float32)
    d["wup"] = np.ascontiguousarray(inp["rwkv_w_up"], np.float32)
    d["aup"] = np.ascontiguousarray(inp["rwkv_a_up"], np.float32)
    d["pp"] = np.stack([make_pp(inp, l) for l in range(DEPTH)])
    d["pr"] = np.stack([make_pr(inp, l) for l in range(DEPTH)])
    d["pr2"] = np.stack([make_pr2(inp, l) for l in range(DEPTH)])
    d["cst"] = make_consts()
    return d


def kernel(**inputs):
    nc, _ = build_program()
    maps = [prep_inputs(inputs, b % 4) for b in range(4)]
    in_maps = [maps[i % 4] for i in range(8)]
    res = run_bass_kernel_spmd(nc, in_maps, core_ids=list(range(8)))
    return np.stack([np.asarray(res.results[b]["y"], np.float32) for b in range(4)], 0)
```

```python
import numpy as np
import concourse.bass as bass
import concourse.mybir as mybir

F32 = mybir.dt.float32
BF16 = mybir.dt.bfloat16
ALU = mybir.AluOpType
AF = mybir.ActivationFunctionType
AX = mybir.AxisListType

import os
DUMP = os.environ.get("MK_DUMP", "")
DMAQ = dict(kv.split(":") for kv in filter(None, os.environ.get("MK_DMAQ", "act:sp").split(",")))
ATTACH = os.environ.get("MK_ATTACH", "1") != "0"
NOSYNC = set(filter(None, os.environ.get("MK_NOSYNC", "").split(",")))
ENGS = ("pe", "dve", "act", "pool", "sp")
N_DMA_SEMS = 20


class Trk:
    __slots__ = ("lw", "rd", "ps")

    def __init__(self, ps=False):
        self.lw = None
        self.rd = []
        self.ps = ps


class V:
    __slots__ = ("ap", "trk")

    def __init__(self, ap, trk):
        self.ap = ap
        self.trk = trk

    def __getitem__(self, idx):
        return V(self.ap[idx], self.trk)

    def m(self, fn):
        return V(fn(self.ap), self.trk)

    def re(self, pat, **kw):
        return V(self.ap.rearrange(pat, **kw), self.trk)

    def bc(self, shape):
        return V(self.ap.broadcast_to(shape), self.trk)

    @property
    def shape(self):
        return self.ap.shape


class Tile:
    def __init__(self, handle):
        self.h = handle
        self.trk = Trk()

    def __getitem__(self, idx):
        return V(self.h[idx], (self.trk,))

    def ap(self):
        return V(self.h.ap() if hasattr(self.h, "ap") else self.h[:], (self.trk,))


class DTile:
    def __init__(self, handle):
        self.h = handle
        self.reg = {}

    def v(self, key, idx=None, fn=None):
        t = self.reg.setdefault(key, Trk())
        ap = self.h.ap()
        if fn is not None:
            ap = fn(ap)
        if idx is not None:
            ap = ap[idx]
        return V(ap, (t,))


class Op:
    __slots__ = ("eng", "fn", "reads", "writes", "idx", "deps", "signal", "sig_count",
                 "is_dma", "dma_sem", "dma_val", "dma_prev", "kind")


class Prog:
    def __init__(self, nc):
        self.nc = nc
        self.ops = {e: [] for e in ENGS}
        self.n_t = 0
        self.all_ops = []

    ARENA = 207 * 1024

    def sb(self, shape, dtype, name=None):
        self.n_t += 1
        if not hasattr(self, "_abase"):
            a = self.nc.alloc_sbuf_tensor("arena", [128, self.ARENA], mybir.dt.uint8)
            self._abase = self.nc.lookup_mloc(a).addr
            self._aoff = 0
        esz = 2 if dtype == BF16 else 4
        n = esz
        for d in shape[1:]:
            n *= d
        off = (self._aoff + 63) // 64 * 64
        assert off + n <= self.ARENA, f"SBUF arena overflow allocating {name} {shape}: {off + n}"
        self._aoff = off + n
        return Tile(self.nc.alloc_sbuf_tensor_at(f"{name or 'sb'}_{self.n_t}", list(shape), dtype, offset=self._abase + off))

    def mark(self):
        return getattr(self, "_aoff", 0)

    def release(self, mark):
        self.barrier()
        self._aoff = mark

    def barrier(self):
        lasts = []
        for e in ENGS:
            for op in reversed(self.ops[e]):
                if not op.is_dma and op.kind != "bar":
                    lasts.append(op)
                    break
        dmas = [op for op in self.all_ops[getattr(self, "_bar_pos", 0):] if op.is_dma]
        self._bar_pos = len(self.all_ops)
        for e in ENGS:
            op = self.rec(e, lambda eh: eh.nop(nofuse=True), [], [], kind="bar")
            op.deps = [d for d in lasts if d.eng != e] + dmas
            for d in op.deps:
                d.signal = True

    def ps(self, shape, dtype=F32, name=None):
        self.n_t += 1
        return Tile(self.nc.alloc_psum_tensor(name or f"ps{self.n_t}", list(shape), dtype))

    def dram(self, name, shape, dtype, kind="Internal"):
        return DTile(self.nc.dram_tensor(name, list(shape), dtype, kind=kind))

    def rec(self, eng, fn, reads, writes, is_dma=False, kind=""):
        op = Op()
        op.eng = eng
        op.fn = fn
        op.kind = kind
        op.is_dma = is_dma
        op.idx = len(self.ops[eng])
        op.signal = False
        op.deps = []
        rt = []
        for v in reads:
            if v is None or not isinstance(v, V):
                continue
            rt.extend(v.trk)
        wt = []
        for v in writes:
            if v is None or not isinstance(v, V):
                continue
            wt.extend(v.trk)
        deps = set()
        for t in rt:
            if t.lw is not None:
                deps.add(t.lw)
            if t.ps:
                for r in t.rd:
                    if r.eng != eng:
                        deps.add(r)
        for t in wt:
            if t.lw is not None:
                deps.add(t.lw)
            for r in t.rd:
                deps.add(r)
        deps.discard(op)
        for t in rt:
            t.rd.append(op)
        for t in wt:
            t.lw = op
            t.rd = []
        final = []
        for d in deps:
            if d.eng == "pe" and eng == "pe" and not d.is_dma and not is_dma:
                continue
            if d.eng == eng and eng in NOSYNC and not d.is_dma and not is_dma:
                continue
            final.append(d)
        op.deps = final
        for d in final:
            d.signal = True
        self.ops[eng].append(op)
        self.all_ops.append(op)
        return op

    def mm(self, out, lhsT, rhs, start=True, stop=True):
        return self.rec("pe", lambda e: e.matmul(out.ap, lhsT.ap, rhs.ap, start=start, stop=stop),
                        [lhsT, rhs], [out], kind="mm")

    def tr(self, out, in_, ident):
        return self.rec("pe", lambda e: e.transpose(out.ap, in_.ap, ident.ap), [in_, ident], [out], kind="tr")

    def act(self, out, in_, func, bias=None, scale=None, accum_out=None, eng="act"):
        kw = {}
        rd = [in_]
        if bias is not None:
            kw["bias"] = bias.ap if isinstance(bias, V) else bias
            rd.append(bias)
        if scale is not None:
            kw["scale"] = scale.ap if isinstance(scale, V) else scale
            rd.append(scale)
        wr = [out]
        if accum_out is not None:
            kw["accum_out"] = accum_out.ap
            wr.append(accum_out)
        return self.rec("act", lambda e: e.activation(out.ap, in_.ap, func, **kw), rd, wr, kind="act")

    def tt(self, eng, out, in0, in1, op):
        return self.rec(eng, lambda e: e.tensor_tensor(out.ap, in0.ap, in1.ap, op), [in0, in1], [out], kind="tt")

    def ts(self, eng, out, in0, s1, s2, op0, op1=None, accum_out=None):
        rd = [in0, s1, s2]
        a1 = s1.ap if isinstance(s1, V) else s1
        a2 = s2.ap if isinstance(s2, V) else s2
        kw = {}
        wr = [out]
        if accum_out is not None:
            kw["accum_out"] = accum_out.ap
            wr.append(accum_out)
        if op1 is None:
            return self.rec(eng, lambda e: e.tensor_scalar(out.ap, in0.ap, a1, None, op0, **kw), rd, wr, kind="ts")
        return self.rec(eng, lambda e: e.tensor_scalar(out.ap, in0.ap, a1, a2, op0, op1, **kw), rd, wr, kind="ts")

    def stt(self, eng, out, in0, scalar, in1, op0, op1):
        a = scalar.ap if isinstance(scalar, V) else scalar
        return self.rec(eng, lambda e: e.scalar_tensor_tensor(out.ap, in0.ap, a, in1.ap, op0, op1),
                        [in0, scalar, in1], [out], kind="stt")

    def copy(self, eng, out, in_):
        if eng == "act":
            return self.rec("act", lambda e: e.copy(out.ap, in_.ap), [in_], [out], kind="copy")
        return self.rec(eng, lambda e: e.tensor_copy(out.ap, in_.ap), [in_], [out], kind="copy")

    def memset(self, eng, out, val):
        return self.rec(eng, lambda e: e.memset(out.ap, val), [], [out], kind="memset")

    def reduce(self, eng, out, in_, op, axis=AX.X):
        return self.rec(eng, lambda e: e.tensor_reduce(out.ap, in_.ap, axis, op), [in_], [out], kind="red")

    def recip(self, out, in_):
        return self.rec("dve", lambda e: e.reciprocal(out.ap, in_.ap), [in_], [out], kind="recip")

    def scan(self, eng, out, d0, d1, initial, op0, op1):
        ini = initial.ap if isinstance(initial, V) else initial
        return self.rec(eng, lambda e: e.tensor_tensor_scan(out.ap, d0.ap, d1.ap, ini, op0, op1),
                        [d0, d1, initial], [out], kind="scan")

    def dma(self, q, out, in_, **kw):
        q = DMAQ.get(q, q)
        return self.rec(q, lambda e: e.dma_start(out.ap, in_.ap, **kw), [in_], [out], is_dma=True, kind="dma")

    def emit(self):
        nc = self.nc
        eng_sem = {e: nc.alloc_semaphore(f"s_{e}") for e in ENGS}
        dma_sems = {e: ([nc.alloc_semaphore(f"d_{e}_{i}") for i in range(N_DMA_SEMS)]
                        if any(o.is_dma for o in self.ops[e]) else []) for e in ENGS}
        for e in ENGS:
            cnt = 0
            nd = 0
            last_on_sem = {}
            for op in self.ops[e]:
                if op.is_dma:
                    j = nd % N_DMA_SEMS
                    nd += 1
                    op.dma_sem = dma_sems[e][j]
                    k = nd_k = (nd - 1) // N_DMA_SEMS + 1
                    op.dma_val = 16 * k
                    op.dma_prev = last_on_sem.get(j)
                    last_on_sem[j] = op
                else:
                    if op.signal:
                        cnt += 1
                        op.sig_count = cnt
        self._eng_sem = eng_sem
        handles = {"pe": "tensor", "dve": "vector", "act": "scalar", "pool": "gpsimd", "sp": "sync"}

        def run_engine(ename, eh):
            seen = {}

            def wait(sem, val):
                key = id(sem)
                if seen.get(key, 0) >= val:
                    return
                seen[key] = val
                eh.wait_ge(sem, val)

            for op in self.ops[ename]:
                need = []
                for d in op.deps:
                    if d.is_dma:
                        need.append((d.dma_sem, d.dma_val))
                    else:
                        need.append((eng_sem[d.eng], d.sig_count))
                if op.is_dma and op.dma_prev is not None:
                    need.append((op.dma_prev.dma_sem, op.dma_prev.dma_val))
                todo = []
                for sem, val in need:
                    key = id(sem)
                    if seen.get(key, 0) >= val:
                        continue
                    seen[key] = val
                    todo = [(s_, v_) for (s_, v_) in todo if s_ is not sem] + [(sem, val)]
                attach = None
                if ATTACH and todo and not op.is_dma and op.kind != "bar":
                    attach = todo.pop()
                for sem, val in todo:
                    eh.wait_ge(sem, val)
                if op.is_dma:
                    ins = op.fn(eh)
                    ins.then_inc(op.dma_sem, 16)
                else:
                    ins = op.fn(eh)
                    if attach is not None:
                        ins._wait_ge(attach[0], attach[1])
                    if op.signal:
                        ins.then_inc(eng_sem[ename], 1)
                    if DUMP and ename in DUMP:
                        print(ename, op.idx, op.kind, ins.concise(), flush=True)
            last = {}
            for op in self.ops[ename]:
                if op.is_dma:
                    last[id(op.dma_sem)] = op
            for op in last.values():
                wait(op.dma_sem, op.dma_val)

        with nc.Block() as block:
            @block.tensor
            def _(eh):
                run_engine("pe", eh)

            @block.vector
            def _(eh):
                run_engine("dve", eh)

            @block.scalar
            def _(eh):
                run_engine("act", eh)

            @block.gpsimd
            def _(eh):
                run_engine("pool", eh)

            @block.sync
            def _(eh):
                run_engine("sp", eh)

    def stats(self):
        return {e: len(self.ops[e]) for e in ENGS}

from concourse.bass_utils import run_bass_kernel_spmd

S = 4352
CTX = 256
TLAT = 4096
DM = 1024
NIN = 3596
NBLK = 34
DEPTH = 2
TILES = [(0, 256)] + [(256 + 512 * i, 512) for i in range(8)]
NEG = -30000.0

C_Q, C_K, C_V, C_G = 0, 384, 512, 640
C_RW = 1024
C_RG = 2048
C_Z = 2304
C_XBC = 2688
C_DT = 3584


def _slots(spec):
    d, o = {}, 0
    for n, w in spec:
        d[n] = (o, w)
        o += w
    return d, o


PP, NPP = _slots([("adab", 24), ("normw", 8), ("sink", 6), ("mu0", 8), ("mu1", 8), ("w0", 4), ("a0", 4),
                  ("kk", 4), ("ka", 4), ("rk", 2), ("lnw", 2), ("lnb", 2), ("convw", 21), ("convb", 7),
                  ("ssmd", 3), ("ssmnw", 3)])
PR, NPR = _slots([("adabg", 1024), ("dtb", 12), ("alog", 12), ("fnw", 1024)])
PR2, NPR2 = _slots([("convr", 3 * 896), ("shr", 3 * 1024)])
CS, NCS = _slots([("ident", 128), ("mband", 3 * 384), ("perm", 128), ("cos", 4096), ("sin", 4096),
                  ("blk", 128), ("rmf", 256), ("rmb", 256), ("snf", 128), ("snb", 128), ("trif", 128),
                  ("trib", 128), ("ones", 128), ("hlo", 128), ("hhi", 128), ("lowf", 128), ("lowb", 128)])


def fm(v):
    v = np.asarray(v, np.float32)
    return np.ascontiguousarray(v.reshape(-1, 128).T)


def rep(v):
    v = np.asarray(v, np.float32).reshape(1, -1)
    return np.ascontiguousarray(np.broadcast_to(v, (128, v.shape[1])))


def make_consts():
    c = np.zeros((128, NCS), np.float32)

    def put(n, a):
        o, w = CS[n]
        c[:, o:o + w] = a
    put("ident", np.eye(128, dtype=np.float32))
    qi = np.arange(128)[:, None]
    kj = np.arange(384)[None, :]
    band = np.abs(kj - 128 - qi) <= 128
    mb = []
    for var in range(3):
        ok = band.copy()
        if var == 0:
            ok &= kj >= 128
        if var == 2:
            ok &= kj < 256
        mb.append(np.where(ok, 0.0, NEG))
    put("mband", np.concatenate(mb, 1))
    perm = np.zeros((128, 128), np.float32)
    for m in range(128):
        d = m % 64
        half = (d % 32) // 16
        partner = m + 16 if half == 0 else m - 16
        perm[partner, m] = 1.0
    put("perm", perm)
    rows = TLAT // 64
    row = np.repeat(np.arange(rows), 64).astype(np.float32)
    col = np.tile(np.arange(64), rows).astype(np.float32)
    pos = np.stack([row, col], -1)
    inv = (np.float32(10000.0) ** (-np.arange(16, dtype=np.float32) / np.float32(16))).astype(np.float32)
    ang = (pos[:, :, None] * inv).astype(np.float32)
    cosv, sinv = np.cos(ang).astype(np.float32), np.sin(ang).astype(np.float32)
    ct = np.zeros((128, TLAT), np.float32)
    st = np.zeros((128, TLAT), np.float32)
    for m in range(128):
        d = m % 64
        ax, half, f = d // 32, (d % 32) // 16, d % 16
        ct[m] = cosv[:, ax, f]
        st[m] = -sinv[:, ax, f] if half == 0 else sinv[:, ax, f]
    put("cos", ct)
    put("sin", st)
    blk = np.zeros((128, 128), np.float32)
    blk[:64, :64] = 1
    blk[64:, 64:] = 1
    put("blk", blk)
    s = np.arange(128)[:, None]
    t = np.arange(128)[None, :]
    same = (s // 64) == (t // 64)
    put("rmf", np.concatenate([(same & (s < t)), (same & (s <= t))], 1).astype(np.float32))
    put("rmb", np.concatenate([(same & (s > t)), (same & (s >= t))], 1).astype(np.float32))
    put("lowf", (same & (t < s)).astype(np.float32))
    put("lowb", (same & (t > s)).astype(np.float32))
    put("snf", np.where(s <= t, 0.0, NEG))
    put("snb", np.where(s >= t, 0.0, NEG))
    put("trif", (s <= t).astype(np.float32))
    put("trib", (s >= t).astype(np.float32))
    put("ones", np.ones((128, 128), np.float32))
    hlo = np.zeros((128, 128), np.float32)
    hlo[:64] = 1
    put("hlo", hlo)
    put("hhi", 1 - hlo)
    return c


def make_pp(inp, l):
    p = np.zeros((128, NPP), np.float32)

    def put(n, a):
        o, w = PP[n]
        assert a.shape == (128, w), (n, a.shape)
        p[:, o:o + w] = a
    put("adab", fm(inp["ada_b"][l]))
    put("normw", fm(inp["norm_w"][l]))
    put("sink", rep(inp["attn_sink"][l]))
    put("mu0", fm(inp["rwkv_mu"][l, 0]))
    put("mu1", fm(inp["rwkv_mu"][l, 1]))
    for n, k in (("w0", "rwkv_w0"), ("a0", "rwkv_a0"), ("kk", "rwkv_k_k"), ("ka", "rwkv_k_a")):
        put(n, np.concatenate([fm(inp[k][l, 0]), fm(inp[k][l, 1])], 1))
    put("rk", fm(inp["rwkv_r_k"][l].reshape(-1)))
    put("lnw", fm(inp["rwkv_ln_w"][l]))
    put("lnb", fm(inp["rwkv_ln_b"][l]))
    cw = inp["ssm_conv_w"][l]
    put("convw", np.stack([fm(cw[0]), fm(cw[1]), fm(cw[2])], -1).reshape(128, 21))
    put("convb", fm(inp["ssm_conv_b"][l]))
    put("ssmd", fm(np.repeat(inp["ssm_d"][l], 64)))
    put("ssmnw", fm(inp["ssm_norm_w"][l]))
    return p


def make_pr(inp, l):
    p = np.zeros((128, NPR), np.float32)

    def put(n, a):
        o, w = PR[n]
        p[:, o:o + w] = a
    put("adabg", rep(inp["ada_b"][l, 2048:3072]))
    put("dtb", rep(inp["ssm_dt_bias"][l].reshape(-1)))
    put("alog", rep(inp["ssm_a_log"][l].reshape(-1)))
    put("fnw", rep(inp["final_norm_w"]))
    return p


def make_pr2(inp, l):
    p = np.zeros((128, NPR2), np.float32)

    def put(n, a):
        o, w = PR2[n]
        p[:, o:o + w] = a
    put("convr", rep(inp["ssm_conv_w"][l].reshape(-1)))
    mu = inp["rwkv_mu"][l]
    put("shr", rep(np.concatenate([mu[0], mu[1], mu[1]], 0)))
    return p


class MK:
    def __init__(self, nc, stop_after=None, dbg=False):
        self.nc = nc
        self.P = P = Prog(nc)
        self.dbg = dbg
        self.stop_after = stop_after
        ok = "ExternalOutput" if dbg else "Internal"
        self.xin = P.dram("xin", [S, DM], F32, kind="ExternalInput")
        self.cvec = P.dram("cvec", [128, 16], F32, kind="ExternalInput")
        self.ada_w = P.dram("ada_w", [DEPTH, DM, 3 * DM], F32, kind="ExternalInput")
        self.w_in = P.dram("w_in", [DEPTH, DM, NIN], F32, kind="ExternalInput")
        self.w_out = P.dram("w_out", [DEPTH, DM, DM], F32, kind="ExternalInput")
        self.wup = P.dram("wup", [DEPTH, 2, 64, 256], F32, kind="ExternalInput")
        self.aup = P.dram("aup", [DEPTH, 2, 64, 256], F32, kind="ExternalInput")
        self.pp = P.dram("pp", [DEPTH, 128, NPP], F32, kind="ExternalInput")
        self.pr = P.dram("pr", [DEPTH, 128, NPR], F32, kind="ExternalInput")
        self.pr2 = P.dram("pr2", [DEPTH, 128, NPR2], F32, kind="ExternalInput")
        self.cst = P.dram("cst", [128, NCS], F32, kind="ExternalInput")
        self.y = P.dram("y", [TLAT, DM], F32, kind="ExternalOutput")
        self.wbin = P.dram("wbin", [DEPTH, 128, 8, NIN], BF16)
        self.wbout = P.dram("wbout", [DEPTH, 128, 8, DM], BF16)
        self.hT = P.dram("hT", [128, 8, S], BF16, kind=ok)
        self.mix = P.dram("mixT", [DM, S], BF16, kind=ok)
        self.xres = P.dram("xres", [S, DM], F32, kind=ok)
        self.yf = P.dram("yfwd", [384, S], F32)
        self.psh = nc.alloc_psum_tensor("psum", [128, 8, 512], F32)
        self.bt = [Trk(ps=True) for _ in range(8)]
        self.c_ident = P.sb([128, 128], F32, "c_ident")
        self.c_identb = P.sb([128, 128], BF16, "c_identb")
        self.load_c(self.c_ident, "ident")
        P.copy("dve", self.c_identb[:], self.c_ident[:])

    def ps(self, b0, nb=1, dt=F32):
        ap = self.psh[:, b0:b0 + nb, :]
        if dt is not F32:
            ap = ap.bitcast(dt)
        return V(ap, tuple(self.bt[b0:b0 + nb]))

    def load_c(self, tile, name, q="sp", sub=None):
        o, w = CS[name]
        if sub is not None:
            o, w = o + sub[0], sub[1]
        self.P.dma(q, tile[:], self.cst.v("c", (slice(None), slice(o, o + w))))

    def phase_w(self):
        P = self.P
        st = [P.sb([128, 8, 512], F32, f"w_st{i}") for i in range(2)]
        sb = [P.sb([128, 8, 512], BF16, f"w_sb{i}") for i in range(2)]
        it = 0
        for l in range(DEPTH):
            for (src, dst, n) in ((self.w_in, self.wbin, NIN), (self.w_out, self.wbout, DM)):
                for c0 in range(0, n, 512):
                    cw = min(512, n - c0)
                    a, b = st[it % 2], sb[it % 2]
                    q = "sp" if it % 2 == 0 else "act"
                    P.dma(q, a[:, :, 0:cw], src.v(("w", l), fn=lambda ap, l=l, c0=c0, cw=cw:
                                                   ap[l].rearrange("(kc p) n -> p kc n", p=128)[:, :, c0:c0 + cw]))
                    P.copy("dve" if it % 2 == 0 else "pool", b[:, :, 0:cw], a[:, :, 0:cw])
                    P.dma(q, dst.v(("wb", l, c0), fn=lambda ap, l=l, c0=c0, cw=cw: ap[l][:, :, c0:c0 + cw]),
                          b[:, :, 0:cw])
                    it += 1

    def wb(self, l, c0, cw):
        t0 = (c0 // 512) * 512
        trk = []
        ap = self.wbin.h.ap()[l][:, :, c0:c0 + cw]
        for t in range(t0, c0 + cw, 512):
            trk.append(self.wbin.reg.setdefault(("wb", l, t), Trk()))
        return V(ap, tuple(trk))

    def wbo(self, l, c0, cw):
        t0 = (c0 // 512) * 512
        trk = []
        ap = self.wbout.h.ap()[l][:, :, c0:c0 + cw]
        for t in range(t0, c0 + cw, 512):
            trk.append(self.wbout.reg.setdefault(("wb", l, t), Trk()))
        return V(ap, tuple(trk))

    def phase_mod(self, l):
        P = self.P
        if not hasattr(self, "ppt"):
            self.ppt = P.sb([128, NPP], F32, "ppt")
            self.prt = P.sb([128, NPR], F32, "prt")
            self.cact = P.sb([128, 8, 2], F32, "cact")
            self.modT = P.sb([128, 24, 2], F32, "modT")
            self.s1T = P.sb([128, 8, 2], F32, "s1T")
            self.gate_bc = P.sb([128, 2, 1024], F32, "gate_bc")
            craw = P.sb([128, 16], F32, "craw")
            P.dma("sp", craw[:], self.cvec.v(0))
            P.act(self.cact[:].re("p k w -> p w k"), craw[:].re("p (w k) -> p w k", w=2), AF.Silu)
        mk_ = P.mark()
        self.crep = P.sb([128, 8, 2, 128], F32, "crep")
        self.aw = [P.sb([128, 8, 512], F32, f"aw{i}") for i in range(2)]
        P.copy("dve", self.crep[:], self.cact[:].m(lambda a: a.unsqueeze(3)).bc([128, 8, 2, 128]))
        P.dma("sp", self.ppt[:], self.pp.v(l, fn=lambda a: a[l]))
        P.dma("act", self.prt[:], self.pr.v(l, fn=lambda a: a[l]))
        for t in range(6):
            a = self.aw[t % 2]
            P.dma("sp" if t % 2 == 0 else "act", a[:],
                  self.ada_w.v(("aw", l), fn=lambda ap, t=t: ap[l].rearrange("(kc p) n -> p kc n", p=128)[:, :, t * 512:(t + 1) * 512]))
            pm = self.ps(0)
            for j in range(4):
                for kc in range(8):
                    P.mm(pm[:, 0, j * 2:(j + 1) * 2], a[:, kc, j * 128:(j + 1) * 128], self.cact[:, kc, :],
                         start=(kc == 0), stop=(kc == 7))
            P.copy("dve", self.modT[:, t * 4:(t + 1) * 4, :], pm[:, 0, 0:8].re("p (j w) -> p j w", w=2))
            if t >= 4:
                for w in range(2):
                    pg = self.ps(1 + w)
                    for kc in range(8):
                        P.mm(pg[:, 0, :], self.crep[:, kc, w, :], a[:, kc, :], start=(kc == 0), stop=(kc == 7))
                    o = PR["adabg"][0] + (t - 4) * 512
                    P.tt("dve", self.gate_bc[:, w, (t - 4) * 512:(t - 3) * 512], pg[:, 0, :], self.prt[:, o:o + 512], ALU.add)
        o = PP["adab"][0]
        P.tt("dve", self.modT[:], self.modT[:], self.ppt[:, o:o + 24].m(lambda a: a.unsqueeze(2)).bc([128, 24, 2]), ALU.add)
        o = PP["normw"][0]
        P.stt("dve", self.s1T[:], self.modT[:, 8:16, :], 1.0,
              self.ppt[:, o:o + 8].m(lambda a: a.unsqueeze(2)).bc([128, 8, 2]), ALU.add, ALU.mult)
        P.release(mk_)

    def phase_norm(self, l):
        P = self.P
        if not hasattr(self, "n_x"):
            self.n_x = [P.sb([128, DM], F32, f"n_x{i}") for i in range(2)]
            self.n_junk = P.sb([128, DM], F32, "n_junk")
            self.n_xb = [P.sb([128, DM], BF16, f"n_xb{i}") for i in range(2)]
            self.n_ss = [P.sb([128, 4], F32, f"n_ss{i}") for i in range(2)]
            self.n_h = [P.sb([128, 8, 512], BF16, f"n_h{i}") for i in range(2)]
            self.n_t = [P.sb([128, 8, 128], F32, f"n_t{i}") for i in range(2)]
        src = self.xin if l == 0 else self.xres
        for ti, (t0, tn) in enumerate(TILES):
            w = 1 if ti == 0 else 0
            hb = self.n_h[ti % 2]
            for bi in range(tn // 128):
                blk = (t0 // 128) + bi
                xt, xb, ss = self.n_x[blk % 2], self.n_xb[blk % 2], self.n_ss[blk % 2]
                P.dma("sp" if blk % 2 == 0 else "act", xt[:], src.v(("x", blk), (slice(blk * 128, blk * 128 + 128), slice(None))))
                P.act(self.n_junk[:], xt[:], AF.Square, accum_out=ss[:, 0:1])
                P.act(ss[:, 1:2], ss[:, 0:1], AF.Sqrt, bias=1e-6, scale=1.0 / DM)
                P.recip(ss[:, 2:3], ss[:, 1:2])
                P.ts("dve", xb[:], xt[:], ss[:, 2:3], None, ALU.mult)
                pt = self.ps(2 + blk % 2, 1, BF16)
                for kc in range(8):
                    P.tr(pt[:, 0, kc * 128:(kc + 1) * 128], xb[:, kc * 128:(kc + 1) * 128], self.c_identb[:])
                tmp = self.n_t[blk % 2]
                P.tt("dve", tmp[:], pt[:, 0, :].re("p (k t) -> p k t", k=8),
                     self.s1T[:, :, w:w + 1].bc([128, 8, 128]), ALU.mult)
                P.tt("pool", hb[:, :, bi * 128:(bi + 1) * 128], tmp[:],
                     self.modT[:, 0:8, w:w + 1].bc([128, 8, 128]), ALU.add)
            P.dma("sp", self.hT.v(("h", ti), (slice(None), slice(None), slice(t0, t0 + tn))), hb[:, :, 0:tn])

    def phase_out(self, l):
        P = self.P
        last = (l == DEPTH - 1)
        if not hasattr(self, "o_w"):
            self.o_w = P.sb([128, 8, DM], BF16, "o_w")
            self.o_m = [P.sb([128, 8, 128], BF16, f"o_m{i}") for i in range(2)]
            self.o_x = [P.sb([128, DM], F32, f"o_x{i}") for i in range(2)]
            self.o_t = [P.sb([128, DM], F32, f"o_t{i}") for i in range(2)]
            self.o_ss = [P.sb([128, 4], F32, f"o_ss{i}") for i in range(2)]
        P.dma("sp", self.o_w[:], self.wbo(l, 0, DM))
        src = self.xin if l == 0 else self.xres
        for blk in range(NBLK):
            if last and blk < 2:
                continue
            w = 1 if blk < 2 else 0
            m, xt, tt_, ss = self.o_m[blk % 2], self.o_x[blk % 2], self.o_t[blk % 2], self.o_ss[blk % 2]
            tsl = slice(blk * 128, blk * 128 + 128)
            P.dma("sp", m[:], self.mix.v(("m", blk), fn=lambda ap, tsl=tsl: ap.rearrange("(kc p) t -> p kc t", p=128)[:, :, tsl]))
            P.dma("act", xt[:], src.v(("x", blk), (tsl, slice(None))))
            for hf in range(2):
                po = self.ps(4 + 2 * (blk % 2) + hf)
                for kc in range(8):
                    P.mm(po[:, 0, :], m[:, kc, :], self.o_w[:, kc, hf * 512:(hf + 1) * 512], start=(kc == 0), stop=(kc == 7))
                P.tt("dve", tt_[:, hf * 512:(hf + 1) * 512], po[:, 0, :], self.gate_bc[:, w, hf * 512:(hf + 1) * 512], ALU.mult)
            P.tt("pool", tt_[:], tt_[:], xt[:], ALU.add)
            if not last:
                P.dma("sp", self.xres.v(("x", blk), (tsl, slice(None))), tt_[:])
            else:
                P.act(xt[:], tt_[:], AF.Square, accum_out=ss[:, 0:1])
                P.act(ss[:, 1:2], ss[:, 0:1], AF.Sqrt, bias=1e-6, scale=1.0 / DM)
                P.recip(ss[:, 2:3], ss[:, 1:2])
                o = PR["fnw"][0]
                P.stt("dve", xt[:], tt_[:], ss[:, 2:3], self.prt[:, o:o + DM], ALU.mult, ALU.mult)
                P.dma("sp", self.y.v(("y", blk), (slice(blk * 128 - CTX, blk * 128 - CTX + 128), slice(None))), xt[:])

    def load_h(self, tile, ti, q="sp"):
        t0, tn = TILES[ti]
        self.P.dma(q, tile[:, :, 0:tn], self.hT.v(("h", ti), (slice(None), slice(None), slice(t0, t0 + tn))))

    def phase_att(self, l):
        P = self.P
        ctx_out = (l < DEPTH - 1)
        import os
        sm = int(os.environ.get("MK_SET", "63"))
        wq = P.sb([128, 8, 384], BF16, "a_wq")
        wg = P.sb([128, 8, 384], BF16, "a_wg")
        wk = P.sb([128, 8, 128], BF16, "a_wk")
        wv = P.sb([128, 8, 128], BF16, "a_wv")
        if sm & 1:
            for j in range(3):
                for hh in range(2):
                    h = j + 3 * hh
                    P.dma("sp", wq[:, :, j * 128 + hh * 64: j * 128 + hh * 64 + 64], self.wb(l, C_Q + h * 64, 64))
                    P.dma("act", wg[:, :, j * 128 + hh * 64: j * 128 + hh * 64 + 64], self.wb(l, C_G + h * 64, 64))
            P.dma("sp", wk[:], self.wb(l, C_K, 128))
            P.dma("act", wv[:], self.wb(l, C_V, 128))
        cos = P.sb([128, TLAT], F32, "a_cos")
        sin = P.sb([128, TLAT], F32, "a_sin")
        if sm & 2:
            self.load_c(cos, "cos", "sp")
            self.load_c(sin, "sin", "act")
        pf = P.sb([128, 128], F32, "a_pf")
        permb = P.sb([128, 128], BF16, "a_permb")
        if sm & 4:
            self.load_c(pf, "perm")
            P.copy("dve", permb[:], pf[:])
        mband = P.sb([128, 3, 384], F32, "a_mband")
        o, w_ = CS["mband"]
        if sm & 8:
            P.dma("sp", mband[:].re("p a b -> p (a b)"), self.cst.v("c", (slice(None), slice(o, o + w_))))
        sink8 = P.sb([128, 6], F32, "a_sink8")
        o = PP["sink"][0]
        if sm & 16:
            P.ts("dve", sink8[:], self.ppt[:, o:o + 6], 8.0, None, ALU.mult)
        KB = P.sb([128, 4608], BF16, "a_KB")
        Vtm = P.sb([128, 36, 128], BF16, "a_V")
        if sm & 32:
            P.memset("pool", KB[:, 256:384], 0.0)
            P.memset("pool", KB[:, 4480:4608], 0.0)
            P.memset("pool", Vtm[:, 2, :], 0.0)
            P.memset("pool", Vtm[:, 35, :], 0.0)
        hb = [P.sb([128, 8, 512], BF16, f"a_h{i}") for i in range(2)]
        kf = P.sb([128, 512], F32, "a_kf")
        t1 = [P.sb([128, 512], F32, f"a_t1{i}") for i in range(2)]
        t2 = [P.sb([128, 512], F32, f"a_t2{i}") for i in range(2)]

        def kcol(tok):
            return tok if tok < CTX else tok + 128

        def proj(dst, wt, c0, cw, hbt, tn, lat_t0, bank, func=None):
            pp_ = self.ps(bank)
            for kc in range(8):
                P.mm(pp_[:, 0, 0:tn], wt[:, kc, c0:c0 + cw], hbt[:, kc, 0:tn], start=(kc == 0), stop=(kc == 7))
            if func is not None:
                P.act(dst, pp_[:, 0, 0:tn], func)
                return
            if lat_t0 is None:
                P.copy("act", dst, pp_[:, 0, 0:tn])
                return
            P.copy("act", kf[:, 0:tn], pp_[:, 0, 0:tn])
            pr_ = self.ps(2)
            P.mm(pr_[:, 0, 0:tn], pf[:], kf[:, 0:tn])
            a, b = t1[bank % 2], t2[bank % 2]
            P.tt("pool", a[:, 0:tn], kf[:, 0:tn], cos[:, lat_t0:lat_t0 + tn], ALU.mult)
            P.tt("dve", b[:, 0:tn], pr_[:, 0, 0:tn], sin[:, lat_t0:lat_t0 + tn], ALU.mult)
            P.tt("pool", dst, a[:, 0:tn], b[:, 0:tn], ALU.add)

        stage = int(os.environ.get("MK_ATT", "9"))
        if stage <= 0:
            return
        for ti, (t0, tn) in enumerate(TILES):
            h = hb[ti % 2]
            self.load_h(h, ti, "sp" if ti % 2 == 0 else "act")
            kc0 = kcol(t0)
            ma = int(os.environ.get("MK_A", "3"))
            if ti >= int(os.environ.get("MK_AT", "9")):
                break
            if ma & 1:
                proj(KB[:, kc0:kc0 + tn], wk, 0, 128, h, tn, None if ti == 0 else t0 - CTX, ti % 2)
            for bi in range(tn // 128):
                if not (ma & 2):
                    break
                vb = (t0 // 128 + bi)
                vb = vb if vb < 2 else vb + 1
                pv = self.ps(3 + bi % 2)
                for kc in range(8):
                    P.mm(pv[:, 0, 0:128], h[:, kc, bi * 128:(bi + 1) * 128], wv[:, kc, :], start=(kc == 0), stop=(kc == 7))
                P.copy("dve" if bi % 2 == 0 else "act", Vtm[:, vb, :], pv[:, 0, 0:128])

        import os
        stage = int(os.environ.get("MK_ATT", "9"))
        if stage <= 1:
            return
        qT = [P.sb([128, 3, 512], BF16, f"a_q{i}") for i in range(2)]
        gT = [P.sb([128, 3, 512], BF16, f"a_g{i}") for i in range(2)]
        om = [P.sb([128, 3, 512], BF16, f"a_om{i}") for i in range(2)]
        sc = [P.sb([128, 3, 641], F32, f"a_sc{i}") for i in range(2)]
        pe = [P.sb([128, 3, 641], F32, f"a_pe{i}") for i in range(2)]
        for kg in range(2):
            P.copy("dve", sc[kg][:, :, 640:641], sink8[:, kg * 3:kg * 3 + 3].m(lambda a: a.unsqueeze(2)))
        pn = [P.sb([128, 3, 640], BF16, f"a_pn{i}") for i in range(4)]
        nblk_att = 0
        pTs = [P.sb([128, 5, 128], BF16, f"a_pT{i}") for i in range(3)]
        st = [P.sb([128, 8, 3], F32, f"a_st{i}") for i in range(2)]
        it = 0
        npt = 0
        pending = None

        def rr(a, b):
            live = [g_ for g_ in (a, b) if g_ is not None]
            while live:
                for g_ in list(live):
                    if next(g_, "done") == "done":
                        live.remove(g_)
        for ti, (t0, tn) in enumerate(TILES):
            if ti == 0 and not ctx_out:
                continue
            h = hb[ti % 2]
            self.load_h(h, ti, "sp" if ti % 2 == 0 else "act")
            q_, g_, o_ = qT[ti % 2], gT[ti % 2], om[ti % 2]
            for j in range(3):
                proj(q_[:, j, 0:tn], wq, j * 128, 128, h, tn, None if ti == 0 else t0 - CTX, j % 2)
                proj(g_[:, j, 0:tn], wg, j * 128, 128, h, tn, None, (j + 1) % 2, func=AF.Silu)
            if stage <= 2:
                break
            for bi in range(tn // 128):
                if stage <= 5 and (bi > 0 or ti > 1):
                    break
                qs = slice(bi * 128, bi * 128 + 128)
                isctx = (ti == 0)
                n = (t0 - CTX) // 128 + bi if not isctx else None
                lo = 384 if isctx else 0
                po = self.ps(2)
                KG = [(kg, slice(kg * 64, kg * 64 + 64), sc[kg], pe[kg], pn[2 * (nblk_att % 2) + kg], st[kg]) for kg in range(2)]
                nblk_att += 1
                chunks = ([] if isctx else [0, 1, 2]) + [3, 4]
                def front(KG=KG, chunks=chunks, n=n, qs=qs, lo=lo, isctx=isctx, q_=q_):
                    for kg, ps_, s_, e_, n_, stt_ in KG:
                        for j in range(3):
                            pb = self.ps(3 + 2 * kg)
                            pc = self.ps(4 + 2 * kg)
                            if not isctx:
                                kb0 = 256 + n * 128
                                P.mm(pb[:, 0, 0:384], q_[ps_, j, qs], KB[ps_, kb0:kb0 + 384])
                                yield
                                var = 0 if n == 0 else (2 if n == 31 else 1)
                                P.tt("dve", s_[:, j, 0:384], pb[:, 0, 0:384], mband[:, var, :], ALU.add)
                                yield
                            P.mm(pc[:, 0, 0:256], q_[ps_, j, qs], KB[ps_, 0:256])
                            yield
                            P.copy("act", s_[:, j, 384:640], pc[:, 0, 0:256])
                            yield
                    for kg, ps_, s_, e_, n_, stt_ in KG:
                        P.reduce("dve", stt_[:, 0, :], s_[:, :, lo:641], ALU.max)
                        yield
                        P.ts("dve", stt_[:, 2, :], stt_[:, 0, :], -0.125, None, ALU.mult)
                        yield
                    for kg, ps_, s_, e_, n_, stt_ in KG:
                        for j in range(3):
                            P.act(e_[:, j, lo:641], s_[:, j, lo:641], AF.Exp, bias=stt_[:, 2, j:j + 1], scale=0.125,
                                  accum_out=stt_[:, 3, j:j + 1])
                            yield
                    for kg, ps_, s_, e_, n_, stt_ in KG:
                        P.recip(stt_[:, 7, :], stt_[:, 3, :])
                        yield
                        P.tt("dve" if kg == 0 else "pool", n_[:, :, lo:640], e_[:, :, lo:640],
                             stt_[:, 7, :].m(lambda a: a.unsqueeze(2)).bc([128, 3, 640 - lo]), ALU.mult)
                        yield
                def back(KG=KG, chunks=chunks, n=n, qs=qs, po=po, o_=o_, g_=g_):
                    nonlocal npt
                    for j in range(3):
                        for kg, ps_, s_, e_, n_, stt_ in KG:
                            pt = self.ps([7, 0, 1][npt % 3], 1, BF16)
                            for c in chunks:
                                P.tr(pt[:, 0, c * 128:(c + 1) * 128], n_[:, j, c * 128:(c + 1) * 128], self.c_identb[:])
                                yield
                            pts = pTs[npt % 3]
                            npt += 1
                            c0 = chunks[0]
                            P.copy("act" if npt % 2 == 0 else "dve", pts[:, c0:5, :], pt[:, 0, c0 * 128:640].re("p (c t) -> p c t", t=128))
                            yield
                            for ci, c in enumerate(chunks):
                                vb = (2 + n + c) if c < 3 else (c - 3)
                                P.mm(po[ps_, 0, j * 128:(j + 1) * 128], Vtm[:, vb, kg * 64:kg * 64 + 64], pts[:, c, :],
                                     start=(ci == 0), stop=(ci == len(chunks) - 1))
                                yield
                    P.tt("dve", o_[:, :, qs], po[:, 0, 0:384].re("p (j t) -> p j t", j=3), g_[:, :, qs], ALU.mult)
                    yield
                rr(front(), pending)
                pending = back()
            rr(None, pending)
            pending = None
            for j in range(3):
                for hh in range(2):
                    hd = j + 3 * hh
                    P.dma("sp" if hh == 0 else "act",
                          self.mix.v(("ma", ti, hd), (slice(hd * 64, hd * 64 + 64), slice(t0, t0 + tn))),
                          o_[hh * 64:hh * 64 + 64, j, 0:tn])

    def load_h_halo(self, tile, ti, q="sp"):
        P = self.P
        t0, tn = TILES[ti]
        lo, hi = t0 - 1, t0 + tn + 1
        if ti <= 1:
            P.memset("pool", tile[:, :, 0:1], 0.0)
            lo = t0
        if ti == 0 or ti == len(TILES) - 1:
            P.memset("pool", tile[:, :, tn + 1:tn + 2], 0.0)
            hi = t0 + tn
        P.dma(q, tile[:, :, lo - t0 + 1:hi - t0 + 1], self.hT.v(("h", ti), (slice(None), slice(None), slice(lo, hi))))

    def scaled_w(self, l, c0, ncol, rows, name, roff=0):
        P = self.P
        rows = [r[:, roff:roff + ncol] for r in rows]
        w3 = P.sb([128, 3, 8, ncol], BF16, name)
        mk_ = P.mark()
        half = ncol // 2
        stg = [P.sb([128, 8, half], F32, f"{name}_st{i}") for i in range(2)]
        for hf in range(2):
            st = stg[hf]
            P.dma("sp" if hf == 0 else "act", st[:],
                  self.w_in.v(("w", l), fn=lambda ap, hf=hf: ap[l].rearrange("(kc p) n -> p kc n", p=128)[:, :, c0 + hf * half:c0 + (hf + 1) * half]))
            for tap in range(3):
                P.tt("dve" if tap != 1 else "pool", w3[:, tap, :, hf * half:(hf + 1) * half], st[:],
                     rows[tap][:, hf * half:(hf + 1) * half].m(lambda a: a.unsqueeze(1)).bc([128, 8, half]), ALU.mult)
        P.release(mk_)
        return w3

    def phase_ssd(self, l):
        P = self.P
        ctx_out = (l < DEPTH - 1)
        xsT = P.sb([128, 3, S], F32, "s_xsT")
        BT = P.sb([128, 2, S], BF16, "s_BT")
        CT = P.sb([128, 2, S], BF16, "s_CT")
        dt_tm = P.sb([128, NBLK, 12], F32, "s_dt")
        dtA_tm = P.sb([128, NBLK, 12], F32, "s_dtA")
        aneg = P.sb([128, 12], F32, "s_aneg")
        o = PR["alog"][0]
        P.act(aneg[:], self.prt[:, o:o + 12], AF.Exp)
        P.ts("dve", aneg[:], aneg[:], -1.0, None, ALU.mult)
        mk1 = P.mark()
        o = PR2["convr"][0]
        crow = P.sb([128, 3 * 896], F32, "s_crow")
        P.dma("sp", crow[:], self.pr2.v(l, fn=lambda a: a[l][:, o:o + 3 * 896]))
        rows = [crow[:, k * 896:(k + 1) * 896] for k in range(3)]
        wx3 = self.scaled_w(l, C_XBC, 896, rows, "s_wx3")
        wdt = P.sb([128, 8, 12], BF16, "s_wdt")
        P.dma("sp", wdt[:], self.wb(l, C_DT, 12))
        hh = [P.sb([128, 8, 514], BF16, f"s_hh{i}") for i in range(2)]
        dtmp = [P.sb([128, 12], F32, f"s_dtmp{i}") for i in range(2)]
        ocb = PP["convb"][0]
        nb = 0
        for ti, (t0, tn) in enumerate(TILES):
            h = hh[ti % 2]
            self.load_h_halo(h, ti, "sp" if ti % 2 == 0 else "act")
            for c in range(7):
                pb = self.ps(c % 2)
                n = 0
                for tap in range(3):
                    for kc in range(8):
                        P.mm(pb[:, 0, 0:tn], wx3[:, tap, kc, c * 128:(c + 1) * 128], h[:, kc, tap:tap + tn],
                             start=(n == 0), stop=(n == 23))
                        n += 1
                if c < 3:
                    dst = xsT[:, c, t0:t0 + tn]
                elif c < 5:
                    dst = BT[:, c - 3, t0:t0 + tn]
                else:
                    dst = CT[:, c - 5, t0:t0 + tn]
                P.act(dst, pb[:, 0, 0:tn], AF.Silu, bias=self.ppt[:, ocb + c:ocb + c + 1])
            for bi in range(tn // 128):
                blk = t0 // 128 + bi
                pd = self.ps(2 + blk % 2)
                for kc in range(8):
                    P.mm(pd[:, 0, 0:12], h[:, kc, 1 + bi * 128:1 + (bi + 1) * 128], wdt[:, kc, :], start=(kc == 0), stop=(kc == 7))
                d_ = dtmp[blk % 2]
                o = PR["dtb"][0]
                P.tt("dve", d_[:], pd[:, 0, 0:12], self.prt[:, o:o + 12], ALU.add)
                P.act(d_[:], d_[:], AF.Exp)
                P.act(dt_tm[:, blk, :], d_[:], AF.Ln, bias=1.0)
                P.tt("dve", dtA_tm[:, blk, :], dt_tm[:, blk, :], aneg[:], ALU.mult)
        P.release(mk1)
        if int(os.environ.get("MK_SSD", "9")) <= 1:
            self.dbg_xsT = xsT
            return
        wz = P.sb([128, 8, 384], BF16, "s_wz")
        P.dma("sp", wz[:], self.wb(l, C_Z, 384))
        cf = {}
        for nme in ("snf", "snb", "trif", "trib", "ones", "hlo", "hhi"):
            cf[nme] = P.sb([128, 128], F32, "s_c_" + nme)
            self.load_c(cf[nme], nme, "act")
        St = P.sb([128, 6, 64], F32, "s_S")
        NB2 = 2
        xs_sb = [P.sb([128, 384], F32, f"s_xs{i}") for i in range(NB2)]
        Bt_sb = [P.sb([128, 2, 128], BF16, f"s_Bt{i}") for i in range(NB2)]
        Gs = [P.sb([128, 2, 128], F32, f"s_G{i}") for i in range(NB2)]
        cumc = [P.sb([128, 6], F32, f"s_cc{i}") for i in range(NB2)]
        dbc = [P.sb([128, 6, 128], F32, f"s_dbc{i}") for i in range(NB2)]
        Dm = [P.sb([128, 6, 128], F32, f"s_D{i}") for i in range(NB2)]
        Em = [P.sb([128, 6, 128], F32, f"s_E{i}") for i in range(NB2)]
        scT = [P.sb([128, 6, 128], BF16, f"s_sc{i}") for i in range(NB2)]
        Er = [P.sb([128, 6, 128], F32, f"s_Er{i}") for i in range(NB2)]
        Cd = [P.sb([128, 6, 128], F32, f"s_Cd{i}") for i in range(NB2)]
        xdt = [P.sb([128, 6, 64], BF16, f"s_xdt{i}") for i in range(NB2)]
        xdtt = [P.sb([128, 6, 64], BF16, f"s_xdtt{i}") for i in range(NB2)]
        ysb = [P.sb([128, 3, 128], F32, f"s_y{i}") for i in range(NB2)]
        yfl = [P.sb([128, 3, 128], F32, f"s_yf{i}") for i in range(NB2)]
        hz = [P.sb([128, 8, 128], BF16, f"s_hz{i}") for i in range(NB2)]
        zs = [P.sb([128, 3, 128], F32, f"s_z{i}") for i in range(NB2)]
        sq = [P.sb([128, 3, 128], F32, f"s_sq{i}") for i in range(NB2)]
        rs = [P.sb([128, 2, 128], F32, f"s_rs{i}") for i in range(NB2)]
        mo = [P.sb([128, 3, 128], BF16, f"s_mo{i}") for i in range(NB2)]
        osd, onw = PP["ssmd"][0], PP["ssmnw"][0]
        it = 0
        for d in range(2):
            order = list(range(NBLK)) if d == 0 else [1, 0] + list(range(NBLK - 1, 1, -1))
            tri = cf["trif"] if d == 0 else cf["trib"]
            sneg = cf["snf"] if d == 0 else cf["snb"]
            last = 127 if d == 0 else 0
            P.memset("dve", St[:], 0.0)
            pending = None
            for c in order:
                if d == 1 and c < 2 and not ctx_out:
                    pass
                k = it % NB2
                it += 1
                cs_ = slice(c * 128, (c + 1) * 128)
                pxs = self.ps(0)
                for j in range(3):
                    P.tr(pxs[:, 0, j * 128:(j + 1) * 128], xsT[:, j, cs_], self.c_ident[:])
                P.copy("act", xs_sb[k][:], pxs[:, 0, 0:384])
                pbt = self.ps(1, 1, BF16)
                for g in range(2):
                    P.tr(pbt[:, 0, g * 128:(g + 1) * 128], BT[:, g, cs_], self.c_identb[:])
                P.copy("dve", Bt_sb[k][:].re("p g n -> p (g n)"), pbt[:, 0, 0:256])
                pg = self.ps(2)
                for g in range(2):
                    P.mm(pg[:, 0, g * 128:(g + 1) * 128], BT[:, g, cs_], CT[:, g, cs_])
                P.copy("act", Gs[k][:].re("p g n -> p (g n)"), pg[:, 0, 0:256])
                dA = dtA_tm[:, c, d * 6:(d + 1) * 6]
                P.mm(pg[:, 0, 256:262], tri[:], dA)
                P.copy("dve", cumc[k][:], pg[:, 0, 256:262])
                P.copy("pool", dbc[k][:], dA.m(lambda a: a.unsqueeze(2)).bc([128, 6, 128]))
                pcr = self.ps(3, 2)
                for h in range(6):
                    P.mm(pcr[:, h // 4, (h % 4) * 128:(h % 4 + 1) * 128], dbc[k][:, h, :], tri[:])
                pcr_v = pcr.re("p b n -> p (b n)")[:, 0:768].re("p (h n) -> p h n", h=6)
                P.tt("dve", Dm[k][:], pcr_v, cumc[k][:].m(lambda a: a.unsqueeze(2)).bc([128, 6, 128]), ALU.subtract)
                P.act(Er[k][:], pcr_v, AF.Exp)
                P.tt("pool", Dm[k][:], Dm[k][:], sneg[:].m(lambda a: a.unsqueeze(1)).bc([128, 6, 128]), ALU.add)
                P.act(Em[k][:], Dm[k][:], AF.Exp)
                P.tt("pool", scT[k][:].re("p (g h) n -> p g h n", g=2), Em[k][:].re("p (g h) n -> p g h n", g=2),
                     Gs[k][:].m(lambda a: a.unsqueeze(2)).bc([128, 2, 3, 128]), ALU.mult)
                P.tt("dve", Cd[k][:].re("p (g h) n -> p g h n", g=2), Er[k][:].re("p (g h) n -> p g h n", g=2),
                     CT[:, :, cs_].m(lambda a: a.unsqueeze(2)).bc([128, 2, 3, 128]), ALU.mult)
                dtc = dt_tm[:, c, d * 6:(d + 1) * 6]
                P.tt("dve", xdt[k][:], xs_sb[k][:].re("p (h q) -> p h q", h=6),
                     dtc.m(lambda a: a.unsqueeze(2)).bc([128, 6, 64]), ALU.mult)
                P.tt("pool", xdtt[k][:], xdt[k][:], Em[k][:, :, last:last + 1].bc([128, 6, 64]), ALU.mult)
                def back(c=c, k=k, cs_=cs_, d=d, last=last):
                    py = self.ps(5)
                    for h in range(6):
                        o_ = py[(h % 2) * 64:(h % 2) * 64 + 64, 0, (h // 2) * 128:(h // 2 + 1) * 128]
                        P.mm(o_, xdt[k][:, h, :], scT[k][:, h, :], start=True, stop=False)
                        P.mm(o_, St[:, h, :], Cd[k][:, h, :], start=False, stop=True)
                    pcs = self.ps(6)
                    for h in range(6):
                        P.mm(pcs[:, 0, h * 64:(h + 1) * 64], Bt_sb[k][:, h // 3, :], xdtt[k][:, h, :])
                    P.tt("dve", St[:], St[:], Er[k][:, :, last:last + 1].bc([128, 6, 64]), ALU.mult)
                    P.tt("dve", St[:], St[:], pcs[:, 0, 0:384].re("p (h q) -> p h q", h=6), ALU.add)
                    if d == 0:
                        P.copy("act", ysb[k][:].re("p j n -> p (j n)"), py[:, 0, 0:384])
                        P.dma("sp", self.yf.v(("yf", c), fn=lambda ap, cs_=cs_: ap.rearrange("(j p) t -> p j t", p=128)[:, :, cs_]), ysb[k][:])
                        return
                    if c < 2 and not ctx_out:
                        return
                    P.dma("act", yfl[k][:], self.yf.v(("yf", c), fn=lambda ap, cs_=cs_: ap.rearrange("(j p) t -> p j t", p=128)[:, :, cs_]))
                    P.dma("sp", hz[k][:], self.hT.v(("h", "z"), (slice(None), slice(None), cs_)))
                    pz = self.ps(7)
                    for j in range(3):
                        for kc in range(8):
                            P.mm(pz[:, 0, j * 128:(j + 1) * 128], wz[:, kc, j * 128:(j + 1) * 128], hz[k][:, kc, :],
                                 start=(kc == 0), stop=(kc == 7))
                    P.act(zs[k][:].re("p j n -> p (j n)"), pz[:, 0, 0:384], AF.Silu)
                    y_ = ysb[k]
                    P.tt("dve", y_[:].re("p j n -> p (j n)"), py[:, 0, 0:384], yfl[k][:].re("p j n -> p (j n)"), ALU.add)
                    P.tt("pool", yfl[k][:], xsT[:, :, cs_], self.ppt[:, osd:osd + 3].m(lambda a: a.unsqueeze(2)).bc([128, 3, 128]), ALU.mult)
                    P.tt("pool", y_[:], y_[:], yfl[k][:], ALU.add)
                    P.tt("dve", y_[:], y_[:], zs[k][:], ALU.mult)
                    P.act(sq[k][:], y_[:], AF.Square)
                    pgs = self.ps(7)
                    P.mm(pgs[:, 0, 384:512].m(lambda a: a), cf["ones"][:], sq[k][:, 0, :], start=True, stop=False)
                    P.mm(pgs[:, 0, 384:512], cf["hlo"][:], sq[k][:, 1, :], start=False, stop=True)
                    pgs2 = self.ps(6)
                    P.mm(pgs2[:, 0, 384:512], cf["hhi"][:], sq[k][:, 1, :], start=True, stop=False)
                    P.mm(pgs2[:, 0, 384:512], cf["ones"][:], sq[k][:, 2, :], start=False, stop=True)
                    P.act(rs[k][:, 0, :], pgs[:, 0, 384:512], AF.Sqrt, bias=1e-5, scale=1.0 / 192)
                    P.act(rs[k][:, 1, :], pgs2[:, 0, 384:512], AF.Sqrt, bias=1e-5, scale=1.0 / 192)
                    P.recip(rs[k][:], rs[k][:])
                    m_ = mo[k]
                    P.stt("dve", m_[:, 0, :], y_[:, 0, :], self.ppt[:, onw:onw + 1], rs[k][:, 0, :], ALU.mult, ALU.mult)
                    P.stt("dve", m_[0:64, 1, :], y_[0:64, 1, :], self.ppt[0:64, onw + 1:onw + 2], rs[k][0:64, 0, :], ALU.mult, ALU.mult)
                    P.stt("dve", m_[64:128, 1, :], y_[64:128, 1, :], self.ppt[64:128, onw + 1:onw + 2], rs[k][64:128, 1, :], ALU.mult, ALU.mult)
                    P.stt("dve", m_[:, 2, :], y_[:, 2, :], self.ppt[:, onw + 2:onw + 3], rs[k][:, 1, :], ALU.mult, ALU.mult)
                    P.dma("sp", self.mix.v(("ms", c), fn=lambda ap, cs_=cs_: ap[640:1024].rearrange("(j p) t -> p j t", p=128)[:, :, cs_]), m_[:])
                if pending is not None:
                    pending()
                pending = back
            if pending is not None:
                pending()

    def phase_rwkv(self, l):
        P = self.P
        ctx_out = (l < DEPTH - 1)
        LWC = -0.6065306597126334
        cf = {}
        for nme in ("blk", "lowf", "lowb"):
            cf[nme] = P.sb([128, 128], F32, "r_c_" + nme)
            self.load_c(cf[nme], nme, "act")
        m4 = {nme: P.sb([128, 2, 256], F32, "r_m4_" + nme) for nme in ("rmf", "rmb")}
        mk0 = P.mark()
        for nme in ("rmf", "rmb"):
            t_ = P.sb([128, 256], F32, "r_c_" + nme)
            self.load_c(t_, nme, "act")
            P.copy("dve", m4[nme][:], t_[:].m(lambda a: a.unsqueeze(1)).bc([128, 2, 256]))
        P.release(mk0)
        ones64 = P.sb([128, 64], F32, "r_ones")
        P.memset("pool", ones64[:], 1.0)
        omka = P.sb([128, 4], F32, "r_omka")
        oka = PP["ka"][0]
        P.ts("dve", omka[:], self.ppt[:, oka:oka + 4], -1.0, 1.0, ALU.mult, ALU.add)
        wupf = P.sb([128, 256], F32, "r_wupf")
        aupf = P.sb([128, 256], F32, "r_aupf")
        wupb = P.sb([128, 256], BF16, "r_wupb")
        aupb = P.sb([128, 256], BF16, "r_aupb")
        for d in range(2):
            P.dma("sp", wupf[d * 64:(d + 1) * 64, :], self.wup.v(("u", l, d), fn=lambda ap, d=d: ap[l][d]))
            P.dma("act", aupf[d * 64:(d + 1) * 64, :], self.aup.v(("u", l, d), fn=lambda ap, d=d: ap[l][d]))
        P.copy("dve", wupb[:], wupf[:])
        P.copy("pool", aupb[:], aupf[:])
        wdT = P.sb([128, S], BF16, "r_wdT")
        adT = P.sb([128, S], BF16, "r_adT")
        mk1 = P.mark()
        rows = self.shift_rows(l)
        hh = [P.sb([128, 8, 514], BF16, f"r_hh{i}") for i in range(2)]
        wl3 = self.scaled_w(l, C_RW + 768, 256, rows, "r_wl3", roff=768)
        for ti, (t0, tn) in enumerate(TILES):
            h = hh[ti % 2]
            self.load_h_halo(h, ti, "sp" if ti % 2 == 0 else "act")
            for c in range(2):
                pb = self.ps(c)
                n = 0
                for tap in range(3):
                    for kc in range(8):
                        P.mm(pb[:, 0, 0:tn], wl3[:, tap, kc, c * 128:(c + 1) * 128], h[:, kc, tap:tap + tn],
                             start=(n == 0), stop=(n == 23))
                        n += 1
                if c == 0:
                    P.act(wdT[:, t0:t0 + tn], pb[:, 0, 0:tn], AF.Tanh)
                else:
                    P.copy("act", adT[:, t0:t0 + tn], pb[:, 0, 0:tn])
        P.release(mk1)
        RW = int(os.environ.get("MK_RW", "99"))
        if RW <= 1:
            return
        mk_hp = P.mark()
        for hp in range(2):
            rT = P.sb([128, S], F32, "r_rT")
            kT = P.sb([128, S], F32, "r_kT")
            vT = P.sb([128, S], F32, "r_vT")
            ysum = P.sb([128, S], F32, "r_ysum")
            mk2 = P.mark()
            rows = self.shift_rows(l)
            hh = [P.sb([128, 8, 514], BF16, f"r_hh{i}") for i in range(2)]
            w3 = [self.scaled_w(l, C_RW + j * 256 + hp * 128, 128, rows, f"r_w3{j}", roff=j * 256 + hp * 128) for j in range(3)]
            for ti, (t0, tn) in enumerate(TILES):
                h = hh[ti % 2]
                self.load_h_halo(h, ti, "sp" if ti % 2 == 0 else "act")
                for j, dst in enumerate((rT, kT, vT)):
                    pb = self.ps(j % 2)
                    n = 0
                    for tap in range(3):
                        for kc in range(8):
                            P.mm(pb[:, 0, 0:tn], w3[j][:, tap, kc, :], h[:, kc, tap:tap + tn], start=(n == 0), stop=(n == 23))
                            n += 1
                    P.copy("act", dst[:, t0:t0 + tn], pb[:, 0, 0:tn])
            P.release(mk2)
            if RW <= 2:
                return
            mk3 = P.mark()
            self.rwkv_scan(l, hp, rT, kT, vT, wdT, adT, ysum, wupb, aupb, cf, m4, ones64, omka, LWC)
            P.release(mk3)
            self.rwkv_finish(l, hp, rT, kT, vT, ysum, cf, ctx_out)
            P.release(mk_hp)

    def shift_rows(self, l):
        P = self.P
        o = PR2["shr"][0]
        sr = P.sb([128, 3 * 1024], F32, "r_shr")
        P.dma("sp", sr[:], self.pr2.v(l, fn=lambda a: a[l][:, o:o + 3 * 1024]))
        r0, r2, r1 = sr[:, 0:1024], sr[:, 1024:2048], sr[:, 2048:3072]
        P.tt("dve", r1, r0, r2, ALU.add)
        P.ts("dve", r1, r1, -1.0, 1.0, ALU.mult, ALU.add)
        return [r0, r1, r2]

    def rwkv_scan(self, l, hp, rT, kT, vT, wdT, adT, ysum, wupb, aupb, cf, m4, ones64, omka, LWC):
        P = self.P
        NB = 2
        f32t = lambda n, w=512: [P.sb([128, w], F32, f"r_{n}")] * NB
        sg, aa, cum = f32t("sg"), f32t("aa"), f32t("cum")
        lw = sg
        cumx = [P.sb([128, 8], F32, "r_tot")] * NB
        e1, e2, e3, e4 = f32t("e1"), f32t("e2"), f32t("e3"), f32t("e4")
        kkr, sqk, nrm, kd, bq = f32t("kkr"), f32t("sqk"), f32t("nrm"), f32t("kd"), f32t("bq")
        kk, tmpa = kkr, nrm
        WC = [P.sb([128, 8], F32, f"r_WC{i}") for i in range(NB)]
        arb = [P.sb([128, 4, 2, 128], F32, "r_arb")] * NB
        kb = [P.sb([128, 512], F32, "r_kb")] * NB
        bb = [P.sb([128, 512], F32, "r_bb")] * NB
        kt = [P.sb([128, 512], F32, "r_kt")] * NB
        btl = [P.sb([128, 512], F32, "r_btl")] * NB
        G4 = 4
        TM = [P.sb([128, 4, 128], F32, f"r_TM{i}") for i in range(G4)]
        AT = [P.sb([128, 2, 4, 128], F32, f"r_AT{i}") for i in range(G4)]
        X1 = [P.sb([128, 2, 128], F32, f"r_X1{i}") for i in range(G4)]
        XX = [P.sb([128, 2, 2, 128], F32, f"r_XX{i}") for i in range(G4)]
        Qf = [P.sb([128, 2, 128], F32, f"r_Qf{i}") for i in range(G4)]
        W1 = [P.sb([128, 128], F32, f"r_W1{i}") for i in range(G4)]
        AU = [P.sb([128, 256], F32, f"r_AU{i}") for i in range(G4)]
        Y0T = [P.sb([128, 128], F32, f"r_Y0T{i}") for i in range(G4)]
        RhT = [P.sb([128, 128], F32, f"r_RhT{i}") for i in range(G4)]
        MTt = [P.sb([128, 128], F32, "r_MTt")] * 2
        MT = [[P.sb([128, 128], F32, f"r_MT{i}_{c}") for c in range(2)] for i in range(G4)]
        Ns = [[P.sb([128, 64], F32, f"r_Ns{i}_{c}") for c in range(2)] for i in range(G4)]
        ytmp = [P.sb([128, 64], F32, f"r_yt{i}") for i in range(2)]
        Sseq = P.sb([128, 4, 64], F32, "r_Sseq")
        ow0, oa0, okk, oka = PP["w0"][0], PP["a0"][0], PP["kk"][0], PP["ka"][0]
        if os.environ.get("MK_MEM"):
            print("rwkv_scan arena used", P._aoff, "of", P.ARENA)
        nblk_done = 0
        for d in range(2):
            if d == 1 and hp == 0 and os.environ.get("MK_RWDBG"):
                if not hasattr(self, "dbgy"):
                    self.dbgy = P.dram("dbgy", [128, S], F32, kind="ExternalOutput")
                P.dma("sp", self.dbgy.v(0), ysum[:])
            col = d * 2 + hp
            mask4 = m4["rmf"] if d == 0 else m4["rmb"]
            low = cf["lowf"] if d == 0 else cf["lowb"]
            P.memset("dve", Sseq[:, 0, :], 0.0)
            si = 0
            tiles = list(range(len(TILES))) if d == 0 else [0] + list(range(len(TILES) - 1, 0, -1))
            def prepA(tix):
                ti = tiles[tix]
                t0, tn = TILES[ti]
                nch = tn // 64
                nbk = tn // 128
                k_ = tix % NB
                ts_ = slice(t0, t0 + tn)
                ds_ = slice(d * 64, (d + 1) * 64)
                pxw, pxa = self.ps(0), self.ps(1)
                P.mm(pxw[:, 0, 0:tn], wupb[ds_, hp * 128:(hp + 1) * 128], wdT[ds_, ts_])
                P.act(sg[k_][:, 0:tn], pxw[:, 0, 0:tn], AF.Sigmoid, bias=self.ppt[:, ow0 + col:ow0 + col + 1])
                P.mm(pxa[:, 0, 0:tn], aupb[ds_, hp * 128:(hp + 1) * 128], adT[ds_, ts_])
                P.act(aa[k_][:, 0:tn], pxa[:, 0, 0:tn], AF.Sigmoid, bias=self.ppt[:, oa0 + col:oa0 + col + 1])
                yield
                P.ts("dve", lw[k_][:, 0:tn], sg[k_][:, 0:tn], LWC, None, ALU.mult)
                yield
                for c in range(nch):
                    cs = slice(c * 64, (c + 1) * 64)
                    P.scan("dve", cum[k_][:, cs], ones64[:], lw[k_][:, cs], 0.0, ALU.mult, ALU.add)
                    yield
                c3 = lambda t_: t_[:, 0:tn].re("p (c n) -> p c n", n=64)
                P.act(WC[k_][:, 0:nch].m(lambda a: a.unsqueeze(2)), c3(cum[k_])[:, :, 63:64], AF.Exp)
                yield
                if d == 1:
                    P.copy("pool", cumx[k_][:, 0:nch].m(lambda a: a.unsqueeze(2)), c3(cum[k_])[:, :, 63:64])
                    yield
                    P.tt("dve", cum[k_][:, 0:tn], lw[k_][:, 0:tn], cum[k_][:, 0:tn], ALU.subtract)
                    yield
                    P.tt("dve", c3(cum[k_]), c3(cum[k_]), cumx[k_][:, 0:nch].m(lambda a: a.unsqueeze(2)).bc([128, nch, 64]), ALU.add)
                    yield
                    tot = None
                P.tt("pool", e4[k_][:, 0:tn], cum[k_][:, 0:tn], lw[k_][:, 0:tn], ALU.subtract)
                yield
                P.act(e1[k_][:, 0:tn], cum[k_][:, 0:tn], AF.Exp)
                yield
                P.act(e2[k_][:, 0:tn], cum[k_][:, 0:tn], AF.Exp, scale=-1.0)
                yield
                P.act(e3[k_][:, 0:tn], e4[k_][:, 0:tn], AF.Exp)
                yield
                P.tt("dve", c3(e4[k_]), c3(e2[k_]), WC[k_][:, 0:nch].m(lambda a: a.unsqueeze(2)).bc([128, nch, 64]), ALU.mult)
                yield
                P.ts("dve", kkr[k_][:, 0:tn], kT[:, ts_], self.ppt[:, okk + col:okk + col + 1], None, ALU.mult)
                yield
                P.act(sqk[k_][:, 0:tn], kkr[k_][:, 0:tn], AF.Square)
                yield
                pss = self.ps(1)
                P.mm(pss[:, 0, 0:tn], cf["blk"][:], sqk[k_][:, 0:tn])
                P.act(nrm[k_][:, 0:tn], pss[:, 0, 0:tn], AF.Sqrt)
                yield
                P.ts("dve", nrm[k_][:, 0:tn], nrm[k_][:, 0:tn], 1e-12, None, ALU.max)
                yield
                P.recip(nrm[k_][:, 0:tn], nrm[k_][:, 0:tn])
                yield
                P.tt("dve", kk[k_][:, 0:tn], kkr[k_][:, 0:tn], nrm[k_][:, 0:tn], ALU.mult)
                yield
                P.ts("pool", tmpa[k_][:, 0:tn], aa[k_][:, 0:tn], self.ppt[:, oka + col:oka + col + 1], omka[:, col:col + 1], ALU.mult, ALU.add)
                yield
                P.tt("pool", kd[k_][:, 0:tn], kT[:, ts_], tmpa[k_][:, 0:tn], ALU.mult)
                yield
                P.tt("dve", bq[k_][:, 0:tn], kk[k_][:, 0:tn], aa[k_][:, 0:tn], ALU.mult)
                yield
            def prepB(tix):
                ti = tiles[tix]
                t0, tn = TILES[ti]
                nch = tn // 64
                nbk = tn // 128
                k_ = tix % NB
                ts_ = slice(t0, t0 + tn)
                ds_ = slice(d * 64, (d + 1) * 64)
                c3 = lambda t_: t_[:, 0:tn].re("p (c n) -> p c n", n=64)
                b3 = lambda t_: t_[:, 0:tn].re("p (b n) -> p b n", n=128)
                P.ts("pool", sqk[k_][:, 0:tn], kk[k_][:, 0:tn], -1.0, None, ALU.mult)
                yield
                P.tt("dve", arb[k_][:, 0:nbk, 0, :], b3(sqk[k_]), b3(e3[k_]), ALU.mult)
                yield
                P.tt("pool", arb[k_][:, 0:nbk, 1, :], rT[:, ts_].re("p (b n) -> p b n", n=128), b3(e1[k_]), ALU.mult)
                yield
                P.tt("dve", kb[k_][:, 0:tn], kd[k_][:, 0:tn], e2[k_][:, 0:tn], ALU.mult)
                yield
                P.tt("pool", bb[k_][:, 0:tn], bq[k_][:, 0:tn], e2[k_][:, 0:tn], ALU.mult)
                yield
                P.tt("dve", kt[k_][:, 0:tn], kd[k_][:, 0:tn], e4[k_][:, 0:tn], ALU.mult)
                yield
                P.tt("pool", btl[k_][:, 0:tn], bq[k_][:, 0:tn], e4[k_][:, 0:tn], ALU.mult)
                yield
            def step(gen, n=10 ** 9):
                for _ in range(n):
                    if next(gen, 'done') == 'done':
                        return
            step(prepA(0))
            step(prepB(0))
            for tix, ti in enumerate(tiles):
                t0, tn = TILES[ti]
                nch = tn // 64
                nbk = tn // 128
                k_ = tix % NB
                ts_ = slice(t0, t0 + tn)
                ds_ = slice(d * 64, (d + 1) * 64)
                nxtA = prepA(tix + 1) if tix + 1 < len(tiles) else iter(())
                blocks = list(range(nbk)) if d == 0 else list(range(nbk - 1, -1, -1))
                G = len(blocks)
                idm = lambda t_: t_[:].m(lambda a: a.unsqueeze(1)).bc([128, 2, 128])
                ev = 0
                for g, bi in enumerate(blocks):
                    bs = slice(bi * 128, (bi + 1) * 128)
                    pt = self.ps(2 + g)
                    P.tr(pt[:, 0, 0:128], arb[k_][:, bi, 0, :], self.c_ident[:])
                    P.tr(pt[:, 0, 128:256], btl[k_][:, bs], self.c_ident[:])
                    P.tr(pt[:, 0, 256:384], kt[k_][:, bs], self.c_ident[:])
                    P.tr(pt[:, 0, 384:512], vT[:, t0 + bi * 128:t0 + (bi + 1) * 128], self.c_ident[:])
                for g, bi in enumerate(blocks):
                    P.copy("act" if g % 2 == 0 else "dve", TM[g][:].re("p a n -> p (a n)"), self.ps(2 + g)[:, 0, 0:512])
                px2 = self.ps(6, 2)
                for half in range(0, G, 2):
                    gs = list(range(half, min(half + 2, G)))
                    for g in gs:
                        bi = blocks[g]
                        bs = slice(bi * 128, (bi + 1) * 128)
                        pa = self.ps(2 + 2 * (g % 2), 2)
                        for h in range(2):
                            hs = slice(h * 64, (h + 1) * 64)
                            ar_ = arb[k_][hs, bi, :, :].re("p a n -> p (a n)")
                            P.mm(pa[:, h, 0:256], bb[k_][hs, bs], ar_)
                            P.mm(pa[:, h, 256:512], kb[k_][hs, bs], ar_)
                            P.mm(px2[:, h, g * 128:(g + 1) * 128], arb[k_][hs, bi, 0, :], bb[k_][hs, bs])
                    for g in gs:
                        pa = self.ps(2 + 2 * (g % 2), 2)
                        P.tt("dve", AT[g][:].re("p h a n -> p h (a n)"), pa,
                             mask4[:].re("p a n -> p (a n)").m(lambda a: a.unsqueeze(1)).bc([128, 2, 512]), ALU.mult)
                for g in range(G):
                    P.tt("dve", X1[g][:], px2[:, :, g * 128:(g + 1) * 128], idm(low), ALU.mult)
                    P.tt("pool", Qf[g][:], AT[g][:, :, 0, :], idm(self.c_ident), ALU.add)
                xk = [[X1[g][:, h, :] for h in range(2)] for g in range(G)]
                xtk = [[AT[g][:, h, 0, :] for h in range(2)] for g in range(G)]
                for lev in range(int(os.environ.get('MK_NLEV', '5'))):
                    for g in range(G):
                        pn = self.ps(2 + g)
                        for h in range(2):
                            P.mm(pn[:, 0, h * 256:h * 256 + 128], xtk[g][h], xk[g][h])
                            if lev < 4:
                                P.mm(pn[:, 0, h * 256 + 128:h * 256 + 256], xk[g][h], xtk[g][h])
                    for g in range(G):
                        P.copy("act", XX[g][:].re("p h a n -> p (h a n)"), self.ps(2 + g)[:, 0, :])
                        xk[g] = [XX[g][:, h, 0, :] for h in range(2)]
                        xtk[g] = [XX[g][:, h, 1, :] for h in range(2)]
                    for g in range(G):
                        pq = self.ps(6 + g // 2)
                        for h in range(2):
                            c0 = (g % 2) * 256 + h * 128
                            P.mm(pq[:, 0, c0:c0 + 128], xk[g][h], Qf[g][:, h, :])
                    for g in range(G):
                        pq = self.ps(6 + g // 2)
                        c0 = (g % 2) * 256
                        P.tt("dve", Qf[g][:], pq[:, 0, c0:c0 + 256].re("p (h n) -> p h n", h=2), Qf[g][:], ALU.add)
                    step(nxtA, 7)
                pw = self.ps(0)
                for g in range(G):
                    for h in range(2):
                        P.mm(pw[:, 0, g * 128 + h * 64:g * 128 + (h + 1) * 64], AT[g][:, h, 2, :], TM[g][:, 3, h * 64:(h + 1) * 64])
                for g in range(G):
                    P.copy("act", W1[g][:], pw[:, 0, g * 128:(g + 1) * 128])
                for g in range(G):
                    pau = self.ps(2 + g // 2)
                    c0 = (g % 2) * 256
                    for h in range(2):
                        P.mm(pau[:, 0, c0 + h * 64:c0 + (h + 1) * 64], Qf[g][:, h, :], TM[g][:, 0, h * 64:(h + 1) * 64])
                        P.mm(pau[:, 0, c0 + 128 + h * 64:c0 + 128 + (h + 1) * 64], Qf[g][:, h, :], W1[g][:, h * 64:(h + 1) * 64])
                for g in range(G):
                    pau = self.ps(2 + g // 2)
                    c0 = (g % 2) * 256
                    P.copy("act" if g % 2 == 0 else "dve", AU[g][:], pau[:, 0, c0:c0 + 256])
                for g, bi in enumerate(blocks):
                    py = self.ps(4 + g // 2)
                    c0 = (g % 2) * 256
                    for h in range(2):
                        hs = slice(h * 64, (h + 1) * 64)
                        P.mm(py[hs, 0, c0:c0 + 128], AU[g][:, 128 + h * 64:128 + (h + 1) * 64], AT[g][:, h, 1, :], start=True, stop=False)
                        P.mm(py[hs, 0, c0:c0 + 128], TM[g][:, 3, h * 64:(h + 1) * 64], AT[g][:, h, 3, :], start=False, stop=True)
                        P.mm(py[hs, 0, c0 + 128:c0 + 256], AU[g][:, h * 64:(h + 1) * 64], AT[g][:, h, 1, :])
                for g, bi in enumerate(blocks):
                    py = self.ps(4 + g // 2)
                    c0 = (g % 2) * 256
                    P.copy("act", Y0T[g][:], py[:, 0, c0:c0 + 128])
                    P.tt("dve", RhT[g][:], py[:, 0, c0 + 128:c0 + 256], arb[k_][:, bi, 1, :], ALU.add)
                step(nxtA)
                if tix + 1 < len(tiles):
                    step(prepB(tix + 1))
                for g, bi in enumerate(blocks):
                    for cc in range(2):
                        cs = slice(cc * 64, (cc + 1) * 64)
                        pmn = self.ps((6 if g < 2 else 2) + cc)
                        c0 = (g % 2) * 192
                        P.mm(pmn[:, 0, c0:c0 + 128], AU[g][cs, 0:128], TM[g][cs, 1, :])
                        for h in range(2):
                            hs = slice(h * 64, (h + 1) * 64)
                            P.mm(pmn[hs, 0, c0 + 128:c0 + 192], TM[g][cs, 1, hs], AU[g][cs, 128 + h * 64:128 + (h + 1) * 64], start=True, stop=False)
                            P.mm(pmn[hs, 0, c0 + 128:c0 + 192], TM[g][cs, 2, hs], TM[g][cs, 3, hs], start=False, stop=True)
                for g, bi in enumerate(blocks):
                    for cc in range(2):
                        pmn = self.ps((6 if g < 2 else 2) + cc)
                        c0 = (g % 2) * 192
                        mt_ = MTt[cc]
                        P.tt("dve", mt_[:], pmn[:, 0, c0:c0 + 128], cf["blk"][:], ALU.mult)
                        wcol = bi * 2 + cc
                        P.stt("dve", MT[g][cc][:], self.c_ident[:], WC[k_][:, wcol:wcol + 1], mt_[:], ALU.mult, ALU.add)
                        P.copy("act", Ns[g][cc][:], pmn[:, 0, c0 + 128:c0 + 192])
                for g, bi in enumerate(blocks if not os.environ.get('MK_NOSEQ') else []):
                    for cc in ([0, 1] if d == 0 else [1, 0]):
                        cs = slice(cc * 64, (cc + 1) * 64)
                        tok = slice(t0 + bi * 128 + cc * 64, t0 + bi * 128 + cc * 64 + 64)
                        pyh = [self.ps(4), self.ps(5)]
                        pss_ = self.ps(0)
                        for h in range(2):
                            hs = slice(h * 64, (h + 1) * 64)
                            P.mm(pyh[h][hs, 0, 0:64], Sseq[hs, si % 4, :], RhT[g][hs, cs])
                        P.mm(pss_[:, 0, 0:64], MT[g][cc][:], Sseq[:, si % 4, :])
                        P.tt("dve", Sseq[:, (si + 1) % 4, :], pss_[:, 0, 0:64], Ns[g][cc][:], ALU.add)
                        for h in range(2):
                            hs = slice(h * 64, (h + 1) * 64)
                            if d == 0:
                                P.tt("pool" if False else "dve", ysum[hs, tok], pyh[h][hs, 0, 0:64], Y0T[g][hs, cs], ALU.add)
                            else:
                                yt_ = ytmp[cc]
                                P.tt("dve", yt_[hs, :], pyh[h][hs, 0, 0:64], Y0T[g][hs, cs], ALU.add)
                                P.tt("pool", ysum[hs, tok], ysum[hs, tok], yt_[hs, :], ALU.add)
                        si += 1

    def rwkv_finish(self, l, hp, rT, kT, vT, ysum, cf, ctx_out):
        P = self.P
        wg = P.sb([128, 8, 128], BF16, "rf_wg")
        P.dma("sp", wg[:], self.wb(l, C_RG + hp * 128, 128))
        hb = [P.sb([128, 8, 512], BF16, f"rf_h{i}") for i in range(2)]
        f = lambda n: [P.sb([128, 512], F32, f"rf_{n}{i}") for i in range(2)]
        yc, sq, rstd, rk, gs = f("yc"), f("sq"), f("rstd"), f("rk"), f("gs")
        mo = [P.sb([128, 512], BF16, f"rf_mo{i}") for i in range(2)]
        olw, olb, ork = PP["lnw"][0] + hp, PP["lnb"][0] + hp, PP["rk"][0] + hp
        for ti, (t0, tn) in enumerate(TILES):
            if ti == 0 and not ctx_out:
                continue
            k_ = ti % 2
            ts_ = slice(t0, t0 + tn)
            self.load_h(hb[k_], ti, "sp" if k_ == 0 else "act")
            pg = self.ps(0)
            for kc in range(8):
                P.mm(pg[:, 0, 0:tn], wg[:, kc, :], hb[k_][:, kc, 0:tn], start=(kc == 0), stop=(kc == 7))
            P.act(gs[k_][:, 0:tn], pg[:, 0, 0:tn], AF.Silu)
            pm = self.ps(1)
            P.mm(pm[:, 0, 0:tn], cf["blk"][:], ysum[:, ts_])
            P.stt("dve", yc[k_][:, 0:tn], pm[:, 0, 0:tn], -1.0 / 64, ysum[:, ts_], ALU.mult, ALU.add)
            P.act(sq[k_][:, 0:tn], yc[k_][:, 0:tn], AF.Square)
            pv = self.ps(2)
            P.mm(pv[:, 0, 0:tn], cf["blk"][:], sq[k_][:, 0:tn])
            P.act(rstd[k_][:, 0:tn], pv[:, 0, 0:tn], AF.Sqrt, bias=64e-5, scale=1.0 / 64)
            P.recip(rstd[k_][:, 0:tn], rstd[k_][:, 0:tn])
            P.tt("dve", yc[k_][:, 0:tn], yc[k_][:, 0:tn], rstd[k_][:, 0:tn], ALU.mult)
            P.ts("dve", yc[k_][:, 0:tn], yc[k_][:, 0:tn], self.ppt[:, olw:olw + 1], self.ppt[:, olb:olb + 1], ALU.mult, ALU.add)
            P.stt("dve", rk[k_][:, 0:tn], rT[:, ts_], self.ppt[:, ork:ork + 1], kT[:, ts_], ALU.mult, ALU.mult)
            pb = self.ps(3)
            P.mm(pb[:, 0, 0:tn], cf["blk"][:], rk[k_][:, 0:tn])
            P.tt("dve", rk[k_][:, 0:tn], pb[:, 0, 0:tn], vT[:, ts_], ALU.mult)
            P.tt("pool", yc[k_][:, 0:tn], yc[k_][:, 0:tn], rk[k_][:, 0:tn], ALU.add)
            P.tt("dve", mo[k_][:, 0:tn], yc[k_][:, 0:tn], gs[k_][:, 0:tn], ALU.mult)
            r0_ = 384 + hp * 128
            P.dma("sp", self.mix.v(("mr", hp, ti), (slice(r0_, r0_ + 128), ts_)), mo[k_][:, 0:tn])

    def zero_mix(self, r0, r1):
        P = self.P
        z = P.sb([128, S], BF16, "zmix")
        P.memset("pool", z[:], 0.0)
        for kc in range(r0 // 128, r1 // 128):
            P.dma("sp", self.mix.v(("mz", kc), (slice(kc * 128, kc * 128 + 128), slice(None))), z[:])

    def build(self, layers=DEPTH):
        P = self.P
        base = P.mark()
        self.phase_w()
        P.release(base)
        for l in range(layers):
            self.phase_mod(l)
            keep = P.mark()
            self.phase_norm(l)
            P.release(keep)
            if self.stop_after == "norm":
                return
            if not os.environ.get("MK_SKIP_ATT"):
                self.phase_att(l)
                P.release(keep)
            if self.stop_after == "att":
                return
            if os.environ.get("MK_SKIP_RWKV"):
                self.zero_mix(384, 640)
            else:
                self.phase_rwkv(l)
            P.release(keep)
            if self.stop_after == "rwkv":
                return
            self.phase_ssd(l)
            P.release(keep)
            if self.stop_after == "ssd":
                return
            self.phase_out(l)
            P.release(keep)
            for a in ("n_x", "o_w"):
                if hasattr(self, a):
                    delattr(self, a)

def build_program(stop_after=None, dbg=False, layers=DEPTH):
    nc = bass.Bass("TRN2", target_bir_lowering=False)
    k = MK(nc, stop_after=stop_after, dbg=dbg)
    k.build(layers)
    k.P.emit()
    return nc, k


def prep_inputs(inp, b):
    inp = {k: np.asarray(v) for k, v in inp.items()}
    d = {}
    d["xin"] = np.ascontiguousarray(np.concatenate([inp["ctx"][b], inp["x"][b]], 0).astype(np.float32))
    d["cvec"] = np.ascontiguousarray(np.concatenate([fm(inp["c"][b]), fm(inp["c_ctx"])], 1))
    d["ada_w"] = np.ascontiguousarray(inp["ada_w"], np.float32)
    d["w_in"] = np.ascontiguousarray(inp["w_in"], np.float32)
    d["w_out"] = np.ascontiguousarray(inp["w_out"], np.float32)
    d["wup"] = np.ascontiguousarray(inp["rwkv_w_up"], np.float32)
    d["aup"] = np.ascontiguousarray(inp["rwkv_a_up"], np.float32)
    d["pp"] = np.stack([make_pp(inp, l) for l in range(DEPTH)])
    d["pr"] = np.stack([make_pr(inp, l) for l in range(DEPTH)])
    d["pr2"] = np.stack([make_pr2(inp, l) for l in range(DEPTH)])
    d["cst"] = make_consts()
    return d


def kernel(**inputs):
    nc, _ = build_program()
    maps = [prep_inputs(inputs, b % 4) for b in range(4)]
    in_maps = [maps[i % 4] for i in range(8)]
    res = run_bass_kernel_spmd(nc, in_maps, core_ids=list(range(8)))
    return np.stack([np.asarray(res.results[b]["y"], np.float32) for b in range(4)], 0)
```

```python
import numpy as np
import concourse.bass as bass
import concourse.mybir as mybir

F32 = mybir.dt.float32
BF16 = mybir.dt.bfloat16
ALU = mybir.AluOpType
AF = mybir.ActivationFunctionType
AX = mybir.AxisListType

import os
DUMP = os.environ.get("MK_DUMP", "")
DMAQ = dict(kv.split(":") for kv in filter(None, os.environ.get("MK_DMAQ", "act:sp").split(",")))
ATTACH = os.environ.get("MK_ATTACH", "1") != "0"
NOSYNC = set(filter(None, os.environ.get("MK_NOSYNC", "").split(",")))
ENGS = ("pe", "dve", "act", "pool", "sp")
N_DMA_SEMS = 20


class Trk:
    __slots__ = ("lw", "rd", "ps")

    def __init__(self, ps=False):
        self.lw = None
        self.rd = []
        self.ps = ps


class V:
    __slots__ = ("ap", "trk")

    def __init__(self, ap, trk):
        self.ap = ap
        self.trk = trk

    def __getitem__(self, idx):
        return V(self.ap[idx], self.trk)

    def m(self, fn):
        return V(fn(self.ap), self.trk)

    def re(self, pat, **kw):
        return V(self.ap.rearrange(pat, **kw), self.trk)

    def bc(self, shape):
        return V(self.ap.broadcast_to(shape), self.trk)

    @property
    def shape(self):
        return self.ap.shape


class Tile:
    def __init__(self, handle):
        self.h = handle
        self.trk = Trk()

    def __getitem__(self, idx):
        return V(self.h[idx], (self.trk,))

    def ap(self):
        return V(self.h.ap() if hasattr(self.h, "ap") else self.h[:], (self.trk,))


class DTile:
    def __init__(self, handle):
        self.h = handle
        self.reg = {}

    def v(self, key, idx=None, fn=None):
        t = self.reg.setdefault(key, Trk())
        ap = self.h.ap()
        if fn is not None:
            ap = fn(ap)
        if idx is not None:
            ap = ap[idx]
        return V(ap, (t,))


class Op:
    __slots__ = ("eng", "fn", "reads", "writes", "idx", "deps", "signal", "sig_count",
                 "is_dma", "dma_sem", "dma_val", "dma_prev", "kind")


class Prog:
    def __init__(self, nc):
        self.nc = nc
        self.ops = {e: [] for e in ENGS}
        self.n_t = 0
        self.all_ops = []

    ARENA = 207 * 1024

    def sb(self, shape, dtype, name=None):
        self.n_t += 1
        if not hasattr(self, "_abase"):
            a = self.nc.alloc_sbuf_tensor("arena", [128, self.ARENA], mybir.dt.uint8)
            self._abase = self.nc.lookup_mloc(a).addr
            self._aoff = 0
        esz = 2 if dtype == BF16 else 4
        n = esz
        for d in shape[1:]:
            n *= d
        off = (self._aoff + 63) // 64 * 64
        assert off + n <= self.ARENA, f"SBUF arena overflow allocating {name} {shape}: {off + n}"
        self._aoff = off + n
        return Tile(self.nc.alloc_sbuf_tensor_at(f"{name or 'sb'}_{self.n_t}", list(shape), dtype, offset=self._abase + off))

    def mark(self):
        return getattr(self, "_aoff", 0)

    def release(self, mark):
        self.barrier()
        self._aoff = mark

    def barrier(self):
        lasts = []
        for e in ENGS:
            for op in reversed(self.ops[e]):
                if not op.is_dma and op.kind != "bar":
                    lasts.append(op)
                    break
        dmas = [op for op in self.all_ops[getattr(self, "_bar_pos", 0):] if op.is_dma]
        self._bar_pos = len(self.all_ops)
        for e in ENGS:
            op = self.rec(e, lambda eh: eh.nop(nofuse=True), [], [], kind="bar")
            op.deps = [d for d in lasts if d.eng != e] + dmas
            for d in op.deps:
                d.signal = True

    def ps(self, shape, dtype=F32, name=None):
        self.n_t += 1
        return Tile(self.nc.alloc_psum_tensor(name or f"ps{self.n_t}", list(shape), dtype))

    def dram(self, name, shape, dtype, kind="Internal"):
        return DTile(self.nc.dram_tensor(name, list(shape), dtype, kind=kind))

    def rec(self, eng, fn, reads, writes, is_dma=False, kind=""):
        op = Op()
        op.eng = eng
        op.fn = fn
        op.kind = kind
        op.is_dma = is_dma
        op.idx = len(self.ops[eng])
        op.signal = False
        op.deps = []
        rt = []
        for v in reads:
            if v is None or not isinstance(v, V):
                continue
            rt.extend(v.trk)
        wt = []
        for v in writes:
            if v is None or not isinstance(v, V):
                continue
            wt.extend(v.trk)
        deps = set()
        for t in rt:
            if t.lw is not None:
                deps.add(t.lw)
            if t.ps:
                for r in t.rd:
                    if r.eng != eng:
                        deps.add(r)
        for t in wt:
            if t.lw is not None:
                deps.add(t.lw)
            for r in t.rd:
                deps.add(r)
        deps.discard(op)
        for t in rt:
            t.rd.append(op)
        for t in wt:
            t.lw = op
            t.rd = []
        final = []
        for d in deps:
            if d.eng == "pe" and eng == "pe" and not d.is_dma and not is_dma:
                continue
            if d.eng == eng and eng in NOSYNC and not d.is_dma and not is_dma:
                continue
            final.append(d)
        op.deps = final
        for d in final:
            d.signal = True
        self.ops[eng].append(op)
        self.all_ops.append(op)
        return op

    def mm(self, out, lhsT, rhs, start=True, stop=True):
        return self.rec("pe", lambda e: e.matmul(out.ap, lhsT.ap, rhs.ap, start=start, stop=stop),
                        [lhsT, rhs], [out], kind="mm")

    def tr(self, out, in_, ident):
        return self.rec("pe", lambda e: e.transpose(out.ap, in_.ap, ident.ap), [in_, ident], [out], kind="tr")

    def act(self, out, in_, func, bias=None, scale=None, accum_out=None, eng="act"):
        kw = {}
        rd = [in_]
        if bias is not None:
            kw["bias"] = bias.ap if isinstance(bias, V) else bias
            rd.append(bias)
        if scale is not None:
            kw["scale"] = scale.ap if isinstance(scale, V) else scale
            rd.append(scale)
        wr = [out]
        if accum_out is not None:
            kw["accum_out"] = accum_out.ap
            wr.append(accum_out)
        return self.rec("act", lambda e: e.activation(out.ap, in_.ap, func, **kw), rd, wr, kind="act")

    def tt(self, eng, out, in0, in1, op):
        return self.rec(eng, lambda e: e.tensor_tensor(out.ap, in0.ap, in1.ap, op), [in0, in1], [out], kind="tt")

    def ts(self, eng, out, in0, s1, s2, op0, op1=None, accum_out=None):
        rd = [in0, s1, s2]
        a1 = s1.ap if isinstance(s1, V) else s1
        a2 = s2.ap if isinstance(s2, V) else s2
        kw = {}
        wr = [out]
        if accum_out is not None:
            kw["accum_out"] = accum_out.ap
            wr.append(accum_out)
        if op1 is None:
            return self.rec(eng, lambda e: e.tensor_scalar(out.ap, in0.ap, a1, None, op0, **kw), rd, wr, kind="ts")
        return self.rec(eng, lambda e: e.tensor_scalar(out.ap, in0.ap, a1, a2, op0, op1, **kw), rd, wr, kind="ts")

    def stt(self, eng, out, in0, scalar, in1, op0, op1):
        a = scalar.ap if isinstance(scalar, V) else scalar
        return self.rec(eng, lambda e: e.scalar_tensor_tensor(out.ap, in0.ap, a, in1.ap, op0, op1),
                        [in0, scalar, in1], [out], kind="stt")

    def copy(self, eng, out, in_):
        if eng == "act":
            return self.rec("act", lambda e: e.copy(out.ap, in_.ap), [in_], [out], kind="copy")
        return self.rec(eng, lambda e: e.tensor_copy(out.ap, in_.ap), [in_], [out], kind="copy")

    def memset(self, eng, out, val):
        return self.rec(eng, lambda e: e.memset(out.ap, val), [], [out], kind="memset")

    def reduce(self, eng, out, in_, op, axis=AX.X):
        return self.rec(eng, lambda e: e.tensor_reduce(out.ap, in_.ap, axis, op), [in_], [out], kind="red")

    def recip(self, out, in_):
        return self.rec("dve", lambda e: e.reciprocal(out.ap, in_.ap), [in_], [out], kind="recip")

    def scan(self, eng, out, d0, d1, initial, op0, op1):
        ini = initial.ap if isinstance(initial, V) else initial
        return self.rec(eng, lambda e: e.tensor_tensor_scan(out.ap, d0.ap, d1.ap, ini, op0, op1),
                        [d0, d1, initial], [out], kind="scan")

    def dma(self, q, out, in_, **kw):
        q = DMAQ.get(q, q)
        return self.rec(q, lambda e: e.dma_start(out.ap, in_.ap, **kw), [in_], [out], is_dma=True, kind="dma")

    def emit(self):
        nc = self.nc
        eng_sem = {e: nc.alloc_semaphore(f"s_{e}") for e in ENGS}
        dma_sems = {e: ([nc.alloc_semaphore(f"d_{e}_{i}") for i in range(N_DMA_SEMS)]
                        if any(o.is_dma for o in self.ops[e]) else []) for e in ENGS}
        for e in ENGS:
            cnt = 0
            nd = 0
            last_on_sem = {}
            for op in self.ops[e]:
                if op.is_dma:
                    j = nd % N_DMA_SEMS
                    nd += 1
                    op.dma_sem = dma_sems[e][j]
                    k = nd_k = (nd - 1) // N_DMA_SEMS + 1
                    op.dma_val = 16 * k
                    op.dma_prev = last_on_sem.get(j)
                    last_on_sem[j] = op
                else:
                    if op.signal:
                        cnt += 1
                        op.sig_count = cnt
        self._eng_sem = eng_sem
        handles = {"pe": "tensor", "dve": "vector", "act": "scalar", "pool": "gpsimd", "sp": "sync"}

        def run_engine(ename, eh):
            seen = {}

            def wait(sem, val):
                key = id(sem)
                if seen.get(key, 0) >= val:
                    return
                seen[key] = val
                eh.wait_ge(sem, val)

            for op in self.ops[ename]:
                need = []
                for d in op.deps:
                    if d.is_dma:
                        need.append((d.dma_sem, d.dma_val))
                    else:
                        need.append((eng_sem[d.eng], d.sig_count))
                if op.is_dma and op.dma_prev is not None:
                    need.append((op.dma_prev.dma_sem, op.dma_prev.dma_val))
                todo = []
                for sem, val in need:
                    key = id(sem)
                    if seen.get(key, 0) >= val:
                        continue
                    seen[key] = val
                    todo = [(s_, v_) for (s_, v_) in todo if s_ is not sem] + [(sem, val)]
                attach = None
                if ATTACH and todo and not op.is_dma and op.kind != "bar":
                    attach = todo.pop()
                for sem, val in todo:
                    eh.wait_ge(sem, val)
                if op.is_dma:
                    ins = op.fn(eh)
                    ins.then_inc(op.dma_sem, 16)
                else:
                    ins = op.fn(eh)
                    if attach is not None:
                        ins._wait_ge(attach[0], attach[1])
                    if op.signal:
                        ins.then_inc(eng_sem[ename], 1)
                    if DUMP and ename in DUMP:
                        print(ename, op.idx, op.kind, ins.concise(), flush=True)
            last = {}
            for op in self.ops[ename]:
                if op.is_dma:
                    last[id(op.dma_sem)] = op
            for op in last.values():
                wait(op.dma_sem, op.dma_val)

        with nc.Block() as block:
            @block.tensor
            def _(eh):
                run_engine("pe", eh)

            @block.vector
            def _(eh):
                run_engine("dve", eh)

            @block.scalar
            def _(eh):
                run_engine("act", eh)

            @block.gpsimd
            def _(eh):
                run_engine("pool", eh)

            @block.sync
            def _(eh):
                run_engine("sp", eh)

    def stats(self):
        return {e: len(self.ops[e]) for e in ENGS}

from concourse.bass_utils import run_bass_kernel_spmd

S = 4352
CTX = 256
TLAT = 4096
DM = 1024
NIN = 3596
NBLK = 34
DEPTH = 2
TILES = [(0, 256)] + [(256 + 512 * i, 512) for i in range(8)]
NEG = -30000.0

C_Q, C_K, C_V, C_G = 0, 384, 512, 640
C_RW = 1024
C_RG = 2048
C_Z = 2304
C_XBC = 2688
C_DT = 3584


def _slots(spec):
    d, o = {}, 0
    for n, w in spec:
        d[n] = (o, w)
        o += w
    return d, o


PP, NPP = _slots([("adab", 24), ("normw", 8), ("sink", 6), ("mu0", 8), ("mu1", 8), ("w0", 4), ("a0", 4),
                  ("kk", 4), ("ka", 4), ("rk", 2), ("lnw", 2), ("lnb", 2), ("convw", 21), ("convb", 7),
                  ("ssmd", 3), ("ssmnw", 3)])
PR, NPR = _slots([("adabg", 1024), ("dtb", 12), ("alog", 12), ("fnw", 1024)])
PR2, NPR2 = _slots([("convr", 3 * 896), ("shr", 3 * 1024)])
CS, NCS = _slots([("ident", 128), ("mband", 3 * 384), ("perm", 128), ("cos", 4096), ("sin", 4096),
                  ("blk", 128), ("rmf", 256), ("rmb", 256), ("snf", 128), ("snb", 128), ("trif", 128),
                  ("trib", 128), ("ones", 128), ("hlo", 128), ("hhi", 128), ("lowf", 128), ("lowb", 128)])


def fm(v):
    v = np.asarray(v, np.float32)
    return np.ascontiguousarray(v.reshape(-1, 128).T)


def rep(v):
    v = np.asarray(v, np.float32).reshape(1, -1)
    return np.ascontiguousarray(np.broadcast_to(v, (128, v.shape[1])))


def make_consts():
    c = np.zeros((128, NCS), np.float32)

    def put(n, a):
        o, w = CS[n]
        c[:, o:o + w] = a
    put("ident", np.eye(128, dtype=np.float32))
    qi = np.arange(128)[:, None]
    kj = np.arange(384)[None, :]
    band = np.abs(kj - 128 - qi) <= 128
    mb = []
    for var in range(3):
        ok = band.copy()
        if var == 0:
            ok &= kj >= 128
        if var == 2:
            ok &= kj < 256
        mb.append(np.where(ok, 0.0, NEG))
    put("mband", np.concatenate(mb, 1))
    perm = np.zeros((128, 128), np.float32)
    for m in range(128):
        d = m % 64
        half = (d % 32) // 16
        partner = m + 16 if half == 0 else m - 16
        perm[partner, m] = 1.0
    put("perm", perm)
    rows = TLAT // 64
    row = np.repeat(np.arange(rows), 64).astype(np.float32)
    col = np.tile(np.arange(64), rows).astype(np.float32)
    pos = np.stack([row, col], -1)
    inv = (np.float32(10000.0) ** (-np.arange(16, dtype=np.float32) / np.float32(16))).astype(np.float32)
    ang = (pos[:, :, None] * inv).astype(np.float32)
    cosv, sinv = np.cos(ang).astype(np.float32), np.sin(ang).astype(np.float32)
    ct = np.zeros((128, TLAT), np.float32)
    st = np.zeros((128, TLAT), np.float32)
    for m in range(128):
        d = m % 64
        ax, half, f = d // 32, (d % 32) // 16, d % 16
        ct[m] = cosv[:, ax, f]
        st[m] = -sinv[:, ax, f] if half == 0 else sinv[:, ax, f]
    put("cos", ct)
    put("sin", st)
    blk = np.zeros((128, 128), np.float32)
    blk[:64, :64] = 1
    blk[64:, 64:] = 1
    put("blk", blk)
    s = np.arange(128)[:, None]
    t = np.arange(128)[None, :]
    same = (s // 64) == (t // 64)
    put("rmf", np.concatenate([(same & (s < t)), (same & (s <= t))], 1).astype(np.float32))
    put("rmb", np.concatenate([(same & (s > t)), (same & (s >= t))], 1).astype(np.float32))
    put("lowf", (same & (t < s)).astype(np.float32))
    put("lowb", (same & (t > s)).astype(np.float32))
    put("snf", np.where(s <= t, 0.0, NEG))
    put("snb", np.where(s >= t, 0.0, NEG))
    put("trif", (s <= t).astype(np.float32))
    put("trib", (s >= t).astype(np.float32))
    put("ones", np.ones((128, 128), np.float32))
    hlo = np.zeros((128, 128), np.float32)
    hlo[:64] = 1
    put("hlo", hlo)
    put("hhi", 1 - hlo)
    return c


def make_pp(inp, l):
    p = np.zeros((128, NPP), np.float32)

    def put(n, a):
        o, w = PP[n]
        assert a.shape == (128, w), (n, a.shape)
        p[:, o:o + w] = a
    put("adab", fm(inp["ada_b"][l]))
    put("normw", fm(inp["norm_w"][l]))
    put("sink", rep(inp["attn_sink"][l]))
    put("mu0", fm(inp["rwkv_mu"][l, 0]))
    put("mu1", fm(inp["rwkv_mu"][l, 1]))
    for n, k in (("w0", "rwkv_w0"), ("a0", "rwkv_a0"), ("kk", "rwkv_k_k"), ("ka", "rwkv_k_a")):
        put(n, np.concatenate([fm(inp[k][l, 0]), fm(inp[k][l, 1])], 1))
    put("rk", fm(inp["rwkv_r_k"][l].reshape(-1)))
    put("lnw", fm(inp["rwkv_ln_w"][l]))
    put("lnb", fm(inp["rwkv_ln_b"][l]))
    cw = inp["ssm_conv_w"][l]
    put("convw", np.stack([fm(cw[0]), fm(cw[1]), fm(cw[2])], -1).reshape(128, 21))
    put("convb", fm(inp["ssm_conv_b"][l]))
    put("ssmd", fm(np.repeat(inp["ssm_d"][l], 64)))
    put("ssmnw", fm(inp["ssm_norm_w"][l]))
    return p


def make_pr(inp, l):
    p = np.zeros((128, NPR), np.float32)

    def put(n, a):
        o, w = PR[n]
        p[:, o:o + w] = a
    put("adabg", rep(inp["ada_b"][l, 2048:3072]))
    put("dtb", rep(inp["ssm_dt_bias"][l].reshape(-1)))
    put("alog", rep(inp["ssm_a_log"][l].reshape(-1)))
    put("fnw", rep(inp["final_norm_w"]))
    return p


def make_pr2(inp, l):
    p = np.zeros((128, NPR2), np.float32)

    def put(n, a):
        o, w = PR2[n]
        p[:, o:o + w] = a
    put("convr", rep(inp["ssm_conv_w"][l].reshape(-1)))
    mu = inp["rwkv_mu"][l]
    put("shr", rep(np.concatenate([mu[0], mu[1], mu[1]], 0)))
    return p


class MK:
    def __init__(self, nc, stop_after=None, dbg=False):
        self.nc = nc
        self.P = P = Prog(nc)
        self.dbg = dbg
        self.stop_after = stop_after
        ok = "ExternalOutput" if dbg else "Internal"
        self.xin = P.dram("xin", [S, DM], F32, kind="ExternalInput")
        self.cvec = P.dram("cvec", [128, 16], F32, kind="ExternalInput")
        self.ada_w = P.dram("ada_w", [DEPTH, DM, 3 * DM], F32, kind="ExternalInput")
        self.w_in = P.dram("w_in", [DEPTH, DM, NIN], F32, kind="ExternalInput")
        self.w_out = P.dram("w_out", [DEPTH, DM, DM], F32, kind="ExternalInput")
        self.wup = P.dram("wup", [DEPTH, 2, 64, 256], F32, kind="ExternalInput")
        self.aup = P.dram("aup", [DEPTH, 2, 64, 256], F32, kind="ExternalInput")
        self.pp = P.dram("pp", [DEPTH, 128, NPP], F32, kind="ExternalInput")
        self.pr = P.dram("pr", [DEPTH, 128, NPR], F32, kind="ExternalInput")
        self.pr2 = P.dram("pr2", [DEPTH, 128, NPR2], F32, kind="ExternalInput")
        self.cst = P.dram("cst", [128, NCS], F32, kind="ExternalInput")
        self.y = P.dram("y", [TLAT, DM], F32, kind="ExternalOutput")
        self.wbin = P.dram("wbin", [DEPTH, 128, 8, NIN], BF16)
        self.wbout = P.dram("wbout", [DEPTH, 128, 8, DM], BF16)
        self.hT = P.dram("hT", [128, 8, S], BF16, kind=ok)
        self.mix = P.dram("mixT", [DM, S], BF16, kind=ok)
        self.xres = P.dram("xres", [S, DM], F32, kind=ok)
        self.yf = P.dram("yfwd", [384, S], F32)
        self.psh = nc.alloc_psum_tensor("psum", [128, 8, 512], F32)
        self.bt = [Trk(ps=True) for _ in range(8)]
        self.c_ident = P.sb([128, 128], F32, "c_ident")
        self.c_identb = P.sb([128, 128], BF16, "c_identb")
        self.load_c(self.c_ident, "ident")
        P.copy("dve", self.c_identb[:], self.c_ident[:])

    def ps(self, b0, nb=1, dt=F32):
        ap = self.psh[:, b0:b0 + nb, :]
        if dt is not F32:
            ap = ap.bitcast(dt)
        return V(ap, tuple(self.bt[b0:b0 + nb]))

    def load_c(self, tile, name, q="sp", sub=None):
        o, w = CS[name]
        if sub is not None:
            o, w = o + sub[0], sub[1]
        self.P.dma(q, tile[:], self.cst.v("c", (slice(None), slice(o, o + w))))

    def phase_w(self):
        P = self.P
        st = [P.sb([128, 8, 512], F32, f"w_st{i}") for i in range(2)]
        sb = [P.sb([128, 8, 512], BF16, f"w_sb{i}") for i in range(2)]
        it = 0
        for l in range(DEPTH):
            for (src, dst, n) in ((self.w_in, self.wbin, NIN), (self.w_out, self.wbout, DM)):
                for c0 in range(0, n, 512):
                    cw = min(512, n - c0)
                    a, b = st[it % 2], sb[it % 2]
                    q = "sp" if it % 2 == 0 else "act"
                    P.dma(q, a[:, :, 0:cw], src.v(("w", l), fn=lambda ap, l=l, c0=c0, cw=cw:
                                                   ap[l].rearrange("(kc p) n -> p kc n", p=128)[:, :, c0:c0 + cw]))
                    P.copy("dve" if it % 2 == 0 else "pool", b[:, :, 0:cw], a[:, :, 0:cw])
                    P.dma(q, dst.v(("wb", l, c0), fn=lambda ap, l=l, c0=c0, cw=cw: ap[l][:, :, c0:c0 + cw]),
                          b[:, :, 0:cw])
                    it += 1

    def wb(self, l, c0, cw):
        t0 = (c0 // 512) * 512
        trk = []
        ap = self.wbin.h.ap()[l][:, :, c0:c0 + cw]
        for t in range(t0, c0 + cw, 512):
            trk.append(self.wbin.reg.setdefault(("wb", l, t), Trk()))
        return V(ap, tuple(trk))

    def wbo(self, l, c0, cw):
        t0 = (c0 // 512) * 512
        trk = []
        ap = self.wbout.h.ap()[l][:, :, c0:c0 + cw]
        for t in range(t0, c0 + cw, 512):
            trk.append(self.wbout.reg.setdefault(("wb", l, t), Trk()))
        return V(ap, tuple(trk))

    def phase_mod(self, l):
        P = self.P
        if not hasattr(self, "ppt"):
            self.ppt = P.sb([128, NPP], F32, "ppt")
            self.prt = P.sb([128, NPR], F32, "prt")
            self.cact = P.sb([128, 8, 2], F32, "cact")
            self.modT = P.sb([128, 24, 2], F32, "modT")
            self.s1T = P.sb([128, 8, 2], F32, "s1T")
            self.gate_bc = P.sb([128, 2, 1024], F32, "gate_bc")
            craw = P.sb([128, 16], F32, "craw")
            P.dma("sp", craw[:], self.cvec.v(0))
            P.act(self.cact[:].re("p k w -> p w k"), craw[:].re("p (w k) -> p w k", w=2), AF.Silu)
        mk_ = P.mark()
        self.crep = P.sb([128, 8, 2, 128], F32, "crep")
        self.aw = [P.sb([128, 8, 512], F32, f"aw{i}") for i in range(2)]
        P.copy("dve", self.crep[:], self.cact[:].m(lambda a: a.unsqueeze(3)).bc([128, 8, 2, 128]))
        P.dma("sp", self.ppt[:], self.pp.v(l, fn=lambda a: a[l]))
        P.dma("act", self.prt[:], self.pr.v(l, fn=lambda a: a[l]))
        for t in range(6):
            a = self.aw[t % 2]
            P.dma("sp" if t % 2 == 0 else "act", a[:],
                  self.ada_w.v(("aw", l), fn=lambda ap, t=t: ap[l].rearrange("(kc p) n -> p kc n", p=128)[:, :, t * 512:(t + 1) * 512]))
            pm = self.ps(0)
            for j in range(4):
                for kc in range(8):
                    P.mm(pm[:, 0, j * 2:(j + 1) * 2], a[:, kc, j * 128:(j + 1) * 128], self.cact[:, kc, :],
                         start=(kc == 0), stop=(kc == 7))
            P.copy("dve", self.modT[:, t * 4:(t + 1) * 4, :], pm[:, 0, 0:8].re("p (j w) -> p j w", w=2))
            if t >= 4:
                for w in range(2):
                    pg = self.ps(1 + w)
                    for kc in range(8):
                        P.mm(pg[:, 0, :], self.crep[:, kc, w, :], a[:, kc, :], start=(kc == 0), stop=(kc == 7))
                    o = PR["adabg"][0] + (t - 4) * 512
                    P.tt("dve", self.gate_bc[:, w, (t - 4) * 512:(t - 3) * 512], pg[:, 0, :], self.prt[:, o:o + 512], ALU.add)
        o = PP["adab"][0]
        P.tt("dve", self.modT[:], self.modT[:], self.ppt[:, o:o + 24].m(lambda a: a.unsqueeze(2)).bc([128, 24, 2]), ALU.add)
        o = PP["normw"][0]
        P.stt("dve", self.s1T[:], self.modT[:, 8:16, :], 1.0,
              self.ppt[:, o:o + 8].m(lambda a: a.unsqueeze(2)).bc([128, 8, 2]), ALU.add, ALU.mult)
        P.release(mk_)

    def phase_norm(self, l):
        P = self.P
        if not hasattr(self, "n_x"):
            self.n_x = [P.sb([128, DM], F32, f"n_x{i}") for i in range(3)]
            self.n_junk = P.sb([128, DM], F32, "n_junk")
            self.n_xb = [P.sb([128, DM], BF16, f"n_xb{i}") for i in range(3)]
            self.n_ss = [P.sb([128, 4], F32, f"n_ss{i}") for i in range(3)]
            self.n_h = [P.sb([128, 8, 512], BF16, f"n_h{i}") for i in range(2)]
            self.n_t = [P.sb([128, 8, 128], F32, f"n_t{i}") for i in range(3)]
        src = self.xin if l == 0 else self.xres
        for ti, (t0, tn) in enumerate(TILES):
            w = 1 if ti == 0 else 0
            hb = self.n_h[ti % 2]
            for bi in range(tn // 128):
                blk = (t0 // 128) + bi
                xt, xb, ss = self.n_x[blk % 3], self.n_xb[blk % 3], self.n_ss[blk % 3]
                P.dma("sp" if blk % 2 == 0 else "act", xt[:], src.v(("x", blk), (slice(blk * 128, blk * 128 + 128), slice(None))))
                P.act(self.n_junk[:], xt[:], AF.Square, accum_out=ss[:, 0:1])
                P.act(ss[:, 1:2], ss[:, 0:1], AF.Sqrt, bias=1e-6, scale=1.0 / DM)
                P.recip(ss[:, 2:3], ss[:, 1:2])
                P.ts("dve", xb[:], xt[:], ss[:, 2:3], None, ALU.mult)
                pt = self.ps(2 + blk % 3, 1, BF16)
                for kc in range(8):
                    P.tr(pt[:, 0, kc * 128:(kc + 1) * 128], xb[:, kc * 128:(kc + 1) * 128], self.c_identb[:])
                tmp = self.n_t[blk % 3]
                P.tt("dve", tmp[:], pt[:, 0, :].re("p (k t) -> p k t", k=8),
                     self.s1T[:, :, w:w + 1].bc([128, 8, 128]), ALU.mult)
                P.tt("pool", hb[:, :, bi * 128:(bi + 1) * 128], tmp[:],
                     self.modT[:, 0:8, w:w + 1].bc([128, 8, 128]), ALU.add)
            P.dma("sp", self.hT.v(("h", ti), (slice(None), slice(None), slice(t0, t0 + tn))), hb[:, :, 0:tn])

    def phase_out(self, l):
        P = self.P
        last = (l == DEPTH - 1)
        if not hasattr(self, "o_w"):
            self.o_w = P.sb([128, 8, DM], BF16, "o_w")
            self.o_m = [P.sb([128, 8, 128], BF16, f"o_m{i}") for i in range(3)]
            self.o_x = [P.sb([128, DM], F32, f"o_x{i}") for i in range(3)]
            self.o_t = [P.sb([128, DM], F32, f"o_t{i}") for i in range(3)]
            self.o_ss = [P.sb([128, 4], F32, f"o_ss{i}") for i in range(3)]
        P.dma("sp", self.o_w[:], self.wbo(l, 0, DM))
        src = self.xin if l == 0 else self.xres
        for blk in range(NBLK):
            if last and blk < 2:
                continue
            w = 1 if blk < 2 else 0
            m, xt, tt_, ss = self.o_m[blk % 3], self.o_x[blk % 3], self.o_t[blk % 3], self.o_ss[blk % 3]
            tsl = slice(blk * 128, blk * 128 + 128)
            P.dma("sp", m[:], self.mix.v(("m", blk), fn=lambda ap, tsl=tsl: ap.rearrange("(kc p) t -> p kc t", p=128)[:, :, tsl]))
            P.dma("act", xt[:], src.v(("x", blk), (tsl, slice(None))))
            for hf in range(2):
                po = self.ps(2 + 2 * (blk % 3) + hf)
                for kc in range(8):
                    P.mm(po[:, 0, :], m[:, kc, :], self.o_w[:, kc, hf * 512:(hf + 1) * 512], start=(kc == 0), stop=(kc == 7))
                P.tt("dve", tt_[:, hf * 512:(hf + 1) * 512], po[:, 0, :], self.gate_bc[:, w, hf * 512:(hf + 1) * 512], ALU.mult)
            P.tt("pool", tt_[:], tt_[:], xt[:], ALU.add)
            if not last:
                P.dma("sp", self.xres.v(("x", blk), (tsl, slice(None))), tt_[:])
            else:
                P.act(xt[:], tt_[:], AF.Square, accum_out=ss[:, 0:1])
                P.act(ss[:, 1:2], ss[:, 0:1], AF.Sqrt, bias=1e-6, scale=1.0 / DM)
                P.recip(ss[:, 2:3], ss[:, 1:2])
                o = PR["fnw"][0]
                P.stt("dve", xt[:], tt_[:], ss[:, 2:3], self.prt[:, o:o + DM], ALU.mult, ALU.mult)
                P.dma("sp", self.y.v(("y", blk), (slice(blk * 128 - CTX, blk * 128 - CTX + 128), slice(None))), xt[:])

    def load_h(self, tile, ti, q="sp"):
        t0, tn = TILES[ti]
        self.P.dma(q, tile[:, :, 0:tn], self.hT.v(("h", ti), (slice(None), slice(None), slice(t0, t0 + tn))))

    def phase_att(self, l):
        P = self.P
        ctx_out = (l < DEPTH - 1)
        import os
        sm = int(os.environ.get("MK_SET", "63"))
        wq = P.sb([128, 8, 384], BF16, "a_wq")
        wg = P.sb([128, 8, 384], BF16, "a_wg")
        wk = P.sb([128, 8, 128], BF16, "a_wk")
        wv = P.sb([128, 8, 128], BF16, "a_wv")
        if sm & 1:
            for j in range(3):
                for hh in range(2):
                    h = j + 3 * hh
                    P.dma("sp", wq[:, :, j * 128 + hh * 64: j * 128 + hh * 64 + 64], self.wb(l, C_Q + h * 64, 64))
                    P.dma("act", wg[:, :, j * 128 + hh * 64: j * 128 + hh * 64 + 64], self.wb(l, C_G + h * 64, 64))
            P.dma("sp", wk[:], self.wb(l, C_K, 128))
            P.dma("act", wv[:], self.wb(l, C_V, 128))
        cos = P.sb([128, TLAT], F32, "a_cos")
        sin = P.sb([128, TLAT], F32, "a_sin")
        if sm & 2:
            self.load_c(cos, "cos", "sp")
            self.load_c(sin, "sin", "act")
        pf = P.sb([128, 128], F32, "a_pf")
        permb = P.sb([128, 128], BF16, "a_permb")
        if sm & 4:
            self.load_c(pf, "perm")
            P.copy("dve", permb[:], pf[:])
        mband = P.sb([128, 3, 384], F32, "a_mband")
        o, w_ = CS["mband"]
        if sm & 8:
            P.dma("sp", mband[:].re("p a b -> p (a b)"), self.cst.v("c", (slice(None), slice(o, o + w_))))
        sink8 = P.sb([128, 6], F32, "a_sink8")
        o = PP["sink"][0]
        if sm & 16:
            P.ts("dve", sink8[:], self.ppt[:, o:o + 6], 8.0, None, ALU.mult)
        KB = P.sb([128, 4608], BF16, "a_KB")
        Vtm = P.sb([128, 36, 128], BF16, "a_V")
        if sm & 32:
            P.memset("pool", KB[:, 256:384], 0.0)
            P.memset("pool", KB[:, 4480:4608], 0.0)
            P.memset("pool", Vtm[:, 2, :], 0.0)
            P.memset("pool", Vtm[:, 35, :], 0.0)
        hb = [P.sb([128, 8, 512], BF16, f"a_h{i}") for i in range(2)]
        kf = P.sb([128, 512], F32, "a_kf")
        t1 = [P.sb([128, 512], F32, f"a_t1{i}") for i in range(2)]
        t2 = [P.sb([128, 512], F32, f"a_t2{i}") for i in range(2)]

        def kcol(tok):
            return tok if tok < CTX else tok + 128

        def proj(dst, wt, c0, cw, hbt, tn, lat_t0, bank, func=None):
            pp_ = self.ps(bank)
            for kc in range(8):
                P.mm(pp_[:, 0, 0:tn], wt[:, kc, c0:c0 + cw], hbt[:, kc, 0:tn], start=(kc == 0), stop=(kc == 7))
            if func is not None:
                P.act(dst, pp_[:, 0, 0:tn], func)
                return
            if lat_t0 is None:
                P.copy("act", dst, pp_[:, 0, 0:tn])
                return
            P.copy("act", kf[:, 0:tn], pp_[:, 0, 0:tn])
            pr_ = self.ps(2)
            P.mm(pr_[:, 0, 0:tn], pf[:], kf[:, 0:tn])
            a, b = t1[bank % 2], t2[bank % 2]
            P.tt("pool", a[:, 0:tn], kf[:, 0:tn], cos[:, lat_t0:lat_t0 + tn], ALU.mult)
            P.tt("dve", b[:, 0:tn], pr_[:, 0, 0:tn], sin[:, lat_t0:lat_t0 + tn], ALU.mult)
            P.tt("pool", dst, a[:, 0:tn], b[:, 0:tn], ALU.add)

        stage = int(os.environ.get("MK_ATT", "9"))
        if stage <= 0:
            return
        for ti, (t0, tn) in enumerate(TILES):
            h = hb[ti % 2]
            self.load_h(h, ti, "sp" if ti % 2 == 0 else "act")
            kc0 = kcol(t0)
            ma = int(os.environ.get("MK_A", "3"))
            if ti >= int(os.environ.get("MK_AT", "9")):
                break
            if ma & 1:
                proj(KB[:, kc0:kc0 + tn], wk, 0, 128, h, tn, None if ti == 0 else t0 - CTX, ti % 2)
            for bi in range(tn // 128):
                if not (ma & 2):
                    break
                vb = (t0 // 128 + bi)
                vb = vb if vb < 2 else vb + 1
                pv = self.ps(3 + bi % 2)
                for kc in range(8):
                    P.mm(pv[:, 0, 0:128], h[:, kc, bi * 128:(bi + 1) * 128], wv[:, kc, :], start=(kc == 0), stop=(kc == 7))
                P.copy("dve" if bi % 2 == 0 else "act", Vtm[:, vb, :], pv[:, 0, 0:128])

        import os
        stage = int(os.environ.get("MK_ATT", "9"))
        if stage <= 1:
            return
        qT = [P.sb([128, 3, 512], BF16, f"a_q{i}") for i in range(2)]
        gT = [P.sb([128, 3, 512], BF16, f"a_g{i}") for i in range(2)]
        om = [P.sb([128, 3, 512], BF16, f"a_om{i}") for i in range(2)]
        sc = [P.sb([128, 3, 641], F32, f"a_sc{i}") for i in range(2)]
        pe = [P.sb([128, 3, 641], F32, f"a_pe{i}") for i in range(2)]
        for kg in range(2):
            P.copy("dve", sc[kg][:, :, 640:641], sink8[:, kg * 3:kg * 3 + 3].m(lambda a: a.unsqueeze(2)))
        pn = [P.sb([128, 3, 640], BF16, f"a_pn{i}") for i in range(4)]
        nblk_att = 0
        pTs = [P.sb([128, 5, 128], BF16, f"a_pT{i}") for i in range(3)]
        st = [P.sb([128, 8, 3], F32, f"a_st{i}") for i in range(2)]
        it = 0
        npt = 0
        pending = None

        def rr(a, b):
            live = [g_ for g_ in (a, b) if g_ is not None]
            while live:
                for g_ in list(live):
                    if next(g_, "done") == "done":
                        live.remove(g_)
        for ti, (t0, tn) in enumerate(TILES):
            if ti == 0 and not ctx_out:
                continue
            h = hb[ti % 2]
            self.load_h(h, ti, "sp" if ti % 2 == 0 else "act")
            q_, g_, o_ = qT[ti % 2], gT[ti % 2], om[ti % 2]
            for j in range(3):
                proj(q_[:, j, 0:tn], wq, j * 128, 128, h, tn, None if ti == 0 else t0 - CTX, j % 2)
                proj(g_[:, j, 0:tn], wg, j * 128, 128, h, tn, None, (j + 1) % 2, func=AF.Silu)
            if stage <= 2:
                break
            for bi in range(tn // 128):
                if stage <= 5 and (bi > 0 or ti > 1):
                    break
                qs = slice(bi * 128, bi * 128 + 128)
                isctx = (ti == 0)
                n = (t0 - CTX) // 128 + bi if not isctx else None
                lo = 384 if isctx else 0
                po = self.ps(2)
                KG = [(kg, slice(kg * 64, kg * 64 + 64), sc[kg], pe[kg], pn[2 * (nblk_att % 2) + kg], st[kg]) for kg in range(2)]
                nblk_att += 1
                chunks = ([] if isctx else [0, 1, 2]) + [3, 4]
                def front(KG=KG, chunks=chunks, n=n, qs=qs, lo=lo, isctx=isctx, q_=q_):
                    for kg, ps_, s_, e_, n_, stt_ in KG:
                        for j in range(3):
                            pb = self.ps(3 + 2 * kg)
                            pc = self.ps(4 + 2 * kg)
                            if not isctx:
                                kb0 = 256 + n * 128
                                P.mm(pb[:, 0, 0:384], q_[ps_, j, qs], KB[ps_, kb0:kb0 + 384])
                                yield
                                var = 0 if n == 0 else (2 if n == 31 else 1)
                                P.tt("dve", s_[:, j, 0:384], pb[:, 0, 0:384], mband[:, var, :], ALU.add)
                                yield
                            P.mm(pc[:, 0, 0:256], q_[ps_, j, qs], KB[ps_, 0:256])
                            yield
                            P.copy("act", s_[:, j, 384:640], pc[:, 0, 0:256])
                            yield
                    for kg, ps_, s_, e_, n_, stt_ in KG:
                        P.reduce("dve", stt_[:, 0, :], s_[:, :, lo:641], ALU.max)
                        yield
                        P.ts("dve", stt_[:, 2, :], stt_[:, 0, :], -0.125, None, ALU.mult)
                        yield
                    for kg, ps_, s_, e_, n_, stt_ in KG:
                        for j in range(3):
                            P.act(e_[:, j, lo:641], s_[:, j, lo:641], AF.Exp, bias=stt_[:, 2, j:j + 1], scale=0.125,
                                  accum_out=stt_[:, 3, j:j + 1])
                            yield
                    for kg, ps_, s_, e_, n_, stt_ in KG:
                        P.recip(stt_[:, 7, :], stt_[:, 3, :])
                        yield
                        P.tt("dve" if kg == 0 else "pool", n_[:, :, lo:640], e_[:, :, lo:640],
                             stt_[:, 7, :].m(lambda a: a.unsqueeze(2)).bc([128, 3, 640 - lo]), ALU.mult)
                        yield
                def back(KG=KG, chunks=chunks, n=n, qs=qs, po=po, o_=o_, g_=g_):
                    nonlocal npt
                    for j in range(3):
                        for kg, ps_, s_, e_, n_, stt_ in KG:
                            pt = self.ps([7, 0, 1][npt % 3], 1, BF16)
                            for c in chunks:
                                P.tr(pt[:, 0, c * 128:(c + 1) * 128], n_[:, j, c * 128:(c + 1) * 128], self.c_identb[:])
                                yield
                            pts = pTs[npt % 3]
                            npt += 1
                            c0 = chunks[0]
                            P.copy("act" if npt % 2 == 0 else "dve", pts[:, c0:5, :], pt[:, 0, c0 * 128:640].re("p (c t) -> p c t", t=128))
                            yield
                            for ci, c in enumerate(chunks):
                                vb = (2 + n + c) if c < 3 else (c - 3)
                                P.mm(po[ps_, 0, j * 128:(j + 1) * 128], Vtm[:, vb, kg * 64:kg * 64 + 64], pts[:, c, :],
                                     start=(ci == 0), stop=(ci == len(chunks) - 1))
                                yield
                    P.tt("dve", o_[:, :, qs], po[:, 0, 0:384].re("p (j t) -> p j t", j=3), g_[:, :, qs], ALU.mult)
                    yield
                rr(front(), pending)
                pending = back()
            rr(None, pending)
            pending = None
            for j in range(3):
                for hh in range(2):
                    hd = j + 3 * hh
                    P.dma("sp" if hh == 0 else "act",
                          self.mix.v(("ma", ti, hd), (slice(hd * 64, hd * 64 + 64), slice(t0, t0 + tn))),
                          o_[hh * 64:hh * 64 + 64, j, 0:tn])

    def load_h_halo(self, tile, ti, q="sp"):
        P = self.P
        t0, tn = TILES[ti]
        lo, hi = t0 - 1, t0 + tn + 1
        if ti <= 1:
            P.memset("pool", tile[:, :, 0:1], 0.0)
            lo = t0
        if ti == 0 or ti == len(TILES) - 1:
            P.memset("pool", tile[:, :, tn + 1:tn + 2], 0.0)
            hi = t0 + tn
        P.dma(q, tile[:, :, lo - t0 + 1:hi - t0 + 1], self.hT.v(("h", ti), (slice(None), slice(None), slice(lo, hi))))

    def scaled_w(self, l, c0, ncol, rows, name, roff=0):
        P = self.P
        rows = [r[:, roff:roff + ncol] for r in rows]
        w3 = P.sb([128, 3, 8, ncol], BF16, name)
        mk_ = P.mark()
        half = ncol // 2
        stg = [P.sb([128, 8, half], F32, f"{name}_st{i}") for i in range(2)]
        for hf in range(2):
            st = stg[hf]
            P.dma("sp" if hf == 0 else "act", st[:],
                  self.w_in.v(("w", l), fn=lambda ap, hf=hf: ap[l].rearrange("(kc p) n -> p kc n", p=128)[:, :, c0 + hf * half:c0 + (hf + 1) * half]))
            for tap in range(3):
                P.tt("dve" if tap != 1 else "pool", w3[:, tap, :, hf * half:(hf + 1) * half], st[:],
                     rows[tap][:, hf * half:(hf + 1) * half].m(lambda a: a.unsqueeze(1)).bc([128, 8, half]), ALU.mult)
        P.release(mk_)
        return w3

    def phase_ssd(self, l):
        P = self.P
        ctx_out = (l < DEPTH - 1)
        xsT = P.sb([128, 3, S], F32, "s_xsT")
        BT = P.sb([128, 2, S], BF16, "s_BT")
        CT = P.sb([128, 2, S], BF16, "s_CT")
        dt_tm = P.sb([128, NBLK, 12], F32, "s_dt")
        dtA_tm = P.sb([128, NBLK, 12], F32, "s_dtA")
        aneg = P.sb([128, 12], F32, "s_aneg")
        o = PR["alog"][0]
        P.act(aneg[:], self.prt[:, o:o + 12], AF.Exp)
        P.ts("dve", aneg[:], aneg[:], -1.0, None, ALU.mult)
        mk1 = P.mark()
        o = PR2["convr"][0]
        crow = P.sb([128, 3 * 896], F32, "s_crow")
        P.dma("sp", crow[:], self.pr2.v(l, fn=lambda a: a[l][:, o:o + 3 * 896]))
        rows = [crow[:, k * 896:(k + 1) * 896] for k in range(3)]
        wx3 = self.scaled_w(l, C_XBC, 896, rows, "s_wx3")
        wdt = P.sb([128, 8, 12], BF16, "s_wdt")
        P.dma("sp", wdt[:], self.wb(l, C_DT, 12))
        hh = [P.sb([128, 8, 514], BF16, f"s_hh{i}") for i in range(2)]
        dtmp = [P.sb([128, 12], F32, f"s_dtmp{i}") for i in range(2)]
        ocb = PP["convb"][0]
        nb = 0
        for ti, (t0, tn) in enumerate(TILES):
            h = hh[ti % 2]
            self.load_h_halo(h, ti, "sp" if ti % 2 == 0 else "act")
            for c in range(7):
                pb = self.ps(c % 2)
                n = 0
                for tap in range(3):
                    for kc in range(8):
                        P.mm(pb[:, 0, 0:tn], wx3[:, tap, kc, c * 128:(c + 1) * 128], h[:, kc, tap:tap + tn],
                             start=(n == 0), stop=(n == 23))
                        n += 1
                if c < 3:
                    dst = xsT[:, c, t0:t0 + tn]
                elif c < 5:
                    dst = BT[:, c - 3, t0:t0 + tn]
                else:
                    dst = CT[:, c - 5, t0:t0 + tn]
                P.act(dst, pb[:, 0, 0:tn], AF.Silu, bias=self.ppt[:, ocb + c:ocb + c + 1])
            for bi in range(tn // 128):
                blk = t0 // 128 + bi
                pd = self.ps(2 + blk % 2)
                for kc in range(8):
                    P.mm(pd[:, 0, 0:12], h[:, kc, 1 + bi * 128:1 + (bi + 1) * 128], wdt[:, kc, :], start=(kc == 0), stop=(kc == 7))
                d_ = dtmp[blk % 2]
                o = PR["dtb"][0]
                P.tt("dve", d_[:], pd[:, 0, 0:12], self.prt[:, o:o + 12], ALU.add)
                P.act(d_[:], d_[:], AF.Exp)
                P.act(dt_tm[:, blk, :], d_[:], AF.Ln, bias=1.0)
                P.tt("dve", dtA_tm[:, blk, :], dt_tm[:, blk, :], aneg[:], ALU.mult)
        P.release(mk1)
        if int(os.environ.get("MK_SSD", "9")) <= 1:
            self.dbg_xsT = xsT
            return
        wz = P.sb([128, 8, 384], BF16, "s_wz")
        P.dma("sp", wz[:], self.wb(l, C_Z, 384))
        cf = {}
        for nme in ("snf", "snb", "trif", "trib", "ones", "hlo", "hhi"):
            cf[nme] = P.sb([128, 128], F32, "s_c_" + nme)
            self.load_c(cf[nme], nme, "act")
        St = P.sb([128, 6, 64], F32, "s_S")
        NB2 = 2
        xs_sb = [P.sb([128, 384], F32, f"s_xs{i}") for i in range(NB2)]
        Bt_sb = [P.sb([128, 2, 128], BF16, f"s_Bt{i}") for i in range(NB2)]
        Gs = [P.sb([128, 2, 128], F32, f"s_G{i}") for i in range(NB2)]
        cumc = [P.sb([128, 6], F32, f"s_cc{i}") for i in range(NB2)]
        dbc = [P.sb([128, 6, 128], F32, f"s_dbc{i}") for i in range(NB2)]
        Dm = [P.sb([128, 6, 128], F32, f"s_D{i}") for i in range(NB2)]
        Em = [P.sb([128, 6, 128], F32, f"s_E{i}") for i in range(NB2)]
        scT = [P.sb([128, 6, 128], BF16, f"s_sc{i}") for i in range(NB2)]
        Er = [P.sb([128, 6, 128], F32, f"s_Er{i}") for i in range(NB2)]
        Cd = [P.sb([128, 6, 128], F32, f"s_Cd{i}") for i in range(NB2)]
        xdt = [P.sb([128, 6, 64], BF16, f"s_xdt{i}") for i in range(NB2)]
        xdtt = [P.sb([128, 6, 64], BF16, f"s_xdtt{i}") for i in range(NB2)]
        ysb = [P.sb([128, 3, 128], F32, f"s_y{i}") for i in range(NB2)]
        yfl = [P.sb([128, 3, 128], F32, f"s_yf{i}") for i in range(NB2)]
        hz = [P.sb([128, 8, 128], BF16, f"s_hz{i}") for i in range(NB2)]
        zs = [P.sb([128, 3, 128], F32, f"s_z{i}") for i in range(NB2)]
        sq = [P.sb([128, 3, 128], F32, f"s_sq{i}") for i in range(NB2)]
        rs = [P.sb([128, 2, 128], F32, f"s_rs{i}") for i in range(NB2)]
        mo = [P.sb([128, 3, 128], BF16, f"s_mo{i}") for i in range(NB2)]
        osd, onw = PP["ssmd"][0], PP["ssmnw"][0]
        it = 0
        for d in range(2):
            order = list(range(NBLK)) if d == 0 else [1, 0] + list(range(NBLK - 1, 1, -1))
            tri = cf["trif"] if d == 0 else cf["trib"]
            sneg = cf["snf"] if d == 0 else cf["snb"]
            last = 127 if d == 0 else 0
            P.memset("dve", St[:], 0.0)
            pending = None
            for c in order:
                if d == 1 and c < 2 and not ctx_out:
                    pass
                k = it % NB2
                it += 1
                cs_ = slice(c * 128, (c + 1) * 128)
                pxs = self.ps(0)
                for j in range(3):
                    P.tr(pxs[:, 0, j * 128:(j + 1) * 128], xsT[:, j, cs_], self.c_ident[:])
                P.copy("act", xs_sb[k][:], pxs[:, 0, 0:384])
                pbt = self.ps(1, 1, BF16)
                for g in range(2):
                    P.tr(pbt[:, 0, g * 128:(g + 1) * 128], BT[:, g, cs_], self.c_identb[:])
                P.copy("dve", Bt_sb[k][:].re("p g n -> p (g n)"), pbt[:, 0, 0:256])
                pg = self.ps(2)
                for g in range(2):
                    P.mm(pg[:, 0, g * 128:(g + 1) * 128], BT[:, g, cs_], CT[:, g, cs_])
                P.copy("act", Gs[k][:].re("p g n -> p (g n)"), pg[:, 0, 0:256])
                dA = dtA_tm[:, c, d * 6:(d + 1) * 6]
                P.mm(pg[:, 0, 256:262], tri[:], dA)
                P.copy("dve", cumc[k][:], pg[:, 0, 256:262])
                P.copy("pool", dbc[k][:], dA.m(lambda a: a.unsqueeze(2)).bc([128, 6, 128]))
                pcr = self.ps(3, 2)
                for h in range(6):
                    P.mm(pcr[:, h // 4, (h % 4) * 128:(h % 4 + 1) * 128], dbc[k][:, h, :], tri[:])
                pcr_v = pcr.re("p b n -> p (b n)")[:, 0:768].re("p (h n) -> p h n", h=6)
                P.tt("dve", Dm[k][:], pcr_v, cumc[k][:].m(lambda a: a.unsqueeze(2)).bc([128, 6, 128]), ALU.subtract)
                P.act(Er[k][:], pcr_v, AF.Exp)
                P.tt("pool", Dm[k][:], Dm[k][:], sneg[:].m(lambda a: a.unsqueeze(1)).bc([128, 6, 128]), ALU.add)
                P.act(Em[k][:], Dm[k][:], AF.Exp)
                P.tt("pool", scT[k][:].re("p (g h) n -> p g h n", g=2), Em[k][:].re("p (g h) n -> p g h n", g=2),
                     Gs[k][:].m(lambda a: a.unsqueeze(2)).bc([128, 2, 3, 128]), ALU.mult)
                P.tt("dve", Cd[k][:].re("p (g h) n -> p g h n", g=2), Er[k][:].re("p (g h) n -> p g h n", g=2),
                     CT[:, :, cs_].m(lambda a: a.unsqueeze(2)).bc([128, 2, 3, 128]), ALU.mult)
                dtc = dt_tm[:, c, d * 6:(d + 1) * 6]
                P.tt("dve", xdt[k][:], xs_sb[k][:].re("p (h q) -> p h q", h=6),
                     dtc.m(lambda a: a.unsqueeze(2)).bc([128, 6, 64]), ALU.mult)
                P.tt("pool", xdtt[k][:], xdt[k][:], Em[k][:, :, last:last + 1].bc([128, 6, 64]), ALU.mult)
                def back(c=c, k=k, cs_=cs_, d=d, last=last):
                    py = self.ps(5)
                    for h in range(6):
                        o_ = py[(h % 2) * 64:(h % 2) * 64 + 64, 0, (h // 2) * 128:(h // 2 + 1) * 128]
                        P.mm(o_, xdt[k][:, h, :], scT[k][:, h, :], start=True, stop=False)
                        P.mm(o_, St[:, h, :], Cd[k][:, h, :], start=False, stop=True)
                    pcs = self.ps(6)
                    for h in range(6):
                        P.mm(pcs[:, 0, h * 64:(h + 1) * 64], Bt_sb[k][:, h // 3, :], xdtt[k][:, h, :])
                    P.tt("dve", St[:], St[:], Er[k][:, :, last:last + 1].bc([128, 6, 64]), ALU.mult)
                    P.tt("dve", St[:], St[:], pcs[:, 0, 0:384].re("p (h q) -> p h q", h=6), ALU.add)
                    if d == 0:
                        P.copy("act", ysb[k][:].re("p j n -> p (j n)"), py[:, 0, 0:384])
                        P.dma("sp", self.yf.v(("yf", c), fn=lambda ap, cs_=cs_: ap.rearrange("(j p) t -> p j t", p=128)[:, :, cs_]), ysb[k][:])
                        return
                    if c < 2 and not ctx_out:
                        return
                    P.dma("act", yfl[k][:], self.yf.v(("yf", c), fn=lambda ap, cs_=cs_: ap.rearrange("(j p) t -> p j t", p=128)[:, :, cs_]))
                    P.dma("sp", hz[k][:], self.hT.v(("h", "z"), (slice(None), slice(None), cs_)))
                    pz = self.ps(7)
                    for j in range(3):
                        for kc in range(8):
                            P.mm(pz[:, 0, j * 128:(j + 1) * 128], wz[:, kc, j * 128:(j + 1) * 128], hz[k][:, kc, :],
                                 start=(kc == 0), stop=(kc == 7))
                    P.act(zs[k][:].re("p j n -> p (j n)"), pz[:, 0, 0:384], AF.Silu)
                    y_ = ysb[k]
                    P.tt("dve", y_[:].re("p j n -> p (j n)"), py[:, 0, 0:384], yfl[k][:].re("p j n -> p (j n)"), ALU.add)
                    P.tt("pool", yfl[k][:], xsT[:, :, cs_], self.ppt[:, osd:osd + 3].m(lambda a: a.unsqueeze(2)).bc([128, 3, 128]), ALU.mult)
                    P.tt("pool", y_[:], y_[:], yfl[k][:], ALU.add)
                    P.tt("dve", y_[:], y_[:], zs[k][:], ALU.mult)
                    P.act(sq[k][:], y_[:], AF.Square)
                    pgs = self.ps(7)
                    P.mm(pgs[:, 0, 384:512].m(lambda a: a), cf["ones"][:], sq[k][:, 0, :], start=True, stop=False)
                    P.mm(pgs[:, 0, 384:512], cf["hlo"][:], sq[k][:, 1, :], start=False, stop=True)
                    pgs2 = self.ps(6)
                    P.mm(pgs2[:, 0, 384:512], cf["hhi"][:], sq[k][:, 1, :], start=True, stop=False)
                    P.mm(pgs2[:, 0, 384:512], cf["ones"][:], sq[k][:, 2, :], start=False, stop=True)
                    P.act(rs[k][:, 0, :], pgs[:, 0, 384:512], AF.Sqrt, bias=1e-5, scale=1.0 / 192)
                    P.act(rs[k][:, 1, :], pgs2[:, 0, 384:512], AF.Sqrt, bias=1e-5, scale=1.0 / 192)
                    P.recip(rs[k][:], rs[k][:])
                    m_ = mo[k]
                    P.stt("dve", m_[:, 0, :], y_[:, 0, :], self.ppt[:, onw:onw + 1], rs[k][:, 0, :], ALU.mult, ALU.mult)
                    P.stt("dve", m_[0:64, 1, :], y_[0:64, 1, :], self.ppt[0:64, onw + 1:onw + 2], rs[k][0:64, 0, :], ALU.mult, ALU.mult)
                    P.stt("dve", m_[64:128, 1, :], y_[64:128, 1, :], self.ppt[64:128, onw + 1:onw + 2], rs[k][64:128, 1, :], ALU.mult, ALU.mult)
                    P.stt("dve", m_[:, 2, :], y_[:, 2, :], self.ppt[:, onw + 2:onw + 3], rs[k][:, 1, :], ALU.mult, ALU.mult)
                    P.dma("sp", self.mix.v(("ms", c), fn=lambda ap, cs_=cs_: ap[640:1024].rearrange("(j p) t -> p j t", p=128)[:, :, cs_]), m_[:])
                if pending is not None:
                    pending()
                pending = back
            if pending is not None:
                pending()

    def phase_rwkv(self, l):
        P = self.P
        ctx_out = (l < DEPTH - 1)
        LWC = -0.6065306597126334
        cf = {}
        for nme in ("blk", "lowf", "lowb"):
            cf[nme] = P.sb([128, 128], F32, "r_c_" + nme)
            self.load_c(cf[nme], nme, "act")
        m4 = {nme: P.sb([128, 2, 256], F32, "r_m4_" + nme) for nme in ("rmf", "rmb")}
        mk0 = P.mark()
        for nme in ("rmf", "rmb"):
            t_ = P.sb([128, 256], F32, "r_c_" + nme)
            self.load_c(t_, nme, "act")
            P.copy("dve", m4[nme][:], t_[:].m(lambda a: a.unsqueeze(1)).bc([128, 2, 256]))
        P.release(mk0)
        ones64 = P.sb([128, 64], F32, "r_ones")
        P.memset("pool", ones64[:], 1.0)
        omka = P.sb([128, 4], F32, "r_omka")
        oka = PP["ka"][0]
        P.ts("dve", omka[:], self.ppt[:, oka:oka + 4], -1.0, 1.0, ALU.mult, ALU.add)
        wupf = P.sb([128, 256], F32, "r_wupf")
        aupf = P.sb([128, 256], F32, "r_aupf")
        wupb = P.sb([128, 256], BF16, "r_wupb")
        aupb = P.sb([128, 256], BF16, "r_aupb")
        for d in range(2):
            P.dma("sp", wupf[d * 64:(d + 1) * 64, :], self.wup.v(("u", l, d), fn=lambda ap, d=d: ap[l][d]))
            P.dma("act", aupf[d * 64:(d + 1) * 64, :], self.aup.v(("u", l, d), fn=lambda ap, d=d: ap[l][d]))
        P.copy("dve", wupb[:], wupf[:])
        P.copy("pool", aupb[:], aupf[:])
        wdT = P.sb([128, S], BF16, "r_wdT")
        adT = P.sb([128, S], BF16, "r_adT")
        mk1 = P.mark()
        rows = self.shift_rows(l)
        hh = [P.sb([128, 8, 514], BF16, f"r_hh{i}") for i in range(2)]
        wl3 = self.scaled_w(l, C_RW + 768, 256, rows, "r_wl3", roff=768)
        for ti, (t0, tn) in enumerate(TILES):
            h = hh[ti % 2]
            self.load_h_halo(h, ti, "sp" if ti % 2 == 0 else "act")
            for c in range(2):
                pb = self.ps(c)
                n = 0
                for tap in range(3):
                    for kc in range(8):
                        P.mm(pb[:, 0, 0:tn], wl3[:, tap, kc, c * 128:(c + 1) * 128], h[:, kc, tap:tap + tn],
                             start=(n == 0), stop=(n == 23))
                        n += 1
                if c == 0:
                    P.act(wdT[:, t0:t0 + tn], pb[:, 0, 0:tn], AF.Tanh)
                else:
                    P.copy("act", adT[:, t0:t0 + tn], pb[:, 0, 0:tn])
        P.release(mk1)
        RW = int(os.environ.get("MK_RW", "99"))
        if RW <= 1:
            return
        mk_hp = P.mark()
        for hp in range(2):
            rT = P.sb([128, S], F32, "r_rT")
            kT = P.sb([128, S], F32, "r_kT")
            vT = P.sb([128, S], F32, "r_vT")
            ysum = P.sb([128, S], F32, "r_ysum")
            mk2 = P.mark()
            rows = self.shift_rows(l)
            hh = [P.sb([128, 8, 514], BF16, f"r_hh{i}") for i in range(2)]
            w3 = [self.scaled_w(l, C_RW + j * 256 + hp * 128, 128, rows, f"r_w3{j}", roff=j * 256 + hp * 128) for j in range(3)]
            for ti, (t0, tn) in enumerate(TILES):
                h = hh[ti % 2]
                self.load_h_halo(h, ti, "sp" if ti % 2 == 0 else "act")
                for j, dst in enumerate((rT, kT, vT)):
                    pb = self.ps(j % 2)
                    n = 0
                    for tap in range(3):
                        for kc in range(8):
                            P.mm(pb[:, 0, 0:tn], w3[j][:, tap, kc, :], h[:, kc, tap:tap + tn], start=(n == 0), stop=(n == 23))
                            n += 1
                    P.copy("act", dst[:, t0:t0 + tn], pb[:, 0, 0:tn])
            P.release(mk2)
            if RW <= 2:
                return
            mk3 = P.mark()
            self.rwkv_scan(l, hp, rT, kT, vT, wdT, adT, ysum, wupb, aupb, cf, m4, ones64, omka, LWC)
            P.release(mk3)
            self.rwkv_finish(l, hp, rT, kT, vT, ysum, cf, ctx_out)
            P.release(mk_hp)

    def shift_rows(self, l):
        P = self.P
        o = PR2["shr"][0]
        sr = P.sb([128, 3 * 1024], F32, "r_shr")
        P.dma("sp", sr[:], self.pr2.v(l, fn=lambda a: a[l][:, o:o + 3 * 1024]))
        r0, r2, r1 = sr[:, 0:1024], sr[:, 1024:2048], sr[:, 2048:3072]
        P.tt("dve", r1, r0, r2, ALU.add)
        P.ts("dve", r1, r1, -1.0, 1.0, ALU.mult, ALU.add)
        return [r0, r1, r2]

    def rwkv_scan(self, l, hp, rT, kT, vT, wdT, adT, ysum, wupb, aupb, cf, m4, ones64, omka, LWC):
        P = self.P
        NB = 2
        f32t = lambda n, w=512: [P.sb([128, w], F32, f"r_{n}")] * NB
        sg, aa, cum = f32t("sg"), f32t("aa"), f32t("cum")
        lw = sg
        cumx = [P.sb([128, 8], F32, "r_tot")] * NB
        e1, e2, e3, e4 = f32t("e1"), f32t("e2"), f32t("e3"), f32t("e4")
        kkr, sqk, nrm, kd, bq = f32t("kkr"), f32t("sqk"), f32t("nrm"), f32t("kd"), f32t("bq")
        kk, tmpa = kkr, nrm
        WC = [P.sb([128, 8], F32, f"r_WC{i}") for i in range(NB)]
        arb = [P.sb([128, 4, 2, 128], F32, "r_arb")] * NB
        kb = [P.sb([128, 512], F32, "r_kb")] * NB
        bb = [P.sb([128, 512], F32, "r_bb")] * NB
        kt = [P.sb([128, 512], F32, "r_kt")] * NB
        btl = [P.sb([128, 512], F32, "r_btl")] * NB
        G4 = 4
        TM = [P.sb([128, 4, 128], F32, f"r_TM{i}") for i in range(G4)]
        AT = [P.sb([128, 2, 4, 128], F32, f"r_AT{i}") for i in range(G4)]
        X1 = [P.sb([128, 2, 128], F32, f"r_X1{i}") for i in range(G4)]
        XX = [P.sb([128, 2, 2, 128], F32, f"r_XX{i}") for i in range(G4)]
        Qf = [P.sb([128, 2, 128], F32, f"r_Qf{i}") for i in range(G4)]
        W1 = [P.sb([128, 128], F32, f"r_W1{i}") for i in range(G4)]
        AU = [P.sb([128, 256], F32, f"r_AU{i}") for i in range(G4)]
        Y0T = [P.sb([128, 128], F32, f"r_Y0T{i}") for i in range(G4)]
        RhT = [P.sb([128, 128], F32, f"r_RhT{i}") for i in range(G4)]
        MTt = [P.sb([128, 128], F32, "r_MTt")] * 2
        MT = [[P.sb([128, 128], F32, f"r_MT{i}_{c}") for c in range(2)] for i in range(G4)]
        Ns = [[P.sb([128, 64], F32, f"r_Ns{i}_{c}") for c in range(2)] for i in range(G4)]
        ytmp = [P.sb([128, 64], F32, f"r_yt{i}") for i in range(2)]
        Sseq = P.sb([128, 4, 64], F32, "r_Sseq")
        ow0, oa0, okk, oka = PP["w0"][0], PP["a0"][0], PP["kk"][0], PP["ka"][0]
        if os.environ.get("MK_MEM"):
            print("rwkv_scan arena used", P._aoff, "of", P.ARENA)
        nblk_done = 0
        for d in range(2):
            if d == 1 and hp == 0 and os.environ.get("MK_RWDBG"):
                if not hasattr(self, "dbgy"):
                    self.dbgy = P.dram("dbgy", [128, S], F32, kind="ExternalOutput")
                P.dma("sp", self.dbgy.v(0), ysum[:])
            col = d * 2 + hp
            mask4 = m4["rmf"] if d == 0 else m4["rmb"]
            low = cf["lowf"] if d == 0 else cf["lowb"]
            P.memset("dve", Sseq[:, 0, :], 0.0)
            si = 0
            tiles = list(range(len(TILES))) if d == 0 else [0] + list(range(len(TILES) - 1, 0, -1))
            def prepA(tix):
                ti = tiles[tix]
                t0, tn = TILES[ti]
                nch = tn // 64
                nbk = tn // 128
                k_ = tix % NB
                ts_ = slice(t0, t0 + tn)
                ds_ = slice(d * 64, (d + 1) * 64)
                pxw, pxa = self.ps(0), self.ps(1)
                P.mm(pxw[:, 0, 0:tn], wupb[ds_, hp * 128:(hp + 1) * 128], wdT[ds_, ts_])
                P.act(sg[k_][:, 0:tn], pxw[:, 0, 0:tn], AF.Sigmoid, bias=self.ppt[:, ow0 + col:ow0 + col + 1])
                P.mm(pxa[:, 0, 0:tn], aupb[ds_, hp * 128:(hp + 1) * 128], adT[ds_, ts_])
                P.act(aa[k_][:, 0:tn], pxa[:, 0, 0:tn], AF.Sigmoid, bias=self.ppt[:, oa0 + col:oa0 + col + 1])
                yield
                P.ts("dve", lw[k_][:, 0:tn], sg[k_][:, 0:tn], LWC, None, ALU.mult)
                yield
                for c in range(nch):
                    cs = slice(c * 64, (c + 1) * 64)
                    P.scan("dve", cum[k_][:, cs], ones64[:], lw[k_][:, cs], 0.0, ALU.mult, ALU.add)
                    yield
                c3 = lambda t_: t_[:, 0:tn].re("p (c n) -> p c n", n=64)
                P.act(WC[k_][:, 0:nch].m(lambda a: a.unsqueeze(2)), c3(cum[k_])[:, :, 63:64], AF.Exp)
                yield
                if d == 1:
                    P.copy("pool", cumx[k_][:, 0:nch].m(lambda a: a.unsqueeze(2)), c3(cum[k_])[:, :, 63:64])
                    yield
                    P.tt("dve", cum[k_][:, 0:tn], lw[k_][:, 0:tn], cum[k_][:, 0:tn], ALU.subtract)
                    yield
                    P.tt("dve", c3(cum[k_]), c3(cum[k_]), cumx[k_][:, 0:nch].m(lambda a: a.unsqueeze(2)).bc([128, nch, 64]), ALU.add)
                    yield
                    tot = None
                P.tt("pool", e4[k_][:, 0:tn], cum[k_][:, 0:tn], lw[k_][:, 0:tn], ALU.subtract)
                yield
                P.act(e1[k_][:, 0:tn], cum[k_][:, 0:tn], AF.Exp)
                yield
                P.act(e2[k_][:, 0:tn], cum[k_][:, 0:tn], AF.Exp, scale=-1.0)
                yield
                P.act(e3[k_][:, 0:tn], e4[k_][:, 0:tn], AF.Exp)
                yield
                P.tt("dve", c3(e4[k_]), c3(e2[k_]), WC[k_][:, 0:nch].m(lambda a: a.unsqueeze(2)).bc([128, nch, 64]), ALU.mult)
                yield
                P.ts("dve", kkr[k_][:, 0:tn], kT[:, ts_], self.ppt[:, okk + col:okk + col + 1], None, ALU.mult)
                yield
                P.act(sqk[k_][:, 0:tn], kkr[k_][:, 0:tn], AF.Square)
                yield
                pss = self.ps(1)
                P.mm(pss[:, 0, 0:tn], cf["blk"][:], sqk[k_][:, 0:tn])
                P.act(nrm[k_][:, 0:tn], pss[:, 0, 0:tn], AF.Sqrt)
                yield
                P.ts("dve", nrm[k_][:, 0:tn], nrm[k_][:, 0:tn], 1e-12, None, ALU.max)
                yield
                P.recip(nrm[k_][:, 0:tn], nrm[k_][:, 0:tn])
                yield
                P.tt("dve", kk[k_][:, 0:tn], kkr[k_][:, 0:tn], nrm[k_][:, 0:tn], ALU.mult)
                yield
                P.ts("pool", tmpa[k_][:, 0:tn], aa[k_][:, 0:tn], self.ppt[:, oka + col:oka + col + 1], omka[:, col:col + 1], ALU.mult, ALU.add)
                yield
                P.tt("pool", kd[k_][:, 0:tn], kT[:, ts_], tmpa[k_][:, 0:tn], ALU.mult)
                yield
                P.tt("dve", bq[k_][:, 0:tn], kk[k_][:, 0:tn], aa[k_][:, 0:tn], ALU.mult)
                yield
            def prepB(tix):
                ti = tiles[tix]
                t0, tn = TILES[ti]
                nch = tn // 64
                nbk = tn // 128
                k_ = tix % NB
                ts_ = slice(t0, t0 + tn)
                ds_ = slice(d * 64, (d + 1) * 64)
                c3 = lambda t_: t_[:, 0:tn].re("p (c n) -> p c n", n=64)
                b3 = lambda t_: t_[:, 0:tn].re("p (b n) -> p b n", n=128)
                P.ts("pool", sqk[k_][:, 0:tn], kk[k_][:, 0:tn], -1.0, None, ALU.mult)
                yield
                P.tt("dve", arb[k_][:, 0:nbk, 0, :], b3(sqk[k_]), b3(e3[k_]), ALU.mult)
                yield
                P.tt("pool", arb[k_][:, 0:nbk, 1, :], rT[:, ts_].re("p (b n) -> p b n", n=128), b3(e1[k_]), ALU.mult)
                yield
                P.tt("dve", kb[k_][:, 0:tn], kd[k_][:, 0:tn], e2[k_][:, 0:tn], ALU.mult)
                yield
                P.tt("pool", bb[k_][:, 0:tn], bq[k_][:, 0:tn], e2[k_][:, 0:tn], ALU.mult)
                yield
                P.tt("dve", kt[k_][:, 0:tn], kd[k_][:, 0:tn], e4[k_][:, 0:tn], ALU.mult)
                yield
                P.tt("pool", btl[k_][:, 0:tn], bq[k_][:, 0:tn], e4[k_][:, 0:tn], ALU.mult)
                yield
            def step(gen, n=10 ** 9):
                for _ in range(n):
                    if next(gen, 'done') == 'done':
                        return
            step(prepA(0))
            step(prepB(0))
            for tix, ti in enumerate(tiles):
                t0, tn = TILES[ti]
                nch = tn // 64
                nbk = tn // 128
                k_ = tix % NB
                ts_ = slice(t0, t0 + tn)
                ds_ = slice(d * 64, (d + 1) * 64)
                nxtA = prepA(tix + 1) if tix + 1 < len(tiles) else iter(())
                blocks = list(range(nbk)) if d == 0 else list(range(nbk - 1, -1, -1))
                G = len(blocks)
                idm = lambda t_: t_[:].m(lambda a: a.unsqueeze(1)).bc([128, 2, 128])
                ev = 0
                for g, bi in enumerate(blocks):
                    bs = slice(bi * 128, (bi + 1) * 128)
                    pt = self.ps(2 + g)
                    P.tr(pt[:, 0, 0:128], arb[k_][:, bi, 0, :], self.c_ident[:])
                    P.tr(pt[:, 0, 128:256], btl[k_][:, bs], self.c_ident[:])
                    P.tr(pt[:, 0, 256:384], kt[k_][:, bs], self.c_ident[:])
                    P.tr(pt[:, 0, 384:512], vT[:, t0 + bi * 128:t0 + (bi + 1) * 128], self.c_ident[:])
                for g, bi in enumerate(blocks):
                    P.copy("act" if g % 2 == 0 else "dve", TM[g][:].re("p a n -> p (a n)"), self.ps(2 + g)[:, 0, 0:512])
                px2 = self.ps(6, 2)
                for half in range(0, G, 2):
                    gs = list(range(half, min(half + 2, G)))
                    for g in gs:
                        bi = blocks[g]
                        bs = slice(bi * 128, (bi + 1) * 128)
                        pa = self.ps(2 + 2 * (g % 2), 2)
                        for h in range(2):
                            hs = slice(h * 64, (h + 1) * 64)
                            ar_ = arb[k_][hs, bi, :, :].re("p a n -> p (a n)")
                            P.mm(pa[:, h, 0:256], bb[k_][hs, bs], ar_)
                            P.mm(pa[:, h, 256:512], kb[k_][hs, bs], ar_)
                            P.mm(px2[:, h, g * 128:(g + 1) * 128], arb[k_][hs, bi, 0, :], bb[k_][hs, bs])
                    for g in gs:
                        pa = self.ps(2 + 2 * (g % 2), 2)
                        P.tt("dve", AT[g][:].re("p h a n -> p h (a n)"), pa,
                             mask4[:].re("p a n -> p (a n)").m(lambda a: a.unsqueeze(1)).bc([128, 2, 512]), ALU.mult)
                for g in range(G):
                    P.tt("dve", X1[g][:], px2[:, :, g * 128:(g + 1) * 128], idm(low), ALU.mult)
                    P.tt("pool", Qf[g][:], AT[g][:, :, 0, :], idm(self.c_ident), ALU.add)
                xk = [[X1[g][:, h, :] for h in range(2)] for g in range(G)]
                xtk = [[AT[g][:, h, 0, :] for h in range(2)] for g in range(G)]
                for lev in range(int(os.environ.get('MK_NLEV', '5'))):
                    for g in range(G):
                        pn = self.ps(2 + g)
                        for h in range(2):
                            P.mm(pn[:, 0, h * 256:h * 256 + 128], xtk[g][h], xk[g][h])
                            if lev < 4:
                                P.mm(pn[:, 0, h * 256 + 128:h * 256 + 256], xk[g][h], xtk[g][h])
                    for g in range(G):
                        P.copy("act", XX[g][:].re("p h a n -> p (h a n)"), self.ps(2 + g)[:, 0, :])
                        xk[g] = [XX[g][:, h, 0, :] for h in range(2)]
                        xtk[g] = [XX[g][:, h, 1, :] for h in range(2)]
                    for g in range(G):
                        pq = self.ps(6 + g // 2)
                        for h in range(2):
                            c0 = (g % 2) * 256 + h * 128
                            P.mm(pq[:, 0, c0:c0 + 128], xk[g][h], Qf[g][:, h, :])
                    for g in range(G):
                        pq = self.ps(6 + g // 2)
                        c0 = (g % 2) * 256
                        P.tt("dve", Qf[g][:], pq[:, 0, c0:c0 + 256].re("p (h n) -> p h n", h=2), Qf[g][:], ALU.add)
                    step(nxtA, 7)
                pw = self.ps(0)
                for g in range(G):
                    for h in range(2):
                        P.mm(pw[:, 0, g * 128 + h * 64:g * 128 + (h + 1) * 64], AT[g][:, h, 2, :], TM[g][:, 3, h * 64:(h + 1) * 64])
                for g in range(G):
                    P.copy("act", W1[g][:], pw[:, 0, g * 128:(g + 1) * 128])
                for g in range(G):
                    pau = self.ps(2 + g // 2)
                    c0 = (g % 2) * 256
                    for h in range(2):
                        P.mm(pau[:, 0, c0 + h * 64:c0 + (h + 1) * 64], Qf[g][:, h, :], TM[g][:, 0, h * 64:(h + 1) * 64])
                        P.mm(pau[:, 0, c0 + 128 + h * 64:c0 + 128 + (h + 1) * 64], Qf[g][:, h, :], W1[g][:, h * 64:(h + 1) * 64])
                for g in range(G):
                    pau = self.ps(2 + g // 2)
                    c0 = (g % 2) * 256
                    P.copy("act" if g % 2 == 0 else "dve", AU[g][:], pau[:, 0, c0:c0 + 256])
                for g, bi in enumerate(blocks):
                    py = self.ps(4 + g // 2)
                    c0 = (g % 2) * 256
                    for h in range(2):
                        hs = slice(h * 64, (h + 1) * 64)
                        P.mm(py[hs, 0, c0:c0 + 128], AU[g][:, 128 + h * 64:128 + (h + 1) * 64], AT[g][:, h, 1, :], start=True, stop=False)
                        P.mm(py[hs, 0, c0:c0 + 128], TM[g][:, 3, h * 64:(h + 1) * 64], AT[g][:, h, 3, :], start=False, stop=True)
                        P.mm(py[hs, 0, c0 + 128:c0 + 256], AU[g][:, h * 64:(h + 1) * 64], AT[g][:, h, 1, :])
                for g, bi in enumerate(blocks):
                    py = self.ps(4 + g // 2)
                    c0 = (g % 2) * 256
                    P.copy("act", Y0T[g][:], py[:, 0, c0:c0 + 128])
                    P.tt("dve", RhT[g][:], py[:, 0, c0 + 128:c0 + 256], arb[k_][:, bi, 1, :], ALU.add)
                step(nxtA)
                if tix + 1 < len(tiles):
                    step(prepB(tix + 1))
                for g, bi in enumerate(blocks):
                    for cc in range(2):
                        cs = slice(cc * 64, (cc + 1) * 64)
                        pmn = self.ps((6 if g < 2 else 2) + cc)
                        c0 = (g % 2) * 192
                        P.mm(pmn[:, 0, c0:c0 + 128], AU[g][cs, 0:128], TM[g][cs, 1, :])
                        for h in range(2):
                            hs = slice(h * 64, (h + 1) * 64)
                            P.mm(pmn[hs, 0, c0 + 128:c0 + 192], TM[g][cs, 1, hs], AU[g][cs, 128 + h * 64:128 + (h + 1) * 64], start=True, stop=False)
                            P.mm(pmn[hs, 0, c0 + 128:c0 + 192], TM[g][cs, 2, hs], TM[g][cs, 3, hs], start=False, stop=True)
                for g, bi in enumerate(blocks):
                    for cc in range(2):
                        pmn = self.ps((6 if g < 2 else 2) + cc)
                        c0 = (g % 2) * 192
                        mt_ = MTt[cc]
                        P.tt("dve", mt_[:], pmn[:, 0, c0:c0 + 128], cf["blk"][:], ALU.mult)
                        wcol = bi * 2 + cc
                        P.stt("dve", MT[g][cc][:], self.c_ident[:], WC[k_][:, wcol:wcol + 1], mt_[:], ALU.mult, ALU.add)
                        P.copy("act", Ns[g][cc][:], pmn[:, 0, c0 + 128:c0 + 192])
                for g, bi in enumerate(blocks if not os.environ.get('MK_NOSEQ') else []):
                    for cc in ([0, 1] if d == 0 else [1, 0]):
                        cs = slice(cc * 64, (cc + 1) * 64)
                        tok = slice(t0 + bi * 128 + cc * 64, t0 + bi * 128 + cc * 64 + 64)
                        pyh = [self.ps(4), self.ps(5)]
                        pss_ = self.ps(0)
                        for h in range(2):
                            hs = slice(h * 64, (h + 1) * 64)
                            P.mm(pyh[h][hs, 0, 0:64], Sseq[hs, si % 4, :], RhT[g][hs, cs])
                        P.mm(pss_[:, 0, 0:64], MT[g][cc][:], Sseq[:, si % 4, :])
                        P.tt("dve", Sseq[:, (si + 1) % 4, :], pss_[:, 0, 0:64], Ns[g][cc][:], ALU.add)
                        for h in range(2):
                            hs = slice(h * 64, (h + 1) * 64)
                            if d == 0:
                                P.tt("pool" if False else "dve", ysum[hs, tok], pyh[h][hs, 0, 0:64], Y0T[g][hs, cs], ALU.add)
                            else:
                                yt_ = ytmp[cc]
                                P.tt("dve", yt_[hs, :], pyh[h][hs, 0, 0:64], Y0T[g][hs, cs], ALU.add)
                                P.tt("pool", ysum[hs, tok], ysum[hs, tok], yt_[hs, :], ALU.add)
                        si += 1

    def rwkv_finish(self, l, hp, rT, kT, vT, ysum, cf, ctx_out):
        P = self.P
        wg = P.sb([128, 8, 128], BF16, "rf_wg")
        P.dma("sp", wg[:], self.wb(l, C_RG + hp * 128, 128))
        hb = [P.sb([128, 8, 512], BF16, f"rf_h{i}") for i in range(2)]
        f = lambda n: [P.sb([128, 512], F32, f"rf_{n}{i}") for i in range(2)]
        yc, sq, rstd, rk, gs = f("yc"), f("sq"), f("rstd"), f("rk"), f("gs")
        mo = [P.sb([128, 512], BF16, f"rf_mo{i}") for i in range(2)]
        olw, olb, ork = PP["lnw"][0] + hp, PP["lnb"][0] + hp, PP["rk"][0] + hp
        for ti, (t0, tn) in enumerate(TILES):
            if ti == 0 and not ctx_out:
                continue
            k_ = ti % 2
            ts_ = slice(t0, t0 + tn)
            self.load_h(hb[k_], ti, "sp" if k_ == 0 else "act")
            pg = self.ps(0)
            for kc in range(8):
                P.mm(pg[:, 0, 0:tn], wg[:, kc, :], hb[k_][:, kc, 0:tn], start=(kc == 0), stop=(kc == 7))
            P.act(gs[k_][:, 0:tn], pg[:, 0, 0:tn], AF.Silu)
            pm = self.ps(1)
            P.mm(pm[:, 0, 0:tn], cf["blk"][:], ysum[:, ts_])
            P.stt("dve", yc[k_][:, 0:tn], pm[:, 0, 0:tn], -1.0 / 64, ysum[:, ts_], ALU.mult, ALU.add)
            P.act(sq[k_][:, 0:tn], yc[k_][:, 0:tn], AF.Square)
            pv = self.ps(2)
            P.mm(pv[:, 0, 0:tn], cf["blk"][:], sq[k_][:, 0:tn])
            P.act(rstd[k_][:, 0:tn], pv[:, 0, 0:tn], AF.Sqrt, bias=64e-5, scale=1.0 / 64)
            P.recip(rstd[k_][:, 0:tn], rstd[k_][:, 0:tn])
            P.tt("dve", yc[k_][:, 0:tn], yc[k_][:, 0:tn], rstd[k_][:, 0:tn], ALU.mult)
            P.ts("dve", yc[k_][:, 0:tn], yc[k_][:, 0:tn], self.ppt[:, olw:olw + 1], self.ppt[:, olb:olb + 1], ALU.mult, ALU.add)
            P.stt("dve", rk[k_][:, 0:tn], rT[:, ts_], self.ppt[:, ork:ork + 1], kT[:, ts_], ALU.mult, ALU.mult)
            pb = self.ps(3)
            P.mm(pb[:, 0, 0:tn], cf["blk"][:], rk[k_][:, 0:tn])
            P.tt("dve", rk[k_][:, 0:tn], pb[:, 0, 0:tn], vT[:, ts_], ALU.mult)
            P.tt("pool", yc[k_][:, 0:tn], yc[k_][:, 0:tn], rk[k_][:, 0:tn], ALU.add)
            P.tt("dve", mo[k_][:, 0:tn], yc[k_][:, 0:tn], gs[k_][:, 0:tn], ALU.mult)
            r0_ = 384 + hp * 128
            P.dma("sp", self.mix.v(("mr", hp, ti), (slice(r0_, r0_ + 128), ts_)), mo[k_][:, 0:tn])

    def zero_mix(self, r0, r1):
        P = self.P
        z = P.sb([128, S], BF16, "zmix")
        P.memset("pool", z[:], 0.0)
        for kc in range(r0 // 128, r1 // 128):
            P.dma("sp", self.mix.v(("mz", kc), (slice(kc * 128, kc * 128 + 128), slice(None))), z[:])

    def build(self, layers=DEPTH):
        P = self.P
        base = P.mark()
        self.phase_w()
        P.release(base)
        for l in range(layers):
            self.phase_mod(l)
            keep = P.mark()
            self.phase_norm(l)
            P.release(keep)
            if self.stop_after == "norm":
                return
            if not os.environ.get("MK_SKIP_ATT"):
                self.phase_att(l)
                P.release(keep)
            if self.stop_after == "att":
                return
            if os.environ.get("MK_SKIP_RWKV"):
                self.zero_mix(384, 640)
            else:
                self.phase_rwkv(l)
            P.release(keep)
            if self.stop_after == "rwkv":
                return
            self.phase_ssd(l)
            P.release(keep)
            if self.stop_after == "ssd":
                return
            self.phase_out(l)
            P.release(keep)
            for a in ("n_x", "o_w"):
                if hasattr(self, a):
                    delattr(self, a)

def build_program(stop_after=None, dbg=False, layers=DEPTH):
    nc = bass.Bass("TRN2", target_bir_lowering=False)
    k = MK(nc, stop_after=stop_after, dbg=dbg)
    k.build(layers)
    k.P.emit()
    return nc, k


def prep_inputs(inp, b):
    inp = {k: np.asarray(v) for k, v in inp.items()}
    d = {}
    d["xin"] = np.ascontiguousarray(np.concatenate([inp["ctx"][b], inp["x"][b]], 0).astype(np.float32))
    d["cvec"] = np.ascontiguousarray(np.concatenate([fm(inp["c"][b]), fm(inp["c_ctx"])], 1))
    d["ada_w"] = np.ascontiguousarray(inp["ada_w"], np.float32)
    d["w_in"] = np.ascontiguousarray(inp["w_in"], np.float32)
    d["w_out"] = np.ascontiguousarray(inp["w_out"], np.float32)
    d["wup"] = np.ascontiguousarray(inp["rwkv_w_up"], np.float32)
    d["aup"] = np.ascontiguousarray(inp["rwkv_a_up"], np.float32)
    d["pp"] = np.stack([make_pp(inp, l) for l in range(DEPTH)])
    d["pr"] = np.stack([make_pr(inp, l) for l in range(DEPTH)])
    d["pr2"] = np.stack([make_pr2(inp, l) for l in range(DEPTH)])
    d["cst"] = make_consts()
    return d


def kernel(**inputs):
    nc, _ = build_program()
    maps = [prep_inputs(inputs, b % 4) for b in range(4)]
    in_maps = [maps[i % 4] for i in range(8)]
    res = run_bass_kernel_spmd(nc, in_maps, core_ids=list(range(8)))
    return np.stack([np.asarray(res.results[b]["y"], np.float32) for b in range(4)], 0)
```
